# Optimizing a Trainium2 kernel written in Bass

```python
import jax, jax.numpy as jnp
from jax import lax
import numpy as np

D_MODEL = 2048
BATCH = 1
SEQ = 16384
DEPTH = 2

N_MEM = 256
ROPE_THETA = 10000.0
BLOCK_Q = 128
NORM_EPS = 1e-5
MLA_HEADS = 8
MLA_Q_LORA = 512
MLA_KV_LORA = 512
MLA_NOPE = 128
MLA_ROPE = 64
MLA_V = 128
DSA_HEADS = 8
DSA_HEAD_DIM = 128
IDX_HEADS = 16
IDX_DIM = 64
IDX_TOPK_MAX = 256
GLA_HEADS = 4
GLA_DK = 128
GLA_DV = 256
GLA_GATE_RANK = 16
GLA_GATE_TAU = 16.0
GLA_CHUNK = 64
XA_HEADS = 4
XA_HEAD_DIM = 128
D_FF = 5632
N_BRANCH = 3
DEEPNORM_ALPHA = (2.0 * DEPTH) ** 0.25
DEEPNORM_BETA = (8.0 * DEPTH) ** -0.25

IN_SPLITS = (
    MLA_Q_LORA, MLA_KV_LORA, MLA_ROPE,
    DSA_HEADS * DSA_HEAD_DIM, DSA_HEADS * DSA_HEAD_DIM, DSA_HEADS * DSA_HEAD_DIM,
    IDX_HEADS * IDX_DIM, IDX_DIM, IDX_HEADS,
    GLA_HEADS * GLA_DK, GLA_HEADS * GLA_DK, GLA_HEADS * GLA_DV,
    GLA_GATE_RANK, GLA_HEADS * GLA_DV,
)
IN_WIDTH = sum(IN_SPLITS)

kernel_name = "hybrid_mla_dsa_gla_deepnorm_macaron"


def layer_norm(x, g, b):
    xf = x.astype(jnp.float32)
    mu = jnp.mean(xf, -1, keepdims=True)
    var = jnp.mean(jnp.square(xf - mu), -1, keepdims=True)
    y = (xf - mu) * lax.rsqrt(var + NORM_EPS) * g.astype(jnp.float32) + b.astype(jnp.float32)
    return y.astype(x.dtype)


def rms_norm(x, g):
    xf = x.astype(jnp.float32)
    y = xf * lax.rsqrt(jnp.mean(xf * xf, -1, keepdims=True) + NORM_EPS) * g.astype(jnp.float32)
    return y.astype(x.dtype)


def rope_tables(positions, dim):
    inv_freq = 1.0 / (ROPE_THETA ** (jnp.arange(0, dim, 2, dtype=jnp.float32) / dim))
    ang = positions.astype(jnp.float32)[..., None] * inv_freq
    return jnp.cos(ang), jnp.sin(ang)


def apply_rope(x, cos, sin):
    xf = x.astype(jnp.float32)
    x1, x2 = jnp.split(xf, 2, axis=-1)
    return jnp.concatenate([x1 * cos - x2 * sin, x2 * cos + x1 * sin], -1).astype(x.dtype)


def to_blocks(a):
    b, s = a.shape[:2]
    a = a.reshape((b, s // BLOCK_Q, BLOCK_Q) + a.shape[2:])
    return jnp.moveaxis(a, 1, 0)


def from_blocks(a):
    a = jnp.moveaxis(a, 0, 1)
    return a.reshape((a.shape[0], a.shape[1] * a.shape[2]) + a.shape[3:])


def swiglu(x, w_gate, w_up, w_down):
    return (jax.nn.silu(x @ w_gate) * (x @ w_up)) @ w_down


def mla_branch(c_q, c_kv, k_rope, q_norm, w_uq, kv_norm, w_ukv, cos64, sin64):
    b, s, _ = c_q.shape
    q = (rms_norm(c_q, q_norm) @ w_uq).reshape(b, s, MLA_HEADS, MLA_NOPE + MLA_ROPE)
    q_nope = q[..., :MLA_NOPE]
    q_rope = apply_rope(q[..., MLA_NOPE:], cos64[:, :, None], sin64[:, :, None])
    kv = (rms_norm(c_kv, kv_norm) @ w_ukv).reshape(b, s, MLA_HEADS, MLA_NOPE + MLA_V)
    k_nope, v = kv[..., :MLA_NOPE], kv[..., MLA_NOPE:]
    k_rope = apply_rope(k_rope, cos64, sin64)
    scale = (MLA_NOPE + MLA_ROPE) ** -0.5
    key_pos = jnp.arange(s)

    def block(args):
        qn, qr, start = args
        sc = (jnp.einsum('bqhd,bshd->bhqs', qn, k_nope, preferred_element_type=jnp.float32)
              + jnp.einsum('bqhd,bsd->bhqs', qr, k_rope, preferred_element_type=jnp.float32)) * scale
        q_pos = start + jnp.arange(BLOCK_Q)
        sc = jnp.where(key_pos[None, :] <= q_pos[:, None], sc, -jnp.inf)
        p = jax.nn.softmax(sc, axis=-1).astype(v.dtype)
        return jnp.einsum('bhqs,bshd->bqhd', p, v)

    starts = jnp.arange(s // BLOCK_Q, dtype=jnp.int32) * BLOCK_Q
    o = lax.map(block, (to_blocks(q_nope), to_blocks(q_rope), starts))
    return from_blocks(o).reshape(b, s, MLA_HEADS * MLA_V)


def dsa_branch(q, k, v, iq, ik, iw, cos128, sin128, cos64, sin64):
    b, s, _ = q.shape
    topk = min(IDX_TOPK_MAX, s // 4)
    q = apply_rope(q.reshape(b, s, DSA_HEADS, DSA_HEAD_DIM), cos128[:, :, None], sin128[:, :, None])
    k = apply_rope(k.reshape(b, s, DSA_HEADS, DSA_HEAD_DIM), cos128[:, :, None], sin128[:, :, None])
    v = v.reshape(b, s, DSA_HEADS, DSA_HEAD_DIM)
    iq = apply_rope(iq.reshape(b, s, IDX_HEADS, IDX_DIM), cos64[:, :, None], sin64[:, :, None])
    ik = apply_rope(ik, cos64, sin64)
    iw = iw.astype(jnp.float32) * IDX_HEADS ** -0.5
    idx_scale = IDX_DIM ** -0.5
    attn_scale = DSA_HEAD_DIM ** -0.5
    key_pos = jnp.arange(s)
    gather = jax.vmap(lambda tab, ids: tab[ids])

    def block(args):
        qb, iqb, iwb, start = args
        q_pos = start + jnp.arange(BLOCK_Q)
        causal = key_pos[None, :] <= q_pos[:, None]
        logits = jnp.einsum('bqhd,bsd->bqhs', iqb, ik, preferred_element_type=jnp.float32) * idx_scale
        score = jnp.einsum('bqhs,bqh->bqs', jax.nn.relu(logits), iwb)
        score = jnp.where(causal, score, -jnp.inf)
        _, sel = lax.top_k(score, topk)
        valid = sel <= q_pos[None, :, None]
        k_sel = gather(k, sel)
        v_sel = gather(v, sel)
        sc = jnp.einsum('bqhd,bqkhd->bhqk', qb, k_sel, preferred_element_type=jnp.float32) * attn_scale
        sc = jnp.where(valid[:, None], sc, -jnp.inf)
        p = jax.nn.softmax(sc, axis=-1).astype(v.dtype)
        return jnp.einsum('bhqk,bqkhd->bqhd', p, v_sel)

    starts = jnp.arange(s // BLOCK_Q, dtype=jnp.int32) * BLOCK_Q
    o = lax.map(block, (to_blocks(q), to_blocks(iq), to_blocks(iw), starts))
    return from_blocks(o).reshape(b, s, DSA_HEADS * DSA_HEAD_DIM)


def gla_branch(q, k, v, g_lr, r, w_gate2, b_gate, norm_g):
    b, s, _ = q.shape
    nc = s // GLA_CHUNK
    f32 = jnp.float32

    def heads(a, d):
        return a.reshape(b, nc, GLA_CHUNK, GLA_HEADS, d).transpose(1, 0, 3, 2, 4)

    log_a = jax.nn.log_sigmoid((g_lr @ w_gate2 + b_gate).astype(f32)) / GLA_GATE_TAU
    qh = heads(q.astype(f32) * GLA_DK ** -0.5, GLA_DK)
    kh = heads(k.astype(f32), GLA_DK)
    vh = heads(v.astype(f32), GLA_DV)
    gh = heads(log_a, GLA_DK)
    tri = jnp.tril(jnp.ones((GLA_CHUNK, GLA_CHUNK), dtype=bool))

    def step(state, inp):
        qc, kc, vc, gc = inp
        cum = jnp.cumsum(gc, axis=-2)
        diff = cum[:, :, :, None, :] - cum[:, :, None, :, :]
        decay = jnp.exp(jnp.where(tri[:, :, None], diff, -jnp.inf))
        attn = jnp.einsum('bhtk,bhsk,bhtsk->bhts', qc, kc, decay)
        o = attn @ vc + jnp.einsum('bhtk,bhkv->bhtv', qc * jnp.exp(cum), state)
        last = cum[:, :, -1:, :]
        state = (jnp.exp(last[:, :, 0, :, None]) * state
                 + jnp.einsum('bhsk,bhsv->bhkv', kc * jnp.exp(last - cum), vc))
        return state, o

    s0 = jnp.zeros((b, GLA_HEADS, GLA_DK, GLA_DV), f32)
    _, o = lax.scan(step, s0, (qh, kh, vh, gh))
    o = o.transpose(1, 0, 3, 2, 4)
    o = rms_norm(o, norm_g.reshape(GLA_HEADS, GLA_DV)).reshape(b, s, GLA_HEADS * GLA_DV)
    return (o * jax.nn.silu(r.astype(f32))).astype(r.dtype)


def hybrid_mixer(x, cos64, sin64, cos128, sin128, w_in,
                 mla_q_norm, mla_w_uq, mla_kv_norm, mla_w_ukv,
                 gla_w_gate2, gla_b_gate, gla_norm,
                 w_branch_mla, w_branch_dsa, w_branch_gla, w_merge, b_merge, w_out):
    points = [int(p) for p in np.cumsum(IN_SPLITS)[:-1]]
    (c_q, c_kv, k_rope, dq, dk, dv, iq, ik, iw,
     gq, gk, gv, g_lr, g_r) = jnp.split(x @ w_in, points, axis=-1)
    y_mla = mla_branch(c_q, c_kv, k_rope, mla_q_norm, mla_w_uq, mla_kv_norm, mla_w_ukv,
                       cos64, sin64) @ w_branch_mla
    y_dsa = dsa_branch(dq, dk, dv, iq, ik, iw, cos128, sin128, cos64, sin64) @ w_branch_dsa
    y_gla = gla_branch(gq, gk, gv, g_lr, g_r, gla_w_gate2, gla_b_gate, gla_norm) @ w_branch_gla
    g_mla, g_dsa, g_gla = jnp.split(jax.nn.sigmoid(x @ w_merge + b_merge), N_BRANCH, axis=-1)
    return (g_mla * y_mla + g_dsa * y_dsa + g_gla * y_gla) @ w_out


def memory_cross_attention(x, mem, w_q, w_kv, w_o):
    b, s, _ = x.shape
    m = mem.shape[1]
    q = (x @ w_q).reshape(b, s, XA_HEADS, XA_HEAD_DIM)
    kv = (mem @ w_kv).reshape(b, m, 2, XA_HEADS, XA_HEAD_DIM)
    k, v = kv[:, :, 0], kv[:, :, 1]
    sc = jnp.einsum('bqhd,bmhd->bhqm', q, k, preferred_element_type=jnp.float32) * XA_HEAD_DIM ** -0.5
    p = jax.nn.softmax(sc, axis=-1).astype(v.dtype)
    o = jnp.einsum('bhqm,bmhd->bqhd', p, v).reshape(b, s, XA_HEADS * XA_HEAD_DIM)
    return o @ w_o


def setup_inputs(seed: int = 0) -> dict:
    key = jax.random.key(seed)
    k = jax.random.split(key, 25)
    f32 = jnp.float32
    hq = XA_HEADS * XA_HEAD_DIM

    def dense(kk, shape, fan_in, scale=1.0):
        return jax.random.normal(kk, shape, f32) * (scale * fan_in ** -0.5)

    def gain(kk, shape):
        return 1.0 + 0.02 * jax.random.normal(kk, shape, f32)

    def bias(kk, shape):
        return 0.02 * jax.random.normal(kk, shape, f32)

    offset = jax.random.randint(k[2], (BATCH, 1), 0, 4096, dtype=jnp.int32)
    return {
        "x": jax.random.normal(k[0], (BATCH, SEQ, D_MODEL), f32),
        "mem": jax.random.normal(k[1], (BATCH, N_MEM, D_MODEL), f32),
        "positions": offset + jnp.arange(SEQ, dtype=jnp.int32)[None, :],
        "ln_gain": gain(k[3], (DEPTH, 4, D_MODEL)),
        "ln_bias": bias(k[4], (DEPTH, 4, D_MODEL)),
        "ffn_w_gate": dense(k[5], (DEPTH, 2, D_MODEL, D_FF), D_MODEL),
        "ffn_w_up": dense(k[6], (DEPTH, 2, D_MODEL, D_FF), D_MODEL),
        "ffn_w_down": dense(k[7], (DEPTH, 2, D_FF, D_MODEL), D_FF, DEEPNORM_BETA),
        "w_in": dense(k[8], (DEPTH, D_MODEL, IN_WIDTH), D_MODEL),
        "mla_q_norm": gain(k[9], (DEPTH, MLA_Q_LORA)),
        "mla_w_uq": dense(k[10], (DEPTH, MLA_Q_LORA, MLA_HEADS * (MLA_NOPE + MLA_ROPE)), MLA_Q_LORA),
        "mla_kv_norm": gain(k[11], (DEPTH, MLA_KV_LORA)),
        "mla_w_ukv": dense(k[12], (DEPTH, MLA_KV_LORA, MLA_HEADS * (MLA_NOPE + MLA_V)), MLA_KV_LORA),
        "gla_w_gate2": dense(k[13], (DEPTH, GLA_GATE_RANK, GLA_HEADS * GLA_DK), GLA_GATE_RANK),
        "gla_b_gate": bias(k[14], (DEPTH, GLA_HEADS * GLA_DK)),
        "gla_norm": gain(k[15], (DEPTH, GLA_HEADS * GLA_DV)),
        "w_branch_mla": dense(k[16], (DEPTH, MLA_HEADS * MLA_V, D_MODEL), MLA_HEADS * MLA_V, DEEPNORM_BETA),
        "w_branch_dsa": dense(k[17], (DEPTH, DSA_HEADS * DSA_HEAD_DIM, D_MODEL), DSA_HEADS * DSA_HEAD_DIM, DEEPNORM_BETA),
        "w_branch_gla": dense(k[18], (DEPTH, GLA_HEADS * GLA_DV, D_MODEL), GLA_HEADS * GLA_DV, DEEPNORM_BETA),
        "w_merge": dense(k[19], (DEPTH, D_MODEL, N_BRANCH * D_MODEL), D_MODEL),
        "b_merge": bias(k[20], (DEPTH, N_BRANCH * D_MODEL)),
        "w_out": dense(k[21], (DEPTH, D_MODEL, D_MODEL), D_MODEL, DEEPNORM_BETA),
        "xa_w_q": dense(k[22], (DEPTH, D_MODEL, hq), D_MODEL),
        "xa_w_kv": dense(k[23], (DEPTH, D_MODEL, 2 * hq), D_MODEL),
        "xa_w_o": dense(k[24], (DEPTH, hq, D_MODEL), hq, DEEPNORM_BETA),
    }


def reference(x, mem, positions, ln_gain, ln_bias, ffn_w_gate, ffn_w_up, ffn_w_down, w_in,
              mla_q_norm, mla_w_uq, mla_kv_norm, mla_w_ukv, gla_w_gate2, gla_b_gate, gla_norm,
              w_branch_mla, w_branch_dsa, w_branch_gla, w_merge, b_merge, w_out,
              xa_w_q, xa_w_kv, xa_w_o):
    cos64, sin64 = rope_tables(positions, MLA_ROPE)
    cos128, sin128 = rope_tables(positions, DSA_HEAD_DIM)
    alpha = DEEPNORM_ALPHA
    for l in range(DEPTH):
        x = layer_norm(alpha * x + 0.5 * swiglu(x, ffn_w_gate[l, 0], ffn_w_up[l, 0], ffn_w_down[l, 0]),
                       ln_gain[l, 0], ln_bias[l, 0])
        h = hybrid_mixer(x, cos64, sin64, cos128, sin128, w_in[l],
                         mla_q_norm[l], mla_w_uq[l], mla_kv_norm[l], mla_w_ukv[l],
                         gla_w_gate2[l], gla_b_gate[l], gla_norm[l],
                         w_branch_mla[l], w_branch_dsa[l], w_branch_gla[l],
                         w_merge[l], b_merge[l], w_out[l])
        x = layer_norm(alpha * x + h, ln_gain[l, 1], ln_bias[l, 1])
        x = layer_norm(alpha * x + memory_cross_attention(x, mem, xa_w_q[l], xa_w_kv[l], xa_w_o[l]),
                       ln_gain[l, 2], ln_bias[l, 2])
        x = layer_norm(alpha * x + 0.5 * swiglu(x, ffn_w_gate[l, 1], ffn_w_up[l, 1], ffn_w_down[l, 1]),
                       ln_gain[l, 3], ln_bias[l, 3])
    return x
```

```python
import contextlib
import math
import numpy as np
import ml_dtypes
import concourse.bass as bass
import concourse.mybir as mybir
from concourse.bass_utils import run_bass_kernel_spmd

F32 = mybir.dt.float32
BF16 = mybir.dt.bfloat16
I32 = mybir.dt.int32
AF = mybir.ActivationFunctionType
ALU = mybir.AluOpType
NPBF = ml_dtypes.bfloat16

NCORES = 8
SEQ = 16384
D = 2048
DFF = 5632
TL = SEQ // NCORES
NT = TL // 128
BLK = 512
NB = TL // BLK
ALPHA = 4.0 ** 0.25
EPS = 1e-5
NEG = -30000.0
IN_W = 8352

ENG_NAMES = ["pe", "act", "dve", "pool", "sp"]


class Reg:
    __slots__ = ("w", "r")

    def __init__(self):
        self.w = None
        self.r = {}


class Buf:
    _uid = [0]

    def __init__(self, kb, name, shape, dt, psum=False):
        Buf._uid[0] += 1
        name = f"{name}_{Buf._uid[0]}"
        if psum:
            self.t = kb.scopes[-1].enter_context(kb.nc.psum_tensor(name, shape, dt))
            self.psum = True
        else:
            self.psum = False
            self.t = kb.scopes[-1].enter_context(kb.nc.sbuf_tensor(name, shape, dt))
        self.r = Reg()

    def __getitem__(self, idx):
        return self.t[idx]


class View:
    def __init__(self, ap):
        self.ap = ap
        self.r = Reg()

    def __getitem__(self, idx):
        return self.ap[idx]


class KB:
    def __init__(self, nc):
        self.nc = nc
        self.es = contextlib.ExitStack()
        self.scopes = [self.es]
        engs = [nc.tensor, nc.scalar, nc.vector, nc.gpsimd, nc.sync]
        self.eng = dict(zip(ENG_NAMES, engs))
        self.idx = {n: i for i, n in enumerate(ENG_NAMES)}
        self.sems = [self.es.enter_context(nc.semaphore("s_" + n)) for n in ENG_NAMES]
        self.cnt = [0] * len(ENG_NAMES)
        self.waited = [dict() for _ in ENG_NAMES]
        self.dpool = {}
        for q, n in (("sp", 24), ("pool", 16), ("act", 8)):
            lst = []
            for i in range(n):
                self.sems.append(self.es.enter_context(nc.semaphore(f"d_{q}{i}")))
                self.cnt.append(0)
                lst.append(len(self.sems) - 1)
            self.dpool[q] = [lst, 0]
        self.ninstr = 0

    def _wait(self, ei, deps):
        w = self.waited[ei]
        for (si, v) in deps.items():
            if si == ei and ei == 0:
                continue
            if w.get(si, 0) < v:
                self.eng[ENG_NAMES[ei]].wait_ge(self.sems[si], v)
                w[si] = v

    @staticmethod
    def _deps(reads, writes):
        deps = {}

        def add(t):
            if t is not None and deps.get(t[0], 0) < t[1]:
                deps[t[0]] = t[1]
        for r in reads:
            add(r.w)
        for r in writes:
            add(r.w)
            for si, v in r.r.items():
                add((si, v))
        return deps

    @staticmethod
    def _mark(t, reads, writes):
        for r in reads:
            if r.r.get(t[0], 0) < t[1]:
                r.r[t[0]] = t[1]
        for r in writes:
            r.w = t
            r.r = {}

    def op(self, eng, fn, reads=(), writes=()):
        if self.ninstr >= getattr(self, "maxops", 1 << 60):
            return None
        ei = self.idx[eng]
        writes = list(writes) + [x for x in reads if isinstance(x, Buf) and x.psum]
        reads = [x for x in reads if not (isinstance(x, Buf) and x.psum)]
        reads = [x if isinstance(x, Reg) else x.r for x in reads]
        writes = [x if isinstance(x, Reg) else x.r for x in writes]
        self._wait(ei, self._deps(reads, writes))
        ins = fn(self.eng[eng])
        self.cnt[ei] += 1
        ins.then_inc(self.sems[ei], 1)
        self._mark((ei, self.cnt[ei]), reads, writes)
        self.ninstr += 1
        return ins

    def dma(self, q, out, in_, reads=(), writes=()):
        if self.ninstr >= getattr(self, "maxops", 1 << 60):
            return None
        ei = self.idx[q]
        reads = [x if isinstance(x, Reg) else x.r for x in reads]
        writes = [x if isinstance(x, Reg) else x.r for x in writes]
        lst, nxt = self.dpool[q]
        si = lst[nxt]
        self.dpool[q][1] = (nxt + 1) % len(lst)
        deps = self._deps(reads, writes)
        if self.cnt[si] > 0 and deps.get(si, 0) < self.cnt[si]:
            deps[si] = self.cnt[si]
        self._wait(ei, deps)
        self.cnt[si] += 16
        self.eng[q].dma_start(out=out, in_=in_).then_inc(self.sems[si], 16)
        self._mark((si, self.cnt[si]), reads, writes)
        self.ninstr += 1

    def push(self):
        self.barrier()
        self.scopes.append(contextlib.ExitStack())

    def pop(self):
        self.barrier()
        self.scopes.pop().close()

    def barrier(self):
        for ei in range(len(ENG_NAMES)):
            deps = {si: v for si, v in enumerate(self.cnt) if v > 0 and si != ei}
            self._wait(ei, deps)

    def finish(self):
        deps = {si: v for si, v in enumerate(self.cnt) if v > 0 and si != self.idx["sp"]}
        self._wait(self.idx["sp"], deps)
        self.es.close()


C_ID, C_ONES, C_TRI, C_RTRI, C_MASKF, C_P128, C_P64, C_CH, C_F64, C_F128, C_EI, C_EPS, C_END = (
    0, 128, 256, 384, 512, 640, 768, 896, 898, 899, 900, 908, 909)


def make_consts(core):
    c = np.zeros((128, C_END), np.float32)
    c[:, C_ID:C_ID + 128] = np.eye(128)
    c[:, C_ONES:C_ONES + 128] = 1.0
    s = np.arange(128)[:, None]
    t = np.arange(128)[None, :]
    same = (s // 64) == (t // 64)
    c[:, C_TRI:C_TRI + 128] = (same & (s <= t))
    c[:, C_RTRI:C_RTRI + 128] = (same & (s > t))
    c[:, C_MASKF:C_MASKF + 128] = (same & (s <= t))
    for half, col in ((64, C_P128), (32, C_P64)):
        p = np.zeros((128, 128), np.float32)
        for m in range(128):
            if (m % (2 * half)) < half:
                p[m + half, m] = -1.0
            else:
                p[m - half, m] = 1.0
        c[:, col:col + 128] = p
    c[:, C_CH] = (np.arange(128) < 64)
    c[:, C_CH + 1] = (np.arange(128) >= 64)
    f64 = (1.0 / (np.float32(10000.0) ** (np.arange(0, 64, 2, dtype=np.float32) / np.float32(64)))).astype(np.float32)
    f128 = (1.0 / (np.float32(10000.0) ** (np.arange(0, 128, 2, dtype=np.float32) / np.float32(128)))).astype(np.float32)
    c[:, C_F64] = f64[np.arange(128) % 32]
    c[:, C_F128] = f128[np.arange(128) % 64]
    c[:, C_EI + core] = 1.0
    c[:, C_EPS] = EPS
    return c


def make_cmask(core):
    m = np.zeros((128, 1024), np.float32)
    q = np.arange(128)[:, None]
    for i in range(8):
        blk = m[:, i * 128:(i + 1) * 128]
        if i > core:
            blk[:] = NEG
        elif i == core:
            blk[np.arange(128)[None, :] > q] = NEG
    return m


class Prog:
    def __init__(self):
        self.nc = bass.Bass("TRN2", target_bir_lowering=False)
        self.kb = KB(self.nc)
        self.in_names = []
        self.out_names = []
        self.dreg = {}
        kb = self.kb
        self.consts_d = self.inp("consts", [128, C_END], F32)
        self.consts = Buf(kb, "consts_sb", [128, C_END], F32)
        kb.dma("sp", self.consts[:, :], self.consts_d[:, :], writes=[self.consts])
        self.identb = Buf(kb, "identb", [128, 128], BF16)
        kb.op("dve", lambda e: e.tensor_copy(out=self.identb[:, :], in_=self.consts[:, C_ID:C_ID + 128]),
              reads=[self.consts], writes=[self.identb])
        self.pb = [Buf(kb, f"pb{i}", [128, 512], F32, psum=True) for i in range(6)]
        self.pt = [Buf(kb, f"pt{i}", [128, 1024], BF16, psum=True) for i in range(2)]
        self.ws = None
        self.wsn = 0
        self.pbn = 0
        self.uid = 0

    def _dt(self, name, shape, dt, kind):
        t = self.nc.dram_tensor(name, list(shape), dt, kind=kind)
        ap = t.ap()
        self.dreg[name] = Reg()
        return ap

    def inp(self, name, shape, dt):
        self.in_names.append(name)
        return self._dt(name, shape, dt, "ExternalInput")

    def out(self, name, shape, dt):
        self.out_names.append(name)
        return self._dt(name, shape, dt, "ExternalOutput")

    def scr(self, name, shape, dt):
        return self._dt(name, shape, dt, "Internal")

    def R(self, name):
        return self.dreg[name]

    def cc(self, col, n=128, rows=128):
        return self.consts[0:rows, col:col + n]

    def load_w(self, w_ap, wname, k0, kc, c0, n):
        slot = self.ws[self.wsn]
        self.wsn = (self.wsn + 1) % len(self.ws)
        assert kc * n <= 8192
        view = slot[:, 0:kc * n].rearrange("p (k n) -> p k n", n=n)
        src = w_ap[k0:k0 + kc * 128, c0:c0 + n].rearrange("(k p) n -> p k n", p=128)
        self.kb.dma("pool", view, src, reads=[self.R(wname)], writes=[slot])
        return slot, view

    def bank(self):
        b = self.pb[self.pbn]
        self.pbn = (self.pbn + 1) % 4
        return b


def mm(pg, out_ap, lhsT, rhs, start, stop, reads, wbank):
    pg.kb.op("pe", lambda e: e.matmul(out_ap, lhsT=lhsT, rhs=rhs, start=start, stop=stop),
             reads=reads, writes=[wbank])


def emit_transpose_out(pg, src_buf, src_ap, ncol_chunks, xb, xts, dst_ap, dst_name, q="sp"):
    kb = pg.kb
    kb.op("act", lambda e: e.activation(out=xb[:, 0:ncol_chunks * 128], in_=src_ap, func=AF.Copy),
          reads=[src_buf], writes=[xb])
    for half in range((ncol_chunks + 7) // 8):
        n = min(8, ncol_chunks - half * 8)
        ptb = pg.pt[half % 2]
        for k in range(n):
            kc = half * 8 + k
            kb.op("pe", lambda e, kc=kc, k=k, ptb=ptb: e.transpose(out=ptb[:, k * 128:(k + 1) * 128],
                                                                    in_=xb[:, kc * 128:(kc + 1) * 128],
                                                                    identity=pg.identb[:, :]),
                  reads=[xb, pg.identb], writes=[ptb])
        kb.op("dve", lambda e, ptb=ptb, half=half, n=n: e.tensor_copy(
            out=xts[:, half * 1024:half * 1024 + n * 128], in_=ptb[:, 0:n * 128]),
            reads=[ptb], writes=[xts])
    kb.dma(q, dst_ap.rearrange("(k p) t -> p k t", p=128),
           xts[:, 0:ncol_chunks * 128].rearrange("p (k t) -> p k t", t=128),
           reads=[xts], writes=[pg.R(dst_name)])


def emit_ln(pg, vbuf, vap, gbc, bbc, st, x_out_ap, x_out_name, xT_out_ap, xT_out_name, xb, xts):
    kb = pg.kb
    stats, mv, rstd = st
    for c in range(4):
        kb.op("dve", lambda e, c=c: e.bn_stats(out=stats[:, c * 6:(c + 1) * 6], in_=vap[:, c * 512:(c + 1) * 512]),
              reads=[vbuf], writes=[stats])
    kb.op("dve", lambda e: e.bn_aggr(out=mv[:, 0:2], in_=stats[:, 0:24]), reads=[stats], writes=[mv])
    kb.op("act", lambda e: e.activation(out=rstd[:, 0:1], in_=mv[:, 1:2], func=AF.Ln, bias=pg.cc(C_EPS, 1), scale=1.0),
          reads=[mv, pg.consts], writes=[rstd])
    kb.op("act", lambda e: e.activation(out=rstd[:, 1:2], in_=rstd[:, 0:1], func=AF.Exp, scale=-0.5),
          reads=[rstd], writes=[rstd])
    kb.op("dve", lambda e: e.tensor_scalar(out=vap, in0=vap, scalar1=mv[:, 0:1], scalar2=rstd[:, 1:2],
                                           op0=ALU.subtract, op1=ALU.mult),
          reads=[vbuf, mv, rstd], writes=[vbuf])
    kb.op("dve", lambda e: e.tensor_tensor(out=vap, in0=vap, in1=gbc[:, :], op=ALU.mult), reads=[vbuf, gbc], writes=[vbuf])
    kb.op("dve", lambda e: e.tensor_tensor(out=vap, in0=vap, in1=bbc[:, :], op=ALU.add), reads=[vbuf, bbc], writes=[vbuf])
    kb.dma("sp", x_out_ap, vap, reads=[vbuf], writes=[pg.R(x_out_name)])
    emit_transpose_out(pg, vbuf, vap, 16, xb, xts, xT_out_ap, xT_out_name)


class Work:
    def __init__(self, pg):
        kb = pg.kb
        pg.uid += 1
        u = str(pg.uid)
        pg.ws = [Buf(kb, f"ws{i}_" + u, [128, 8192], BF16) for i in range(3)]
        self.v = [Buf(kb, f"v{i}", [128, D], F32) for i in range(4)]
        self.xT = Buf(kb, "xTblk", [128, 16 * BLK], BF16)
        self.hT = Buf(kb, "hT", [128, 22 * BLK], BF16)
        self.gbc = Buf(kb, "gbc", [128, D], F32)
        self.bbc = Buf(kb, "bbc", [128, D], F32)
        self.sg = [Buf(kb, f"sg{i}", [128, BLK], F32) for i in range(2)]
        self.xb = Buf(kb, "xb", [128, D], BF16)
        self.xts = Buf(kb, "xts", [128, D], BF16)
        self.stats = Buf(kb, "stats", [128, 24], F32)
        self.mv = Buf(kb, "mv", [128, 2], F32)
        self.rstd = Buf(kb, "rstd", [128, 2], F32)
        self.st = (self.stats, self.mv, self.rstd)


def load_xT_block(pg, wk, xT_ap, xT_name, b):
    view = wk.xT[:, :].rearrange("p (k t) -> p k t", t=BLK)
    pg.kb.dma("sp", view, xT_ap[:, b * BLK:(b + 1) * BLK].rearrange("(k p) t -> p k t", p=128),
              reads=[pg.R(xT_name)], writes=[wk.xT])
    return view


def load_ln_params(pg, wk, g_ap, b_ap, gname, bname):
    pg.kb.dma("sp", wk.gbc[:, :], g_ap.partition_broadcast(128), reads=[pg.R(gname)], writes=[wk.gbc])
    pg.kb.dma("sp", wk.bbc[:, :], b_ap.partition_broadcast(128), reads=[pg.R(bname)], writes=[wk.bbc])


def emit_ffn_block(pg, wk, b, x_in, xT_in, wg, wu, wd, ln, x_out, xT_out):
    kb = pg.kb
    kb.barrier()
    load_ln_params(pg, wk, ln[0][0], ln[1][0], ln[0][1], ln[1][1])
    xT = load_xT_block(pg, wk, xT_in[0], xT_in[1], b)
    for t in range(4):
        kb.dma("sp", wk.v[t][:, :], x_in[0][b * BLK + t * 128: b * BLK + (t + 1) * 128, :],
               reads=[pg.R(x_in[1])], writes=[wk.v[t]])
    hT = wk.hT[:, :].rearrange("p (f t) -> p f t", t=BLK)
    for half in range(2):
        for grp in range(6):
            nf = 4 if grp < 5 else 2
            f0 = half * 22 + grp * 4
            sg_, wgv = pg.load_w(wg[0], wg[1], 0, 16, f0 * 128, nf * 128)
            su_, wuv = pg.load_w(wu[0], wu[1], 0, 16, f0 * 128, nf * 128)
            for fi in range(nf):
                pgate = pg.bank()
                pup = pg.bank()
                for kc in range(16):
                    mm(pg, pgate[:, :], wgv[:, kc, fi * 128:(fi + 1) * 128], xT[:, kc, :], kc == 0, kc == 15,
                       [sg_, wk.xT], pgate)
                for kc in range(16):
                    mm(pg, pup[:, :], wuv[:, kc, fi * 128:(fi + 1) * 128], xT[:, kc, :], kc == 0, kc == 15,
                       [su_, wk.xT], pup)
                sgb = wk.sg[(grp * 4 + fi) % 2]
                kb.op("act", lambda e, pgate=pgate, sgb=sgb: e.activation(out=sgb[:, :], in_=pgate[:, :], func=AF.Silu),
                      reads=[pgate], writes=[sgb])
                fl = grp * 4 + fi
                kb.op("dve", lambda e, pup=pup, sgb=sgb, fl=fl: e.scalar_tensor_tensor(
                    out=hT[:, fl, :], in0=pup[:, :], scalar=0.5, in1=sgb[:, :], op0=ALU.mult, op1=ALU.mult),
                    reads=[pup, sgb], writes=[wk.hT])
        for n in range(4):
            s0, w0 = pg.load_w(wd[0], wd[1], (half * 22) * 128, 11, n * 512, 512)
            s1, w1 = pg.load_w(wd[0], wd[1], (half * 22 + 11) * 128, 11, n * 512, 512)
            for t in range(4):
                pd = pg.bank()
                for q_, (s_, w_) in enumerate(((s0, w0), (s1, w1))):
                    for fc in range(11):
                        f = q_ * 11 + fc
                        mm(pg, pd[:, :], hT[:, f, t * 128:(t + 1) * 128], w_[:, fc, :], f == 0, f == 21, [wk.hT, s_], pd)
                vslice = wk.v[t][:, n * 512:(n + 1) * 512]
                kb.op("dve", lambda e, pd=pd, vslice=vslice, half=half: e.scalar_tensor_tensor(
                    out=vslice, in0=vslice, scalar=(ALPHA if half == 0 else 1.0), in1=pd[:, :], op0=ALU.mult, op1=ALU.add),
                    reads=[pd, wk.v[t]], writes=[wk.v[t]])
    for t in range(4):
        r0 = b * BLK + t * 128
        emit_ln(pg, wk.v[t], wk.v[t][:, :], wk.gbc, wk.bbc, wk.st,
                x_out[0][r0:r0 + 128, :], x_out[1], xT_out[0][:, r0:r0 + 128], xT_out[1], wk.xb, wk.xts)


def emit_prep_xT(pg, wk, x_in, xT_out):
    for j in range(NT):
        vb = wk.v[j % 4]
        pg.kb.dma("sp", vb[:, :], x_in[0][j * 128:(j + 1) * 128, :], reads=[pg.R(x_in[1])], writes=[vb])
        emit_transpose_out(pg, vb, vb[:, :], 16, wk.xb, wk.xts, xT_out[0][:, j * 128:(j + 1) * 128], xT_out[1])


TWO_PI = 2.0 * math.pi
CW1 = 6.28125
CW2 = TWO_PI - 6.28125


def emit_rope_tables(pg, wk, pos_ap, pos_name, tabs, tabs_name):
    kb = pg.kb
    posf, ang, kf, tmp = wk.v[0], wk.v[1], wk.v[2], wk.v[3]
    ki = View(wk.xT[:, 0:2 * TL].bitcast(I32))
    kb.dma("pool", posf[:, :], pos_ap.partition_broadcast(128), reads=[pg.R(pos_name)], writes=[posf])
    PI = math.pi

    def fold(buf):
        kb.op("dve", lambda e: e.tensor_scalar(out=tmp[:, :], in0=buf[:, :], scalar1=PI, scalar2=-TWO_PI,
                                               op0=ALU.is_gt, op1=ALU.mult), reads=[buf], writes=[tmp])
        kb.op("dve", lambda e: e.tensor_tensor(out=buf[:, :], in0=buf[:, :], in1=tmp[:, :], op=ALU.add),
              reads=[buf, tmp], writes=[buf])
        kb.op("dve", lambda e: e.tensor_scalar(out=tmp[:, :], in0=buf[:, :], scalar1=-PI, scalar2=TWO_PI,
                                               op0=ALU.is_lt, op1=ALU.mult), reads=[buf], writes=[tmp])
        kb.op("dve", lambda e: e.tensor_tensor(out=buf[:, :], in0=buf[:, :], in1=tmp[:, :], op=ALU.add),
              reads=[buf, tmp], writes=[buf])
        kb.op("dve", lambda e: e.tensor_scalar(out=buf[:, :], in0=buf[:, :], scalar1=-PI, scalar2=PI,
                                               op0=ALU.max, op1=ALU.min), reads=[buf], writes=[buf])

    for ti, fcol in ((0, C_F64), (1, C_F128)):
        kb.op("dve", lambda e: e.tensor_scalar(out=ang[:, :], in0=posf[:, :], scalar1=pg.cc(fcol, 1), scalar2=None,
                                               op0=ALU.mult), reads=[posf, pg.consts], writes=[ang])
        kb.op("dve", lambda e: e.tensor_scalar(out=ki[:, :], in0=ang[:, :], scalar1=1.0 / TWO_PI, scalar2=None,
                                               op0=ALU.mult), reads=[ang], writes=[ki, wk.xT])
        kb.op("dve", lambda e: e.tensor_copy(out=kf[:, :], in_=ki[:, :]), reads=[ki, wk.xT], writes=[kf])
        kb.op("dve", lambda e: e.scalar_tensor_tensor(out=ang[:, :], in0=kf[:, :], scalar=-CW1, in1=ang[:, :],
                                                      op0=ALU.mult, op1=ALU.add), reads=[kf, ang], writes=[ang])
        kb.op("dve", lambda e: e.scalar_tensor_tensor(out=ang[:, :], in0=kf[:, :], scalar=-CW2, in1=ang[:, :],
                                                      op0=ALU.mult, op1=ALU.add), reads=[kf, ang], writes=[ang])
        fold(ang)
        kb.op("act", lambda e: e.activation(out=kf[:, :], in_=ang[:, :], func=AF.Sin), reads=[ang], writes=[kf])
        kb.dma("sp", tabs[2 * ti + 1], kf[:, :], reads=[kf], writes=[pg.R(tabs_name)])
        kb.op("dve", lambda e: e.tensor_scalar(out=ang[:, :], in0=ang[:, :], scalar1=PI / 2, scalar2=None, op0=ALU.add),
              reads=[ang], writes=[ang])
        fold(ang)
        kb.op("act", lambda e: e.activation(out=kf[:, :], in_=ang[:, :], func=AF.Sin), reads=[ang], writes=[kf])
        kb.dma("sp", tabs[2 * ti], kf[:, :], reads=[kf], writes=[pg.R(tabs_name)])
    kb.barrier()


O_CQ, O_CKV, O_KR, O_DQ, O_DK, O_DV, O_IQ, O_IK, O_IW, O_GQ, O_GK, O_GV, O_GLR, O_GR = (
    0, 512, 1024, 1088, 2112, 3136, 4160, 5184, 5248, 5264, 5776, 6288, 7312, 7328)


class MixParams:
    def __init__(self, pg, tag):
        kb = pg.kb
        self.sm = Buf(kb, "sm" + tag, [128, 8], F32)
        self.wg2 = Buf(kb, "wg2" + tag, [16, 512], F32)
        self.bg = Buf(kb, "bg" + tag, [1, 512], F32)

    def load(self, pg, sm_ap, sm_name, wg2_ap, wg2_name, bg_ap, bg_name):
        kb = pg.kb
        kb.dma("sp", self.sm[:, :], sm_ap[:, :], reads=[pg.R(sm_name)], writes=[self.sm])
        kb.dma("sp", self.wg2[:, :], wg2_ap[:, :], reads=[pg.R(wg2_name)], writes=[self.wg2])
        kb.dma("sp", self.bg[:, :], bg_ap[:, :], reads=[pg.R(bg_name)], writes=[self.bg])


def fm_proj(pg, wview, wslot, c0, M, xT, xTbuf, K=16):
    ps = pg.bank()
    for kc in range(K):
        mm(pg, ps[0:M, :], wview[:, kc, c0:c0 + M], xT[:, kc, :], kc == 0, kc == K - 1, [wslot, xTbuf], ps)
    return ps


def tm_proj(pg, wview_cols, wslot, N, xT, xTbuf, t, K=16, outv=None):
    ps = pg.bank()
    o = ps[:, 0:N] if outv is None else outv(ps)
    for kc in range(K):
        mm(pg, o, xT[:, kc, t * 128:(t + 1) * 128], wview_cols(kc), kc == 0, kc == K - 1, [wslot, xTbuf], ps)
    return ps


def emit_mixproj_block(pg, wk, mp, b, L):
    kb = pg.kb
    kb.barrier()
    nm = L["n"]
    tok = slice(b * BLK, (b + 1) * BLK)
    xT = load_xT_block(pg, wk, nm["xT"][0], nm["xT"][1], b)
    xTb = wk.xT
    tab = View(wk.v[0][:, :].rearrange("p (a t) -> p a t", t=BLK))
    kb.dma("sp", tab[:, :, :], nm["tabs"][0][:, :, tok].rearrange("a p t -> p a t"), reads=[pg.R(nm["tabs"][1])], writes=[tab])
    C64, S64, C128, S128 = (tab[:, i, :] for i in range(4))
    cT = View(wk.v[1][:, :].rearrange("p (a t) -> p a t", t=BLK))
    sq = View(wk.v[2][:, :].rearrange("p (a t) -> p a t", t=BLK))
    rstd = View(wk.v[3][:, 0:512])
    xs = View(wk.v[3][:, 512:1024])
    t1 = View(wk.v[3][:, 1024:1536])
    cnT = View(wk.hT[:, 0:2048].rearrange("p (a t) -> p a t", t=BLK))
    stg = [View(wk.hT[:, 2048 + i * 512: 2048 + (i + 1) * 512]) for i in range(4)]
    stgn = [0]

    def stage():
        s = stg[stgn[0] % 4]
        stgn[0] += 1
        return s

    def store_fm(ps, M, dst_ap, dst_name):
        s = stage()
        kb.op("act", lambda e: e.activation(out=s[0:M, :], in_=ps[0:M, :], func=AF.Copy), reads=[ps], writes=[s])
        kb.dma("sp", dst_ap, s[0:M, :], reads=[s], writes=[pg.R(dst_name)])

    def rope_store(ps, M, Cb, Sb, pcol, dst_ap, dst_name):
        kb.op("act", lambda e: e.activation(out=xs[0:M, :], in_=ps[0:M, :], func=AF.Copy), reads=[ps], writes=[xs])
        ps2 = pg.bank()
        mm(pg, ps2[0:M, :], pg.consts[0:M, pcol:pcol + M], xs[0:M, :], True, True, [pg.consts, xs], ps2)
        kb.op("dve", lambda e: e.tensor_tensor(out=t1[0:M, :], in0=xs[0:M, :], in1=Cb[0:M, :], op=ALU.mult),
              reads=[xs, tab], writes=[t1])
        kb.op("dve", lambda e: e.tensor_tensor(out=xs[0:M, :], in0=ps2[0:M, :], in1=Sb[0:M, :], op=ALU.mult),
              reads=[ps2, tab, xs], writes=[xs])
        s = stage()
        kb.op("dve", lambda e: e.tensor_tensor(out=s[0:M, :], in0=t1[0:M, :], in1=xs[0:M, :], op=ALU.add),
              reads=[t1, xs], writes=[s])
        kb.dma("sp", dst_ap, s[0:M, :], reads=[s], writes=[pg.R(dst_name)])

    def rms_fm(c0, gcol0):
        slot, wv = pg.load_w(nm["w_in"][0], nm["w_in"][1], 0, 16, c0, 512)
        for c in range(4):
            ps = fm_proj(pg, wv, slot, c * 128, 128, xT, xTb)
            kb.op("act", lambda e, ps=ps, c=c: e.activation(out=sq[:, c, :], in_=ps[:, :], func=(AF.Copy if getattr(pg, "nosq", False) else AF.Square)), reads=[ps], writes=[sq])
            kb.op("dve", lambda e, ps=ps, c=c: e.tensor_copy(out=cT[:, c, :], in_=ps[:, :]), reads=[ps], writes=[cT])
        if getattr(pg, 'dbg', 99) < 2.1:
            return
        ss = pg.bank()
        for c in range(4):
            mm(pg, ss[:, :], pg.cc(C_ONES), sq[:, c, :], c == 0, c == 3, [pg.consts, sq], ss)
        if getattr(pg, 'dbg', 99) < 2.2:
            return
        kb.op("act", lambda e: e.activation(out=rstd[:, :], in_=ss[:, :], func=AF.Ln, bias=pg.cc(C_EPS, 1), scale=1.0 / 512),
              reads=[ss, pg.consts], writes=[rstd])
        kb.op("act", lambda e: e.activation(out=rstd[:, :], in_=rstd[:, :], func=AF.Exp, scale=-0.5), reads=[rstd], writes=[rstd])
        for c in range(4):
            kb.op("dve", lambda e, c=c: e.scalar_tensor_tensor(out=cnT[:, c, :], in0=cT[:, c, :], scalar=mp.sm[:, gcol0 + c:gcol0 + c + 1],
                                                               in1=rstd[:, :], op0=ALU.mult, op1=ALU.mult),
                  reads=[cT, mp.sm, rstd], writes=[cnT])

    if getattr(pg, 'dbg', 99) < 2:
        return
    rms_fm(O_CQ, 0)
    if getattr(pg, 'dbg', 99) < 2.3:
        return
    slot, wq = pg.load_w(nm["w_uq"][0], nm["w_uq"][1], 0, 4, 0, 1536)
    for h in range(8):
        ps = fm_proj(pg, wq, slot, h * 192, 128, cnT, cnT, K=4)
        store_fm(ps, 128, nm["QnT"][0][h, :, tok], nm["QnT"][1])
        if getattr(pg, 'dbg', 99) < 2.6:
            continue
        ps = fm_proj(pg, wq, slot, h * 192 + 128, 64, cnT, cnT, K=4)
        rope_store(ps, 64, C64, S64, C_P64, nm["QrT"][0][h, :, tok], nm["QrT"][1])
    if getattr(pg, 'dbg', 99) < 3:
        return
    rms_fm(O_CKV, 4)
    slot, wkv = pg.load_w(nm["w_ukv"][0], nm["w_ukv"][1], 0, 4, 0, 2048)
    for h in range(8):
        ps = fm_proj(pg, wkv, slot, h * 256, 128, cnT, cnT, K=4)
        store_fm(ps, 128, nm["KnT_o"][0][h, :, tok], nm["KnT_o"][1])
    for t in range(4):
        for i in range(2):
            wsel = lambda kc, i=i: wkv[:, kc, :].rearrange("p (h two d) -> p h two d", two=2, d=128)[:, 4 * i:4 * i + 4, 1, :]
            ps = tm_proj(pg, wsel, slot, 512, cnT, cnT, t, K=4, outv=lambda p: p[:, :].rearrange("p (h d) -> p h d", d=128))
            s = stage()
            kb.op("act", lambda e, ps=ps, s=s: e.activation(out=s[:, :], in_=ps[:, :], func=AF.Copy), reads=[ps], writes=[s])
            r0 = b * BLK + t * 128
            kb.dma("sp", nm["Vm_o"][0][r0:r0 + 128, i * 512:(i + 1) * 512], s[:, :], reads=[s], writes=[pg.R(nm["Vm_o"][1])])
    if getattr(pg, 'dbg', 99) < 4:
        return
    slot, wv = pg.load_w(nm["w_in"][0], nm["w_in"][1], 0, 16, O_KR, 64)
    ps = fm_proj(pg, wv, slot, 0, 64, xT, xTb)
    rope_store(ps, 64, C64, S64, C_P64, nm["KrT_o"][0][:, tok], nm["KrT_o"][1])
    for (c0, key, Cb, Sb, pcol) in ((O_DQ, "QdT", C128, S128, C_P128), (O_DK, "KdT_o", C128, S128, C_P128),
                                    (O_IQ, "IqT", C64, S64, C_P64)):
        for g in range(2):
            slot, wv = pg.load_w(nm["w_in"][0], nm["w_in"][1], 0, 16, c0 + g * 512, 512)
            for hh in range(4):
                ps = fm_proj(pg, wv, slot, hh * 128, 128, xT, xTb)
                rope_store(ps, 128, Cb, Sb, pcol, nm[key][0][g * 4 + hh, :, tok], nm[key][1])
    if getattr(pg, 'dbg', 99) < 5:
        return
    for g in range(2):
        slot, wv = pg.load_w(nm["w_in"][0], nm["w_in"][1], 0, 16, O_DV + g * 512, 512)
        for t in range(4):
            ps = tm_proj(pg, lambda kc, wv=wv: wv[:, kc, :], slot, 512, xT, xTb, t)
            s = stage()
            kb.op("act", lambda e, ps=ps, s=s: e.activation(out=s[:, :], in_=ps[:, :], func=AF.Copy), reads=[ps], writes=[s])
            r0 = b * BLK + t * 128
            kb.dma("sp", nm["Vd_o"][0][r0:r0 + 128, g * 512:(g + 1) * 512], s[:, :], reads=[s], writes=[pg.R(nm["Vd_o"][1])])
    slot, wv = pg.load_w(nm["w_in"][0], nm["w_in"][1], 0, 16, O_IK, 80)
    ps = fm_proj(pg, wv, slot, 0, 64, xT, xTb)
    rope_store(ps, 64, C64, S64, C_P64, nm["IkT_o"][0][:, tok], nm["IkT_o"][1])
    for t in range(4):
        ps = tm_proj(pg, lambda kc, wv=wv: wv[:, kc, 64:80], slot, 16, xT, xTb, t)
        kb.op("act", lambda e, ps=ps: e.activation(out=xs[:, 0:16], in_=ps[:, 0:16], func=AF.Copy), reads=[ps], writes=[xs])
        r0 = b * BLK + t * 128
        kb.dma("sp", nm["iw"][0][r0:r0 + 128, :], xs[:, 0:16], reads=[xs], writes=[pg.R(nm["iw"][1])])
    if getattr(pg, 'dbg', 99) < 6:
        return
    emit_gla_proj_block(pg, wk, mp, b, L, xT)


def emit_gla_proj_block(pg, wk, mp, b, L, xT):
    kb = pg.kb
    kb.barrier()
    nm = L["n"]
    xTb = wk.xT
    win = nm["w_in"]
    gq_raw = View(wk.v[0][:, :].rearrange("p (a t) -> p a t", t=512))
    gk_raw = View(wk.v[1][:, :].rearrange("p (a t) -> p a t", t=512))
    lbuf = View(wk.v[2][:, 0:512])
    Eq = View(wk.v[2][:, 512:1024])
    Ek = View(wk.v[2][:, 1024:1536])
    Er = View(wk.v[2][:, 1536:2048])
    oi = View(wk.v[3][:, 0:1024])
    U0 = View(wk.v[3][:, 1024:2048])
    Ut = View(wk.gbc[:, 0:1024])
    gv = View(wk.hT[:, 0:4096].rearrange("p (a t) -> p a t", t=1024))
    sgr = View(wk.hT[:, 4096:8192].rearrange("p (a t) -> p a t", t=1024))
    qt = View(wk.hT[:, 8192:8704])
    kt = View(wk.hT[:, 8704:9216])
    kh = View(wk.hT[:, 9216:9728])
    qkT = View(wk.hT[:, 9728:10752].rearrange("p (a t) -> p a t", t=128))
    AT = View(wk.hT[:, 10752:11264].rearrange("p (a t) -> p a t", t=128))
    glrT = View(wk.sg[0][0:16, :])
    Acol = View(wk.sg[1][:, 0:8])
    Atl = View(wk.sg[1][:, 8:12])

    slot, wv = pg.load_w(win[0], win[1], 0, 16, O_GLR, 16)
    ps = fm_proj(pg, wv, slot, 0, 16, xT, xTb)
    kb.op("act", lambda e: e.activation(out=glrT[:, :], in_=ps[0:16, :], func=AF.Copy), reads=[ps], writes=[glrT])
    for (c0, dst) in ((O_GQ, gq_raw), (O_GK, gk_raw)):
        slot, wv = pg.load_w(win[0], win[1], 0, 16, c0, 512)
        for t in range(4):
            ps = tm_proj(pg, lambda kc, wv=wv: wv[:, kc, :], slot, 512, xT, xTb, t)
            kb.op("act", lambda e, ps=ps, t=t, dst=dst: e.activation(out=dst[:, t, :], in_=ps[:, :], func=AF.Copy), reads=[ps], writes=[dst])
    for (c0, dst, fn) in ((O_GV, gv, AF.Copy), (O_GR, sgr, AF.Silu)):
        for g in range(2):
            slot, wv = pg.load_w(win[0], win[1], 0, 16, c0 + g * 512, 512)
            for t in range(4):
                ps = tm_proj(pg, lambda kc, wv=wv: wv[:, kc, :], slot, 512, xT, xTb, t)
                kb.op("act", lambda e, ps=ps, t=t, g=g, dst=dst, fn=fn: e.activation(out=dst[:, t, g * 512:(g + 1) * 512], in_=ps[:, :], func=fn),
                      reads=[ps], writes=[dst])
    for t in range(4):
        r0 = b * BLK + t * 128
        kb.dma("sp", nm["gsr"][0][r0:r0 + 128, :], sgr[:, t, :], reads=[sgr], writes=[pg.R(nm["gsr"][1])])
    for t in range(4):
        j = b * 4 + t
        r0 = b * BLK + t * 128
        Z = pg.bank()
        mm(pg, Z[:, :], glrT[0:16, t * 128:(t + 1) * 128], mp.wg2[0:16, :], True, False, [glrT, mp.wg2], Z)
        mm(pg, Z[:, :], pg.consts[0:1, C_ONES:C_ONES + 128], mp.bg[0:1, :], False, True, [pg.consts, mp.bg], Z)
        kb.op("act", lambda e: e.activation(out=Eq[:, :], in_=Z[:, :], func=AF.Exp, scale=-1.0), reads=[Z], writes=[Eq])
        kb.op("act", lambda e: e.activation(out=lbuf[:, :], in_=Eq[:, :], func=AF.Ln, bias=pg.cc(C_ONES, 1), scale=1.0),
              reads=[Eq, pg.consts], writes=[lbuf])
        cum = pg.bank()
        mm(pg, cum[:, :], pg.cc(C_TRI), lbuf[:, :], True, True, [pg.consts, lbuf], cum)
        rev = pg.bank()
        mm(pg, rev[:, :], pg.cc(C_RTRI), lbuf[:, :], True, True, [pg.consts, lbuf], rev)
        tot = pg.pb[4]
        for h in range(4):
            mm(pg, tot[:, h * 2:h * 2 + 2], lbuf[:, h * 128:(h + 1) * 128], pg.consts[:, C_CH:C_CH + 2], True, True, [pg.consts, lbuf], tot)
        kb.op("act", lambda e: e.activation(out=Eq[:, :], in_=cum[:, :], func=AF.Exp, scale=-1.0 / 16), reads=[cum], writes=[Eq])
        kb.op("act", lambda e: e.activation(out=Ek[:, :], in_=cum[:, :], func=AF.Exp, scale=1.0 / 16), reads=[cum], writes=[Ek])
        kb.op("act", lambda e: e.activation(out=Er[:, :], in_=rev[:, :], func=AF.Exp, scale=-1.0 / 16), reads=[rev], writes=[Er])
        kb.op("act", lambda e: e.activation(out=Acol[:, :], in_=tot[:, 0:8], func=AF.Exp, scale=-1.0 / 16), reads=[tot], writes=[Acol])
        kb.op("dve", lambda e, t=t: e.scalar_tensor_tensor(out=qt[:, :], in0=gq_raw[:, t, :], scalar=128.0 ** -0.5, in1=Eq[:, :],
                                                          op0=ALU.mult, op1=ALU.mult), reads=[gq_raw, Eq], writes=[qt])
        kb.op("dve", lambda e, t=t: e.tensor_tensor(out=kt[:, :], in0=gk_raw[:, t, :], in1=Ek[:, :], op=ALU.mult), reads=[gk_raw, Ek], writes=[kt])
        kb.op("dve", lambda e, t=t: e.tensor_tensor(out=kh[:, :], in0=gk_raw[:, t, :], in1=Er[:, :], op=ALU.mult), reads=[gk_raw, Er], writes=[kh])
        ptb = pg.pt[0]
        for i in range(8):
            src = qt if i < 4 else kt
            hh = i % 4
            kb.op("pe", lambda e, i=i, src=src, hh=hh: e.transpose(out=ptb[:, i * 128:(i + 1) * 128], in_=src[:, hh * 128:(hh + 1) * 128],
                                                                   identity=pg.identb[:, :]), reads=[src, pg.identb], writes=[ptb])
        kb.op("dve", lambda e: e.tensor_copy(out=qkT[:, :, :], in_=ptb[:, :].rearrange("p (a t) -> p a t", t=128)), reads=[ptb], writes=[qkT])
        kb.dma("sp", nm["gqT"][0][:, :, r0:r0 + 128].rearrange("h p t -> p h t"), qkT[:, 0:4, :], reads=[qkT], writes=[pg.R(nm["gqT"][1])])
        for h in range(4):
            at = pg.bank()
            mm(pg, at[:, 0:128], qkT[:, 4 + h, :], qkT[:, h, :], True, True, [qkT], at)
            kb.op("dve", lambda e, at=at, h=h: e.tensor_tensor(out=AT[:, h, :], in0=at[:, 0:128], in1=pg.cc(C_MASKF), op=ALU.mult),
                  reads=[at, pg.consts], writes=[AT])
        for h in range(4):
            o = pg.bank()
            mm(pg, o[:, 0:256], AT[:, h, :], gv[:, t, h * 256:(h + 1) * 256], True, True, [AT, gv], o)
            kb.op("act", lambda e, o=o, h=h: e.activation(out=oi[:, h * 256:(h + 1) * 256], in_=o[:, 0:256], func=AF.Copy), reads=[o], writes=[oi])
        kb.dma("sp", nm["goi"][0][r0:r0 + 128, :], oi[:, :], reads=[oi], writes=[pg.R(nm["goi"][1])])
        for h in range(4):
            u0 = pg.bank()
            mm(pg, u0[:, 0:256], kh[0:64, h * 128:(h + 1) * 128], gv[0:64, t, h * 256:(h + 1) * 256], True, True, [kh, gv], u0)
            u1 = pg.bank()
            mm(pg, u1[:, 0:256], kh[64:128, h * 128:(h + 1) * 128], gv[64:128, t, h * 256:(h + 1) * 256], True, True, [kh, gv], u1)
            kb.op("act", lambda e, u0=u0, h=h: e.activation(out=U0[:, h * 256:(h + 1) * 256], in_=u0[:, 0:256], func=AF.Copy), reads=[u0], writes=[U0])
            kb.op("dve", lambda e, u1=u1, h=h: e.scalar_tensor_tensor(out=Ut[:, h * 256:(h + 1) * 256], in0=U0[:, h * 256:(h + 1) * 256],
                                                                      scalar=Acol[:, 2 * h + 1:2 * h + 2], in1=u1[:, 0:256],
                                                                      op0=ALU.mult, op1=ALU.add), reads=[U0, Acol, u1], writes=[Ut])
        kb.dma("sp", nm["gU0"][0][j], U0[:, :], reads=[U0], writes=[pg.R(nm["gU0"][1])])
        kb.dma("sp", nm["gUt_o"][0][j], Ut[:, :], reads=[Ut], writes=[pg.R(nm["gUt_o"][1])])
        Ac3 = Acol[:, :].rearrange("p (h c) -> p h c", c=2)
        kb.op("dve", lambda e: e.tensor_tensor(out=Atl[:, :], in0=Ac3[:, :, 0], in1=Ac3[:, :, 1], op=ALU.mult), reads=[Acol], writes=[Atl])
        kb.dma("sp", nm["gA"][0][j], Acol[:, :], reads=[Acol], writes=[pg.R(nm["gA"][1])])
        kb.dma("sp", nm["gAt_o"][0][j], Atl[:, :], reads=[Atl], writes=[pg.R(nm["gAt_o"][1])])
    kb.barrier()


BIS_L = -64.0
BIS_W = 128.0
BIS_N = 28


def emit_indexer(pg, nm, tiles=None):
    kb = pg.kb
    kb.push()
    ik2 = Buf(kb, "ik2", [128, SEQ], BF16)
    score = Buf(kb, "score", [128, SEQ], F32)
    mneg = Buf(kb, "mnegw", [128, SEQ], BF16)
    cm = Buf(kb, "cm32", [128, 1024], F32)
    iq = Buf(kb, "iqt", [128, 1024], BF16)
    iwt = Buf(kb, "iwt", [128, 16], F32)
    wab = Buf(kb, "wab", [128, 16], F32)
    wsg = Buf(kb, "wsg", [128, 16], F32)
    mid = Buf(kb, "mid", [128, 1], F32)
    cnt = Buf(kb, "cnt", [128, 1], F32)
    dlt = Buf(kb, "dlt", [128, 1], F32)
    nmid = Buf(kb, "nmid", [128, 1], F32)
    sgs = Buf(kb, "sgs", [128, 1], F32)
    ajunk = Buf(kb, "ajunk", [128, 9216], BF16)
    kb.dma("sp", ik2[0:64, :], nm["IkT_g"][0][:, :], reads=[pg.R(nm["IkT_g"][1])], writes=[ik2])
    kb.dma("sp", ik2[64:128, :], nm["IkT_g"][0][:, :], reads=[pg.R(nm["IkT_g"][1])], writes=[ik2])
    kb.dma("sp", cm[:, :], nm["cmask"][0][:, :], reads=[pg.R(nm["cmask"][1])], writes=[cm])
    iq3 = iq[:, :].rearrange("p (a t) -> p a t", t=128)
    for j in (range(NT) if tiles is None else tiles):
        nk = 1024 * (j + 1)
        ts = slice(j * 128, (j + 1) * 128)
        kb.dma("sp", iq3, nm["IqT"][0][:, :, ts].rearrange("a p t -> p a t"), reads=[pg.R(nm["IqT"][1])], writes=[iq])
        kb.dma("sp", iwt[:, :], nm["iw"][0][ts, :], reads=[pg.R(nm["iw"][1])], writes=[iwt])
        kb.op("act", lambda e: e.activation(out=wab[:, :], in_=iwt[:, :], func=AF.Abs, scale=(64.0 ** -0.5) * (16.0 ** -0.5)),
              reads=[iwt], writes=[wab])
        kb.op("act", lambda e: e.activation(out=wsg[:, :], in_=iwt[:, :], func=AF.Sign), reads=[iwt], writes=[wsg])
        for kbi in range(2 * (j + 1)):
            ks = slice(kbi * 512, (kbi + 1) * 512)
            diag = kbi >= 2 * j
            for hh in range(16):
                p0 = (hh % 2) * 64
                ps = pg.bank()
                mm(pg, ps[:, :], iq3[p0:p0 + 64, hh // 2, :], ik2[p0:p0 + 64, ks], True, True, [iq, ik2], ps)
                kb.op("act", lambda e, ps=ps, hh=hh: e.activation(out=ps[:, :], in_=ps[:, :], func=AF.Relu, scale=wab[:, hh:hh + 1]),
                      reads=[ps, wab], writes=[ps])
                if hh == 0 and not diag:
                    kb.op("dve", lambda e, ps=ps, ks=ks: e.tensor_scalar(out=score[:, ks], in0=ps[:, :], scalar1=wsg[:, 0:1], scalar2=None,
                                                                         op0=ALU.mult), reads=[ps, wsg], writes=[score])
                else:
                    in1 = cm[:, (kbi - 2 * j) * 512:(kbi - 2 * j + 1) * 512] if hh == 0 else score[:, ks]
                    kb.op("dve", lambda e, ps=ps, ks=ks, hh=hh, in1=in1: e.scalar_tensor_tensor(
                        out=score[:, ks], in0=ps[:, :], scalar=wsg[:, hh:hh + 1], in1=in1, op0=ALU.mult, op1=ALU.add),
                        reads=[ps, wsg, cm], writes=[score])
        nd = max(512, int(round(0.45 * nk / 512)) * 512)
        na = nk - nd
        kb.op("dve", lambda e: e.memset(mid[:, :], BIS_L + BIS_W / 2), writes=[mid])
        kb.op("dve", lambda e: e.memset(nmid[:, :], -(BIS_L + BIS_W / 2)), writes=[nmid])
        for k in range(1, BIS_N + 1):
            kb.op("dve", lambda e: e.tensor_scalar(out=mneg[:, 0:nd], in0=score[:, 0:nd], scalar1=mid[:, 0:1], scalar2=None,
                                                   op0=ALU.is_ge, op1=ALU.add, accum_out=cnt[:, 0:1]),
                  reads=[score, mid], writes=[mneg, cnt])
            kb.op("act", lambda e: e.activation(out=ajunk[:, 0:na], in_=score[:, nd:nk], func=AF.Sign, bias=nmid[:, 0:1], scale=1.0,
                                                accum_out=sgs[:, 0:1]), reads=[score, nmid], writes=[ajunk, sgs])
            kb.op("dve", lambda e: e.scalar_tensor_tensor(out=cnt[:, :], in0=sgs[:, :], scalar=0.5, in1=cnt[:, :], op0=ALU.mult, op1=ALU.add),
                  reads=[sgs], writes=[cnt])
            hk = BIS_W / 2 ** (k + 1) if k < BIS_N else BIS_W / 2 ** BIS_N
            mul = 2 * hk if k < BIS_N else hk
            kb.op("dve", lambda e, mul=mul: e.tensor_scalar(out=dlt[:, :], in0=cnt[:, :], scalar1=255.5 - 0.5 * na, scalar2=mul,
                                                            op0=ALU.is_ge, op1=ALU.mult), reads=[cnt], writes=[dlt])
            kb.op("dve", lambda e, hk=hk: e.scalar_tensor_tensor(out=mid[:, :], in0=dlt[:, :], scalar=-hk, in1=mid[:, :],
                                                                op0=ALU.add, op1=ALU.add), reads=[dlt, mid], writes=[mid])
            if k < BIS_N:
                kb.op("dve", lambda e: e.tensor_scalar(out=nmid[:, :], in0=mid[:, :], scalar1=-1.0, scalar2=None, op0=ALU.mult),
                      reads=[mid], writes=[nmid])
        kb.op("dve", lambda e: e.tensor_scalar(out=mneg[:, 0:nk], in0=score[:, 0:nk], scalar1=mid[:, 0:1], scalar2=NEG,
                                               op0=ALU.is_lt, op1=ALU.mult), reads=[score, mid], writes=[mneg])
        kb.dma("sp", nm["MnegD"][0][j, :, 0:nk], mneg[:, 0:nk], reads=[mneg], writes=[pg.R(nm["MnegD"][1])])
        if "thr_dbg" in nm:
            kb.dma("sp", nm["thr_dbg"][0][j], mid[:, :], reads=[mid], writes=[pg.R(nm["thr_dbg"][1])])
    kb.pop()


def emit_attention(pg, nm, kind, heads=None, tiles=None):
    kb = pg.kb
    kb.push()
    mla = kind == "mla"
    K = Buf(kb, "attK", [128, SEQ], BF16)
    V = Buf(kb, "attV", [128, SEQ], BF16)
    Kg = [View(K[:, g * 1024:(g + 1) * 1024]) for g in range(16)]
    Vg = [View(V[:, g * 1024:(g + 1) * 1024]) for g in range(16)]
    if mla:
        KR = Buf(kb, "attKR", [64, SEQ], BF16)
        cm32 = Buf(kb, "attcm32", [128, 1024], F32)
        cmb = Buf(kb, "attcmb", [128, 1024], BF16)
        kb.dma("sp", KR[:, :], nm["KrT_g"][0][:, :], reads=[pg.R(nm["KrT_g"][1])], writes=[KR])
        kb.dma("sp", cm32[:, :], nm["cmask"][0][:, :], reads=[pg.R(nm["cmask"][1])], writes=[cm32])
        kb.op("dve", lambda e: e.tensor_copy(out=cmb[:, :], in_=cm32[:, :]), reads=[cm32], writes=[cmb])
        Qr = [Buf(kb, f"attQr{i}", [64, 128], BF16) for i in range(2)]
        scale = 192.0 ** -0.5
        Kd, Vd, Qd, oT = nm["KnT_g"], nm["Vm_g"], nm["QnT"], nm["oT_mla"]
    else:
        MN = [Buf(kb, f"attMN{i}", [128, SEQ], BF16) for i in range(2)]
        scale = 128.0 ** -0.5
        Kd, Vd, Qd, oT = nm["KdT_g"], nm["Vd_g"], nm["QdT"], nm["oT_dsa"]
    Qn = [Buf(kb, f"attQn{i}", [128, 128], BF16) for i in range(2)]
    P = [Buf(kb, f"attP{i}", [128, 512], BF16) for i in range(2)]
    PT = [Buf(kb, f"attPT{i}", [128, 512], BF16) for i in range(2)]
    rsb = [Buf(kb, f"attrs{i}", [128, 32], F32) for i in range(2)]
    rsum = Buf(kb, "attrsum", [128, 2], F32)
    ob = Buf(kb, "attob", [128, 128], BF16)
    oTs = Buf(kb, "attoTs", [128, 128], BF16)
    Ob = [pg.pb[4], pg.pb[5]]
    tasks = []
    it = 0
    for h in (range(8) if heads is None else heads):
        first = True
        for j in (range(NT) if tiles is None else tiles):
            nblk = 2 * (j + 1)
            for kbi in range(nblk):
                tasks.append(dict(h=h, j=j, kbi=kbi, nblk=nblk, it=it, newhead=(first and kbi == 0)))
            first = False
            it += 1
    state = {"maxg": -1}

    def stageA(s_, T):
        h, j, kbi, it_ = T["h"], T["j"], T["kbi"], T["it"]
        ts = slice(j * 128, (j + 1) * 128)
        nk = 1024 * (j + 1)
        qn = Qn[it_ % 2]
        if kbi == 0:
            if T["newhead"]:
                state["maxg"] = -1
            for g in range(state["maxg"] + 1, j + 1):
                gs = slice(g * 1024, (g + 1) * 1024)
                kb.dma("sp", Kg[g][:, :], Kd[0][h, :, gs], reads=[pg.R(Kd[1])], writes=[Kg[g]])
                kb.dma("sp", Vg[g][:, :], Vd[0][h, :, gs], reads=[pg.R(Vd[1])], writes=[Vg[g]])
            state["maxg"] = max(state["maxg"], j)
            kb.dma("pool", qn[:, :], Qd[0][h, :, ts], reads=[pg.R(Qd[1])], writes=[qn])
            if mla:
                kb.dma("pool", Qr[it_ % 2][:, :], nm["QrT"][0][h, :, ts], reads=[pg.R(nm["QrT"][1])], writes=[Qr[it_ % 2]])
            else:
                kb.dma("pool", MN[it_ % 2][:, 0:nk], nm["MnegD"][0][j, :, 0:nk], reads=[pg.R(nm["MnegD"][1])], writes=[MN[it_ % 2]])
        g = kbi // 2
        ks = slice(kbi * 512, (kbi + 1) * 512)
        diag = g == j
        ps = pg.bank()
        if mla:
            qr = Qr[it_ % 2]
            mm(pg, ps[:, :], qn[:, :], K[:, ks], True, False, [qn, Kg[g]], ps)
            mm(pg, ps[:, :], qr[:, :], KR[:, ks], False, not diag, [qr, KR], ps)
            if diag:
                mm(pg, ps[:, :], pg.identb[:, :], cmb[:, (kbi - 2 * j) * 512:(kbi - 2 * j + 1) * 512], False, True, [pg.identb, cmb], ps)
        else:
            mn = MN[it_ % 2]
            mm(pg, ps[:, :], qn[:, :], K[:, ks], True, False, [qn, Kg[g]], ps)
            mm(pg, ps[:, :], pg.identb[:, :], mn[:, ks], False, True, [pg.identb, mn], ps)
        p = P[s_ % 2]
        rs = rsb[it_ % 2]
        kb.op("act", lambda e: e.activation(out=p[:, :], in_=ps[:, :], func=AF.Exp, scale=scale, accum_out=rs[:, kbi:kbi + 1]),
              reads=[ps], writes=[p, rs])

    def stageB(s_, T):
        p = P[s_ % 2]
        ptb = pg.pt[s_ % 2]
        for i in range(4):
            kb.op("pe", lambda e, i=i: e.transpose(out=ptb[:, i * 128:(i + 1) * 128], in_=p[:, i * 128:(i + 1) * 128],
                                                   identity=pg.identb[:, :]), reads=[p, pg.identb], writes=[ptb])
        pt_ = PT[s_ % 2]
        kb.op("dve", lambda e: e.tensor_copy(out=pt_[:, :], in_=ptb[:, 0:512]), reads=[ptb], writes=[pt_])

    def stageC(s_, T):
        h, j, kbi, nblk, it_ = T["h"], T["j"], T["kbi"], T["nblk"], T["it"]
        g = kbi // 2
        pt_ = PT[s_ % 2]
        O = Ob[it_ % 2]
        for i in range(4):
            n = kbi * 4 + i
            mm(pg, O[:, 0:128], pt_[:, i * 128:(i + 1) * 128], V[:, n * 128:(n + 1) * 128],
               kbi == 0 and i == 0, kbi == nblk - 1 and i == 3, [pt_, Vg[g]], O)
        if kbi == nblk - 1:
            ts = slice(j * 128, (j + 1) * 128)
            rs = rsb[it_ % 2]
            kb.op("dve", lambda e: e.reduce_sum(out=rsum[:, 0:1], in_=rs[:, 0:nblk], axis=mybir.AxisListType.X), reads=[rs], writes=[rsum])
            kb.op("dve", lambda e: e.reciprocal(out=rsum[:, 1:2], in_=rsum[:, 0:1]), reads=[rsum], writes=[rsum])
            kb.op("act", lambda e: e.activation(out=ob[:, :], in_=O[:, 0:128], func=AF.Copy, scale=rsum[:, 1:2]), reads=[O, rsum], writes=[ob])
            ptb = pg.pt[(s_ + 1) % 2]
            kb.op("pe", lambda e: e.transpose(out=ptb[:, 0:128], in_=ob[:, :], identity=pg.identb[:, :]), reads=[ob, pg.identb], writes=[ptb])
            kb.op("dve", lambda e: e.tensor_copy(out=oTs[:, :], in_=ptb[:, 0:128]), reads=[ptb], writes=[oTs])
            kb.dma("sp", oT[0][h * 128:(h + 1) * 128, ts], oTs[:, :], reads=[oTs], writes=[pg.R(oT[1])])

    N = len(tasks)
    for s_ in range(N + 2):
        if s_ < N:
            stageA(s_, tasks[s_])
        if 0 <= s_ - 1 < N:
            stageB(s_ - 1, tasks[s_ - 1])
        if 0 <= s_ - 2 < N:
            stageC(s_ - 2, tasks[s_ - 2])
    kb.pop()


def emit_gla_out(pg, nm, tiles=None):
    kb = pg.kb
    kb.push()
    S = Buf(kb, "glaS", [128, 1024], F32)
    snap = Buf(kb, "glasnap", [128, 1024], F32)
    At = Buf(kb, "glaAt", [128, 512], F32)
    Ub = [Buf(kb, f"glaUb{i}", [128, 1024], F32) for i in range(3)]
    U0b = Buf(kb, "glaU0", [128, 1024], F32)
    Ab = Buf(kb, "glaAb", [128, 8], F32)
    qTb = Buf(kb, "glaqT", [128, 512], BF16)
    qA = Buf(kb, "glaqA", [128, 512], BF16)
    qB = Buf(kb, "glaqB", [128, 512], BF16)
    oib = Buf(kb, "glaoi", [128, 1024], F32)
    srb = Buf(kb, "glasr", [128, 1024], BF16)
    gn = Buf(kb, "glagn", [128, 1024], F32)
    S0b = Buf(kb, "glaS0b", [128, 1024], BF16)
    S1b = Buf(kb, "glaS1b", [128, 1024], BF16)
    junk = Buf(kb, "glajunk", [128, 256], F32)
    ss = Buf(kb, "glass", [128, 8], F32)
    ob = Buf(kb, "glaob", [128, 1024], BF16)
    oTs = Buf(kb, "glaoTs", [128, 1024], BF16)
    kb.op("dve", lambda e: e.memset(S[:, :], 0.0), writes=[S])
    kb.op("dve", lambda e: e.memset(qA[:, :], 0.0), writes=[qA])
    kb.op("dve", lambda e: e.memset(qB[:, :], 0.0), writes=[qB])
    kb.dma("sp", At[:, :], nm["gAt_g"][0][:, :], reads=[pg.R(nm["gAt_g"][1])], writes=[At])
    kb.dma("sp", gn[:, :], nm["gla_norm"][0].partition_broadcast(128), reads=[pg.R(nm["gla_norm"][1])], writes=[gn])
    q3 = qTb[:, :].rearrange("p (h t) -> p h t", t=128)
    qA3 = qA[:, :].rearrange("p (h t) -> p h t", t=128)
    qB3 = qB[:, :].rearrange("p (h t) -> p h t", t=128)
    for j in range(NT):
        for i in range(8):
            g = 8 * j + i
            ub = Ub[g % 3]
            kb.dma("sp", ub[:, :], nm["gUt_g"][0][g], reads=[pg.R(nm["gUt_g"][1])], writes=[ub])
            if i == 0:
                kb.op("dve", lambda e: e.tensor_scalar(out=snap[:, :], in0=S[:, :], scalar1=pg.cc(C_EI, 1), scalar2=None, op0=ALU.mult),
                      reads=[S, pg.consts], writes=[snap])
            else:
                kb.op("dve", lambda e, i=i: e.scalar_tensor_tensor(out=snap[:, :], in0=S[:, :], scalar=pg.cc(C_EI + i, 1), in1=snap[:, :],
                                                                   op0=ALU.mult, op1=ALU.add), reads=[S, pg.consts, snap], writes=[snap])
            for h in range(4):
                hs = slice(h * 256, (h + 1) * 256)
                kb.op("dve", lambda e, hs=hs, g=g, h=h, ub=ub: e.scalar_tensor_tensor(
                    out=S[:, hs], in0=S[:, hs], scalar=At[:, g * 4 + h:g * 4 + h + 1], in1=ub[:, hs], op0=ALU.mult, op1=ALU.add),
                    reads=[At, ub], writes=[S])
        if tiles is not None and j not in tiles:
            continue
        ts = slice(j * 128, (j + 1) * 128)
        kb.dma("pool", U0b[:, :], nm["gU0"][0][j], reads=[pg.R(nm["gU0"][1])], writes=[U0b])
        kb.dma("pool", Ab[:, :], nm["gA"][0][j], reads=[pg.R(nm["gA"][1])], writes=[Ab])
        kb.dma("pool", q3, nm["gqT"][0][:, :, ts].rearrange("h p t -> p h t"), reads=[pg.R(nm["gqT"][1])], writes=[qTb])
        kb.dma("pool", oib[:, :], nm["goi"][0][ts, :], reads=[pg.R(nm["goi"][1])], writes=[oib])
        kb.dma("pool", srb[:, :], nm["gsr"][0][ts, :], reads=[pg.R(nm["gsr"][1])], writes=[srb])
        kb.op("act", lambda e: e.activation(out=S0b[:, :], in_=snap[:, :], func=AF.Copy), reads=[snap], writes=[S0b])
        for h in range(4):
            hs = slice(h * 256, (h + 1) * 256)
            kb.op("dve", lambda e, hs=hs, h=h: e.scalar_tensor_tensor(out=S1b[:, hs], in0=snap[:, hs], scalar=Ab[:, 2 * h:2 * h + 1],
                                                                      in1=U0b[:, hs], op0=ALU.mult, op1=ALU.add),
                  reads=[snap, Ab, U0b], writes=[S1b])
        kb.op("dve", lambda e: e.tensor_copy(out=qA3[:, :, 0:64], in_=q3[:, :, 0:64]), reads=[qTb], writes=[qA])
        kb.op("dve", lambda e: e.tensor_copy(out=qB3[:, :, 64:128], in_=q3[:, :, 64:128]), reads=[qTb], writes=[qB])
        for h in range(4):
            hs = slice(h * 256, (h + 1) * 256)
            o = pg.bank()
            mm(pg, o[:, 0:256], qA3[:, h, :], S0b[:, hs], True, False, [qA, S0b], o)
            mm(pg, o[:, 0:256], qB3[:, h, :], S1b[:, hs], False, True, [qB, S1b], o)
            kb.op("dve", lambda e, o=o, hs=hs: e.tensor_tensor(out=oib[:, hs], in0=o[:, 0:256], in1=oib[:, hs], op=ALU.add), reads=[o], writes=[oib])
            kb.op("act", lambda e, hs=hs, h=h: e.activation(out=junk[:, :], in_=oib[:, hs], func=AF.Square, accum_out=ss[:, h:h + 1]),
                  reads=[oib], writes=[junk, ss])
        kb.op("act", lambda e: e.activation(out=ss[:, 4:8], in_=ss[:, 0:4], func=AF.Ln, bias=pg.cc(C_EPS, 1), scale=1.0 / 256),
              reads=[ss, pg.consts], writes=[ss])
        kb.op("act", lambda e: e.activation(out=ss[:, 4:8], in_=ss[:, 4:8], func=AF.Exp, scale=-0.5), reads=[ss], writes=[ss])
        for h in range(4):
            hs = slice(h * 256, (h + 1) * 256)
            kb.op("dve", lambda e, hs=hs, h=h: e.scalar_tensor_tensor(out=oib[:, hs], in0=oib[:, hs], scalar=ss[:, 4 + h:5 + h], in1=gn[:, hs],
                                                                      op0=ALU.mult, op1=ALU.mult), reads=[ss, gn], writes=[oib])
        kb.op("dve", lambda e: e.tensor_tensor(out=ob[:, :], in0=oib[:, :], in1=srb[:, :], op=ALU.mult), reads=[oib, srb], writes=[ob])
        ptb = pg.pt[j % 2]
        for i in range(8):
            kb.op("pe", lambda e, i=i, ptb=ptb: e.transpose(out=ptb[:, i * 128:(i + 1) * 128], in_=ob[:, i * 128:(i + 1) * 128],
                                                            identity=pg.identb[:, :]), reads=[ob, pg.identb], writes=[ptb])
        kb.op("dve", lambda e, ptb=ptb: e.tensor_copy(out=oTs[:, :], in_=ptb[:, :]), reads=[ptb], writes=[oTs])
        kb.dma("sp", nm["oT_gla"][0][:, ts].rearrange("(k p) t -> p k t", p=128), oTs[:, :].rearrange("p (k t) -> p k t", t=128),
               reads=[oTs], writes=[pg.R(nm["oT_gla"][1])])
    kb.pop()


def emit_merge_block(pg, wk, b, nm):
    kb = pg.kb
    kb.barrier()
    load_ln_params(pg, wk, nm["ln_g1"][0], nm["ln_b1"][0], nm["ln_g1"][1], nm["ln_b1"][1])
    tok = slice(b * BLK, (b + 1) * BLK)
    xT = load_xT_block(pg, wk, nm["x1T"][0], nm["x1T"][1], b)
    oTm = View(wk.hT[:, 0:4096].rearrange("p (k t) -> p k t", t=BLK))
    oTd = View(wk.hT[:, 4096:8192].rearrange("p (k t) -> p k t", t=BLK))
    oTg0 = View(wk.xb[:, :].rearrange("p (k t) -> p k t", t=BLK))
    oTg1 = View(wk.xts[:, :].rearrange("p (k t) -> p k t", t=BLK))
    bm = View(wk.stats[0:1, 0:24])
    brow = View(wk.hT[0:1, 8192:8192 + 1024].bitcast(F32))
    kb.dma("sp", oTm[:, :, :], nm["oT_mla"][0][:, tok].rearrange("(k p) t -> p k t", p=128), reads=[pg.R(nm["oT_mla"][1])], writes=[oTm])
    kb.dma("sp", oTd[:, :, :], nm["oT_dsa"][0][:, tok].rearrange("(k p) t -> p k t", p=128), reads=[pg.R(nm["oT_dsa"][1])], writes=[oTd])
    kb.dma("sp", oTg0[:, :, :], nm["oT_gla"][0][0:512, tok].rearrange("(k p) t -> p k t", p=128), reads=[pg.R(nm["oT_gla"][1])], writes=[oTg0])
    kb.dma("sp", oTg1[:, :, :], nm["oT_gla"][0][512:1024, tok].rearrange("(k p) t -> p k t", p=128), reads=[pg.R(nm["oT_gla"][1])], writes=[oTg1])
    brs = [("w_br_mla", lambda kc: oTm[:, kc, :], [oTm]), ("w_br_dsa", lambda kc: oTd[:, kc, :], [oTd]),
           ("w_br_gla", lambda kc: (oTg0[:, kc, :] if kc < 4 else oTg1[:, kc - 4, :]), [oTg0, oTg1])]
    for n in range(4):
        for bi, (wname, osel, oregs) in enumerate(brs):
            sm_, wm = pg.load_w(nm["w_merge"][0], nm["w_merge"][1], 0, 16, bi * 2048 + n * 512, 512)
            sb_, wb = pg.load_w(nm[wname][0], nm[wname][1], 0, 8, n * 512, 512)
            c0 = bi * 2048 + n * 512
            kb.dma("sp", brow[:, :], nm["b_merge"][0][0:1, c0:c0 + 512], reads=[pg.R(nm["b_merge"][1])], writes=[brow])
            for t in range(4):
                pgt = pg.bank()
                for kc in range(16):
                    mm(pg, pgt[:, :], xT[:, kc, t * 128:(t + 1) * 128], wm[:, kc, :], kc == 0, False, [wk.xT, sm_], pgt)
                mm(pg, pgt[:, :], pg.consts[0:1, C_ONES:C_ONES + 128], brow[0:1, :], False, True, [pg.consts, brow], pgt)
                sgb = wk.sg[t % 2]
                kb.op("act", lambda e, pgt=pgt, sgb=sgb: e.activation(out=sgb[:, :], in_=pgt[:, :], func=AF.Sigmoid), reads=[pgt], writes=[sgb])
                py = pg.bank()
                for kc in range(8):
                    mm(pg, py[:, :], osel(kc)[:, t * 128:(t + 1) * 128], wb[:, kc, :], kc == 0, kc == 7, oregs + [sb_], py)
                ms = wk.v[t][:, n * 512:(n + 1) * 512]
                if bi == 0:
                    kb.op("dve", lambda e, py=py, sgb=sgb, ms=ms: e.tensor_tensor(out=ms, in0=py[:, :], in1=sgb[:, :], op=ALU.mult),
                          reads=[py, sgb], writes=[wk.v[t]])
                else:
                    kb.op("dve", lambda e, py=py, sgb=sgb: e.tensor_tensor(out=sgb[:, :], in0=py[:, :], in1=sgb[:, :], op=ALU.mult),
                          reads=[py], writes=[sgb])
                    kb.op("dve", lambda e, sgb=sgb, ms=ms: e.tensor_tensor(out=ms, in0=ms, in1=sgb[:, :], op=ALU.add),
                          reads=[sgb], writes=[wk.v[t]])
    kb.barrier()
    mT = View(wk.hT[:, 0:8192].rearrange("p (k t) -> p k t", t=BLK))
    for t in range(4):
        kb.op("act", lambda e, t=t: e.activation(out=wk.xb[:, :], in_=wk.v[t][:, :], func=AF.Copy), reads=[wk.v[t]], writes=[wk.xb])
        for half in range(2):
            ptb = pg.pt[half]
            for k in range(8):
                kc = half * 8 + k
                kb.op("pe", lambda e, kc=kc, k=k, ptb=ptb: e.transpose(out=ptb[:, k * 128:(k + 1) * 128], in_=wk.xb[:, kc * 128:(kc + 1) * 128],
                                                                      identity=pg.identb[:, :]), reads=[wk.xb, pg.identb], writes=[ptb])
            kb.op("dve", lambda e, ptb=ptb, half=half, t=t: e.tensor_copy(
                out=mT[:, half * 8:half * 8 + 8, t * 128:(t + 1) * 128], in_=ptb[:, :].rearrange("p (k t) -> p k t", t=128)),
                reads=[ptb], writes=[mT])
        r0 = b * BLK + t * 128
        kb.dma("sp", wk.v[t][:, :], nm["x1"][0][r0:r0 + 128, :], reads=[pg.R(nm["x1"][1])], writes=[wk.v[t]])
    for n in range(4):
        so_, wo = pg.load_w(nm["w_out"][0], nm["w_out"][1], 0, 16, n * 512, 512)
        for t in range(4):
            ph = pg.bank()
            for kc in range(16):
                mm(pg, ph[:, :], mT[:, kc, t * 128:(t + 1) * 128], wo[:, kc, :], kc == 0, kc == 15, [mT, so_], ph)
            vs = wk.v[t][:, n * 512:(n + 1) * 512]
            kb.op("dve", lambda e, ph=ph, vs=vs: e.scalar_tensor_tensor(out=vs, in0=vs, scalar=ALPHA, in1=ph[:, :], op0=ALU.mult, op1=ALU.add),
                  reads=[ph], writes=[wk.v[t]])
    kb.barrier()
    for t in range(4):
        r0 = b * BLK + t * 128
        emit_ln(pg, wk.v[t], wk.v[t][:, :], wk.gbc, wk.bbc, wk.st, nm["x2"][0][r0:r0 + 128, :], nm["x2"][1],
                nm["x2T"][0][:, r0:r0 + 128], nm["x2T"][1], wk.xb, wk.xts)


class XAState:
    def __init__(self, pg, wk, nm):
        kb = pg.kb
        kb.barrier()
        self.KxT = Buf(kb, "xaK", [128, 1024], BF16)
        self.Vx = Buf(kb, "xaV", [128, 1024], BF16)
        memT = View(wk.hT[:, 0:4096].rearrange("p (k m) -> p k m", m=256))
        for mt in range(2):
            vb = wk.v[mt]
            kb.dma("sp", vb[:, :], nm["mem"][0][mt * 128:(mt + 1) * 128, :], reads=[pg.R(nm["mem"][1])], writes=[vb])
            kb.op("act", lambda e, vb=vb: e.activation(out=wk.xb[:, :], in_=vb[:, :], func=AF.Copy), reads=[vb], writes=[wk.xb])
            for half in range(2):
                ptb = pg.pt[half]
                for k in range(8):
                    kc = half * 8 + k
                    kb.op("pe", lambda e, kc=kc, k=k, ptb=ptb: e.transpose(out=ptb[:, k * 128:(k + 1) * 128], in_=wk.xb[:, kc * 128:(kc + 1) * 128],
                                                                          identity=pg.identb[:, :]), reads=[wk.xb, pg.identb], writes=[ptb])
                kb.op("dve", lambda e, ptb=ptb, half=half, mt=mt: e.tensor_copy(
                    out=memT[:, half * 8:half * 8 + 8, mt * 128:(mt + 1) * 128], in_=ptb[:, :].rearrange("p (k t) -> p k t", t=128)),
                    reads=[ptb], writes=[memT])
        K3 = self.KxT[:, :].rearrange("p (h m) -> p h m", m=256)
        V3 = self.Vx[:, :].rearrange("p (a c) -> p a c", c=512)
        sk_, wkk = pg.load_w(nm["xa_w_kv"][0], nm["xa_w_kv"][1], 0, 16, 0, 512)
        for h in range(4):
            ps = pg.bank()
            for kc in range(16):
                mm(pg, ps[:, 0:256], wkk[:, kc, h * 128:(h + 1) * 128], memT[:, kc, :], kc == 0, kc == 15, [sk_, memT], ps)
            kb.op("act", lambda e, ps=ps, h=h: e.activation(out=K3[:, h, :], in_=ps[:, 0:256], func=AF.Copy), reads=[ps], writes=[self.KxT])
        sv_, wvv = pg.load_w(nm["xa_w_kv"][0], nm["xa_w_kv"][1], 0, 16, 512, 512)
        for mt in range(2):
            ps = pg.bank()
            for kc in range(16):
                mm(pg, ps[:, :], memT[:, kc, mt * 128:(mt + 1) * 128], wvv[:, kc, :], kc == 0, kc == 15, [sv_, memT], ps)
            kb.op("act", lambda e, ps=ps, mt=mt: e.activation(out=V3[:, mt, :], in_=ps[:, :], func=AF.Copy), reads=[ps], writes=[self.Vx])
        self.K3, self.V3 = K3, V3
        kb.barrier()


def emit_xattn_block(pg, wk, xa, b, nm):
    kb = pg.kb
    kb.barrier()
    load_ln_params(pg, wk, nm["ln_g2"][0], nm["ln_b2"][0], nm["ln_g2"][1], nm["ln_b2"][1])
    xT = load_xT_block(pg, wk, nm["x2T"][0], nm["x2T"][1], b)
    qT = View(wk.hT[:, 0:2048].rearrange("p (h t) -> p h t", t=BLK))
    oxT = View(wk.hT[:, 2048:4096].rearrange("p (h t) -> p h t", t=BLK))
    Pb = [View(wk.hT[:, 4096 + i * 256:4096 + (i + 1) * 256]) for i in range(2)]
    PTb = [View(wk.hT[:, 4608 + i * 256:4608 + (i + 1) * 256]) for i in range(2)]
    ob = View(wk.hT[:, 5120:5632])
    rs = View(wk.sg[0][:, 0:8])
    sq_, wq = pg.load_w(nm["xa_w_q"][0], nm["xa_w_q"][1], 0, 16, 0, 512)
    for h in range(4):
        ps = fm_proj(pg, wq, sq_, h * 128, 128, xT, wk.xT)
        kb.op("act", lambda e, ps=ps, h=h: e.activation(out=qT[:, h, :], in_=ps[:, :], func=AF.Copy), reads=[ps], writes=[qT])
    for t in range(4):
        r0 = b * BLK + t * 128
        kb.dma("sp", wk.v[t][:, :], nm["x2"][0][r0:r0 + 128, :], reads=[pg.R(nm["x2"][1])], writes=[wk.v[t]])
        for h in range(4):
            ps = pg.bank()
            mm(pg, ps[:, 0:256], qT[:, h, t * 128:(t + 1) * 128], xa.K3[:, h, :], True, True, [qT, xa.KxT], ps)
            p = Pb[h % 2]
            kb.op("act", lambda e, ps=ps, p=p, h=h: e.activation(out=p[:, :], in_=ps[:, 0:256], func=AF.Exp, scale=128.0 ** -0.5,
                                                                 accum_out=rs[:, h:h + 1]), reads=[ps], writes=[p, rs])
            ptb = pg.pt[h % 2]
            for i in range(2):
                kb.op("pe", lambda e, i=i, p=p, ptb=ptb: e.transpose(out=ptb[:, i * 128:(i + 1) * 128], in_=p[:, i * 128:(i + 1) * 128],
                                                                    identity=pg.identb[:, :]), reads=[p, pg.identb], writes=[ptb])
            pt_ = PTb[h % 2]
            kb.op("dve", lambda e, ptb=ptb, pt_=pt_: e.tensor_copy(out=pt_[:, :], in_=ptb[:, 0:256]), reads=[ptb], writes=[pt_])
            po = pg.bank()
            for i in range(2):
                mm(pg, po[:, 0:128], pt_[:, i * 128:(i + 1) * 128], xa.V3[:, i, h * 128:(h + 1) * 128], i == 0, i == 1, [pt_, xa.Vx], po)
            kb.op("dve", lambda e, h=h: e.reciprocal(out=rs[:, 4 + h:5 + h], in_=rs[:, h:h + 1]), reads=[rs], writes=[rs])
            kb.op("act", lambda e, po=po, h=h: e.activation(out=ob[:, h * 128:(h + 1) * 128], in_=po[:, 0:128], func=AF.Copy, scale=rs[:, 4 + h:5 + h]),
                  reads=[po, rs], writes=[ob])
        ptb = pg.pt[0]
        for h in range(4):
            kb.op("pe", lambda e, h=h, ptb=ptb: e.transpose(out=ptb[:, h * 128:(h + 1) * 128], in_=ob[:, h * 128:(h + 1) * 128],
                                                            identity=pg.identb[:, :]), reads=[ob, pg.identb], writes=[ptb])
        kb.op("dve", lambda e, ptb=ptb, t=t: e.tensor_copy(out=oxT[:, :, t * 128:(t + 1) * 128], in_=ptb[:, 0:512].rearrange("p (h t) -> p h t", t=128)),
              reads=[ptb], writes=[oxT])
    for n in range(4):
        so_, wo = pg.load_w(nm["xa_w_o"][0], nm["xa_w_o"][1], 0, 4, n * 512, 512)
        for t in range(4):
            ph = pg.bank()
            for kc in range(4):
                mm(pg, ph[:, :], oxT[:, kc, t * 128:(t + 1) * 128], wo[:, kc, :], kc == 0, kc == 3, [oxT, so_], ph)
            vs = wk.v[t][:, n * 512:(n + 1) * 512]
            kb.op("dve", lambda e, ph=ph, vs=vs: e.scalar_tensor_tensor(out=vs, in0=vs, scalar=ALPHA, in1=ph[:, :], op0=ALU.mult, op1=ALU.add),
                  reads=[ph], writes=[wk.v[t]])
    kb.barrier()
    for t in range(4):
        r0 = b * BLK + t * 128
        emit_ln(pg, wk.v[t], wk.v[t][:, :], wk.gbc, wk.bbc, wk.st, nm["x3"][0][r0:r0 + 128, :], nm["x3"][1],
                nm["x3T"][0][:, r0:r0 + 128], nm["x3T"][1], wk.xb, wk.xts)


A_OUT = [("x1", [TL, D], F32), ("x1T", [D, TL], BF16), ("QnT", [8, 128, TL], BF16), ("QrT", [8, 64, TL], BF16),
         ("QdT", [8, 128, TL], BF16), ("IqT", [8, 128, TL], BF16), ("iw", [TL, 16], F32), ("gqT", [4, 128, TL], BF16),
         ("gU0", [NT, 128, 1024], F32), ("gA", [NT, 128, 8], F32), ("goi", [TL, 1024], F32), ("gsr", [TL, 1024], BF16),
         ("KnT_o", [8, 128, TL], BF16), ("KrT_o", [64, TL], BF16), ("Vm_o", [TL, 1024], BF16), ("KdT_o", [8, 128, TL], BF16),
         ("Vd_o", [TL, 1024], BF16), ("IkT_o", [64, TL], BF16), ("gUt_o", [NT, 128, 1024], F32), ("gAt_o", [NT, 128, 4], F32)]
A_LOCAL = ["x1", "x1T", "QnT", "QrT", "QdT", "IqT", "iw", "gqT", "gU0", "gA", "goi", "gsr"]
B_GLOBAL = [("KnT_g", [8, 128, SEQ], BF16), ("KrT_g", [64, SEQ], BF16), ("Vm_g", [8, 128, SEQ], BF16), ("KdT_g", [8, 128, SEQ], BF16),
            ("Vd_g", [8, 128, SEQ], BF16), ("IkT_g", [64, SEQ], BF16), ("gUt_g", [SEQ // 128, 128, 1024], F32), ("gAt_g", [128, 512], F32)]
A_W = [("ffn1_g", [D, DFF]), ("ffn1_u", [D, DFF]), ("ffn1_d", [DFF, D]), ("ln_g0", [D]), ("ln_b0", [D]), ("w_in", [D, IN_W]),
       ("w_uq", [512, 1536]), ("w_ukv", [512, 2048]), ("sm", [128, 8]), ("wg2", [16, 512]), ("bg", [1, 512])]
B_W = [("w_br_mla", [1024, D]), ("w_br_dsa", [1024, D]), ("w_br_gla", [1024, D]), ("w_merge", [D, 3 * D]), ("b_merge", [1, 3 * D]),
       ("w_out", [D, D]), ("xa_w_q", [D, 512]), ("xa_w_kv", [D, 1024]), ("xa_w_o", [512, D]), ("mem", [256, D]),
       ("ln_g1", [D]), ("ln_b1", [D]), ("ln_g2", [D]), ("ln_b2", [D]), ("ffn2_g", [D, DFF]), ("ffn2_u", [D, DFF]), ("ffn2_d", [DFF, D]),
       ("ln_g3", [D]), ("ln_b3", [D]), ("gla_norm", [1024])]


def build_program(has_B, has_A, dbg_out=()):
    pg = Prog()
    kb = pg.kb
    nmB, nmA = {}, {}
    if has_B:
        for k in A_LOCAL:
            shape, dt = next((s, d) for (n, s, d) in A_OUT if n == k)
            nmB[k] = (pg.inp("i_" + k, shape, dt), "i_" + k)
        for (k, shape, dt) in B_GLOBAL:
            nmB[k] = (pg.inp(k, shape, dt), k)
        nmB["cmask"] = (pg.inp("cmask", [128, 1024], F32), "cmask")
        for (k, shape) in B_W:
            nmB[k] = (pg.inp("B_" + k, shape, F32), "B_" + k)
        for (k, shape, dt) in (("MnegD", [NT, 128, SEQ], BF16), ("oT_mla", [1024, TL], BF16), ("oT_dsa", [1024, TL], BF16),
                               ("oT_gla", [1024, TL], BF16), ("x2", [TL, D], F32), ("x2T", [D, TL], BF16),
                               ("x3", [TL, D], F32), ("x3T", [D, TL], BF16), ("x4T", [D, TL], BF16)):
            if k in dbg_out:
                nmB[k] = (pg.out(k, shape, dt), k)
            else:
                nmB[k] = (pg.scr(k, shape, dt), k)
        if has_A:
            nmB["x4"] = (pg.out("x4", [TL, D], F32), "x4") if "x4" in dbg_out else (pg.scr("x4", [TL, D], F32), "x4")
        else:
            nmB["x4"] = (pg.out("y", [TL, D], F32), "y")
    if has_A:
        for (k, shape) in A_W:
            nmA[k] = (pg.inp("A_" + k, shape, F32), "A_" + k)
        nmA["pos"] = (pg.inp("pos", [TL], I32), "pos")
        nmA["tabs"] = (pg.scr("tabs", [4, 128, TL], F32), "tabs")
        for (k, shape, dt) in A_OUT:
            nmA[k] = (pg.out("o_" + k, shape, dt), "o_" + k)
        nmA["xT"] = nmA["x1T"]
        if not has_B:
            nmA["x0"] = (pg.inp("x_in", [TL, D], F32), "x_in")
            nmA["x0T"] = (pg.scr("x0T", [D, TL], BF16), "x0T")
        else:
            nmA["x0"] = nmB["x4"]
            nmA["x0T"] = nmB["x4T"]
    if has_B:
        emit_indexer(pg, nmB)
        emit_attention(pg, nmB, "mla")
        emit_attention(pg, nmB, "dsa")
        emit_gla_out(pg, nmB)
    kb.push()
    wk = Work(pg)
    if has_A:
        mp = MixParams(pg, "A")
        mp.load(pg, nmA["sm"][0], nmA["sm"][1], nmA["wg2"][0], nmA["wg2"][1], nmA["bg"][0], nmA["bg"][1])
        emit_rope_tables(pg, wk, nmA["pos"][0], nmA["pos"][1], nmA["tabs"][0], nmA["tabs"][1])
        if not has_B:
            emit_prep_xT(pg, wk, nmA["x0"], nmA["x0T"])
    if has_B:
        xa = XAState(pg, wk, nmB)
        for b in range(NB):
            emit_merge_block(pg, wk, b, nmB)
        for b in range(NB):
            emit_xattn_block(pg, wk, xa, b, nmB)
        for b in range(NB):
            emit_ffn_block(pg, wk, b, nmB["x3"], nmB["x3T"], nmB["ffn2_g"], nmB["ffn2_u"], nmB["ffn2_d"],
                           (nmB["ln_g3"], nmB["ln_b3"]), nmB["x4"], nmB["x4T"])
    if has_A:
        for b in range(NB):
            emit_ffn_block(pg, wk, b, nmA["x0"], nmA["x0T"], nmA["ffn1_g"], nmA["ffn1_u"], nmA["ffn1_d"],
                           (nmA["ln_g0"], nmA["ln_b0"]), nmA["x1"], nmA["x1T"])
        for b in range(NB):
            emit_mixproj_block(pg, wk, mp, b, {"n": nmA})
    kb.pop()
    kb.finish()
    return pg


def _loc(a, c):
    return np.ascontiguousarray(a.reshape(NT, NCORES, 128, *a.shape[1:])[:, c].reshape(TL, *a.shape[1:]))


def _glob_cols(parts):
    lead = parts[0].shape[:-1]
    st = np.stack([p.reshape(*lead, NT, 128) for p in parts], axis=-2)
    return np.ascontiguousarray(st.reshape(*lead, SEQ))


def _glob_rows(parts):
    f = parts[0].shape[1:]
    st = np.stack([p.reshape(NT, 128, *f) for p in parts], axis=1)
    return np.ascontiguousarray(st.reshape(SEQ, *f))


def _vlay(v):
    return np.ascontiguousarray(v.reshape(SEQ // 128, 128, 8, 128).transpose(2, 1, 0, 3).reshape(8, 128, SEQ))


def _a_weights(inp, l):
    sm = np.zeros((128, 8), np.float32)
    sm[:, 0:4] = inp["mla_q_norm"][l].reshape(4, 128).T
    sm[:, 4:8] = inp["mla_kv_norm"][l].reshape(4, 128).T
    return {"A_ffn1_g": inp["ffn_w_gate"][l, 0], "A_ffn1_u": inp["ffn_w_up"][l, 0], "A_ffn1_d": inp["ffn_w_down"][l, 0],
            "A_ln_g0": inp["ln_gain"][l, 0], "A_ln_b0": inp["ln_bias"][l, 0], "A_w_in": inp["w_in"][l],
            "A_w_uq": inp["mla_w_uq"][l], "A_w_ukv": inp["mla_w_ukv"][l], "A_sm": sm,
            "A_wg2": inp["gla_w_gate2"][l], "A_bg": inp["gla_b_gate"][l][None, :]}


def _b_weights(inp, l):
    return {"B_w_br_mla": inp["w_branch_mla"][l], "B_w_br_dsa": inp["w_branch_dsa"][l], "B_w_br_gla": inp["w_branch_gla"][l],
            "B_w_merge": inp["w_merge"][l], "B_b_merge": inp["b_merge"][l][None, :], "B_w_out": inp["w_out"][l],
            "B_xa_w_q": inp["xa_w_q"][l], "B_xa_w_kv": inp["xa_w_kv"][l], "B_xa_w_o": inp["xa_w_o"][l], "B_mem": inp["mem"][0],
            "B_ln_g1": inp["ln_gain"][l, 1], "B_ln_b1": inp["ln_bias"][l, 1], "B_ln_g2": inp["ln_gain"][l, 2], "B_ln_b2": inp["ln_bias"][l, 2],
            "B_ffn2_g": inp["ffn_w_gate"][l, 1], "B_ffn2_u": inp["ffn_w_up"][l, 1], "B_ffn2_d": inp["ffn_w_down"][l, 1],
            "B_ln_g3": inp["ln_gain"][l, 3], "B_ln_b3": inp["ln_bias"][l, 3], "B_gla_norm": inp["gla_norm"][l]}


def _gather(res):
    loc = [{"i_" + k: r["o_" + k] for k in A_LOCAL} for r in res]
    g = {"KnT_g": _glob_cols([r["o_KnT_o"] for r in res]), "KrT_g": _glob_cols([r["o_KrT_o"] for r in res]),
         "KdT_g": _glob_cols([r["o_KdT_o"] for r in res]), "IkT_g": _glob_cols([r["o_IkT_o"] for r in res]),
         "Vm_g": _vlay(_glob_rows([r["o_Vm_o"] for r in res])), "Vd_g": _vlay(_glob_rows([r["o_Vd_o"] for r in res]))}
    ut = np.stack([r["o_gUt_o"] for r in res], axis=1)
    g["gUt_g"] = np.ascontiguousarray(ut.reshape(SEQ // 128, 128, 1024))
    at = np.stack([r["o_gAt_o"] for r in res], axis=1)
    g["gAt_g"] = np.ascontiguousarray(at.reshape(SEQ // 128, 128, 4).transpose(1, 0, 2).reshape(128, 512))
    return loc, g


_PROGS = {}


def _prog(has_B, has_A):
    key = (has_B, has_A)
    if key not in _PROGS:
        _PROGS[key] = build_program(has_B, has_A)
    return _PROGS[key]


def kernel(**inp):
    inp = {k: np.asarray(v) for k, v in inp.items()}
    cores = list(range(NCORES))
    consts = [make_consts(c) for c in cores]
    cmask = [make_cmask(c) for c in cores]
    pos = [_loc(inp["positions"][0].astype(np.int32), c) for c in cores]
    pg = _prog(False, True)
    wa = _a_weights(inp, 0)
    maps = [dict(wa, consts=consts[c], pos=pos[c], x_in=_loc(inp["x"][0], c)) for c in cores]
    res = run_bass_kernel_spmd(pg.nc, maps, core_ids=cores).results
    loc, g = _gather(res)
    pg = _prog(True, True)
    wb, wa = _b_weights(inp, 0), _a_weights(inp, 1)
    maps = [dict(wb, **wa, **g, **loc[c], consts=consts[c], cmask=cmask[c], pos=pos[c]) for c in cores]
    res = run_bass_kernel_spmd(pg.nc, maps, core_ids=cores).results
    loc, g = _gather(res)
    pg = _prog(True, False)
    wb = _b_weights(inp, 1)
    maps = [dict(wb, **g, **loc[c], consts=consts[c], cmask=cmask[c]) for c in cores]
    res = run_bass_kernel_spmd(pg.nc, maps, core_ids=cores).results
    y = _glob_rows([r["y"] for r in res])
    return y[None].astype(np.float32)
```

```python
import contextlib
import math
import numpy as np
import ml_dtypes
import concourse.bass as bass
import concourse.mybir as mybir
from concourse.bass_utils import run_bass_kernel_spmd

F32 = mybir.dt.float32
BF16 = mybir.dt.bfloat16
I32 = mybir.dt.int32
AF = mybir.ActivationFunctionType
ALU = mybir.AluOpType
NPBF = ml_dtypes.bfloat16

NCORES = 8
SEQ = 16384
D = 2048
DFF = 5632
TL = SEQ // NCORES
NT = TL // 128
BLK = 512
NB = TL // BLK
ALPHA = 4.0 ** 0.25
EPS = 1e-5
NEG = -30000.0
IN_W = 8352

ENG_NAMES = ["pe", "act", "dve", "pool", "sp"]


class Reg:
    __slots__ = ("w", "r")

    def __init__(self):
        self.w = None
        self.r = {}


class Buf:
    _uid = [0]

    def __init__(self, kb, name, shape, dt, psum=False):
        Buf._uid[0] += 1
        name = f"{name}_{Buf._uid[0]}"
        if psum:
            self.t = kb.scopes[-1].enter_context(kb.nc.psum_tensor(name, shape, dt))
            self.psum = True
        else:
            self.psum = False
            self.t = kb.scopes[-1].enter_context(kb.nc.sbuf_tensor(name, shape, dt))
        self.r = Reg()

    def __getitem__(self, idx):
        return self.t[idx]


class View:
    def __init__(self, ap):
        self.ap = ap
        self.r = Reg()

    def __getitem__(self, idx):
        return self.ap[idx]


class KB:
    def __init__(self, nc):
        self.nc = nc
        self.es = contextlib.ExitStack()
        self.scopes = [self.es]
        engs = [nc.tensor, nc.scalar, nc.vector, nc.gpsimd, nc.sync]
        self.eng = dict(zip(ENG_NAMES, engs))
        self.idx = {n: i for i, n in enumerate(ENG_NAMES)}
        self.sems = [self.es.enter_context(nc.semaphore("s_" + n)) for n in ENG_NAMES]
        self.cnt = [0] * len(ENG_NAMES)
        self.waited = [dict() for _ in ENG_NAMES]
        self.dpool = {}
        for q, n in (("sp", 24), ("pool", 16), ("act", 8)):
            lst = []
            for i in range(n):
                self.sems.append(self.es.enter_context(nc.semaphore(f"d_{q}{i}")))
                self.cnt.append(0)
                lst.append(len(self.sems) - 1)
            self.dpool[q] = [lst, 0]
        self.ninstr = 0

    def _wait(self, ei, deps):
        w = self.waited[ei]
        for (si, v) in deps.items():
            if si == ei and ei == 0:
                continue
            if w.get(si, 0) < v:
                self.eng[ENG_NAMES[ei]].wait_ge(self.sems[si], v)
                w[si] = v

    @staticmethod
    def _deps(reads, writes):
        deps = {}

        def add(t):
            if t is not None and deps.get(t[0], 0) < t[1]:
                deps[t[0]] = t[1]
        for r in reads:
            add(r.w)
        for r in writes:
            add(r.w)
            for si, v in r.r.items():
                add((si, v))
        return deps

    @staticmethod
    def _mark(t, reads, writes):
        for r in reads:
            if r.r.get(t[0], 0) < t[1]:
                r.r[t[0]] = t[1]
        for r in writes:
            r.w = t
            r.r = {}

    def op(self, eng, fn, reads=(), writes=()):
        if self.ninstr >= getattr(self, "maxops", 1 << 60):
            return None
        ei = self.idx[eng]
        writes = list(writes) + [x for x in reads if isinstance(x, Buf) and x.psum]
        reads = [x for x in reads if not (isinstance(x, Buf) and x.psum)]
        reads = [x if isinstance(x, Reg) else x.r for x in reads]
        writes = [x if isinstance(x, Reg) else x.r for x in writes]
        self._wait(ei, self._deps(reads, writes))
        ins = fn(self.eng[eng])
        self.cnt[ei] += 1
        ins.then_inc(self.sems[ei], 1)
        self._mark((ei, self.cnt[ei]), reads, writes)
        self.ninstr += 1
        return ins

    def dma(self, q, out, in_, reads=(), writes=()):
        if self.ninstr >= getattr(self, "maxops", 1 << 60):
            return None
        ei = self.idx[q]
        reads = [x if isinstance(x, Reg) else x.r for x in reads]
        writes = [x if isinstance(x, Reg) else x.r for x in writes]
        lst, nxt = self.dpool[q]
        si = lst[nxt]
        self.dpool[q][1] = (nxt + 1) % len(lst)
        deps = self._deps(reads, writes)
        if self.cnt[si] > 0 and deps.get(si, 0) < self.cnt[si]:
            deps[si] = self.cnt[si]
        self._wait(ei, deps)
        self.cnt[si] += 16
        self.eng[q].dma_start(out=out, in_=in_).then_inc(self.sems[si], 16)
        self._mark((si, self.cnt[si]), reads, writes)
        self.ninstr += 1

    def push(self):
        self.barrier()
        self.scopes.append(contextlib.ExitStack())

    def pop(self):
        self.barrier()
        self.scopes.pop().close()

    def barrier(self):
        for ei in range(len(ENG_NAMES)):
            deps = {si: v for si, v in enumerate(self.cnt) if v > 0 and si != ei}
            self._wait(ei, deps)

    def finish(self):
        deps = {si: v for si, v in enumerate(self.cnt) if v > 0 and si != self.idx["sp"]}
        self._wait(self.idx["sp"], deps)
        self.es.close()


C_ID, C_ONES, C_TRI, C_RTRI, C_MASKF, C_P128, C_P64, C_CH, C_F64, C_F128, C_EI, C_EPS, C_END = (
    0, 128, 256, 384, 512, 640, 768, 896, 898, 899, 900, 908, 909)


def make_consts(core):
    c = np.zeros((128, C_END), np.float32)
    c[:, C_ID:C_ID + 128] = np.eye(128)
    c[:, C_ONES:C_ONES + 128] = 1.0
    s = np.arange(128)[:, None]
    t = np.arange(128)[None, :]
    same = (s // 64) == (t // 64)
    c[:, C_TRI:C_TRI + 128] = (same & (s <= t))
    c[:, C_RTRI:C_RTRI + 128] = (same & (s > t))
    c[:, C_MASKF:C_MASKF + 128] = (same & (s <= t))
    for half, col in ((64, C_P128), (32, C_P64)):
        p = np.zeros((128, 128), np.float32)
        for m in range(128):
            if (m % (2 * half)) < half:
                p[m + half, m] = -1.0
            else:
                p[m - half, m] = 1.0
        c[:, col:col + 128] = p
    c[:, C_CH] = (np.arange(128) < 64)
    c[:, C_CH + 1] = (np.arange(128) >= 64)
    f64 = (1.0 / (np.float32(10000.0) ** (np.arange(0, 64, 2, dtype=np.float32) / np.float32(64)))).astype(np.float32)
    f128 = (1.0 / (np.float32(10000.0) ** (np.arange(0, 128, 2, dtype=np.float32) / np.float32(128)))).astype(np.float32)
    c[:, C_F64] = f64[np.arange(128) % 32]
    c[:, C_F128] = f128[np.arange(128) % 64]
    c[:, C_EI + core] = 1.0
    c[:, C_EPS] = EPS
    return c


def make_cmask(core):
    m = np.zeros((128, 1024), np.float32)
    q = np.arange(128)[:, None]
    for i in range(8):
        blk = m[:, i * 128:(i + 1) * 128]
        if i > core:
            blk[:] = NEG
        elif i == core:
            blk[np.arange(128)[None, :] > q] = NEG
    return m


class Prog:
    def __init__(self):
        self.nc = bass.Bass("TRN2", target_bir_lowering=False)
        self.kb = KB(self.nc)
        self.in_names = []
        self.out_names = []
        self.dreg = {}
        kb = self.kb
        self.consts_d = self.inp("consts", [128, C_END], F32)
        self.consts = Buf(kb, "consts_sb", [128, C_END], F32)
        kb.dma("sp", self.consts[:, :], self.consts_d[:, :], writes=[self.consts])
        self.identb = Buf(kb, "identb", [128, 128], BF16)
        kb.op("dve", lambda e: e.tensor_copy(out=self.identb[:, :], in_=self.consts[:, C_ID:C_ID + 128]),
              reads=[self.consts], writes=[self.identb])
        self.pb = [Buf(kb, f"pb{i}", [128, 512], F32, psum=True) for i in range(6)]
        self.pt = [Buf(kb, f"pt{i}", [128, 1024], BF16, psum=True) for i in range(2)]
        self.ws = None
        self.wsn = 0
        self.pbn = 0
        self.uid = 0

    def _dt(self, name, shape, dt, kind):
        t = self.nc.dram_tensor(name, list(shape), dt, kind=kind)
        ap = t.ap()
        self.dreg[name] = Reg()
        return ap

    def inp(self, name, shape, dt):
        self.in_names.append(name)
        return self._dt(name, shape, dt, "ExternalInput")

    def out(self, name, shape, dt):
        self.out_names.append(name)
        return self._dt(name, shape, dt, "ExternalOutput")

    def scr(self, name, shape, dt):
        return self._dt(name, shape, dt, "Internal")

    def R(self, name):
        return self.dreg[name]

    def cc(self, col, n=128, rows=128):
        return self.consts[0:rows, col:col + n]

    def load_w(self, w_ap, wname, k0, kc, c0, n):
        slot = self.ws[self.wsn]
        self.wsn = (self.wsn + 1) % len(self.ws)
        assert kc * n <= 8192
        view = slot[:, 0:kc * n].rearrange("p (k n) -> p k n", n=n)
        src = w_ap[k0:k0 + kc * 128, c0:c0 + n].rearrange("(k p) n -> p k n", p=128)
        self.kb.dma("pool", view, src, reads=[self.R(wname)], writes=[slot])
        return slot, view

    def bank(self):
        b = self.pb[self.pbn]
        self.pbn = (self.pbn + 1) % 4
        return b


def mm(pg, out_ap, lhsT, rhs, start, stop, reads, wbank):
    pg.kb.op("pe", lambda e: e.matmul(out_ap, lhsT=lhsT, rhs=rhs, start=start, stop=stop),
             reads=reads, writes=[wbank])


def emit_transpose_out(pg, src_buf, src_ap, ncol_chunks, xb, xts, dst_ap, dst_name, q="sp"):
    kb = pg.kb
    kb.op("act", lambda e: e.activation(out=xb[:, 0:ncol_chunks * 128], in_=src_ap, func=AF.Copy),
          reads=[src_buf], writes=[xb])
    for half in range((ncol_chunks + 7) // 8):
        n = min(8, ncol_chunks - half * 8)
        ptb = pg.pt[half % 2]
        for k in range(n):
            kc = half * 8 + k
            kb.op("pe", lambda e, kc=kc, k=k, ptb=ptb: e.transpose(out=ptb[:, k * 128:(k + 1) * 128],
                                                                    in_=xb[:, kc * 128:(kc + 1) * 128],
                                                                    identity=pg.identb[:, :]),
                  reads=[xb, pg.identb], writes=[ptb])
        kb.op("dve", lambda e, ptb=ptb, half=half, n=n: e.tensor_copy(
            out=xts[:, half * 1024:half * 1024 + n * 128], in_=ptb[:, 0:n * 128]),
            reads=[ptb], writes=[xts])
    kb.dma(q, dst_ap.rearrange("(k p) t -> p k t", p=128),
           xts[:, 0:ncol_chunks * 128].rearrange("p (k t) -> p k t", t=128),
           reads=[xts], writes=[pg.R(dst_name)])


def emit_ln(pg, vbuf, vap, gbc, bbc, st, x_out_ap, x_out_name, xT_out_ap, xT_out_name, xb, xts):
    kb = pg.kb
    stats, mv, rstd = st
    for c in range(4):
        kb.op("dve", lambda e, c=c: e.bn_stats(out=stats[:, c * 6:(c + 1) * 6], in_=vap[:, c * 512:(c + 1) * 512]),
              reads=[vbuf], writes=[stats])
    kb.op("dve", lambda e: e.bn_aggr(out=mv[:, 0:2], in_=stats[:, 0:24]), reads=[stats], writes=[mv])
    kb.op("act", lambda e: e.activation(out=rstd[:, 0:1], in_=mv[:, 1:2], func=AF.Ln, bias=pg.cc(C_EPS, 1), scale=1.0),
          reads=[mv, pg.consts], writes=[rstd])
    kb.op("act", lambda e: e.activation(out=rstd[:, 1:2], in_=rstd[:, 0:1], func=AF.Exp, scale=-0.5),
          reads=[rstd], writes=[rstd])
    kb.op("dve", lambda e: e.tensor_scalar(out=vap, in0=vap, scalar1=mv[:, 0:1], scalar2=rstd[:, 1:2],
                                           op0=ALU.subtract, op1=ALU.mult),
          reads=[vbuf, mv, rstd], writes=[vbuf])
    kb.op("dve", lambda e: e.tensor_tensor(out=vap, in0=vap, in1=gbc[:, :], op=ALU.mult), reads=[vbuf, gbc], writes=[vbuf])
    kb.op("dve", lambda e: e.tensor_tensor(out=vap, in0=vap, in1=bbc[:, :], op=ALU.add), reads=[vbuf, bbc], writes=[vbuf])
    kb.dma("sp", x_out_ap, vap, reads=[vbuf], writes=[pg.R(x_out_name)])
    emit_transpose_out(pg, vbuf, vap, 16, xb, xts, xT_out_ap, xT_out_name)


class Work:
    def __init__(self, pg):
        kb = pg.kb
        pg.uid += 1
        u = str(pg.uid)
        pg.ws = [Buf(kb, f"ws{i}_" + u, [128, 8192], BF16) for i in range(3)]
        self.v = [Buf(kb, f"v{i}", [128, D], F32) for i in range(4)]
        self.xT = Buf(kb, "xTblk", [128, 16 * BLK], BF16)
        self.hT = Buf(kb, "hT", [128, 22 * BLK], BF16)
        self.gbc = Buf(kb, "gbc", [128, D], F32)
        self.bbc = Buf(kb, "bbc", [128, D], F32)
        self.sg = [Buf(kb, f"sg{i}", [128, BLK], F32) for i in range(2)]
        self.xb = Buf(kb, "xb", [128, D], BF16)
        self.xts = Buf(kb, "xts", [128, D], BF16)
        self.stats = Buf(kb, "stats", [128, 24], F32)
        self.mv = Buf(kb, "mv", [128, 2], F32)
        self.rstd = Buf(kb, "rstd", [128, 2], F32)
        self.st = (self.stats, self.mv, self.rstd)


def load_xT_block(pg, wk, xT_ap, xT_name, b):
    view = wk.xT[:, :].rearrange("p (k t) -> p k t", t=BLK)
    pg.kb.dma("sp", view, xT_ap[:, b * BLK:(b + 1) * BLK].rearrange("(k p) t -> p k t", p=128),
              reads=[pg.R(xT_name)], writes=[wk.xT])
    return view


def load_ln_params(pg, wk, g_ap, b_ap, gname, bname):
    pg.kb.dma("sp", wk.gbc[:, :], g_ap.partition_broadcast(128), reads=[pg.R(gname)], writes=[wk.gbc])
    pg.kb.dma("sp", wk.bbc[:, :], b_ap.partition_broadcast(128), reads=[pg.R(bname)], writes=[wk.bbc])


def emit_ffn_block(pg, wk, b, x_in, xT_in, wg, wu, wd, ln, x_out, xT_out):
    kb = pg.kb
    kb.barrier()
    load_ln_params(pg, wk, ln[0][0], ln[1][0], ln[0][1], ln[1][1])
    xT = load_xT_block(pg, wk, xT_in[0], xT_in[1], b)
    for t in range(4):
        kb.dma("sp", wk.v[t][:, :], x_in[0][b * BLK + t * 128: b * BLK + (t + 1) * 128, :],
               reads=[pg.R(x_in[1])], writes=[wk.v[t]])
    hT = wk.hT[:, :].rearrange("p (f t) -> p f t", t=BLK)
    for half in range(2):
        for grp in range(6):
            nf = 4 if grp < 5 else 2
            f0 = half * 22 + grp * 4
            sg_, wgv = pg.load_w(wg[0], wg[1], 0, 16, f0 * 128, nf * 128)
            su_, wuv = pg.load_w(wu[0], wu[1], 0, 16, f0 * 128, nf * 128)
            for fi in range(nf):
                pgate = pg.bank()
                pup = pg.bank()
                for kc in range(16):
                    mm(pg, pgate[:, :], wgv[:, kc, fi * 128:(fi + 1) * 128], xT[:, kc, :], kc == 0, kc == 15,
                       [sg_, wk.xT], pgate)
                for kc in range(16):
                    mm(pg, pup[:, :], wuv[:, kc, fi * 128:(fi + 1) * 128], xT[:, kc, :], kc == 0, kc == 15,
                       [su_, wk.xT], pup)
                sgb = wk.sg[(grp * 4 + fi) % 2]
                kb.op("act", lambda e, pgate=pgate, sgb=sgb: e.activation(out=sgb[:, :], in_=pgate[:, :], func=AF.Silu),
                      reads=[pgate], writes=[sgb])
                fl = grp * 4 + fi
                kb.op("dve", lambda e, pup=pup, sgb=sgb, fl=fl: e.scalar_tensor_tensor(
                    out=hT[:, fl, :], in0=pup[:, :], scalar=0.5, in1=sgb[:, :], op0=ALU.mult, op1=ALU.mult),
                    reads=[pup, sgb], writes=[wk.hT])
        for n in range(4):
            s0, w0 = pg.load_w(wd[0], wd[1], (half * 22) * 128, 11, n * 512, 512)
            s1, w1 = pg.load_w(wd[0], wd[1], (half * 22 + 11) * 128, 11, n * 512, 512)
            for t in range(4):
                pd = pg.bank()
                for q_, (s_, w_) in enumerate(((s0, w0), (s1, w1))):
                    for fc in range(11):
                        f = q_ * 11 + fc
                        mm(pg, pd[:, :], hT[:, f, t * 128:(t + 1) * 128], w_[:, fc, :], f == 0, f == 21, [wk.hT, s_], pd)
                vslice = wk.v[t][:, n * 512:(n + 1) * 512]
                kb.op("dve", lambda e, pd=pd, vslice=vslice, half=half: e.scalar_tensor_tensor(
                    out=vslice, in0=vslice, scalar=(ALPHA if half == 0 else 1.0), in1=pd[:, :], op0=ALU.mult, op1=ALU.add),
                    reads=[pd, wk.v[t]], writes=[wk.v[t]])
    for t in range(4):
        r0 = b * BLK + t * 128
        emit_ln(pg, wk.v[t], wk.v[t][:, :], wk.gbc, wk.bbc, wk.st,
                x_out[0][r0:r0 + 128, :], x_out[1], xT_out[0][:, r0:r0 + 128], xT_out[1], wk.xb, wk.xts)


def emit_prep_xT(pg, wk, x_in, xT_out):
    for j in range(NT):
        vb = wk.v[j % 4]
        pg.kb.dma("sp", vb[:, :], x_in[0][j * 128:(j + 1) * 128, :], reads=[pg.R(x_in[1])], writes=[vb])
        emit_transpose_out(pg, vb, vb[:, :], 16, wk.xb, wk.xts, xT_out[0][:, j * 128:(j + 1) * 128], xT_out[1])


TWO_PI = 2.0 * math.pi
CW1 = 6.28125
CW2 = TWO_PI - 6.28125


def emit_rope_tables(pg, wk, pos_ap, pos_name, tabs, tabs_name):
    kb = pg.kb
    posf, ang, kf, tmp = wk.v[0], wk.v[1], wk.v[2], wk.v[3]
    ki = View(wk.xT[:, 0:2 * TL].bitcast(I32))
    kb.dma("pool", posf[:, :], pos_ap.partition_broadcast(128), reads=[pg.R(pos_name)], writes=[posf])
    PI = math.pi

    def fold(buf):
        kb.op("dve", lambda e: e.tensor_scalar(out=tmp[:, :], in0=buf[:, :], scalar1=PI, scalar2=-TWO_PI,
                                               op0=ALU.is_gt, op1=ALU.mult), reads=[buf], writes=[tmp])
        kb.op("dve", lambda e: e.tensor_tensor(out=buf[:, :], in0=buf[:, :], in1=tmp[:, :], op=ALU.add),
              reads=[buf, tmp], writes=[buf])
        kb.op("dve", lambda e: e.tensor_scalar(out=tmp[:, :], in0=buf[:, :], scalar1=-PI, scalar2=TWO_PI,
                                               op0=ALU.is_lt, op1=ALU.mult), reads=[buf], writes=[tmp])
        kb.op("dve", lambda e: e.tensor_tensor(out=buf[:, :], in0=buf[:, :], in1=tmp[:, :], op=ALU.add),
              reads=[buf, tmp], writes=[buf])
        kb.op("dve", lambda e: e.tensor_scalar(out=buf[:, :], in0=buf[:, :], scalar1=-PI, scalar2=PI,
                                               op0=ALU.max, op1=ALU.min), reads=[buf], writes=[buf])

    for ti, fcol in ((0, C_F64), (1, C_F128)):
        kb.op("dve", lambda e: e.tensor_scalar(out=ang[:, :], in0=posf[:, :], scalar1=pg.cc(fcol, 1), scalar2=None,
                                               op0=ALU.mult), reads=[posf, pg.consts], writes=[ang])
        kb.op("dve", lambda e: e.tensor_scalar(out=ki[:, :], in0=ang[:, :], scalar1=1.0 / TWO_PI, scalar2=None,
                                               op0=ALU.mult), reads=[ang], writes=[ki, wk.xT])
        kb.op("dve", lambda e: e.tensor_copy(out=kf[:, :], in_=ki[:, :]), reads=[ki, wk.xT], writes=[kf])
        kb.op("dve", lambda e: e.scalar_tensor_tensor(out=ang[:, :], in0=kf[:, :], scalar=-CW1, in1=ang[:, :],
                                                      op0=ALU.mult, op1=ALU.add), reads=[kf, ang], writes=[ang])
        kb.op("dve", lambda e: e.scalar_tensor_tensor(out=ang[:, :], in0=kf[:, :], scalar=-CW2, in1=ang[:, :],
                                                      op0=ALU.mult, op1=ALU.add), reads=[kf, ang], writes=[ang])
        fold(ang)
        kb.op("act", lambda e: e.activation(out=kf[:, :], in_=ang[:, :], func=AF.Sin), reads=[ang], writes=[kf])
        kb.dma("sp", tabs[2 * ti + 1], kf[:, :], reads=[kf], writes=[pg.R(tabs_name)])
        kb.op("dve", lambda e: e.tensor_scalar(out=ang[:, :], in0=ang[:, :], scalar1=PI / 2, scalar2=None, op0=ALU.add),
              reads=[ang], writes=[ang])
        fold(ang)
        kb.op("act", lambda e: e.activation(out=kf[:, :], in_=ang[:, :], func=AF.Sin), reads=[ang], writes=[kf])
        kb.dma("sp", tabs[2 * ti], kf[:, :], reads=[kf], writes=[pg.R(tabs_name)])
    kb.barrier()


O_CQ, O_CKV, O_KR, O_DQ, O_DK, O_DV, O_IQ, O_IK, O_IW, O_GQ, O_GK, O_GV, O_GLR, O_GR = (
    0, 512, 1024, 1088, 2112, 3136, 4160, 5184, 5248, 5264, 5776, 6288, 7312, 7328)


class MixParams:
    def __init__(self, pg, tag):
        kb = pg.kb
        self.sm = Buf(kb, "sm" + tag, [128, 8], F32)
        self.wg2 = Buf(kb, "wg2" + tag, [16, 512], F32)
        self.bg = Buf(kb, "bg" + tag, [1, 512], F32)

    def load(self, pg, sm_ap, sm_name, wg2_ap, wg2_name, bg_ap, bg_name):
        kb = pg.kb
        kb.dma("sp", self.sm[:, :], sm_ap[:, :], reads=[pg.R(sm_name)], writes=[self.sm])
        kb.dma("sp", self.wg2[:, :], wg2_ap[:, :], reads=[pg.R(wg2_name)], writes=[self.wg2])
        kb.dma("sp", self.bg[:, :], bg_ap[:, :], reads=[pg.R(bg_name)], writes=[self.bg])


def fm_proj(pg, wview, wslot, c0, M, xT, xTbuf, K=16):
    ps = pg.bank()
    for kc in range(K):
        mm(pg, ps[0:M, :], wview[:, kc, c0:c0 + M], xT[:, kc, :], kc == 0, kc == K - 1, [wslot, xTbuf], ps)
    return ps


def tm_proj(pg, wview_cols, wslot, N, xT, xTbuf, t, K=16, outv=None):
    ps = pg.bank()
    o = ps[:, 0:N] if outv is None else outv(ps)
    for kc in range(K):
        mm(pg, o, xT[:, kc, t * 128:(t + 1) * 128], wview_cols(kc), kc == 0, kc == K - 1, [wslot, xTbuf], ps)
    return ps


def emit_mixproj_block(pg, wk, mp, b, L):
    kb = pg.kb
    kb.barrier()
    nm = L["n"]
    tok = slice(b * BLK, (b + 1) * BLK)
    xT = load_xT_block(pg, wk, nm["xT"][0], nm["xT"][1], b)
    xTb = wk.xT
    tab = View(wk.v[0][:, :].rearrange("p (a t) -> p a t", t=BLK))
    kb.dma("sp", tab[:, :, :], nm["tabs"][0][:, :, tok].rearrange("a p t -> p a t"), reads=[pg.R(nm["tabs"][1])], writes=[tab])
    C64, S64, C128, S128 = (tab[:, i, :] for i in range(4))
    cT = View(wk.v[1][:, :].rearrange("p (a t) -> p a t", t=BLK))
    sq = View(wk.v[2][:, :].rearrange("p (a t) -> p a t", t=BLK))
    rstd = View(wk.v[3][:, 0:512])
    xs = View(wk.v[3][:, 512:1024])
    t1 = View(wk.v[3][:, 1024:1536])
    cnT = View(wk.hT[:, 0:2048].rearrange("p (a t) -> p a t", t=BLK))
    stg = [View(wk.hT[:, 2048 + i * 512: 2048 + (i + 1) * 512]) for i in range(4)]
    stgn = [0]

    def stage():
        s = stg[stgn[0] % 4]
        stgn[0] += 1
        return s

    def store_fm(ps, M, dst_ap, dst_name):
        s = stage()
        kb.op("act", lambda e: e.activation(out=s[0:M, :], in_=ps[0:M, :], func=AF.Copy), reads=[ps], writes=[s])
        kb.dma("sp", dst_ap, s[0:M, :], reads=[s], writes=[pg.R(dst_name)])

    def rope_store(ps, M, Cb, Sb, pcol, dst_ap, dst_name):
        kb.op("act", lambda e: e.activation(out=xs[0:M, :], in_=ps[0:M, :], func=AF.Copy), reads=[ps], writes=[xs])
        ps2 = pg.bank()
        mm(pg, ps2[0:M, :], pg.consts[0:M, pcol:pcol + M], xs[0:M, :], True, True, [pg.consts, xs], ps2)
        kb.op("dve", lambda e: e.tensor_tensor(out=t1[0:M, :], in0=xs[0:M, :], in1=Cb[0:M, :], op=ALU.mult),
              reads=[xs, tab], writes=[t1])
        kb.op("dve", lambda e: e.tensor_tensor(out=xs[0:M, :], in0=ps2[0:M, :], in1=Sb[0:M, :], op=ALU.mult),
              reads=[ps2, tab, xs], writes=[xs])
        s = stage()
        kb.op("dve", lambda e: e.tensor_tensor(out=s[0:M, :], in0=t1[0:M, :], in1=xs[0:M, :], op=ALU.add),
              reads=[t1, xs], writes=[s])
        kb.dma("sp", dst_ap, s[0:M, :], reads=[s], writes=[pg.R(dst_name)])

    def rms_fm(c0, gcol0):
        slot, wv = pg.load_w(nm["w_in"][0], nm["w_in"][1], 0, 16, c0, 512)
        for c in range(4):
            ps = fm_proj(pg, wv, slot, c * 128, 128, xT, xTb)
            kb.op("act", lambda e, ps=ps, c=c: e.activation(out=sq[:, c, :], in_=ps[:, :], func=(AF.Copy if getattr(pg, "nosq", False) else AF.Square)), reads=[ps], writes=[sq])
            kb.op("dve", lambda e, ps=ps, c=c: e.tensor_copy(out=cT[:, c, :], in_=ps[:, :]), reads=[ps], writes=[cT])
        if getattr(pg, 'dbg', 99) < 2.1:
            return
        ss = pg.bank()
        for c in range(4):
            mm(pg, ss[:, :], pg.cc(C_ONES), sq[:, c, :], c == 0, c == 3, [pg.consts, sq], ss)
        if getattr(pg, 'dbg', 99) < 2.2:
            return
        kb.op("act", lambda e: e.activation(out=rstd[:, :], in_=ss[:, :], func=AF.Ln, bias=pg.cc(C_EPS, 1), scale=1.0 / 512),
              reads=[ss, pg.consts], writes=[rstd])
        kb.op("act", lambda e: e.activation(out=rstd[:, :], in_=rstd[:, :], func=AF.Exp, scale=-0.5), reads=[rstd], writes=[rstd])
        for c in range(4):
            kb.op("dve", lambda e, c=c: e.scalar_tensor_tensor(out=cnT[:, c, :], in0=cT[:, c, :], scalar=mp.sm[:, gcol0 + c:gcol0 + c + 1],
                                                               in1=rstd[:, :], op0=ALU.mult, op1=ALU.mult),
                  reads=[cT, mp.sm, rstd], writes=[cnT])

    if getattr(pg, 'dbg', 99) < 2:
        return
    rms_fm(O_CQ, 0)
    if getattr(pg, 'dbg', 99) < 2.3:
        return
    slot, wq = pg.load_w(nm["w_uq"][0], nm["w_uq"][1], 0, 4, 0, 1536)
    for h in range(8):
        ps = fm_proj(pg, wq, slot, h * 192, 128, cnT, cnT, K=4)
        store_fm(ps, 128, nm["QnT"][0][h, :, tok], nm["QnT"][1])
        if getattr(pg, 'dbg', 99) < 2.6:
            continue
        ps = fm_proj(pg, wq, slot, h * 192 + 128, 64, cnT, cnT, K=4)
        rope_store(ps, 64, C64, S64, C_P64, nm["QrT"][0][h, :, tok], nm["QrT"][1])
    if getattr(pg, 'dbg', 99) < 3:
        return
    rms_fm(O_CKV, 4)
    slot, wkv = pg.load_w(nm["w_ukv"][0], nm["w_ukv"][1], 0, 4, 0, 2048)
    for h in range(8):
        ps = fm_proj(pg, wkv, slot, h * 256, 128, cnT, cnT, K=4)
        store_fm(ps, 128, nm["KnT_o"][0][h, :, tok], nm["KnT_o"][1])
    for t in range(4):
        for i in range(2):
            wsel = lambda kc, i=i: wkv[:, kc, :].rearrange("p (h two d) -> p h two d", two=2, d=128)[:, 4 * i:4 * i + 4, 1, :]
            ps = tm_proj(pg, wsel, slot, 512, cnT, cnT, t, K=4, outv=lambda p: p[:, :].rearrange("p (h d) -> p h d", d=128))
            s = stage()
            kb.op("act", lambda e, ps=ps, s=s: e.activation(out=s[:, :], in_=ps[:, :], func=AF.Copy), reads=[ps], writes=[s])
            r0 = b * BLK + t * 128
            kb.dma("sp", nm["Vm_o"][0][r0:r0 + 128, i * 512:(i + 1) * 512], s[:, :], reads=[s], writes=[pg.R(nm["Vm_o"][1])])
    if getattr(pg, 'dbg', 99) < 4:
        return
    slot, wv = pg.load_w(nm["w_in"][0], nm["w_in"][1], 0, 16, O_KR, 64)
    ps = fm_proj(pg, wv, slot, 0, 64, xT, xTb)
    rope_store(ps, 64, C64, S64, C_P64, nm["KrT_o"][0][:, tok], nm["KrT_o"][1])
    for (c0, key, Cb, Sb, pcol) in ((O_DQ, "QdT", C128, S128, C_P128), (O_DK, "KdT_o", C128, S128, C_P128),
                                    (O_IQ, "IqT", C64, S64, C_P64)):
        for g in range(2):
            slot, wv = pg.load_w(nm["w_in"][0], nm["w_in"][1], 0, 16, c0 + g * 512, 512)
            for hh in range(4):
                ps = fm_proj(pg, wv, slot, hh * 128, 128, xT, xTb)
                rope_store(ps, 128, Cb, Sb, pcol, nm[key][0][g * 4 + hh, :, tok], nm[key][1])
    if getattr(pg, 'dbg', 99) < 5:
        return
    for g in range(2):
        slot, wv = pg.load_w(nm["w_in"][0], nm["w_in"][1], 0, 16, O_DV + g * 512, 512)
        for t in range(4):
            ps = tm_proj(pg, lambda kc, wv=wv: wv[:, kc, :], slot, 512, xT, xTb, t)
            s = stage()
            kb.op("act", lambda e, ps=ps, s=s: e.activation(out=s[:, :], in_=ps[:, :], func=AF.Copy), reads=[ps], writes=[s])
            r0 = b * BLK + t * 128
            kb.dma("sp", nm["Vd_o"][0][r0:r0 + 128, g * 512:(g + 1) * 512], s[:, :], reads=[s], writes=[pg.R(nm["Vd_o"][1])])
    slot, wv = pg.load_w(nm["w_in"][0], nm["w_in"][1], 0, 16, O_IK, 80)
    ps = fm_proj(pg, wv, slot, 0, 64, xT, xTb)
    rope_store(ps, 64, C64, S64, C_P64, nm["IkT_o"][0][:, tok], nm["IkT_o"][1])
    for t in range(4):
        ps = tm_proj(pg, lambda kc, wv=wv: wv[:, kc, 64:80], slot, 16, xT, xTb, t)
        kb.op("act", lambda e, ps=ps: e.activation(out=xs[:, 0:16], in_=ps[:, 0:16], func=AF.Copy), reads=[ps], writes=[xs])
        r0 = b * BLK + t * 128
        kb.dma("sp", nm["iw"][0][r0:r0 + 128, :], xs[:, 0:16], reads=[xs], writes=[pg.R(nm["iw"][1])])
    if getattr(pg, 'dbg', 99) < 6:
        return
    emit_gla_proj_block(pg, wk, mp, b, L, xT)


def emit_gla_proj_block(pg, wk, mp, b, L, xT):
    kb = pg.kb
    kb.barrier()
    nm = L["n"]
    xTb = wk.xT
    win = nm["w_in"]
    gq_raw = View(wk.v[0][:, :].rearrange("p (a t) -> p a t", t=512))
    gk_raw = View(wk.v[1][:, :].rearrange("p (a t) -> p a t", t=512))
    lbuf = View(wk.v[2][:, 0:512])
    Eq = View(wk.v[2][:, 512:1024])
    Ek = View(wk.v[2][:, 1024:1536])
    Er = View(wk.v[2][:, 1536:2048])
    oi = View(wk.v[3][:, 0:1024])
    U0 = View(wk.v[3][:, 1024:2048])
    Ut = View(wk.gbc[:, 0:1024])
    gv = View(wk.hT[:, 0:4096].rearrange("p (a t) -> p a t", t=1024))
    sgr = View(wk.hT[:, 4096:8192].rearrange("p (a t) -> p a t", t=1024))
    qt = View(wk.hT[:, 8192:8704])
    kt = View(wk.hT[:, 8704:9216])
    kh = View(wk.hT[:, 9216:9728])
    qkT = View(wk.hT[:, 9728:10752].rearrange("p (a t) -> p a t", t=128))
    AT = View(wk.hT[:, 10752:11264].rearrange("p (a t) -> p a t", t=128))
    glrT = View(wk.sg[0][0:16, :])
    Acol = View(wk.sg[1][:, 0:8])
    Atl = View(wk.sg[1][:, 8:12])

    slot, wv = pg.load_w(win[0], win[1], 0, 16, O_GLR, 16)
    ps = fm_proj(pg, wv, slot, 0, 16, xT, xTb)
    kb.op("act", lambda e: e.activation(out=glrT[:, :], in_=ps[0:16, :], func=AF.Copy), reads=[ps], writes=[glrT])
    for (c0, dst) in ((O_GQ, gq_raw), (O_GK, gk_raw)):
        slot, wv = pg.load_w(win[0], win[1], 0, 16, c0, 512)
        for t in range(4):
            ps = tm_proj(pg, lambda kc, wv=wv: wv[:, kc, :], slot, 512, xT, xTb, t)
            kb.op("act", lambda e, ps=ps, t=t, dst=dst: e.activation(out=dst[:, t, :], in_=ps[:, :], func=AF.Copy), reads=[ps], writes=[dst])
    for (c0, dst, fn) in ((O_GV, gv, AF.Copy), (O_GR, sgr, AF.Silu)):
        for g in range(2):
            slot, wv = pg.load_w(win[0], win[1], 0, 16, c0 + g * 512, 512)
            for t in range(4):
                ps = tm_proj(pg, lambda kc, wv=wv: wv[:, kc, :], slot, 512, xT, xTb, t)
                kb.op("act", lambda e, ps=ps, t=t, g=g, dst=dst, fn=fn: e.activation(out=dst[:, t, g * 512:(g + 1) * 512], in_=ps[:, :], func=fn),
                      reads=[ps], writes=[dst])
    for t in range(4):
        r0 = b * BLK + t * 128
        kb.dma("sp", nm["gsr"][0][r0:r0 + 128, :], sgr[:, t, :], reads=[sgr], writes=[pg.R(nm["gsr"][1])])
    for t in range(4):
        j = b * 4 + t
        r0 = b * BLK + t * 128
        Z = pg.bank()
        mm(pg, Z[:, :], glrT[0:16, t * 128:(t + 1) * 128], mp.wg2[0:16, :], True, False, [glrT, mp.wg2], Z)
        mm(pg, Z[:, :], pg.consts[0:1, C_ONES:C_ONES + 128], mp.bg[0:1, :], False, True, [pg.consts, mp.bg], Z)
        kb.op("act", lambda e: e.activation(out=Eq[:, :], in_=Z[:, :], func=AF.Exp, scale=-1.0), reads=[Z], writes=[Eq])
        kb.op("act", lambda e: e.activation(out=lbuf[:, :], in_=Eq[:, :], func=AF.Ln, bias=pg.cc(C_ONES, 1), scale=1.0),
              reads=[Eq, pg.consts], writes=[lbuf])
        cum = pg.bank()
        mm(pg, cum[:, :], pg.cc(C_TRI), lbuf[:, :], True, True, [pg.consts, lbuf], cum)
        rev = pg.bank()
        mm(pg, rev[:, :], pg.cc(C_RTRI), lbuf[:, :], True, True, [pg.consts, lbuf], rev)
        tot = pg.pb[4]
        for h in range(4):
            mm(pg, tot[:, h * 2:h * 2 + 2], lbuf[:, h * 128:(h + 1) * 128], pg.consts[:, C_CH:C_CH + 2], True, True, [pg.consts, lbuf], tot)
        kb.op("act", lambda e: e.activation(out=Eq[:, :], in_=cum[:, :], func=AF.Exp, scale=-1.0 / 16), reads=[cum], writes=[Eq])
        kb.op("act", lambda e: e.activation(out=Ek[:, :], in_=cum[:, :], func=AF.Exp, scale=1.0 / 16), reads=[cum], writes=[Ek])
        kb.op("act", lambda e: e.activation(out=Er[:, :], in_=rev[:, :], func=AF.Exp, scale=-1.0 / 16), reads=[rev], writes=[Er])
        kb.op("act", lambda e: e.activation(out=Acol[:, :], in_=tot[:, 0:8], func=AF.Exp, scale=-1.0 / 16), reads=[tot], writes=[Acol])
        kb.op("dve", lambda e, t=t: e.scalar_tensor_tensor(out=qt[:, :], in0=gq_raw[:, t, :], scalar=128.0 ** -0.5, in1=Eq[:, :],
                                                          op0=ALU.mult, op1=ALU.mult), reads=[gq_raw, Eq], writes=[qt])
        kb.op("dve", lambda e, t=t: e.tensor_tensor(out=kt[:, :], in0=gk_raw[:, t, :], in1=Ek[:, :], op=ALU.mult), reads=[gk_raw, Ek], writes=[kt])
        kb.op("dve", lambda e, t=t: e.tensor_tensor(out=kh[:, :], in0=gk_raw[:, t, :], in1=Er[:, :], op=ALU.mult), reads=[gk_raw, Er], writes=[kh])
        ptb = pg.pt[0]
        for i in range(8):
            src = qt if i < 4 else kt
            hh = i % 4
            kb.op("pe", lambda e, i=i, src=src, hh=hh: e.transpose(out=ptb[:, i * 128:(i + 1) * 128], in_=src[:, hh * 128:(hh + 1) * 128],
                                                                   identity=pg.identb[:, :]), reads=[src, pg.identb], writes=[ptb])
        kb.op("dve", lambda e: e.tensor_copy(out=qkT[:, :, :], in_=ptb[:, :].rearrange("p (a t) -> p a t", t=128)), reads=[ptb], writes=[qkT])
        kb.dma("sp", nm["gqT"][0][:, :, r0:r0 + 128].rearrange("h p t -> p h t"), qkT[:, 0:4, :], reads=[qkT], writes=[pg.R(nm["gqT"][1])])
        for h in range(4):
            at = pg.bank()
            mm(pg, at[:, 0:128], qkT[:, 4 + h, :], qkT[:, h, :], True, True, [qkT], at)
            kb.op("dve", lambda e, at=at, h=h: e.tensor_tensor(out=AT[:, h, :], in0=at[:, 0:128], in1=pg.cc(C_MASKF), op=ALU.mult),
                  reads=[at, pg.consts], writes=[AT])
        for h in range(4):
            o = pg.bank()
            mm(pg, o[:, 0:256], AT[:, h, :], gv[:, t, h * 256:(h + 1) * 256], True, True, [AT, gv], o)
            kb.op("act", lambda e, o=o, h=h: e.activation(out=oi[:, h * 256:(h + 1) * 256], in_=o[:, 0:256], func=AF.Copy), reads=[o], writes=[oi])
        kb.dma("sp", nm["goi"][0][r0:r0 + 128, :], oi[:, :], reads=[oi], writes=[pg.R(nm["goi"][1])])
        for h in range(4):
            u0 = pg.bank()
            mm(pg, u0[:, 0:256], kh[0:64, h * 128:(h + 1) * 128], gv[0:64, t, h * 256:(h + 1) * 256], True, True, [kh, gv], u0)
            u1 = pg.bank()
            mm(pg, u1[:, 0:256], kh[64:128, h * 128:(h + 1) * 128], gv[64:128, t, h * 256:(h + 1) * 256], True, True, [kh, gv], u1)
            kb.op("act", lambda e, u0=u0, h=h: e.activation(out=U0[:, h * 256:(h + 1) * 256], in_=u0[:, 0:256], func=AF.Copy), reads=[u0], writes=[U0])
            kb.op("dve", lambda e, u1=u1, h=h: e.scalar_tensor_tensor(out=Ut[:, h * 256:(h + 1) * 256], in0=U0[:, h * 256:(h + 1) * 256],
                                                                      scalar=Acol[:, 2 * h + 1:2 * h + 2], in1=u1[:, 0:256],
                                                                      op0=ALU.mult, op1=ALU.add), reads=[U0, Acol, u1], writes=[Ut])
        kb.dma("sp", nm["gU0"][0][j], U0[:, :], reads=[U0], writes=[pg.R(nm["gU0"][1])])
        kb.dma("sp", nm["gUt_o"][0][j], Ut[:, :], reads=[Ut], writes=[pg.R(nm["gUt_o"][1])])
        Ac3 = Acol[:, :].rearrange("p (h c) -> p h c", c=2)
        kb.op("dve", lambda e: e.tensor_tensor(out=Atl[:, :], in0=Ac3[:, :, 0], in1=Ac3[:, :, 1], op=ALU.mult), reads=[Acol], writes=[Atl])
        kb.dma("sp", nm["gA"][0][j], Acol[:, :], reads=[Acol], writes=[pg.R(nm["gA"][1])])
        kb.dma("sp", nm["gAt_o"][0][j], Atl[:, :], reads=[Atl], writes=[pg.R(nm["gAt_o"][1])])
    kb.barrier()


BIS_L = -64.0
BIS_W = 128.0
BIS_N = 24


def emit_indexer(pg, nm, tiles=None):
    kb = pg.kb
    kb.push()
    ik2 = Buf(kb, "ik2", [128, SEQ], BF16)
    score = Buf(kb, "score", [128, SEQ], F32)
    mneg = Buf(kb, "mnegw", [128, SEQ], BF16)
    cm = Buf(kb, "cm32", [128, 1024], F32)
    iq = Buf(kb, "iqt", [128, 1024], BF16)
    iwt = Buf(kb, "iwt", [128, 16], F32)
    wab = Buf(kb, "wab", [128, 16], F32)
    wsg = Buf(kb, "wsg", [128, 16], F32)
    mid = Buf(kb, "mid", [128, 1], F32)
    cnt = Buf(kb, "cnt", [128, 1], F32)
    dlt = Buf(kb, "dlt", [128, 1], F32)
    nmid = Buf(kb, "nmid", [128, 1], F32)
    sgs = Buf(kb, "sgs", [128, 1], F32)
    ajunk = Buf(kb, "ajunk", [128, 10240], BF16)
    accBs = [Buf(kb, f"accB{i}", [128, 512], F32) for i in range(2)]
    kb.dma("sp", ik2[0:64, :], nm["IkT_g"][0][:, :], reads=[pg.R(nm["IkT_g"][1])], writes=[ik2])
    kb.dma("sp", ik2[64:128, :], nm["IkT_g"][0][:, :], reads=[pg.R(nm["IkT_g"][1])], writes=[ik2])
    kb.dma("sp", cm[:, :], nm["cmask"][0][:, :], reads=[pg.R(nm["cmask"][1])], writes=[cm])
    iq3 = iq[:, :].rearrange("p (a t) -> p a t", t=128)
    for j in (range(NT) if tiles is None else tiles):
        nk = 1024 * (j + 1)
        ts = slice(j * 128, (j + 1) * 128)
        kb.dma("sp", iq3, nm["IqT"][0][:, :, ts].rearrange("a p t -> p a t"), reads=[pg.R(nm["IqT"][1])], writes=[iq])
        kb.dma("sp", iwt[:, :], nm["iw"][0][ts, :], reads=[pg.R(nm["iw"][1])], writes=[iwt])
        kb.op("act", lambda e: e.activation(out=wab[:, :], in_=iwt[:, :], func=AF.Abs, scale=(64.0 ** -0.5) * (16.0 ** -0.5)),
              reads=[iwt], writes=[wab])
        kb.op("act", lambda e: e.activation(out=wsg[:, :], in_=iwt[:, :], func=AF.Sign), reads=[iwt], writes=[wsg])
        for kbi in range(2 * (j + 1)):
            ks = slice(kbi * 512, (kbi + 1) * 512)
            diag = kbi >= 2 * j
            accB = accBs[kbi % 2]
            for hh in range(16):
                p0 = (hh % 2) * 64
                ps = pg.bank()
                mm(pg, ps[:, :], iq3[p0:p0 + 64, hh // 2, :], ik2[p0:p0 + 64, ks], True, True, [iq, ik2], ps)
                kb.op("act", lambda e, ps=ps, hh=hh: e.activation(out=ps[:, :], in_=ps[:, :], func=AF.Relu, scale=wab[:, hh:hh + 1]),
                      reads=[ps, wab], writes=[ps])
                if hh % 2 == 0:
                    dst, dreg = score[:, ks], score
                else:
                    dst, dreg = accB[:, :], accB
                if hh == 0 and diag:
                    in1 = cm[:, (kbi - 2 * j) * 512:(kbi - 2 * j + 1) * 512]
                    kb.op("dve", lambda e, ps=ps, dst=dst, hh=hh, in1=in1: e.scalar_tensor_tensor(
                        out=dst, in0=ps[:, :], scalar=wsg[:, hh:hh + 1], in1=in1, op0=ALU.mult, op1=ALU.add),
                        reads=[ps, wsg, cm], writes=[dreg])
                elif hh < 2:
                    kb.op("dve", lambda e, ps=ps, dst=dst, hh=hh: e.tensor_scalar(out=dst, in0=ps[:, :], scalar1=wsg[:, hh:hh + 1], scalar2=None,
                                                                               op0=ALU.mult), reads=[ps, wsg], writes=[dreg])
                else:
                    kb.op("dve", lambda e, ps=ps, dst=dst, hh=hh: e.scalar_tensor_tensor(
                        out=dst, in0=ps[:, :], scalar=wsg[:, hh:hh + 1], in1=dst, op0=ALU.mult, op1=ALU.add),
                        reads=[ps, wsg], writes=[dreg])
            kb.op("dve", lambda e, ks=ks, accB=accB: e.tensor_tensor(out=score[:, ks], in0=score[:, ks], in1=accB[:, :], op=ALU.add),
                  reads=[accB], writes=[score])
        nd = max(512, int(round(0.40 * nk / 512)) * 512)
        na = nk - nd
        kb.op("dve", lambda e: e.memset(mid[:, :], BIS_L + BIS_W / 2), writes=[mid])
        for k in range(1, BIS_N + 1):
            kb.op("dve", lambda e: e.tensor_scalar(out=mneg[:, 0:nd], in0=score[:, 0:nd], scalar1=mid[:, 0:1], scalar2=None,
                                                   op0=ALU.is_ge, op1=ALU.add, accum_out=cnt[:, 0:1]),
                  reads=[score, mid], writes=[mneg, cnt])
            kb.op("act", lambda e: e.activation(out=ajunk[:, 0:na], in_=score[:, nd:nk], func=AF.Sign, bias=mid[:, 0:1], scale=-1.0,
                                                accum_out=sgs[:, 0:1]), reads=[score, mid], writes=[ajunk, sgs])
            kb.op("dve", lambda e: e.scalar_tensor_tensor(out=cnt[:, :], in0=sgs[:, :], scalar=-0.5, in1=cnt[:, :], op0=ALU.mult, op1=ALU.add),
                  reads=[sgs], writes=[cnt])
            hk = BIS_W / 2 ** (k + 1) if k < BIS_N else BIS_W / 2 ** BIS_N
            mul = 2 * hk if k < BIS_N else hk
            kb.op("dve", lambda e, mul=mul: e.tensor_scalar(out=dlt[:, :], in0=cnt[:, :], scalar1=255.5 - 0.5 * na, scalar2=mul,
                                                            op0=ALU.is_ge, op1=ALU.mult), reads=[cnt], writes=[dlt])
            kb.op("dve", lambda e, hk=hk: e.scalar_tensor_tensor(out=mid[:, :], in0=dlt[:, :], scalar=-hk, in1=mid[:, :],
                                                                op0=ALU.add, op1=ALU.add), reads=[dlt, mid], writes=[mid])
        kb.op("dve", lambda e: e.tensor_scalar(out=mneg[:, 0:nk], in0=score[:, 0:nk], scalar1=mid[:, 0:1], scalar2=NEG,
                                               op0=ALU.is_lt, op1=ALU.mult), reads=[score, mid], writes=[mneg])
        kb.dma("sp", nm["MnegD"][0][j, :, 0:nk], mneg[:, 0:nk], reads=[mneg], writes=[pg.R(nm["MnegD"][1])])
        if "thr_dbg" in nm:
            kb.dma("sp", nm["thr_dbg"][0][j], mid[:, :], reads=[mid], writes=[pg.R(nm["thr_dbg"][1])])
    kb.pop()


def emit_attention(pg, nm, kind, heads=None, tiles=None):
    kb = pg.kb
    kb.push()
    mla = kind == "mla"
    K = Buf(kb, "attK", [128, SEQ], BF16)
    V = Buf(kb, "attV", [128, SEQ], BF16)
    Kg = [View(K[:, g * 1024:(g + 1) * 1024]) for g in range(16)]
    Vg = [View(V[:, g * 1024:(g + 1) * 1024]) for g in range(16)]
    if mla:
        KR = Buf(kb, "attKR", [64, SEQ], BF16)
        cm32 = Buf(kb, "attcm32", [128, 1024], F32)
        cmb = Buf(kb, "attcmb", [128, 1024], BF16)
        kb.dma("sp", KR[:, :], nm["KrT_g"][0][:, :], reads=[pg.R(nm["KrT_g"][1])], writes=[KR])
        kb.dma("sp", cm32[:, :], nm["cmask"][0][:, :], reads=[pg.R(nm["cmask"][1])], writes=[cm32])
        kb.op("dve", lambda e: e.tensor_copy(out=cmb[:, :], in_=cm32[:, :]), reads=[cm32], writes=[cmb])
        Qr = [Buf(kb, f"attQr{i}", [64, 128], BF16) for i in range(2)]
        scale = 192.0 ** -0.5
        Kd, Vd, Qd, oT = nm["KnT_g"], nm["Vm_g"], nm["QnT"], nm["oT_mla"]
    else:
        MN = [Buf(kb, f"attMN{i}", [128, SEQ], BF16) for i in range(2)]
        scale = 128.0 ** -0.5
        Kd, Vd, Qd, oT = nm["KdT_g"], nm["Vd_g"], nm["QdT"], nm["oT_dsa"]
    Qn = [Buf(kb, f"attQn{i}", [128, 128], BF16) for i in range(2)]
    P = [Buf(kb, f"attP{i}", [128, 512], BF16) for i in range(2)]
    PT = [Buf(kb, f"attPT{i}", [128, 512], BF16) for i in range(2)]
    rsb = [Buf(kb, f"attrs{i}", [128, 32], F32) for i in range(2)]
    rsum = Buf(kb, "attrsum", [128, 2], F32)
    ob = Buf(kb, "attob", [128, 128], BF16)
    oTs = Buf(kb, "attoTs", [128, 128], BF16)
    Ob = [pg.pb[4], pg.pb[5]]
    tasks = []
    it = 0
    for h in (range(8) if heads is None else heads):
        first = True
        for j in (range(NT) if tiles is None else tiles):
            nblk = 2 * (j + 1)
            for kbi in range(nblk):
                tasks.append(dict(h=h, j=j, kbi=kbi, nblk=nblk, it=it, newhead=(first and kbi == 0)))
            first = False
            it += 1
    state = {"maxg": -1}

    def stageA(s_, T):
        h, j, kbi, it_ = T["h"], T["j"], T["kbi"], T["it"]
        ts = slice(j * 128, (j + 1) * 128)
        nk = 1024 * (j + 1)
        qn = Qn[it_ % 2]
        if kbi == 0:
            if T["newhead"]:
                state["maxg"] = -1
            for g in range(state["maxg"] + 1, j + 1):
                gs = slice(g * 1024, (g + 1) * 1024)
                kb.dma("sp", Kg[g][:, :], Kd[0][h, :, gs], reads=[pg.R(Kd[1])], writes=[Kg[g]])
                kb.dma("sp", Vg[g][:, :], Vd[0][h, :, gs], reads=[pg.R(Vd[1])], writes=[Vg[g]])
            state["maxg"] = max(state["maxg"], j)
            kb.dma("pool", qn[:, :], Qd[0][h, :, ts], reads=[pg.R(Qd[1])], writes=[qn])
            if mla:
                kb.dma("pool", Qr[it_ % 2][:, :], nm["QrT"][0][h, :, ts], reads=[pg.R(nm["QrT"][1])], writes=[Qr[it_ % 2]])
            else:
                kb.dma("pool", MN[it_ % 2][:, 0:nk], nm["MnegD"][0][j, :, 0:nk], reads=[pg.R(nm["MnegD"][1])], writes=[MN[it_ % 2]])
        g = kbi // 2
        ks = slice(kbi * 512, (kbi + 1) * 512)
        diag = g == j
        ps = pg.bank()
        if mla:
            qr = Qr[it_ % 2]
            mm(pg, ps[:, :], qn[:, :], K[:, ks], True, False, [qn, Kg[g]], ps)
            mm(pg, ps[:, :], qr[:, :], KR[:, ks], False, not diag, [qr, KR], ps)
            if diag:
                mm(pg, ps[:, :], pg.identb[:, :], cmb[:, (kbi - 2 * j) * 512:(kbi - 2 * j + 1) * 512], False, True, [pg.identb, cmb], ps)
        else:
            mn = MN[it_ % 2]
            mm(pg, ps[:, :], qn[:, :], K[:, ks], True, False, [qn, Kg[g]], ps)
            mm(pg, ps[:, :], pg.identb[:, :], mn[:, ks], False, True, [pg.identb, mn], ps)
        p = P[s_ % 2]
        rs = rsb[it_ % 2]
        kb.op("act", lambda e: e.activation(out=p[:, :], in_=ps[:, :], func=AF.Exp, scale=scale, accum_out=rs[:, kbi:kbi + 1]),
              reads=[ps], writes=[p, rs])

    def stageB(s_, T):
        p = P[s_ % 2]
        ptb = pg.pt[s_ % 2]
        for i in range(4):
            kb.op("pe", lambda e, i=i: e.transpose(out=ptb[:, i * 128:(i + 1) * 128], in_=p[:, i * 128:(i + 1) * 128],
                                                   identity=pg.identb[:, :]), reads=[p, pg.identb], writes=[ptb])
        pt_ = PT[s_ % 2]
        kb.op("dve", lambda e: e.tensor_copy(out=pt_[:, :], in_=ptb[:, 0:512]), reads=[ptb], writes=[pt_])

    def stageC(s_, T):
        h, j, kbi, nblk, it_ = T["h"], T["j"], T["kbi"], T["nblk"], T["it"]
        g = kbi // 2
        pt_ = PT[s_ % 2]
        O = Ob[it_ % 2]
        for i in range(4):
            n = kbi * 4 + i
            mm(pg, O[:, 0:128], pt_[:, i * 128:(i + 1) * 128], V[:, n * 128:(n + 1) * 128],
               kbi == 0 and i == 0, kbi == nblk - 1 and i == 3, [pt_, Vg[g]], O)
        if kbi == nblk - 1:
            ts = slice(j * 128, (j + 1) * 128)
            rs = rsb[it_ % 2]
            kb.op("dve", lambda e: e.reduce_sum(out=rsum[:, 0:1], in_=rs[:, 0:nblk], axis=mybir.AxisListType.X), reads=[rs], writes=[rsum])
            kb.op("dve", lambda e: e.reciprocal(out=rsum[:, 1:2], in_=rsum[:, 0:1]), reads=[rsum], writes=[rsum])
            kb.op("act", lambda e: e.activation(out=ob[:, :], in_=O[:, 0:128], func=AF.Copy, scale=rsum[:, 1:2]), reads=[O, rsum], writes=[ob])
            ptb = pg.pt[(s_ + 1) % 2]
            kb.op("pe", lambda e: e.transpose(out=ptb[:, 0:128], in_=ob[:, :], identity=pg.identb[:, :]), reads=[ob, pg.identb], writes=[ptb])
            kb.op("dve", lambda e: e.tensor_copy(out=oTs[:, :], in_=ptb[:, 0:128]), reads=[ptb], writes=[oTs])
            kb.dma("sp", oT[0][h * 128:(h + 1) * 128, ts], oTs[:, :], reads=[oTs], writes=[pg.R(oT[1])])

    N = len(tasks)
    for s_ in range(N + 2):
        if s_ < N:
            stageA(s_, tasks[s_])
        if 0 <= s_ - 1 < N:
            stageB(s_ - 1, tasks[s_ - 1])
        if 0 <= s_ - 2 < N:
            stageC(s_ - 2, tasks[s_ - 2])
    kb.pop()


def emit_gla_out(pg, nm, tiles=None):
    kb = pg.kb
    kb.push()
    S = Buf(kb, "glaS", [128, 1024], F32)
    snap = Buf(kb, "glasnap", [128, 1024], F32)
    At = Buf(kb, "glaAt", [128, 512], F32)
    Ub = [Buf(kb, f"glaUb{i}", [128, 1024], F32) for i in range(3)]
    U0b = Buf(kb, "glaU0", [128, 1024], F32)
    Ab = Buf(kb, "glaAb", [128, 8], F32)
    qTb = Buf(kb, "glaqT", [128, 512], BF16)
    qA = Buf(kb, "glaqA", [128, 512], BF16)
    qB = Buf(kb, "glaqB", [128, 512], BF16)
    oib = Buf(kb, "glaoi", [128, 1024], F32)
    srb = Buf(kb, "glasr", [128, 1024], BF16)
    gn = Buf(kb, "glagn", [128, 1024], F32)
    S0b = Buf(kb, "glaS0b", [128, 1024], BF16)
    S1b = Buf(kb, "glaS1b", [128, 1024], BF16)
    junk = Buf(kb, "glajunk", [128, 256], F32)
    ss = Buf(kb, "glass", [128, 8], F32)
    ob = Buf(kb, "glaob", [128, 1024], BF16)
    oTs = Buf(kb, "glaoTs", [128, 1024], BF16)
    kb.op("dve", lambda e: e.memset(S[:, :], 0.0), writes=[S])
    kb.op("dve", lambda e: e.memset(qA[:, :], 0.0), writes=[qA])
    kb.op("dve", lambda e: e.memset(qB[:, :], 0.0), writes=[qB])
    kb.dma("sp", At[:, :], nm["gAt_g"][0][:, :], reads=[pg.R(nm["gAt_g"][1])], writes=[At])
    kb.dma("sp", gn[:, :], nm["gla_norm"][0].partition_broadcast(128), reads=[pg.R(nm["gla_norm"][1])], writes=[gn])
    q3 = qTb[:, :].rearrange("p (h t) -> p h t", t=128)
    qA3 = qA[:, :].rearrange("p (h t) -> p h t", t=128)
    qB3 = qB[:, :].rearrange("p (h t) -> p h t", t=128)
    for j in range(NT):
        for i in range(8):
            g = 8 * j + i
            ub = Ub[g % 3]
            kb.dma("sp", ub[:, :], nm["gUt_g"][0][g], reads=[pg.R(nm["gUt_g"][1])], writes=[ub])
            if i == 0:
                kb.op("dve", lambda e: e.tensor_scalar(out=snap[:, :], in0=S[:, :], scalar1=pg.cc(C_EI, 1), scalar2=None, op0=ALU.mult),
                      reads=[S, pg.consts], writes=[snap])
            else:
                kb.op("dve", lambda e, i=i: e.scalar_tensor_tensor(out=snap[:, :], in0=S[:, :], scalar=pg.cc(C_EI + i, 1), in1=snap[:, :],
                                                                   op0=ALU.mult, op1=ALU.add), reads=[S, pg.consts, snap], writes=[snap])
            for h in range(4):
                hs = slice(h * 256, (h + 1) * 256)
                kb.op("dve", lambda e, hs=hs, g=g, h=h, ub=ub: e.scalar_tensor_tensor(
                    out=S[:, hs], in0=S[:, hs], scalar=At[:, g * 4 + h:g * 4 + h + 1], in1=ub[:, hs], op0=ALU.mult, op1=ALU.add),
                    reads=[At, ub], writes=[S])
        if tiles is not None and j not in tiles:
            continue
        ts = slice(j * 128, (j + 1) * 128)
        kb.dma("pool", U0b[:, :], nm["gU0"][0][j], reads=[pg.R(nm["gU0"][1])], writes=[U0b])
        kb.dma("pool", Ab[:, :], nm["gA"][0][j], reads=[pg.R(nm["gA"][1])], writes=[Ab])
        kb.dma("pool", q3, nm["gqT"][0][:, :, ts].rearrange("h p t -> p h t"), reads=[pg.R(nm["gqT"][1])], writes=[qTb])
        kb.dma("pool", oib[:, :], nm["goi"][0][ts, :], reads=[pg.R(nm["goi"][1])], writes=[oib])
        kb.dma("pool", srb[:, :], nm["gsr"][0][ts, :], reads=[pg.R(nm["gsr"][1])], writes=[srb])
        kb.op("act", lambda e: e.activation(out=S0b[:, :], in_=snap[:, :], func=AF.Copy), reads=[snap], writes=[S0b])
        for h in range(4):
            hs = slice(h * 256, (h + 1) * 256)
            kb.op("dve", lambda e, hs=hs, h=h: e.scalar_tensor_tensor(out=S1b[:, hs], in0=snap[:, hs], scalar=Ab[:, 2 * h:2 * h + 1],
                                                                      in1=U0b[:, hs], op0=ALU.mult, op1=ALU.add),
                  reads=[snap, Ab, U0b], writes=[S1b])
        kb.op("dve", lambda e: e.tensor_copy(out=qA3[:, :, 0:64], in_=q3[:, :, 0:64]), reads=[qTb], writes=[qA])
        kb.op("dve", lambda e: e.tensor_copy(out=qB3[:, :, 64:128], in_=q3[:, :, 64:128]), reads=[qTb], writes=[qB])
        for h in range(4):
            hs = slice(h * 256, (h + 1) * 256)
            o = pg.bank()
            mm(pg, o[:, 0:256], qA3[:, h, :], S0b[:, hs], True, False, [qA, S0b], o)
            mm(pg, o[:, 0:256], qB3[:, h, :], S1b[:, hs], False, True, [qB, S1b], o)
            kb.op("dve", lambda e, o=o, hs=hs: e.tensor_tensor(out=oib[:, hs], in0=o[:, 0:256], in1=oib[:, hs], op=ALU.add), reads=[o], writes=[oib])
            kb.op("act", lambda e, hs=hs, h=h: e.activation(out=junk[:, :], in_=oib[:, hs], func=AF.Square, accum_out=ss[:, h:h + 1]),
                  reads=[oib], writes=[junk, ss])
        kb.op("act", lambda e: e.activation(out=ss[:, 4:8], in_=ss[:, 0:4], func=AF.Ln, bias=pg.cc(C_EPS, 1), scale=1.0 / 256),
              reads=[ss, pg.consts], writes=[ss])
        kb.op("act", lambda e: e.activation(out=ss[:, 4:8], in_=ss[:, 4:8], func=AF.Exp, scale=-0.5), reads=[ss], writes=[ss])
        for h in range(4):
            hs = slice(h * 256, (h + 1) * 256)
            kb.op("dve", lambda e, hs=hs, h=h: e.scalar_tensor_tensor(out=oib[:, hs], in0=oib[:, hs], scalar=ss[:, 4 + h:5 + h], in1=gn[:, hs],
                                                                      op0=ALU.mult, op1=ALU.mult), reads=[ss, gn], writes=[oib])
        kb.op("dve", lambda e: e.tensor_tensor(out=ob[:, :], in0=oib[:, :], in1=srb[:, :], op=ALU.mult), reads=[oib, srb], writes=[ob])
        ptb = pg.pt[j % 2]
        for i in range(8):
            kb.op("pe", lambda e, i=i, ptb=ptb: e.transpose(out=ptb[:, i * 128:(i + 1) * 128], in_=ob[:, i * 128:(i + 1) * 128],
                                                            identity=pg.identb[:, :]), reads=[ob, pg.identb], writes=[ptb])
        kb.op("dve", lambda e, ptb=ptb: e.tensor_copy(out=oTs[:, :], in_=ptb[:, :]), reads=[ptb], writes=[oTs])
        kb.dma("sp", nm["oT_gla"][0][:, ts].rearrange("(k p) t -> p k t", p=128), oTs[:, :].rearrange("p (k t) -> p k t", t=128),
               reads=[oTs], writes=[pg.R(nm["oT_gla"][1])])
    kb.pop()


def emit_merge_block(pg, wk, b, nm):
    kb = pg.kb
    kb.barrier()
    load_ln_params(pg, wk, nm["ln_g1"][0], nm["ln_b1"][0], nm["ln_g1"][1], nm["ln_b1"][1])
    tok = slice(b * BLK, (b + 1) * BLK)
    xT = load_xT_block(pg, wk, nm["x1T"][0], nm["x1T"][1], b)
    oTm = View(wk.hT[:, 0:4096].rearrange("p (k t) -> p k t", t=BLK))
    oTd = View(wk.hT[:, 4096:8192].rearrange("p (k t) -> p k t", t=BLK))
    oTg0 = View(wk.xb[:, :].rearrange("p (k t) -> p k t", t=BLK))
    oTg1 = View(wk.xts[:, :].rearrange("p (k t) -> p k t", t=BLK))
    bm = View(wk.stats[0:1, 0:24])
    brow = View(wk.hT[0:1, 8192:8192 + 1024].bitcast(F32))
    kb.dma("sp", oTm[:, :, :], nm["oT_mla"][0][:, tok].rearrange("(k p) t -> p k t", p=128), reads=[pg.R(nm["oT_mla"][1])], writes=[oTm])
    kb.dma("sp", oTd[:, :, :], nm["oT_dsa"][0][:, tok].rearrange("(k p) t -> p k t", p=128), reads=[pg.R(nm["oT_dsa"][1])], writes=[oTd])
    kb.dma("sp", oTg0[:, :, :], nm["oT_gla"][0][0:512, tok].rearrange("(k p) t -> p k t", p=128), reads=[pg.R(nm["oT_gla"][1])], writes=[oTg0])
    kb.dma("sp", oTg1[:, :, :], nm["oT_gla"][0][512:1024, tok].rearrange("(k p) t -> p k t", p=128), reads=[pg.R(nm["oT_gla"][1])], writes=[oTg1])
    brs = [("w_br_mla", lambda kc: oTm[:, kc, :], [oTm]), ("w_br_dsa", lambda kc: oTd[:, kc, :], [oTd]),
           ("w_br_gla", lambda kc: (oTg0[:, kc, :] if kc < 4 else oTg1[:, kc - 4, :]), [oTg0, oTg1])]
    for n in range(4):
        for bi, (wname, osel, oregs) in enumerate(brs):
            sm_, wm = pg.load_w(nm["w_merge"][0], nm["w_merge"][1], 0, 16, bi * 2048 + n * 512, 512)
            sb_, wb = pg.load_w(nm[wname][0], nm[wname][1], 0, 8, n * 512, 512)
            c0 = bi * 2048 + n * 512
            kb.dma("sp", brow[:, :], nm["b_merge"][0][0:1, c0:c0 + 512], reads=[pg.R(nm["b_merge"][1])], writes=[brow])
            for t in range(4):
                pgt = pg.bank()
                for kc in range(16):
                    mm(pg, pgt[:, :], xT[:, kc, t * 128:(t + 1) * 128], wm[:, kc, :], kc == 0, False, [wk.xT, sm_], pgt)
                mm(pg, pgt[:, :], pg.consts[0:1, C_ONES:C_ONES + 128], brow[0:1, :], False, True, [pg.consts, brow], pgt)
                sgb = wk.sg[t % 2]
                kb.op("act", lambda e, pgt=pgt, sgb=sgb: e.activation(out=sgb[:, :], in_=pgt[:, :], func=AF.Sigmoid), reads=[pgt], writes=[sgb])
                py = pg.bank()
                for kc in range(8):
                    mm(pg, py[:, :], osel(kc)[:, t * 128:(t + 1) * 128], wb[:, kc, :], kc == 0, kc == 7, oregs + [sb_], py)
                ms = wk.v[t][:, n * 512:(n + 1) * 512]
                if bi == 0:
                    kb.op("dve", lambda e, py=py, sgb=sgb, ms=ms: e.tensor_tensor(out=ms, in0=py[:, :], in1=sgb[:, :], op=ALU.mult),
                          reads=[py, sgb], writes=[wk.v[t]])
                else:
                    kb.op("dve", lambda e, py=py, sgb=sgb: e.tensor_tensor(out=sgb[:, :], in0=py[:, :], in1=sgb[:, :], op=ALU.mult),
                          reads=[py], writes=[sgb])
                    kb.op("dve", lambda e, sgb=sgb, ms=ms: e.tensor_tensor(out=ms, in0=ms, in1=sgb[:, :], op=ALU.add),
                          reads=[sgb], writes=[wk.v[t]])
    kb.barrier()
    mT = View(wk.hT[:, 0:8192].rearrange("p (k t) -> p k t", t=BLK))
    for t in range(4):
        kb.op("act", lambda e, t=t: e.activation(out=wk.xb[:, :], in_=wk.v[t][:, :], func=AF.Copy), reads=[wk.v[t]], writes=[wk.xb])
        for half in range(2):
            ptb = pg.pt[half]
            for k in range(8):
                kc = half * 8 + k
                kb.op("pe", lambda e, kc=kc, k=k, ptb=ptb: e.transpose(out=ptb[:, k * 128:(k + 1) * 128], in_=wk.xb[:, kc * 128:(kc + 1) * 128],
                                                                      identity=pg.identb[:, :]), reads=[wk.xb, pg.identb], writes=[ptb])
            kb.op("dve", lambda e, ptb=ptb, half=half, t=t: e.tensor_copy(
                out=mT[:, half * 8:half * 8 + 8, t * 128:(t + 1) * 128], in_=ptb[:, :].rearrange("p (k t) -> p k t", t=128)),
                reads=[ptb], writes=[mT])
        r0 = b * BLK + t * 128
        kb.dma("sp", wk.v[t][:, :], nm["x1"][0][r0:r0 + 128, :], reads=[pg.R(nm["x1"][1])], writes=[wk.v[t]])
    for n in range(4):
        so_, wo = pg.load_w(nm["w_out"][0], nm["w_out"][1], 0, 16, n * 512, 512)
        for t in range(4):
            ph = pg.bank()
            for kc in range(16):
                mm(pg, ph[:, :], mT[:, kc, t * 128:(t + 1) * 128], wo[:, kc, :], kc == 0, kc == 15, [mT, so_], ph)
            vs = wk.v[t][:, n * 512:(n + 1) * 512]
            kb.op("dve", lambda e, ph=ph, vs=vs: e.scalar_tensor_tensor(out=vs, in0=vs, scalar=ALPHA, in1=ph[:, :], op0=ALU.mult, op1=ALU.add),
                  reads=[ph], writes=[wk.v[t]])
    kb.barrier()
    for t in range(4):
        r0 = b * BLK + t * 128
        emit_ln(pg, wk.v[t], wk.v[t][:, :], wk.gbc, wk.bbc, wk.st, nm["x2"][0][r0:r0 + 128, :], nm["x2"][1],
                nm["x2T"][0][:, r0:r0 + 128], nm["x2T"][1], wk.xb, wk.xts)


class XAState:
    def __init__(self, pg, wk, nm):
        kb = pg.kb
        kb.barrier()
        self.KxT = Buf(kb, "xaK", [128, 1024], BF16)
        self.Vx = Buf(kb, "xaV", [128, 1024], BF16)
        memT = View(wk.hT[:, 0:4096].rearrange("p (k m) -> p k m", m=256))
        for mt in range(2):
            vb = wk.v[mt]
            kb.dma("sp", vb[:, :], nm["mem"][0][mt * 128:(mt + 1) * 128, :], reads=[pg.R(nm["mem"][1])], writes=[vb])
            kb.op("act", lambda e, vb=vb: e.activation(out=wk.xb[:, :], in_=vb[:, :], func=AF.Copy), reads=[vb], writes=[wk.xb])
            for half in range(2):
                ptb = pg.pt[half]
                for k in range(8):
                    kc = half * 8 + k
                    kb.op("pe", lambda e, kc=kc, k=k, ptb=ptb: e.transpose(out=ptb[:, k * 128:(k + 1) * 128], in_=wk.xb[:, kc * 128:(kc + 1) * 128],
                                                                          identity=pg.identb[:, :]), reads=[wk.xb, pg.identb], writes=[ptb])
                kb.op("dve", lambda e, ptb=ptb, half=half, mt=mt: e.tensor_copy(
                    out=memT[:, half * 8:half * 8 + 8, mt * 128:(mt + 1) * 128], in_=ptb[:, :].rearrange("p (k t) -> p k t", t=128)),
                    reads=[ptb], writes=[memT])
        K3 = self.KxT[:, :].rearrange("p (h m) -> p h m", m=256)
        V3 = self.Vx[:, :].rearrange("p (a c) -> p a c", c=512)
        sk_, wkk = pg.load_w(nm["xa_w_kv"][0], nm["xa_w_kv"][1], 0, 16, 0, 512)
        for h in range(4):
            ps = pg.bank()
            for kc in range(16):
                mm(pg, ps[:, 0:256], wkk[:, kc, h * 128:(h + 1) * 128], memT[:, kc, :], kc == 0, kc == 15, [sk_, memT], ps)
            kb.op("act", lambda e, ps=ps, h=h: e.activation(out=K3[:, h, :], in_=ps[:, 0:256], func=AF.Copy), reads=[ps], writes=[self.KxT])
        sv_, wvv = pg.load_w(nm["xa_w_kv"][0], nm["xa_w_kv"][1], 0, 16, 512, 512)
        for mt in range(2):
            ps = pg.bank()
            for kc in range(16):
                mm(pg, ps[:, :], memT[:, kc, mt * 128:(mt + 1) * 128], wvv[:, kc, :], kc == 0, kc == 15, [sv_, memT], ps)
            kb.op("act", lambda e, ps=ps, mt=mt: e.activation(out=V3[:, mt, :], in_=ps[:, :], func=AF.Copy), reads=[ps], writes=[self.Vx])
        self.K3, self.V3 = K3, V3
        kb.barrier()


def emit_xattn_block(pg, wk, xa, b, nm):
    kb = pg.kb
    kb.barrier()
    load_ln_params(pg, wk, nm["ln_g2"][0], nm["ln_b2"][0], nm["ln_g2"][1], nm["ln_b2"][1])
    xT = load_xT_block(pg, wk, nm["x2T"][0], nm["x2T"][1], b)
    qT = View(wk.hT[:, 0:2048].rearrange("p (h t) -> p h t", t=BLK))
    oxT = View(wk.hT[:, 2048:4096].rearrange("p (h t) -> p h t", t=BLK))
    Pb = [View(wk.hT[:, 4096 + i * 256:4096 + (i + 1) * 256]) for i in range(2)]
    PTb = [View(wk.hT[:, 4608 + i * 256:4608 + (i + 1) * 256]) for i in range(2)]
    ob = View(wk.hT[:, 5120:5632])
    rs = View(wk.sg[0][:, 0:8])
    sq_, wq = pg.load_w(nm["xa_w_q"][0], nm["xa_w_q"][1], 0, 16, 0, 512)
    for h in range(4):
        ps = fm_proj(pg, wq, sq_, h * 128, 128, xT, wk.xT)
        kb.op("act", lambda e, ps=ps, h=h: e.activation(out=qT[:, h, :], in_=ps[:, :], func=AF.Copy), reads=[ps], writes=[qT])
    for t in range(4):
        r0 = b * BLK + t * 128
        kb.dma("sp", wk.v[t][:, :], nm["x2"][0][r0:r0 + 128, :], reads=[pg.R(nm["x2"][1])], writes=[wk.v[t]])
        for h in range(4):
            ps = pg.bank()
            mm(pg, ps[:, 0:256], qT[:, h, t * 128:(t + 1) * 128], xa.K3[:, h, :], True, True, [qT, xa.KxT], ps)
            p = Pb[h % 2]
            kb.op("act", lambda e, ps=ps, p=p, h=h: e.activation(out=p[:, :], in_=ps[:, 0:256], func=AF.Exp, scale=128.0 ** -0.5,
                                                                 accum_out=rs[:, h:h + 1]), reads=[ps], writes=[p, rs])
            ptb = pg.pt[h % 2]
            for i in range(2):
                kb.op("pe", lambda e, i=i, p=p, ptb=ptb: e.transpose(out=ptb[:, i * 128:(i + 1) * 128], in_=p[:, i * 128:(i + 1) * 128],
                                                                    identity=pg.identb[:, :]), reads=[p, pg.identb], writes=[ptb])
            pt_ = PTb[h % 2]
            kb.op("dve", lambda e, ptb=ptb, pt_=pt_: e.tensor_copy(out=pt_[:, :], in_=ptb[:, 0:256]), reads=[ptb], writes=[pt_])
            po = pg.bank()
            for i in range(2):
                mm(pg, po[:, 0:128], pt_[:, i * 128:(i + 1) * 128], xa.V3[:, i, h * 128:(h + 1) * 128], i == 0, i == 1, [pt_, xa.Vx], po)
            kb.op("dve", lambda e, h=h: e.reciprocal(out=rs[:, 4 + h:5 + h], in_=rs[:, h:h + 1]), reads=[rs], writes=[rs])
            kb.op("act", lambda e, po=po, h=h: e.activation(out=ob[:, h * 128:(h + 1) * 128], in_=po[:, 0:128], func=AF.Copy, scale=rs[:, 4 + h:5 + h]),
                  reads=[po, rs], writes=[ob])
        ptb = pg.pt[0]
        for h in range(4):
            kb.op("pe", lambda e, h=h, ptb=ptb: e.transpose(out=ptb[:, h * 128:(h + 1) * 128], in_=ob[:, h * 128:(h + 1) * 128],
                                                            identity=pg.identb[:, :]), reads=[ob, pg.identb], writes=[ptb])
        kb.op("dve", lambda e, ptb=ptb, t=t: e.tensor_copy(out=oxT[:, :, t * 128:(t + 1) * 128], in_=ptb[:, 0:512].rearrange("p (h t) -> p h t", t=128)),
              reads=[ptb], writes=[oxT])
    for n in range(4):
        so_, wo = pg.load_w(nm["xa_w_o"][0], nm["xa_w_o"][1], 0, 4, n * 512, 512)
        for t in range(4):
            ph = pg.bank()
            for kc in range(4):
                mm(pg, ph[:, :], oxT[:, kc, t * 128:(t + 1) * 128], wo[:, kc, :], kc == 0, kc == 3, [oxT, so_], ph)
            vs = wk.v[t][:, n * 512:(n + 1) * 512]
            kb.op("dve", lambda e, ph=ph, vs=vs: e.scalar_tensor_tensor(out=vs, in0=vs, scalar=ALPHA, in1=ph[:, :], op0=ALU.mult, op1=ALU.add),
                  reads=[ph], writes=[wk.v[t]])
    kb.barrier()
    for t in range(4):
        r0 = b * BLK + t * 128
        emit_ln(pg, wk.v[t], wk.v[t][:, :], wk.gbc, wk.bbc, wk.st, nm["x3"][0][r0:r0 + 128, :], nm["x3"][1],
                nm["x3T"][0][:, r0:r0 + 128], nm["x3T"][1], wk.xb, wk.xts)


A_OUT = [("x1", [TL, D], F32), ("x1T", [D, TL], BF16), ("QnT", [8, 128, TL], BF16), ("QrT", [8, 64, TL], BF16),
         ("QdT", [8, 128, TL], BF16), ("IqT", [8, 128, TL], BF16), ("iw", [TL, 16], F32), ("gqT", [4, 128, TL], BF16),
         ("gU0", [NT, 128, 1024], F32), ("gA", [NT, 128, 8], F32), ("goi", [TL, 1024], F32), ("gsr", [TL, 1024], BF16),
         ("KnT_o", [8, 128, TL], BF16), ("KrT_o", [64, TL], BF16), ("Vm_o", [TL, 1024], BF16), ("KdT_o", [8, 128, TL], BF16),
         ("Vd_o", [TL, 1024], BF16), ("IkT_o", [64, TL], BF16), ("gUt_o", [NT, 128, 1024], F32), ("gAt_o", [NT, 128, 4], F32)]
A_LOCAL = ["x1", "x1T", "QnT", "QrT", "QdT", "IqT", "iw", "gqT", "gU0", "gA", "goi", "gsr"]
B_GLOBAL = [("KnT_g", [8, 128, SEQ], BF16), ("KrT_g", [64, SEQ], BF16), ("Vm_g", [8, 128, SEQ], BF16), ("KdT_g", [8, 128, SEQ], BF16),
            ("Vd_g", [8, 128, SEQ], BF16), ("IkT_g", [64, SEQ], BF16), ("gUt_g", [SEQ // 128, 128, 1024], F32), ("gAt_g", [128, 512], F32)]
A_W = [("ffn1_g", [D, DFF]), ("ffn1_u", [D, DFF]), ("ffn1_d", [DFF, D]), ("ln_g0", [D]), ("ln_b0", [D]), ("w_in", [D, IN_W]),
       ("w_uq", [512, 1536]), ("w_ukv", [512, 2048]), ("sm", [128, 8]), ("wg2", [16, 512]), ("bg", [1, 512])]
B_W = [("w_br_mla", [1024, D]), ("w_br_dsa", [1024, D]), ("w_br_gla", [1024, D]), ("w_merge", [D, 3 * D]), ("b_merge", [1, 3 * D]),
       ("w_out", [D, D]), ("xa_w_q", [D, 512]), ("xa_w_kv", [D, 1024]), ("xa_w_o", [512, D]), ("mem", [256, D]),
       ("ln_g1", [D]), ("ln_b1", [D]), ("ln_g2", [D]), ("ln_b2", [D]), ("ffn2_g", [D, DFF]), ("ffn2_u", [D, DFF]), ("ffn2_d", [DFF, D]),
       ("ln_g3", [D]), ("ln_b3", [D]), ("gla_norm", [1024])]


def build_program(has_B, has_A, dbg_out=()):
    pg = Prog()
    kb = pg.kb
    nmB, nmA = {}, {}
    if has_B:
        for k in A_LOCAL:
            shape, dt = next((s, d) for (n, s, d) in A_OUT if n == k)
            nmB[k] = (pg.inp("i_" + k, shape, dt), "i_" + k)
        for (k, shape, dt) in B_GLOBAL:
            nmB[k] = (pg.inp(k, shape, dt), k)
        nmB["cmask"] = (pg.inp("cmask", [128, 1024], F32), "cmask")
        for (k, shape) in B_W:
            nmB[k] = (pg.inp("B_" + k, shape, F32), "B_" + k)
        for (k, shape, dt) in (("MnegD", [NT, 128, SEQ], BF16), ("oT_mla", [1024, TL], BF16), ("oT_dsa", [1024, TL], BF16),
                               ("oT_gla", [1024, TL], BF16), ("x2", [TL, D], F32), ("x2T", [D, TL], BF16),
                               ("x3", [TL, D], F32), ("x3T", [D, TL], BF16), ("x4T", [D, TL], BF16)):
            if k in dbg_out:
                nmB[k] = (pg.out(k, shape, dt), k)
            else:
                nmB[k] = (pg.scr(k, shape, dt), k)
        if has_A:
            nmB["x4"] = (pg.out("x4", [TL, D], F32), "x4") if "x4" in dbg_out else (pg.scr("x4", [TL, D], F32), "x4")
        else:
            nmB["x4"] = (pg.out("y", [TL, D], F32), "y")
    if has_A:
        for (k, shape) in A_W:
            nmA[k] = (pg.inp("A_" + k, shape, F32), "A_" + k)
        nmA["pos"] = (pg.inp("pos", [TL], I32), "pos")
        nmA["tabs"] = (pg.scr("tabs", [4, 128, TL], F32), "tabs")
        for (k, shape, dt) in A_OUT:
            nmA[k] = (pg.out("o_" + k, shape, dt), "o_" + k)
        nmA["xT"] = nmA["x1T"]
        if not has_B:
            nmA["x0"] = (pg.inp("x_in", [TL, D], F32), "x_in")
            nmA["x0T"] = (pg.scr("x0T", [D, TL], BF16), "x0T")
        else:
            nmA["x0"] = nmB["x4"]
            nmA["x0T"] = nmB["x4T"]
    if has_B:
        emit_indexer(pg, nmB)
        emit_attention(pg, nmB, "mla")
        emit_attention(pg, nmB, "dsa")
        emit_gla_out(pg, nmB)
    kb.push()
    wk = Work(pg)
    if has_A:
        mp = MixParams(pg, "A")
        mp.load(pg, nmA["sm"][0], nmA["sm"][1], nmA["wg2"][0], nmA["wg2"][1], nmA["bg"][0], nmA["bg"][1])
        emit_rope_tables(pg, wk, nmA["pos"][0], nmA["pos"][1], nmA["tabs"][0], nmA["tabs"][1])
        if not has_B:
            emit_prep_xT(pg, wk, nmA["x0"], nmA["x0T"])
    if has_B:
        xa = XAState(pg, wk, nmB)
        for b in range(NB):
            emit_merge_block(pg, wk, b, nmB)
        for b in range(NB):
            emit_xattn_block(pg, wk, xa, b, nmB)
        for b in range(NB):
            emit_ffn_block(pg, wk, b, nmB["x3"], nmB["x3T"], nmB["ffn2_g"], nmB["ffn2_u"], nmB["ffn2_d"],
                           (nmB["ln_g3"], nmB["ln_b3"]), nmB["x4"], nmB["x4T"])
    if has_A:
        for b in range(NB):
            emit_ffn_block(pg, wk, b, nmA["x0"], nmA["x0T"], nmA["ffn1_g"], nmA["ffn1_u"], nmA["ffn1_d"],
                           (nmA["ln_g0"], nmA["ln_b0"]), nmA["x1"], nmA["x1T"])
        for b in range(NB):
            emit_mixproj_block(pg, wk, mp, b, {"n": nmA})
    kb.pop()
    kb.finish()
    return pg


def _loc(a, c):
    return np.ascontiguousarray(a.reshape(NT, NCORES, 128, *a.shape[1:])[:, c].reshape(TL, *a.shape[1:]))


def _glob_cols(parts):
    lead = parts[0].shape[:-1]
    st = np.stack([p.reshape(*lead, NT, 128) for p in parts], axis=-2)
    return np.ascontiguousarray(st.reshape(*lead, SEQ))


def _glob_rows(parts):
    f = parts[0].shape[1:]
    st = np.stack([p.reshape(NT, 128, *f) for p in parts], axis=1)
    return np.ascontiguousarray(st.reshape(SEQ, *f))


def _vlay(v):
    return np.ascontiguousarray(v.reshape(SEQ // 128, 128, 8, 128).transpose(2, 1, 0, 3).reshape(8, 128, SEQ))


def _a_weights(inp, l):
    sm = np.zeros((128, 8), np.float32)
    sm[:, 0:4] = inp["mla_q_norm"][l].reshape(4, 128).T
    sm[:, 4:8] = inp["mla_kv_norm"][l].reshape(4, 128).T
    return {"A_ffn1_g": inp["ffn_w_gate"][l, 0], "A_ffn1_u": inp["ffn_w_up"][l, 0], "A_ffn1_d": inp["ffn_w_down"][l, 0],
            "A_ln_g0": inp["ln_gain"][l, 0], "A_ln_b0": inp["ln_bias"][l, 0], "A_w_in": inp["w_in"][l],
            "A_w_uq": inp["mla_w_uq"][l], "A_w_ukv": inp["mla_w_ukv"][l], "A_sm": sm,
            "A_wg2": inp["gla_w_gate2"][l], "A_bg": inp["gla_b_gate"][l][None, :]}


def _b_weights(inp, l):
    return {"B_w_br_mla": inp["w_branch_mla"][l], "B_w_br_dsa": inp["w_branch_dsa"][l], "B_w_br_gla": inp["w_branch_gla"][l],
            "B_w_merge": inp["w_merge"][l], "B_b_merge": inp["b_merge"][l][None, :], "B_w_out": inp["w_out"][l],
            "B_xa_w_q": inp["xa_w_q"][l], "B_xa_w_kv": inp["xa_w_kv"][l], "B_xa_w_o": inp["xa_w_o"][l], "B_mem": inp["mem"][0],
            "B_ln_g1": inp["ln_gain"][l, 1], "B_ln_b1": inp["ln_bias"][l, 1], "B_ln_g2": inp["ln_gain"][l, 2], "B_ln_b2": inp["ln_bias"][l, 2],
            "B_ffn2_g": inp["ffn_w_gate"][l, 1], "B_ffn2_u": inp["ffn_w_up"][l, 1], "B_ffn2_d": inp["ffn_w_down"][l, 1],
            "B_ln_g3": inp["ln_gain"][l, 3], "B_ln_b3": inp["ln_bias"][l, 3], "B_gla_norm": inp["gla_norm"][l]}


def _gather(res):
    loc = [{"i_" + k: r["o_" + k] for k in A_LOCAL} for r in res]
    g = {"KnT_g": _glob_cols([r["o_KnT_o"] for r in res]), "KrT_g": _glob_cols([r["o_KrT_o"] for r in res]),
         "KdT_g": _glob_cols([r["o_KdT_o"] for r in res]), "IkT_g": _glob_cols([r["o_IkT_o"] for r in res]),
         "Vm_g": _vlay(_glob_rows([r["o_Vm_o"] for r in res])), "Vd_g": _vlay(_glob_rows([r["o_Vd_o"] for r in res]))}
    ut = np.stack([r["o_gUt_o"] for r in res], axis=1)
    g["gUt_g"] = np.ascontiguousarray(ut.reshape(SEQ // 128, 128, 1024))
    at = np.stack([r["o_gAt_o"] for r in res], axis=1)
    g["gAt_g"] = np.ascontiguousarray(at.reshape(SEQ // 128, 128, 4).transpose(1, 0, 2).reshape(128, 512))
    return loc, g


_PROGS = {}


def _prog(has_B, has_A):
    key = (has_B, has_A)
    if key not in _PROGS:
        _PROGS[key] = build_program(has_B, has_A)
    return _PROGS[key]


def kernel(**inp):
    inp = {k: np.asarray(v) for k, v in inp.items()}
    cores = list(range(NCORES))
    consts = [make_consts(c) for c in cores]
    cmask = [make_cmask(c) for c in cores]
    pos = [_loc(inp["positions"][0].astype(np.int32), c) for c in cores]
    pg = _prog(False, True)
    wa = _a_weights(inp, 0)
    maps = [dict(wa, consts=consts[c], pos=pos[c], x_in=_loc(inp["x"][0], c)) for c in cores]
    res = run_bass_kernel_spmd(pg.nc, maps, core_ids=cores).results
    loc, g = _gather(res)
    pg = _prog(True, True)
    wb, wa = _b_weights(inp, 0), _a_weights(inp, 1)
    maps = [dict(wb, **wa, **g, **loc[c], consts=consts[c], cmask=cmask[c], pos=pos[c]) for c in cores]
    res = run_bass_kernel_spmd(pg.nc, maps, core_ids=cores).results
    loc, g = _gather(res)
    pg = _prog(True, False)
    wb = _b_weights(inp, 1)
    maps = [dict(wb, **g, **loc[c], consts=consts[c], cmask=cmask[c]) for c in cores]
    res = run_bass_kernel_spmd(pg.nc, maps, core_ids=cores).results
    y = _glob_rows([r["y"] for r in res])
    return y[None].astype(np.float32)
```

```python
import contextlib
import math
import numpy as np
import ml_dtypes
import concourse.bass as bass
import concourse.mybir as mybir
from concourse.bass_utils import run_bass_kernel_spmd

F32 = mybir.dt.float32
BF16 = mybir.dt.bfloat16
I32 = mybir.dt.int32
AF = mybir.ActivationFunctionType
ALU = mybir.AluOpType
NPBF = ml_dtypes.bfloat16

NCORES = 8
SEQ = 16384
D = 2048
DFF = 5632
TL = SEQ // NCORES
NT = TL // 128
BLK = 512
NB = TL // BLK
ALPHA = 4.0 ** 0.25
EPS = 1e-5
NEG = -30000.0
IN_W = 8352

ENG_NAMES = ["pe", "act", "dve", "pool", "sp"]


class Reg:
    __slots__ = ("w", "r")

    def __init__(self):
        self.w = None
        self.r = {}


class Buf:
    _uid = [0]

    def __init__(self, kb, name, shape, dt, psum=False):
        Buf._uid[0] += 1
        name = f"{name}_{Buf._uid[0]}"
        if psum:
            self.t = kb.scopes[-1].enter_context(kb.nc.psum_tensor(name, shape, dt))
            self.psum = True
        else:
            self.psum = False
            self.t = kb.scopes[-1].enter_context(kb.nc.sbuf_tensor(name, shape, dt))
        self.r = Reg()

    def __getitem__(self, idx):
        return self.t[idx]


class View:
    def __init__(self, ap):
        self.ap = ap
        self.r = Reg()

    def __getitem__(self, idx):
        return self.ap[idx]


class KB:
    def __init__(self, nc):
        self.nc = nc
        self.es = contextlib.ExitStack()
        self.scopes = [self.es]
        engs = [nc.tensor, nc.scalar, nc.vector, nc.gpsimd, nc.sync]
        self.eng = dict(zip(ENG_NAMES, engs))
        self.idx = {n: i for i, n in enumerate(ENG_NAMES)}
        self.sems = [self.es.enter_context(nc.semaphore("s_" + n)) for n in ENG_NAMES]
        self.cnt = [0] * len(ENG_NAMES)
        self.waited = [dict() for _ in ENG_NAMES]
        self.dpool = {}
        for q, n in (("sp", 24), ("pool", 16), ("act", 8)):
            lst = []
            for i in range(n):
                self.sems.append(self.es.enter_context(nc.semaphore(f"d_{q}{i}")))
                self.cnt.append(0)
                lst.append(len(self.sems) - 1)
            self.dpool[q] = [lst, 0]
        self.ninstr = 0

    def _wait(self, ei, deps):
        w = self.waited[ei]
        for (si, v) in deps.items():
            if si == ei and ei == 0:
                continue
            if w.get(si, 0) < v:
                self.eng[ENG_NAMES[ei]].wait_ge(self.sems[si], v)
                w[si] = v

    @staticmethod
    def _deps(reads, writes):
        deps = {}

        def add(t):
            if t is not None and deps.get(t[0], 0) < t[1]:
                deps[t[0]] = t[1]
        for r in reads:
            add(r.w)
        for r in writes:
            add(r.w)
            for si, v in r.r.items():
                add((si, v))
        return deps

    @staticmethod
    def _mark(t, reads, writes):
        for r in reads:
            if r.r.get(t[0], 0) < t[1]:
                r.r[t[0]] = t[1]
        for r in writes:
            r.w = t
            r.r = {}

    def op(self, eng, fn, reads=(), writes=()):
        if self.ninstr >= getattr(self, "maxops", 1 << 60):
            return None
        ei = self.idx[eng]
        writes = list(writes) + [x for x in reads if isinstance(x, Buf) and x.psum]
        reads = [x for x in reads if not (isinstance(x, Buf) and x.psum)]
        reads = [x if isinstance(x, Reg) else x.r for x in reads]
        writes = [x if isinstance(x, Reg) else x.r for x in writes]
        self._wait(ei, self._deps(reads, writes))
        ins = fn(self.eng[eng])
        self.cnt[ei] += 1
        ins.then_inc(self.sems[ei], 1)
        self._mark((ei, self.cnt[ei]), reads, writes)
        self.ninstr += 1
        return ins

    def dma(self, q, out, in_, reads=(), writes=()):
        if self.ninstr >= getattr(self, "maxops", 1 << 60):
            return None
        ei = self.idx[q]
        reads = [x if isinstance(x, Reg) else x.r for x in reads]
        writes = [x if isinstance(x, Reg) else x.r for x in writes]
        lst, nxt = self.dpool[q]
        si = lst[nxt]
        self.dpool[q][1] = (nxt + 1) % len(lst)
        deps = self._deps(reads, writes)
        if self.cnt[si] > 0 and deps.get(si, 0) < self.cnt[si]:
            deps[si] = self.cnt[si]
        self._wait(ei, deps)
        self.cnt[si] += 16
        self.eng[q].dma_start(out=out, in_=in_).then_inc(self.sems[si], 16)
        self._mark((si, self.cnt[si]), reads, writes)
        self.ninstr += 1

    def push(self):
        self.barrier()
        self.scopes.append(contextlib.ExitStack())

    def pop(self):
        self.barrier()
        self.scopes.pop().close()

    def barrier(self):
        for ei in range(len(ENG_NAMES)):
            deps = {si: v for si, v in enumerate(self.cnt) if v > 0 and si != ei}
            self._wait(ei, deps)

    def finish(self):
        deps = {si: v for si, v in enumerate(self.cnt) if v > 0 and si != self.idx["sp"]}
        self._wait(self.idx["sp"], deps)
        self.es.close()


C_ID, C_ONES, C_TRI, C_RTRI, C_MASKF, C_P128, C_P64, C_CH, C_F64, C_F128, C_EI, C_EPS, C_END = (
    0, 128, 256, 384, 512, 640, 768, 896, 898, 899, 900, 908, 909)


def make_consts(core):
    c = np.zeros((128, C_END), np.float32)
    c[:, C_ID:C_ID + 128] = np.eye(128)
    c[:, C_ONES:C_ONES + 128] = 1.0
    s = np.arange(128)[:, None]
    t = np.arange(128)[None, :]
    same = (s // 64) == (t // 64)
    c[:, C_TRI:C_TRI + 128] = (same & (s <= t))
    c[:, C_RTRI:C_RTRI + 128] = (same & (s > t))
    c[:, C_MASKF:C_MASKF + 128] = (same & (s <= t))
    for half, col in ((64, C_P128), (32, C_P64)):
        p = np.zeros((128, 128), np.float32)
        for m in range(128):
            if (m % (2 * half)) < half:
                p[m + half, m] = -1.0
            else:
                p[m - half, m] = 1.0
        c[:, col:col + 128] = p
    c[:, C_CH] = (np.arange(128) < 64)
    c[:, C_CH + 1] = (np.arange(128) >= 64)
    f64 = (1.0 / (np.float32(10000.0) ** (np.arange(0, 64, 2, dtype=np.float32) / np.float32(64)))).astype(np.float32)
    f128 = (1.0 / (np.float32(10000.0) ** (np.arange(0, 128, 2, dtype=np.float32) / np.float32(128)))).astype(np.float32)
    c[:, C_F64] = f64[np.arange(128) % 32]
    c[:, C_F128] = f128[np.arange(128) % 64]
    c[:, C_EI + core] = 1.0
    c[:, C_EPS] = EPS
    return c


def make_cmask(core):
    m = np.zeros((128, 1024), np.float32)
    q = np.arange(128)[:, None]
    for i in range(8):
        blk = m[:, i * 128:(i + 1) * 128]
        if i > core:
            blk[:] = NEG
        elif i == core:
            blk[np.arange(128)[None, :] > q] = NEG
    return m


class Prog:
    def __init__(self):
        self.nc = bass.Bass("TRN2", target_bir_lowering=False)
        self.kb = KB(self.nc)
        self.in_names = []
        self.out_names = []
        self.dreg = {}
        kb = self.kb
        self.consts_d = self.inp("consts", [128, C_END], F32)
        self.consts = Buf(kb, "consts_sb", [128, C_END], F32)
        kb.dma("sp", self.consts[:, :], self.consts_d[:, :], writes=[self.consts])
        self.identb = Buf(kb, "identb", [128, 128], BF16)
        kb.op("dve", lambda e: e.tensor_copy(out=self.identb[:, :], in_=self.consts[:, C_ID:C_ID + 128]),
              reads=[self.consts], writes=[self.identb])
        self.pb = [Buf(kb, f"pb{i}", [128, 512], F32, psum=True) for i in range(6)]
        self.pt = [Buf(kb, f"pt{i}", [128, 1024], BF16, psum=True) for i in range(2)]
        self.ws = None
        self.wsn = 0
        self.pbn = 0
        self.uid = 0

    def _dt(self, name, shape, dt, kind):
        t = self.nc.dram_tensor(name, list(shape), dt, kind=kind)
        ap = t.ap()
        self.dreg[name] = Reg()
        return ap

    def inp(self, name, shape, dt):
        self.in_names.append(name)
        return self._dt(name, shape, dt, "ExternalInput")

    def out(self, name, shape, dt):
        self.out_names.append(name)
        return self._dt(name, shape, dt, "ExternalOutput")

    def scr(self, name, shape, dt):
        return self._dt(name, shape, dt, "Internal")

    def R(self, name):
        return self.dreg[name]

    def cc(self, col, n=128, rows=128):
        return self.consts[0:rows, col:col + n]

    def load_w(self, w_ap, wname, k0, kc, c0, n):
        slot = self.ws[self.wsn]
        self.wsn = (self.wsn + 1) % len(self.ws)
        assert kc * n <= 8192
        view = slot[:, 0:kc * n].rearrange("p (k n) -> p k n", n=n)
        src = w_ap[k0:k0 + kc * 128, c0:c0 + n].rearrange("(k p) n -> p k n", p=128)
        self.kb.dma("pool", view, src, reads=[self.R(wname)], writes=[slot])
        return slot, view

    def bank(self):
        b = self.pb[self.pbn]
        self.pbn = (self.pbn + 1) % 4
        return b


def mm(pg, out_ap, lhsT, rhs, start, stop, reads, wbank):
    pg.kb.op("pe", lambda e: e.matmul(out_ap, lhsT=lhsT, rhs=rhs, start=start, stop=stop),
             reads=reads, writes=[wbank])


def emit_transpose_out(pg, src_buf, src_ap, ncol_chunks, xb, xts, dst_ap, dst_name, q="sp"):
    kb = pg.kb
    kb.op("act", lambda e: e.activation(out=xb[:, 0:ncol_chunks * 128], in_=src_ap, func=AF.Copy),
          reads=[src_buf], writes=[xb])
    for half in range((ncol_chunks + 7) // 8):
        n = min(8, ncol_chunks - half * 8)
        ptb = pg.pt[half % 2]
        for k in range(n):
            kc = half * 8 + k
            kb.op("pe", lambda e, kc=kc, k=k, ptb=ptb: e.transpose(out=ptb[:, k * 128:(k + 1) * 128],
                                                                    in_=xb[:, kc * 128:(kc + 1) * 128],
                                                                    identity=pg.identb[:, :]),
                  reads=[xb, pg.identb], writes=[ptb])
        kb.op("dve", lambda e, ptb=ptb, half=half, n=n: e.tensor_copy(
            out=xts[:, half * 1024:half * 1024 + n * 128], in_=ptb[:, 0:n * 128]),
            reads=[ptb], writes=[xts])
    kb.dma(q, dst_ap.rearrange("(k p) t -> p k t", p=128),
           xts[:, 0:ncol_chunks * 128].rearrange("p (k t) -> p k t", t=128),
           reads=[xts], writes=[pg.R(dst_name)])


def emit_ln(pg, vbuf, vap, gbc, bbc, st, x_out_ap, x_out_name, xT_out_ap, xT_out_name, xb, xts):
    kb = pg.kb
    stats, mv, rstd = st
    for c in range(4):
        kb.op("dve", lambda e, c=c: e.bn_stats(out=stats[:, c * 6:(c + 1) * 6], in_=vap[:, c * 512:(c + 1) * 512]),
              reads=[vbuf], writes=[stats])
    kb.op("dve", lambda e: e.bn_aggr(out=mv[:, 0:2], in_=stats[:, 0:24]), reads=[stats], writes=[mv])
    kb.op("act", lambda e: e.activation(out=rstd[:, 0:1], in_=mv[:, 1:2], func=AF.Ln, bias=pg.cc(C_EPS, 1), scale=1.0),
          reads=[mv, pg.consts], writes=[rstd])
    kb.op("act", lambda e: e.activation(out=rstd[:, 1:2], in_=rstd[:, 0:1], func=AF.Exp, scale=-0.5),
          reads=[rstd], writes=[rstd])
    kb.op("dve", lambda e: e.tensor_scalar(out=vap, in0=vap, scalar1=mv[:, 0:1], scalar2=rstd[:, 1:2],
                                           op0=ALU.subtract, op1=ALU.mult),
          reads=[vbuf, mv, rstd], writes=[vbuf])
    kb.op("dve", lambda e: e.tensor_tensor(out=vap, in0=vap, in1=gbc[:, :], op=ALU.mult), reads=[vbuf, gbc], writes=[vbuf])
    kb.op("dve", lambda e: e.tensor_tensor(out=vap, in0=vap, in1=bbc[:, :], op=ALU.add), reads=[vbuf, bbc], writes=[vbuf])
    kb.dma("sp", x_out_ap, vap, reads=[vbuf], writes=[pg.R(x_out_name)])
    emit_transpose_out(pg, vbuf, vap, 16, xb, xts, xT_out_ap, xT_out_name)


class Work:
    def __init__(self, pg):
        kb = pg.kb
        pg.uid += 1
        u = str(pg.uid)
        pg.ws = [Buf(kb, f"ws{i}_" + u, [128, 8192], BF16) for i in range(3)]
        self.v = [Buf(kb, f"v{i}", [128, D], F32) for i in range(4)]
        self.xT = Buf(kb, "xTblk", [128, 16 * BLK], BF16)
        self.hT = Buf(kb, "hT", [128, 22 * BLK], BF16)
        self.gbc = Buf(kb, "gbc", [128, D], F32)
        self.bbc = Buf(kb, "bbc", [128, D], F32)
        self.sg = [Buf(kb, f"sg{i}", [128, BLK], F32) for i in range(2)]
        self.xb = Buf(kb, "xb", [128, D], BF16)
        self.xts = Buf(kb, "xts", [128, D], BF16)
        self.stats = Buf(kb, "stats", [128, 24], F32)
        self.mv = Buf(kb, "mv", [128, 2], F32)
        self.rstd = Buf(kb, "rstd", [128, 2], F32)
        self.st = (self.stats, self.mv, self.rstd)


def load_xT_block(pg, wk, xT_ap, xT_name, b):
    view = wk.xT[:, :].rearrange("p (k t) -> p k t", t=BLK)
    pg.kb.dma("sp", view, xT_ap[:, b * BLK:(b + 1) * BLK].rearrange("(k p) t -> p k t", p=128),
              reads=[pg.R(xT_name)], writes=[wk.xT])
    return view


def load_ln_params(pg, wk, g_ap, b_ap, gname, bname):
    pg.kb.dma("sp", wk.gbc[:, :], g_ap.partition_broadcast(128), reads=[pg.R(gname)], writes=[wk.gbc])
    pg.kb.dma("sp", wk.bbc[:, :], b_ap.partition_broadcast(128), reads=[pg.R(bname)], writes=[wk.bbc])


def emit_ffn_block(pg, wk, b, x_in, xT_in, wg, wu, wd, ln, x_out, xT_out):
    kb = pg.kb
    kb.barrier()
    load_ln_params(pg, wk, ln[0][0], ln[1][0], ln[0][1], ln[1][1])
    xT = load_xT_block(pg, wk, xT_in[0], xT_in[1], b)
    for t in range(4):
        kb.dma("sp", wk.v[t][:, :], x_in[0][b * BLK + t * 128: b * BLK + (t + 1) * 128, :],
               reads=[pg.R(x_in[1])], writes=[wk.v[t]])
    hT = wk.hT[:, :].rearrange("p (f t) -> p f t", t=BLK)
    for half in range(2):
        for grp in range(6):
            nf = 4 if grp < 5 else 2
            f0 = half * 22 + grp * 4
            sg_, wgv = pg.load_w(wg[0], wg[1], 0, 16, f0 * 128, nf * 128)
            su_, wuv = pg.load_w(wu[0], wu[1], 0, 16, f0 * 128, nf * 128)
            for fi in range(nf):
                pgate = pg.bank()
                pup = pg.bank()
                for kc in range(16):
                    mm(pg, pgate[:, :], wgv[:, kc, fi * 128:(fi + 1) * 128], xT[:, kc, :], kc == 0, kc == 15,
                       [sg_, wk.xT], pgate)
                for kc in range(16):
                    mm(pg, pup[:, :], wuv[:, kc, fi * 128:(fi + 1) * 128], xT[:, kc, :], kc == 0, kc == 15,
                       [su_, wk.xT], pup)
                sgb = wk.sg[(grp * 4 + fi) % 2]
                kb.op("act", lambda e, pgate=pgate, sgb=sgb: e.activation(out=sgb[:, :], in_=pgate[:, :], func=AF.Silu),
                      reads=[pgate], writes=[sgb])
                fl = grp * 4 + fi
                kb.op("dve", lambda e, pup=pup, sgb=sgb, fl=fl: e.scalar_tensor_tensor(
                    out=hT[:, fl, :], in0=pup[:, :], scalar=0.5, in1=sgb[:, :], op0=ALU.mult, op1=ALU.mult),
                    reads=[pup, sgb], writes=[wk.hT])
        for n in range(4):
            s0, w0 = pg.load_w(wd[0], wd[1], (half * 22) * 128, 11, n * 512, 512)
            s1, w1 = pg.load_w(wd[0], wd[1], (half * 22 + 11) * 128, 11, n * 512, 512)
            for t in range(4):
                pd = pg.bank()
                for q_, (s_, w_) in enumerate(((s0, w0), (s1, w1))):
                    for fc in range(11):
                        f = q_ * 11 + fc
                        mm(pg, pd[:, :], hT[:, f, t * 128:(t + 1) * 128], w_[:, fc, :], f == 0, f == 21, [wk.hT, s_], pd)
                vslice = wk.v[t][:, n * 512:(n + 1) * 512]
                kb.op("dve", lambda e, pd=pd, vslice=vslice, half=half: e.scalar_tensor_tensor(
                    out=vslice, in0=vslice, scalar=(ALPHA if half == 0 else 1.0), in1=pd[:, :], op0=ALU.mult, op1=ALU.add),
                    reads=[pd, wk.v[t]], writes=[wk.v[t]])
    for t in range(4):
        r0 = b * BLK + t * 128
        emit_ln(pg, wk.v[t], wk.v[t][:, :], wk.gbc, wk.bbc, wk.st,
                x_out[0][r0:r0 + 128, :], x_out[1], xT_out[0][:, r0:r0 + 128], xT_out[1], wk.xb, wk.xts)


def emit_prep_xT(pg, wk, x_in, xT_out):
    for j in range(NT):
        vb = wk.v[j % 4]
        pg.kb.dma("sp", vb[:, :], x_in[0][j * 128:(j + 1) * 128, :], reads=[pg.R(x_in[1])], writes=[vb])
        emit_transpose_out(pg, vb, vb[:, :], 16, wk.xb, wk.xts, xT_out[0][:, j * 128:(j + 1) * 128], xT_out[1])


TWO_PI = 2.0 * math.pi
CW1 = 6.28125
CW2 = TWO_PI - 6.28125


def emit_rope_tables(pg, wk, pos_ap, pos_name, tabs, tabs_name):
    kb = pg.kb
    posf, ang, kf, tmp = wk.v[0], wk.v[1], wk.v[2], wk.v[3]
    ki = View(wk.xT[:, 0:2 * TL].bitcast(I32))
    kb.dma("pool", posf[:, :], pos_ap.partition_broadcast(128), reads=[pg.R(pos_name)], writes=[posf])
    PI = math.pi

    def fold(buf):
        kb.op("dve", lambda e: e.tensor_scalar(out=tmp[:, :], in0=buf[:, :], scalar1=PI, scalar2=-TWO_PI,
                                               op0=ALU.is_gt, op1=ALU.mult), reads=[buf], writes=[tmp])
        kb.op("dve", lambda e: e.tensor_tensor(out=buf[:, :], in0=buf[:, :], in1=tmp[:, :], op=ALU.add),
              reads=[buf, tmp], writes=[buf])
        kb.op("dve", lambda e: e.tensor_scalar(out=tmp[:, :], in0=buf[:, :], scalar1=-PI, scalar2=TWO_PI,
                                               op0=ALU.is_lt, op1=ALU.mult), reads=[buf], writes=[tmp])
        kb.op("dve", lambda e: e.tensor_tensor(out=buf[:, :], in0=buf[:, :], in1=tmp[:, :], op=ALU.add),
              reads=[buf, tmp], writes=[buf])
        kb.op("dve", lambda e: e.tensor_scalar(out=buf[:, :], in0=buf[:, :], scalar1=-PI, scalar2=PI,
                                               op0=ALU.max, op1=ALU.min), reads=[buf], writes=[buf])

    for ti, fcol in ((0, C_F64), (1, C_F128)):
        kb.op("dve", lambda e: e.tensor_scalar(out=ang[:, :], in0=posf[:, :], scalar1=pg.cc(fcol, 1), scalar2=None,
                                               op0=ALU.mult), reads=[posf, pg.consts], writes=[ang])
        kb.op("dve", lambda e: e.tensor_scalar(out=ki[:, :], in0=ang[:, :], scalar1=1.0 / TWO_PI, scalar2=None,
                                               op0=ALU.mult), reads=[ang], writes=[ki, wk.xT])
        kb.op("dve", lambda e: e.tensor_copy(out=kf[:, :], in_=ki[:, :]), reads=[ki, wk.xT], writes=[kf])
        kb.op("dve", lambda e: e.scalar_tensor_tensor(out=ang[:, :], in0=kf[:, :], scalar=-CW1, in1=ang[:, :],
                                                      op0=ALU.mult, op1=ALU.add), reads=[kf, ang], writes=[ang])
        kb.op("dve", lambda e: e.scalar_tensor_tensor(out=ang[:, :], in0=kf[:, :], scalar=-CW2, in1=ang[:, :],
                                                      op0=ALU.mult, op1=ALU.add), reads=[kf, ang], writes=[ang])
        fold(ang)
        kb.op("act", lambda e: e.activation(out=kf[:, :], in_=ang[:, :], func=AF.Sin), reads=[ang], writes=[kf])
        kb.dma("sp", tabs[2 * ti + 1], kf[:, :], reads=[kf], writes=[pg.R(tabs_name)])
        kb.op("dve", lambda e: e.tensor_scalar(out=ang[:, :], in0=ang[:, :], scalar1=PI / 2, scalar2=None, op0=ALU.add),
              reads=[ang], writes=[ang])
        fold(ang)
        kb.op("act", lambda e: e.activation(out=kf[:, :], in_=ang[:, :], func=AF.Sin), reads=[ang], writes=[kf])
        kb.dma("sp", tabs[2 * ti], kf[:, :], reads=[kf], writes=[pg.R(tabs_name)])
    kb.barrier()


O_CQ, O_CKV, O_KR, O_DQ, O_DK, O_DV, O_IQ, O_IK, O_IW, O_GQ, O_GK, O_GV, O_GLR, O_GR = (
    0, 512, 1024, 1088, 2112, 3136, 4160, 5184, 5248, 5264, 5776, 6288, 7312, 7328)


class MixParams:
    def __init__(self, pg, tag):
        kb = pg.kb
        self.sm = Buf(kb, "sm" + tag, [128, 8], F32)
        self.wg2 = Buf(kb, "wg2" + tag, [16, 512], F32)
        self.bg = Buf(kb, "bg" + tag, [1, 512], F32)

    def load(self, pg, sm_ap, sm_name, wg2_ap, wg2_name, bg_ap, bg_name):
        kb = pg.kb
        kb.dma("sp", self.sm[:, :], sm_ap[:, :], reads=[pg.R(sm_name)], writes=[self.sm])
        kb.dma("sp", self.wg2[:, :], wg2_ap[:, :], reads=[pg.R(wg2_name)], writes=[self.wg2])
        kb.dma("sp", self.bg[:, :], bg_ap[:, :], reads=[pg.R(bg_name)], writes=[self.bg])


def fm_proj(pg, wview, wslot, c0, M, xT, xTbuf, K=16):
    ps = pg.bank()
    for kc in range(K):
        mm(pg, ps[0:M, :], wview[:, kc, c0:c0 + M], xT[:, kc, :], kc == 0, kc == K - 1, [wslot, xTbuf], ps)
    return ps


def tm_proj(pg, wview_cols, wslot, N, xT, xTbuf, t, K=16, outv=None):
    ps = pg.bank()
    o = ps[:, 0:N] if outv is None else outv(ps)
    for kc in range(K):
        mm(pg, o, xT[:, kc, t * 128:(t + 1) * 128], wview_cols(kc), kc == 0, kc == K - 1, [wslot, xTbuf], ps)
    return ps


def emit_mixproj_block(pg, wk, mp, b, L):
    kb = pg.kb
    kb.barrier()
    nm = L["n"]
    tok = slice(b * BLK, (b + 1) * BLK)
    xT = load_xT_block(pg, wk, nm["xT"][0], nm["xT"][1], b)
    xTb = wk.xT
    tab = View(wk.v[0][:, :].rearrange("p (a t) -> p a t", t=BLK))
    kb.dma("sp", tab[:, :, :], nm["tabs"][0][:, :, tok].rearrange("a p t -> p a t"), reads=[pg.R(nm["tabs"][1])], writes=[tab])
    C64, S64, C128, S128 = (tab[:, i, :] for i in range(4))
    cT = View(wk.v[1][:, :].rearrange("p (a t) -> p a t", t=BLK))
    sq = View(wk.v[2][:, :].rearrange("p (a t) -> p a t", t=BLK))
    rstd = View(wk.v[3][:, 0:512])
    xs = View(wk.v[3][:, 512:1024])
    t1 = View(wk.v[3][:, 1024:1536])
    cnT = View(wk.hT[:, 0:2048].rearrange("p (a t) -> p a t", t=BLK))
    stg = [View(wk.hT[:, 2048 + i * 512: 2048 + (i + 1) * 512]) for i in range(4)]
    stgn = [0]

    def stage():
        s = stg[stgn[0] % 4]
        stgn[0] += 1
        return s

    def store_fm(ps, M, dst_ap, dst_name):
        s = stage()
        kb.op("act", lambda e: e.activation(out=s[0:M, :], in_=ps[0:M, :], func=AF.Copy), reads=[ps], writes=[s])
        kb.dma("sp", dst_ap, s[0:M, :], reads=[s], writes=[pg.R(dst_name)])

    def rope_store(ps, M, Cb, Sb, pcol, dst_ap, dst_name):
        kb.op("act", lambda e: e.activation(out=xs[0:M, :], in_=ps[0:M, :], func=AF.Copy), reads=[ps], writes=[xs])
        ps2 = pg.bank()
        mm(pg, ps2[0:M, :], pg.consts[0:M, pcol:pcol + M], xs[0:M, :], True, True, [pg.consts, xs], ps2)
        kb.op("dve", lambda e: e.tensor_tensor(out=t1[0:M, :], in0=xs[0:M, :], in1=Cb[0:M, :], op=ALU.mult),
              reads=[xs, tab], writes=[t1])
        kb.op("dve", lambda e: e.tensor_tensor(out=xs[0:M, :], in0=ps2[0:M, :], in1=Sb[0:M, :], op=ALU.mult),
              reads=[ps2, tab, xs], writes=[xs])
        s = stage()
        kb.op("dve", lambda e: e.tensor_tensor(out=s[0:M, :], in0=t1[0:M, :], in1=xs[0:M, :], op=ALU.add),
              reads=[t1, xs], writes=[s])
        kb.dma("sp", dst_ap, s[0:M, :], reads=[s], writes=[pg.R(dst_name)])

    def rms_fm(c0, gcol0):
        slot, wv = pg.load_w(nm["w_in"][0], nm["w_in"][1], 0, 16, c0, 512)
        for c in range(4):
            ps = fm_proj(pg, wv, slot, c * 128, 128, xT, xTb)
            kb.op("act", lambda e, ps=ps, c=c: e.activation(out=sq[:, c, :], in_=ps[:, :], func=(AF.Copy if getattr(pg, "nosq", False) else AF.Square)), reads=[ps], writes=[sq])
            kb.op("dve", lambda e, ps=ps, c=c: e.tensor_copy(out=cT[:, c, :], in_=ps[:, :]), reads=[ps], writes=[cT])
        if getattr(pg, 'dbg', 99) < 2.1:
            return
        ss = pg.bank()
        for c in range(4):
            mm(pg, ss[:, :], pg.cc(C_ONES), sq[:, c, :], c == 0, c == 3, [pg.consts, sq], ss)
        if getattr(pg, 'dbg', 99) < 2.2:
            return
        kb.op("act", lambda e: e.activation(out=rstd[:, :], in_=ss[:, :], func=AF.Ln, bias=pg.cc(C_EPS, 1), scale=1.0 / 512),
              reads=[ss, pg.consts], writes=[rstd])
        kb.op("act", lambda e: e.activation(out=rstd[:, :], in_=rstd[:, :], func=AF.Exp, scale=-0.5), reads=[rstd], writes=[rstd])
        for c in range(4):
            kb.op("dve", lambda e, c=c: e.scalar_tensor_tensor(out=cnT[:, c, :], in0=cT[:, c, :], scalar=mp.sm[:, gcol0 + c:gcol0 + c + 1],
                                                               in1=rstd[:, :], op0=ALU.mult, op1=ALU.mult),
                  reads=[cT, mp.sm, rstd], writes=[cnT])

    if getattr(pg, 'dbg', 99) < 2:
        return
    rms_fm(O_CQ, 0)
    if getattr(pg, 'dbg', 99) < 2.3:
        return
    slot, wq = pg.load_w(nm["w_uq"][0], nm["w_uq"][1], 0, 4, 0, 1536)
    for h in range(8):
        ps = fm_proj(pg, wq, slot, h * 192, 128, cnT, cnT, K=4)
        store_fm(ps, 128, nm["QnT"][0][h, :, tok], nm["QnT"][1])
        if getattr(pg, 'dbg', 99) < 2.6:
            continue
        ps = fm_proj(pg, wq, slot, h * 192 + 128, 64, cnT, cnT, K=4)
        rope_store(ps, 64, C64, S64, C_P64, nm["QrT"][0][h, :, tok], nm["QrT"][1])
    if getattr(pg, 'dbg', 99) < 3:
        return
    rms_fm(O_CKV, 4)
    slot, wkv = pg.load_w(nm["w_ukv"][0], nm["w_ukv"][1], 0, 4, 0, 2048)
    for h in range(8):
        ps = fm_proj(pg, wkv, slot, h * 256, 128, cnT, cnT, K=4)
        store_fm(ps, 128, nm["KnT_o"][0][h, :, tok], nm["KnT_o"][1])
    for t in range(4):
        for i in range(2):
            wsel = lambda kc, i=i: wkv[:, kc, :].rearrange("p (h two d) -> p h two d", two=2, d=128)[:, 4 * i:4 * i + 4, 1, :]
            ps = tm_proj(pg, wsel, slot, 512, cnT, cnT, t, K=4, outv=lambda p: p[:, :].rearrange("p (h d) -> p h d", d=128))
            s = stage()
            kb.op("act", lambda e, ps=ps, s=s: e.activation(out=s[:, :], in_=ps[:, :], func=AF.Copy), reads=[ps], writes=[s])
            r0 = b * BLK + t * 128
            kb.dma("sp", nm["Vm_o"][0][r0:r0 + 128, i * 512:(i + 1) * 512], s[:, :], reads=[s], writes=[pg.R(nm["Vm_o"][1])])
    if getattr(pg, 'dbg', 99) < 4:
        return
    slot, wv = pg.load_w(nm["w_in"][0], nm["w_in"][1], 0, 16, O_KR, 64)
    ps = fm_proj(pg, wv, slot, 0, 64, xT, xTb)
    rope_store(ps, 64, C64, S64, C_P64, nm["KrT_o"][0][:, tok], nm["KrT_o"][1])
    for (c0, key, Cb, Sb, pcol) in ((O_DQ, "QdT", C128, S128, C_P128), (O_DK, "KdT_o", C128, S128, C_P128),
                                    (O_IQ, "IqT", C64, S64, C_P64)):
        for g in range(2):
            slot, wv = pg.load_w(nm["w_in"][0], nm["w_in"][1], 0, 16, c0 + g * 512, 512)
            for hh in range(4):
                ps = fm_proj(pg, wv, slot, hh * 128, 128, xT, xTb)
                rope_store(ps, 128, Cb, Sb, pcol, nm[key][0][g * 4 + hh, :, tok], nm[key][1])
    if getattr(pg, 'dbg', 99) < 5:
        return
    for g in range(2):
        slot, wv = pg.load_w(nm["w_in"][0], nm["w_in"][1], 0, 16, O_DV + g * 512, 512)
        for t in range(4):
            ps = tm_proj(pg, lambda kc, wv=wv: wv[:, kc, :], slot, 512, xT, xTb, t)
            s = stage()
            kb.op("act", lambda e, ps=ps, s=s: e.activation(out=s[:, :], in_=ps[:, :], func=AF.Copy), reads=[ps], writes=[s])
            r0 = b * BLK + t * 128
            kb.dma("sp", nm["Vd_o"][0][r0:r0 + 128, g * 512:(g + 1) * 512], s[:, :], reads=[s], writes=[pg.R(nm["Vd_o"][1])])
    slot, wv = pg.load_w(nm["w_in"][0], nm["w_in"][1], 0, 16, O_IK, 80)
    ps = fm_proj(pg, wv, slot, 0, 64, xT, xTb)
    rope_store(ps, 64, C64, S64, C_P64, nm["IkT_o"][0][:, tok], nm["IkT_o"][1])
    for t in range(4):
        ps = tm_proj(pg, lambda kc, wv=wv: wv[:, kc, 64:80], slot, 16, xT, xTb, t)
        kb.op("act", lambda e, ps=ps: e.activation(out=xs[:, 0:16], in_=ps[:, 0:16], func=AF.Copy), reads=[ps], writes=[xs])
        r0 = b * BLK + t * 128
        kb.dma("sp", nm["iw"][0][r0:r0 + 128, :], xs[:, 0:16], reads=[xs], writes=[pg.R(nm["iw"][1])])
    if getattr(pg, 'dbg', 99) < 6:
        return
    emit_gla_proj_block(pg, wk, mp, b, L, xT)


def emit_gla_proj_block(pg, wk, mp, b, L, xT):
    kb = pg.kb
    kb.barrier()
    nm = L["n"]
    xTb = wk.xT
    win = nm["w_in"]
    gq_raw = View(wk.v[0][:, :].rearrange("p (a t) -> p a t", t=512))
    gk_raw = View(wk.v[1][:, :].rearrange("p (a t) -> p a t", t=512))
    lbuf = View(wk.v[2][:, 0:512])
    Eq = View(wk.v[2][:, 512:1024])
    Ek = View(wk.v[2][:, 1024:1536])
    Er = View(wk.v[2][:, 1536:2048])
    oi = View(wk.v[3][:, 0:1024])
    U0 = View(wk.v[3][:, 1024:2048])
    Ut = View(wk.gbc[:, 0:1024])
    gv = View(wk.hT[:, 0:4096].rearrange("p (a t) -> p a t", t=1024))
    sgr = View(wk.hT[:, 4096:8192].rearrange("p (a t) -> p a t", t=1024))
    qt = View(wk.hT[:, 8192:8704])
    kt = View(wk.hT[:, 8704:9216])
    kh = View(wk.hT[:, 9216:9728])
    qkT = View(wk.hT[:, 9728:10752].rearrange("p (a t) -> p a t", t=128))
    AT = View(wk.hT[:, 10752:11264].rearrange("p (a t) -> p a t", t=128))
    glrT = View(wk.sg[0][0:16, :])
    Acol = View(wk.sg[1][:, 0:8])
    Atl = View(wk.sg[1][:, 8:12])

    slot, wv = pg.load_w(win[0], win[1], 0, 16, O_GLR, 16)
    ps = fm_proj(pg, wv, slot, 0, 16, xT, xTb)
    kb.op("act", lambda e: e.activation(out=glrT[:, :], in_=ps[0:16, :], func=AF.Copy), reads=[ps], writes=[glrT])
    for (c0, dst) in ((O_GQ, gq_raw), (O_GK, gk_raw)):
        slot, wv = pg.load_w(win[0], win[1], 0, 16, c0, 512)
        for t in range(4):
            ps = tm_proj(pg, lambda kc, wv=wv: wv[:, kc, :], slot, 512, xT, xTb, t)
            kb.op("act", lambda e, ps=ps, t=t, dst=dst: e.activation(out=dst[:, t, :], in_=ps[:, :], func=AF.Copy), reads=[ps], writes=[dst])
    for (c0, dst, fn) in ((O_GV, gv, AF.Copy), (O_GR, sgr, AF.Silu)):
        for g in range(2):
            slot, wv = pg.load_w(win[0], win[1], 0, 16, c0 + g * 512, 512)
            for t in range(4):
                ps = tm_proj(pg, lambda kc, wv=wv: wv[:, kc, :], slot, 512, xT, xTb, t)
                kb.op("act", lambda e, ps=ps, t=t, g=g, dst=dst, fn=fn: e.activation(out=dst[:, t, g * 512:(g + 1) * 512], in_=ps[:, :], func=fn),
                      reads=[ps], writes=[dst])
    for t in range(4):
        r0 = b * BLK + t * 128
        kb.dma("sp", nm["gsr"][0][r0:r0 + 128, :], sgr[:, t, :], reads=[sgr], writes=[pg.R(nm["gsr"][1])])
    for t in range(4):
        j = b * 4 + t
        r0 = b * BLK + t * 128
        Z = pg.bank()
        mm(pg, Z[:, :], glrT[0:16, t * 128:(t + 1) * 128], mp.wg2[0:16, :], True, False, [glrT, mp.wg2], Z)
        mm(pg, Z[:, :], pg.consts[0:1, C_ONES:C_ONES + 128], mp.bg[0:1, :], False, True, [pg.consts, mp.bg], Z)
        kb.op("act", lambda e: e.activation(out=Eq[:, :], in_=Z[:, :], func=AF.Exp, scale=-1.0), reads=[Z], writes=[Eq])
        kb.op("act", lambda e: e.activation(out=lbuf[:, :], in_=Eq[:, :], func=AF.Ln, bias=pg.cc(C_ONES, 1), scale=1.0),
              reads=[Eq, pg.consts], writes=[lbuf])
        cum = pg.bank()
        mm(pg, cum[:, :], pg.cc(C_TRI), lbuf[:, :], True, True, [pg.consts, lbuf], cum)
        rev = pg.bank()
        mm(pg, rev[:, :], pg.cc(C_RTRI), lbuf[:, :], True, True, [pg.consts, lbuf], rev)
        tot = pg.pb[4]
        for h in range(4):
            mm(pg, tot[:, h * 2:h * 2 + 2], lbuf[:, h * 128:(h + 1) * 128], pg.consts[:, C_CH:C_CH + 2], True, True, [pg.consts, lbuf], tot)
        kb.op("act", lambda e: e.activation(out=Eq[:, :], in_=cum[:, :], func=AF.Exp, scale=-1.0 / 16), reads=[cum], writes=[Eq])
        kb.op("act", lambda e: e.activation(out=Ek[:, :], in_=cum[:, :], func=AF.Exp, scale=1.0 / 16), reads=[cum], writes=[Ek])
        kb.op("act", lambda e: e.activation(out=Er[:, :], in_=rev[:, :], func=AF.Exp, scale=-1.0 / 16), reads=[rev], writes=[Er])
        kb.op("act", lambda e: e.activation(out=Acol[:, :], in_=tot[:, 0:8], func=AF.Exp, scale=-1.0 / 16), reads=[tot], writes=[Acol])
        kb.op("dve", lambda e, t=t: e.scalar_tensor_tensor(out=qt[:, :], in0=gq_raw[:, t, :], scalar=128.0 ** -0.5, in1=Eq[:, :],
                                                          op0=ALU.mult, op1=ALU.mult), reads=[gq_raw, Eq], writes=[qt])
        kb.op("dve", lambda e, t=t: e.tensor_tensor(out=kt[:, :], in0=gk_raw[:, t, :], in1=Ek[:, :], op=ALU.mult), reads=[gk_raw, Ek], writes=[kt])
        kb.op("dve", lambda e, t=t: e.tensor_tensor(out=kh[:, :], in0=gk_raw[:, t, :], in1=Er[:, :], op=ALU.mult), reads=[gk_raw, Er], writes=[kh])
        ptb = pg.pt[0]
        for i in range(8):
            src = qt if i < 4 else kt
            hh = i % 4
            kb.op("pe", lambda e, i=i, src=src, hh=hh: e.transpose(out=ptb[:, i * 128:(i + 1) * 128], in_=src[:, hh * 128:(hh + 1) * 128],
                                                                   identity=pg.identb[:, :]), reads=[src, pg.identb], writes=[ptb])
        kb.op("dve", lambda e: e.tensor_copy(out=qkT[:, :, :], in_=ptb[:, :].rearrange("p (a t) -> p a t", t=128)), reads=[ptb], writes=[qkT])
        kb.dma("sp", nm["gqT"][0][:, :, r0:r0 + 128].rearrange("h p t -> p h t"), qkT[:, 0:4, :], reads=[qkT], writes=[pg.R(nm["gqT"][1])])
        for h in range(4):
            at = pg.bank()
            mm(pg, at[:, 0:128], qkT[:, 4 + h, :], qkT[:, h, :], True, True, [qkT], at)
            kb.op("dve", lambda e, at=at, h=h: e.tensor_tensor(out=AT[:, h, :], in0=at[:, 0:128], in1=pg.cc(C_MASKF), op=ALU.mult),
                  reads=[at, pg.consts], writes=[AT])
        for h in range(4):
            o = pg.bank()
            mm(pg, o[:, 0:256], AT[:, h, :], gv[:, t, h * 256:(h + 1) * 256], True, True, [AT, gv], o)
            kb.op("act", lambda e, o=o, h=h: e.activation(out=oi[:, h * 256:(h + 1) * 256], in_=o[:, 0:256], func=AF.Copy), reads=[o], writes=[oi])
        kb.dma("sp", nm["goi"][0][r0:r0 + 128, :], oi[:, :], reads=[oi], writes=[pg.R(nm["goi"][1])])
        for h in range(4):
            u0 = pg.bank()
            mm(pg, u0[:, 0:256], kh[0:64, h * 128:(h + 1) * 128], gv[0:64, t, h * 256:(h + 1) * 256], True, True, [kh, gv], u0)
            u1 = pg.bank()
            mm(pg, u1[:, 0:256], kh[64:128, h * 128:(h + 1) * 128], gv[64:128, t, h * 256:(h + 1) * 256], True, True, [kh, gv], u1)
            kb.op("act", lambda e, u0=u0, h=h: e.activation(out=U0[:, h * 256:(h + 1) * 256], in_=u0[:, 0:256], func=AF.Copy), reads=[u0], writes=[U0])
            kb.op("dve", lambda e, u1=u1, h=h: e.scalar_tensor_tensor(out=Ut[:, h * 256:(h + 1) * 256], in0=U0[:, h * 256:(h + 1) * 256],
                                                                      scalar=Acol[:, 2 * h + 1:2 * h + 2], in1=u1[:, 0:256],
                                                                      op0=ALU.mult, op1=ALU.add), reads=[U0, Acol, u1], writes=[Ut])
        kb.dma("sp", nm["gU0"][0][j], U0[:, :], reads=[U0], writes=[pg.R(nm["gU0"][1])])
        kb.dma("sp", nm["gUt_o"][0][j], Ut[:, :], reads=[Ut], writes=[pg.R(nm["gUt_o"][1])])
        Ac3 = Acol[:, :].rearrange("p (h c) -> p h c", c=2)
        kb.op("dve", lambda e: e.tensor_tensor(out=Atl[:, :], in0=Ac3[:, :, 0], in1=Ac3[:, :, 1], op=ALU.mult), reads=[Acol], writes=[Atl])
        kb.dma("sp", nm["gA"][0][j], Acol[:, :], reads=[Acol], writes=[pg.R(nm["gA"][1])])
        kb.dma("sp", nm["gAt_o"][0][j], Atl[:, :], reads=[Atl], writes=[pg.R(nm["gAt_o"][1])])
    kb.barrier()


BIS_L = -64.0
BIS_W = 128.0
BIS_N = 24


def emit_indexer(pg, nm, tiles=None):
    kb = pg.kb
    kb.push()
    ik2 = Buf(kb, "ik2", [128, SEQ], BF16)
    score = Buf(kb, "score", [128, SEQ], F32)
    mneg = Buf(kb, "mnegw", [128, SEQ], BF16)
    cm = Buf(kb, "cm32", [128, 1024], F32)
    iq = Buf(kb, "iqt", [128, 1024], BF16)
    iwt = Buf(kb, "iwt", [128, 16], F32)
    wab = Buf(kb, "wab", [128, 16], F32)
    wsg = Buf(kb, "wsg", [128, 16], F32)
    mid = Buf(kb, "mid", [128, 1], F32)
    cnt = Buf(kb, "cnt", [128, 1], F32)
    dlt = Buf(kb, "dlt", [128, 1], F32)
    nmid = Buf(kb, "nmid", [128, 1], F32)
    sgs = Buf(kb, "sgs", [128, 1], F32)
    ajunk = Buf(kb, "ajunk", [128, 10240], BF16)
    accBs = [Buf(kb, f"accB{i}", [128, 512], F32) for i in range(2)]
    kb.dma("sp", ik2[0:64, :], nm["IkT_g"][0][:, :], reads=[pg.R(nm["IkT_g"][1])], writes=[ik2])
    kb.dma("sp", ik2[64:128, :], nm["IkT_g"][0][:, :], reads=[pg.R(nm["IkT_g"][1])], writes=[ik2])
    kb.dma("sp", cm[:, :], nm["cmask"][0][:, :], reads=[pg.R(nm["cmask"][1])], writes=[cm])
    iq3 = iq[:, :].rearrange("p (a t) -> p a t", t=128)
    for j in (range(NT) if tiles is None else tiles):
        nk = 1024 * (j + 1)
        ts = slice(j * 128, (j + 1) * 128)
        kb.dma("sp", iq3, nm["IqT"][0][:, :, ts].rearrange("a p t -> p a t"), reads=[pg.R(nm["IqT"][1])], writes=[iq])
        kb.dma("sp", iwt[:, :], nm["iw"][0][ts, :], reads=[pg.R(nm["iw"][1])], writes=[iwt])
        kb.op("act", lambda e: e.activation(out=wab[:, :], in_=iwt[:, :], func=AF.Abs, scale=(64.0 ** -0.5) * (16.0 ** -0.5)),
              reads=[iwt], writes=[wab])
        kb.op("act", lambda e: e.activation(out=wsg[:, :], in_=iwt[:, :], func=AF.Sign), reads=[iwt], writes=[wsg])
        for kbi in range(2 * (j + 1)):
            ks = slice(kbi * 512, (kbi + 1) * 512)
            diag = kbi >= 2 * j
            accB = accBs[kbi % 2]
            for hh in range(16):
                p0 = (hh % 2) * 64
                ps = pg.bank()
                mm(pg, ps[:, :], iq3[p0:p0 + 64, hh // 2, :], ik2[p0:p0 + 64, ks], True, True, [iq, ik2], ps)
                kb.op("act", lambda e, ps=ps, hh=hh: e.activation(out=ps[:, :], in_=ps[:, :], func=AF.Relu, scale=wab[:, hh:hh + 1]),
                      reads=[ps, wab], writes=[ps])
                if hh % 2 == 0:
                    dst, dreg = score[:, ks], score
                else:
                    dst, dreg = accB[:, :], accB
                if hh == 0 and diag:
                    in1 = cm[:, (kbi - 2 * j) * 512:(kbi - 2 * j + 1) * 512]
                    kb.op("dve", lambda e, ps=ps, dst=dst, hh=hh, in1=in1: e.scalar_tensor_tensor(
                        out=dst, in0=ps[:, :], scalar=wsg[:, hh:hh + 1], in1=in1, op0=ALU.mult, op1=ALU.add),
                        reads=[ps, wsg, cm], writes=[dreg])
                elif hh < 2:
                    kb.op("dve", lambda e, ps=ps, dst=dst, hh=hh: e.tensor_scalar(out=dst, in0=ps[:, :], scalar1=wsg[:, hh:hh + 1], scalar2=None,
                                                                               op0=ALU.mult), reads=[ps, wsg], writes=[dreg])
                else:
                    kb.op("dve", lambda e, ps=ps, dst=dst, hh=hh: e.scalar_tensor_tensor(
                        out=dst, in0=ps[:, :], scalar=wsg[:, hh:hh + 1], in1=dst, op0=ALU.mult, op1=ALU.add),
                        reads=[ps, wsg], writes=[dreg])
            kb.op("dve", lambda e, ks=ks, accB=accB: e.tensor_tensor(out=score[:, ks], in0=score[:, ks], in1=accB[:, :], op=ALU.add),
                  reads=[accB], writes=[score])
        nd = max(512, int(round(0.40 * nk / 512)) * 512)
        na = nk - nd
        kb.op("dve", lambda e: e.memset(mid[:, :], BIS_L + BIS_W / 2), writes=[mid])
        for k in range(1, BIS_N + 1):
            kb.op("dve", lambda e: e.tensor_scalar(out=mneg[:, 0:nd], in0=score[:, 0:nd], scalar1=mid[:, 0:1], scalar2=None,
                                                   op0=ALU.is_ge, op1=ALU.add, accum_out=cnt[:, 0:1]),
                  reads=[score, mid], writes=[mneg, cnt])
            kb.op("act", lambda e: e.activation(out=ajunk[:, 0:na], in_=score[:, nd:nk], func=AF.Sign, bias=mid[:, 0:1], scale=-1.0,
                                                accum_out=sgs[:, 0:1]), reads=[score, mid], writes=[ajunk, sgs])
            kb.op("dve", lambda e: e.scalar_tensor_tensor(out=cnt[:, :], in0=sgs[:, :], scalar=-0.5, in1=cnt[:, :], op0=ALU.mult, op1=ALU.add),
                  reads=[sgs], writes=[cnt])
            hk = BIS_W / 2 ** (k + 1) if k < BIS_N else BIS_W / 2 ** BIS_N
            mul = 2 * hk if k < BIS_N else hk
            kb.op("dve", lambda e, mul=mul: e.tensor_scalar(out=dlt[:, :], in0=cnt[:, :], scalar1=255.5 - 0.5 * na, scalar2=mul,
                                                            op0=ALU.is_ge, op1=ALU.mult), reads=[cnt], writes=[dlt])
            kb.op("dve", lambda e, hk=hk: e.scalar_tensor_tensor(out=mid[:, :], in0=dlt[:, :], scalar=-hk, in1=mid[:, :],
                                                                op0=ALU.add, op1=ALU.add), reads=[dlt, mid], writes=[mid])
        kb.op("dve", lambda e: e.tensor_scalar(out=mneg[:, 0:nk], in0=score[:, 0:nk], scalar1=mid[:, 0:1], scalar2=NEG,
                                               op0=ALU.is_lt, op1=ALU.mult), reads=[score, mid], writes=[mneg])
        kb.dma("sp", nm["MnegD"][0][j, :, 0:nk], mneg[:, 0:nk], reads=[mneg], writes=[pg.R(nm["MnegD"][1])])
        if "thr_dbg" in nm:
            kb.dma("sp", nm["thr_dbg"][0][j], mid[:, :], reads=[mid], writes=[pg.R(nm["thr_dbg"][1])])
    kb.pop()


def emit_attention(pg, nm, kind, heads=None, tiles=None):
    kb = pg.kb
    kb.push()
    mla = kind == "mla"
    K = Buf(kb, "attK", [128, SEQ], BF16)
    V = Buf(kb, "attV", [128, 128 * 129], BF16)
    V3 = V[:, :].rearrange("p (n d) -> p n d", d=129)
    Kg = [View(K[:, g * 1024:(g + 1) * 1024]) for g in range(16)]
    Vg = [View(V3[:, g * 8:(g + 1) * 8, :]) for g in range(16)]
    kb.op("dve", lambda e: e.memset(V[:, :], 1.0), writes=[V] + Vg)
    if mla:
        KR = Buf(kb, "attKR", [128, SEQ], BF16)
        kb.op("dve", lambda e: e.memset(KR[:, :], 0.0), writes=[KR])
        cm32 = Buf(kb, "attcm32", [128, 1024], F32)
        cmb = Buf(kb, "attcmb", [128, 1024], BF16)
        kb.dma("sp", KR[0:64, :], nm["KrT_g"][0][:, :], reads=[pg.R(nm["KrT_g"][1])], writes=[KR])
        kb.dma("sp", cm32[:, :], nm["cmask"][0][:, :], reads=[pg.R(nm["cmask"][1])], writes=[cm32])
        kb.op("dve", lambda e: e.tensor_copy(out=cmb[:, :], in_=cm32[:, :]), reads=[cm32], writes=[cmb])
        Qr = [Buf(kb, f"attQr{i}", [128, 128], BF16) for i in range(2)]
        for q_ in Qr:
            kb.op("dve", lambda e, q_=q_: e.memset(q_[:, :], 0.0), writes=[q_])
        scale = 192.0 ** -0.5
        Kd, Vd, Qd, oT = nm["KnT_g"], nm["Vm_g"], nm["QnT"], nm["oT_mla"]
    else:
        MN = [Buf(kb, f"attMN{i}", [128, SEQ], BF16) for i in range(2)]
        scale = 128.0 ** -0.5
        Kd, Vd, Qd, oT = nm["KdT_g"], nm["Vd_g"], nm["QdT"], nm["oT_dsa"]
    Qn = [Buf(kb, f"attQn{i}", [128, 128], BF16) for i in range(2)]
    PT = [Buf(kb, f"attPT{i}", [128, 512], BF16) for i in range(3)]
    rsum = Buf(kb, "attrsum", [128, 2], F32)
    ob = Buf(kb, "attob", [128, 128], BF16)
    oTs = Buf(kb, "attoTs", [128, 128], BF16)
    Ob = [pg.pb[4], pg.pb[5]]
    tasks = []
    it = 0
    for h in (range(8) if heads is None else heads):
        first = True
        for j in (range(NT) if tiles is None else tiles):
            nblk = 2 * (j + 1)
            for kbi in range(nblk):
                tasks.append(dict(h=h, j=j, kbi=kbi, nblk=nblk, it=it, newhead=(first and kbi == 0)))
            first = False
            it += 1
    state = {"maxg": -1}

    def stageA(s_, T):
        h, j, kbi, it_ = T["h"], T["j"], T["kbi"], T["it"]
        ts = slice(j * 128, (j + 1) * 128)
        nk = 1024 * (j + 1)
        qn = Qn[it_ % 2]
        if kbi == 0:
            if T["newhead"]:
                state["maxg"] = -1
            for g in range(state["maxg"] + 1, j + 1):
                gs = slice(g * 1024, (g + 1) * 1024)
                kb.dma("sp", Kg[g][:, :], Kd[0][h, :, gs], reads=[pg.R(Kd[1])], writes=[Kg[g]])
                kb.dma("sp", Vg[g][:, :, 0:128], Vd[0][h, :, gs].rearrange("p (n d) -> p n d", d=128), reads=[pg.R(Vd[1])], writes=[Vg[g]])
            state["maxg"] = max(state["maxg"], j)
            kb.dma("pool", qn[:, :], Qd[0][h, :, ts], reads=[pg.R(Qd[1])], writes=[qn])
            if mla:
                kb.dma("pool", Qr[it_ % 2][0:64, :], nm["QrT"][0][h, :, ts], reads=[pg.R(nm["QrT"][1])], writes=[Qr[it_ % 2]])
            else:
                kb.dma("pool", MN[it_ % 2][:, 0:nk], nm["MnegD"][0][j, :, 0:nk], reads=[pg.R(nm["MnegD"][1])], writes=[MN[it_ % 2]])
        g = kbi // 2
        diag = g == j
        ps = pg.bank()
        for i in range(4):
            k0 = kbi * 512 + i * 128
            o = ps[:, i * 128:(i + 1) * 128]
            if mla:
                qr = Qr[it_ % 2]
                mm(pg, o, K[:, k0:k0 + 128], qn[:, :], True, False, [qn, Kg[g]], ps)
                mm(pg, o, KR[:, k0:k0 + 128], qr[:, :], False, not diag, [qr, KR], ps)
                if diag:
                    c0 = k0 - j * 1024
                    mm(pg, o, cmb[:, c0:c0 + 128], pg.identb[:, :], False, True, [pg.identb, cmb], ps)
            else:
                mn = MN[it_ % 2]
                mm(pg, o, K[:, k0:k0 + 128], qn[:, :], True, False, [qn, Kg[g]], ps)
                mm(pg, o, mn[:, k0:k0 + 128], pg.identb[:, :], False, True, [pg.identb, mn], ps)
        p = PT[s_ % 3]
        kb.op("act", lambda e: e.activation(out=p[:, :], in_=ps[:, :], func=AF.Exp, scale=scale), reads=[ps], writes=[p])

    def stageC(s_, T):
        h, j, kbi, nblk, it_ = T["h"], T["j"], T["kbi"], T["nblk"], T["it"]
        g = kbi // 2
        pt_ = PT[s_ % 3]
        O = Ob[it_ % 2]
        for i in range(4):
            n = kbi * 4 + i
            mm(pg, O[:, 0:129], pt_[:, i * 128:(i + 1) * 128], V3[:, n, :],
               kbi == 0 and i == 0, kbi == nblk - 1 and i == 3, [pt_, Vg[g]], O)
        if kbi == nblk - 1:
            ts = slice(j * 128, (j + 1) * 128)
            kb.op("dve", lambda e: e.reciprocal(out=rsum[:, 1:2], in_=O[:, 128:129]), reads=[O], writes=[rsum])
            kb.op("act", lambda e: e.activation(out=ob[:, :], in_=O[:, 0:128], func=AF.Copy, scale=rsum[:, 1:2]), reads=[O, rsum], writes=[ob])
            ptb = pg.pt[it_ % 2]
            kb.op("pe", lambda e: e.transpose(out=ptb[:, 0:128], in_=ob[:, :], identity=pg.identb[:, :]), reads=[ob, pg.identb], writes=[ptb])
            kb.op("dve", lambda e: e.tensor_copy(out=oTs[:, :], in_=ptb[:, 0:128]), reads=[ptb], writes=[oTs])
            kb.dma("sp", oT[0][h * 128:(h + 1) * 128, ts], oTs[:, :], reads=[oTs], writes=[pg.R(oT[1])])

    N = len(tasks)
    for s_ in range(N + 1):
        if s_ < N:
            stageA(s_, tasks[s_])
        if 0 <= s_ - 1 < N:
            stageC(s_ - 1, tasks[s_ - 1])
    kb.pop()


def emit_gla_out(pg, nm, tiles=None):
    kb = pg.kb
    kb.push()
    S = Buf(kb, "glaS", [128, 1024], F32)
    snap = Buf(kb, "glasnap", [128, 1024], F32)
    At = Buf(kb, "glaAt", [128, 512], F32)
    Ub = [Buf(kb, f"glaUb{i}", [128, 1024], F32) for i in range(3)]
    U0b = Buf(kb, "glaU0", [128, 1024], F32)
    Ab = Buf(kb, "glaAb", [128, 8], F32)
    qTb = Buf(kb, "glaqT", [128, 512], BF16)
    qA = Buf(kb, "glaqA", [128, 512], BF16)
    qB = Buf(kb, "glaqB", [128, 512], BF16)
    oib = Buf(kb, "glaoi", [128, 1024], F32)
    srb = Buf(kb, "glasr", [128, 1024], BF16)
    gn = Buf(kb, "glagn", [128, 1024], F32)
    S0b = Buf(kb, "glaS0b", [128, 1024], BF16)
    S1b = Buf(kb, "glaS1b", [128, 1024], BF16)
    junk = Buf(kb, "glajunk", [128, 256], F32)
    ss = Buf(kb, "glass", [128, 8], F32)
    ob = Buf(kb, "glaob", [128, 1024], BF16)
    oTs = Buf(kb, "glaoTs", [128, 1024], BF16)
    kb.op("dve", lambda e: e.memset(S[:, :], 0.0), writes=[S])
    kb.op("dve", lambda e: e.memset(qA[:, :], 0.0), writes=[qA])
    kb.op("dve", lambda e: e.memset(qB[:, :], 0.0), writes=[qB])
    kb.dma("sp", At[:, :], nm["gAt_g"][0][:, :], reads=[pg.R(nm["gAt_g"][1])], writes=[At])
    kb.dma("sp", gn[:, :], nm["gla_norm"][0].partition_broadcast(128), reads=[pg.R(nm["gla_norm"][1])], writes=[gn])
    q3 = qTb[:, :].rearrange("p (h t) -> p h t", t=128)
    qA3 = qA[:, :].rearrange("p (h t) -> p h t", t=128)
    qB3 = qB[:, :].rearrange("p (h t) -> p h t", t=128)
    for j in range(NT):
        for i in range(8):
            g = 8 * j + i
            ub = Ub[g % 3]
            kb.dma("sp", ub[:, :], nm["gUt_g"][0][g], reads=[pg.R(nm["gUt_g"][1])], writes=[ub])
            if i == 0:
                kb.op("dve", lambda e: e.tensor_scalar(out=snap[:, :], in0=S[:, :], scalar1=pg.cc(C_EI, 1), scalar2=None, op0=ALU.mult),
                      reads=[S, pg.consts], writes=[snap])
            else:
                kb.op("dve", lambda e, i=i: e.scalar_tensor_tensor(out=snap[:, :], in0=S[:, :], scalar=pg.cc(C_EI + i, 1), in1=snap[:, :],
                                                                   op0=ALU.mult, op1=ALU.add), reads=[S, pg.consts, snap], writes=[snap])
            for h in range(4):
                hs = slice(h * 256, (h + 1) * 256)
                kb.op("dve", lambda e, hs=hs, g=g, h=h, ub=ub: e.scalar_tensor_tensor(
                    out=S[:, hs], in0=S[:, hs], scalar=At[:, g * 4 + h:g * 4 + h + 1], in1=ub[:, hs], op0=ALU.mult, op1=ALU.add),
                    reads=[At, ub], writes=[S])
        if tiles is not None and j not in tiles:
            continue
        ts = slice(j * 128, (j + 1) * 128)
        kb.dma("pool", U0b[:, :], nm["gU0"][0][j], reads=[pg.R(nm["gU0"][1])], writes=[U0b])
        kb.dma("pool", Ab[:, :], nm["gA"][0][j], reads=[pg.R(nm["gA"][1])], writes=[Ab])
        kb.dma("pool", q3, nm["gqT"][0][:, :, ts].rearrange("h p t -> p h t"), reads=[pg.R(nm["gqT"][1])], writes=[qTb])
        kb.dma("pool", oib[:, :], nm["goi"][0][ts, :], reads=[pg.R(nm["goi"][1])], writes=[oib])
        kb.dma("pool", srb[:, :], nm["gsr"][0][ts, :], reads=[pg.R(nm["gsr"][1])], writes=[srb])
        kb.op("act", lambda e: e.activation(out=S0b[:, :], in_=snap[:, :], func=AF.Copy), reads=[snap], writes=[S0b])
        for h in range(4):
            hs = slice(h * 256, (h + 1) * 256)
            kb.op("dve", lambda e, hs=hs, h=h: e.scalar_tensor_tensor(out=S1b[:, hs], in0=snap[:, hs], scalar=Ab[:, 2 * h:2 * h + 1],
                                                                      in1=U0b[:, hs], op0=ALU.mult, op1=ALU.add),
                  reads=[snap, Ab, U0b], writes=[S1b])
        kb.op("dve", lambda e: e.tensor_copy(out=qA3[:, :, 0:64], in_=q3[:, :, 0:64]), reads=[qTb], writes=[qA])
        kb.op("dve", lambda e: e.tensor_copy(out=qB3[:, :, 64:128], in_=q3[:, :, 64:128]), reads=[qTb], writes=[qB])
        for h in range(4):
            hs = slice(h * 256, (h + 1) * 256)
            o = pg.bank()
            mm(pg, o[:, 0:256], qA3[:, h, :], S0b[:, hs], True, False, [qA, S0b], o)
            mm(pg, o[:, 0:256], qB3[:, h, :], S1b[:, hs], False, True, [qB, S1b], o)
            kb.op("dve", lambda e, o=o, hs=hs: e.tensor_tensor(out=oib[:, hs], in0=o[:, 0:256], in1=oib[:, hs], op=ALU.add), reads=[o], writes=[oib])
            kb.op("act", lambda e, hs=hs, h=h: e.activation(out=junk[:, :], in_=oib[:, hs], func=AF.Square, accum_out=ss[:, h:h + 1]),
                  reads=[oib], writes=[junk, ss])
        kb.op("act", lambda e: e.activation(out=ss[:, 4:8], in_=ss[:, 0:4], func=AF.Ln, bias=pg.cc(C_EPS, 1), scale=1.0 / 256),
              reads=[ss, pg.consts], writes=[ss])
        kb.op("act", lambda e: e.activation(out=ss[:, 4:8], in_=ss[:, 4:8], func=AF.Exp, scale=-0.5), reads=[ss], writes=[ss])
        for h in range(4):
            hs = slice(h * 256, (h + 1) * 256)
            kb.op("dve", lambda e, hs=hs, h=h: e.scalar_tensor_tensor(out=oib[:, hs], in0=oib[:, hs], scalar=ss[:, 4 + h:5 + h], in1=gn[:, hs],
                                                                      op0=ALU.mult, op1=ALU.mult), reads=[ss, gn], writes=[oib])
        kb.op("dve", lambda e: e.tensor_tensor(out=ob[:, :], in0=oib[:, :], in1=srb[:, :], op=ALU.mult), reads=[oib, srb], writes=[ob])
        ptb = pg.pt[j % 2]
        for i in range(8):
            kb.op("pe", lambda e, i=i, ptb=ptb: e.transpose(out=ptb[:, i * 128:(i + 1) * 128], in_=ob[:, i * 128:(i + 1) * 128],
                                                            identity=pg.identb[:, :]), reads=[ob, pg.identb], writes=[ptb])
        kb.op("dve", lambda e, ptb=ptb: e.tensor_copy(out=oTs[:, :], in_=ptb[:, :]), reads=[ptb], writes=[oTs])
        kb.dma("sp", nm["oT_gla"][0][:, ts].rearrange("(k p) t -> p k t", p=128), oTs[:, :].rearrange("p (k t) -> p k t", t=128),
               reads=[oTs], writes=[pg.R(nm["oT_gla"][1])])
    kb.pop()


def emit_merge_block(pg, wk, b, nm):
    kb = pg.kb
    kb.barrier()
    load_ln_params(pg, wk, nm["ln_g1"][0], nm["ln_b1"][0], nm["ln_g1"][1], nm["ln_b1"][1])
    tok = slice(b * BLK, (b + 1) * BLK)
    xT = load_xT_block(pg, wk, nm["x1T"][0], nm["x1T"][1], b)
    oTm = View(wk.hT[:, 0:4096].rearrange("p (k t) -> p k t", t=BLK))
    oTd = View(wk.hT[:, 4096:8192].rearrange("p (k t) -> p k t", t=BLK))
    oTg0 = View(wk.xb[:, :].rearrange("p (k t) -> p k t", t=BLK))
    oTg1 = View(wk.xts[:, :].rearrange("p (k t) -> p k t", t=BLK))
    bm = View(wk.stats[0:1, 0:24])
    brow = View(wk.hT[0:1, 8192:8192 + 1024].bitcast(F32))
    kb.dma("sp", oTm[:, :, :], nm["oT_mla"][0][:, tok].rearrange("(k p) t -> p k t", p=128), reads=[pg.R(nm["oT_mla"][1])], writes=[oTm])
    kb.dma("sp", oTd[:, :, :], nm["oT_dsa"][0][:, tok].rearrange("(k p) t -> p k t", p=128), reads=[pg.R(nm["oT_dsa"][1])], writes=[oTd])
    kb.dma("sp", oTg0[:, :, :], nm["oT_gla"][0][0:512, tok].rearrange("(k p) t -> p k t", p=128), reads=[pg.R(nm["oT_gla"][1])], writes=[oTg0])
    kb.dma("sp", oTg1[:, :, :], nm["oT_gla"][0][512:1024, tok].rearrange("(k p) t -> p k t", p=128), reads=[pg.R(nm["oT_gla"][1])], writes=[oTg1])
    brs = [("w_br_mla", lambda kc: oTm[:, kc, :], [oTm]), ("w_br_dsa", lambda kc: oTd[:, kc, :], [oTd]),
           ("w_br_gla", lambda kc: (oTg0[:, kc, :] if kc < 4 else oTg1[:, kc - 4, :]), [oTg0, oTg1])]
    for n in range(4):
        for bi, (wname, osel, oregs) in enumerate(brs):
            sm_, wm = pg.load_w(nm["w_merge"][0], nm["w_merge"][1], 0, 16, bi * 2048 + n * 512, 512)
            sb_, wb = pg.load_w(nm[wname][0], nm[wname][1], 0, 8, n * 512, 512)
            c0 = bi * 2048 + n * 512
            kb.dma("sp", brow[:, :], nm["b_merge"][0][0:1, c0:c0 + 512], reads=[pg.R(nm["b_merge"][1])], writes=[brow])
            for t in range(4):
                pgt = pg.bank()
                for kc in range(16):
                    mm(pg, pgt[:, :], xT[:, kc, t * 128:(t + 1) * 128], wm[:, kc, :], kc == 0, False, [wk.xT, sm_], pgt)
                mm(pg, pgt[:, :], pg.consts[0:1, C_ONES:C_ONES + 128], brow[0:1, :], False, True, [pg.consts, brow], pgt)
                sgb = wk.sg[t % 2]
                kb.op("act", lambda e, pgt=pgt, sgb=sgb: e.activation(out=sgb[:, :], in_=pgt[:, :], func=AF.Sigmoid), reads=[pgt], writes=[sgb])
                py = pg.bank()
                for kc in range(8):
                    mm(pg, py[:, :], osel(kc)[:, t * 128:(t + 1) * 128], wb[:, kc, :], kc == 0, kc == 7, oregs + [sb_], py)
                ms = wk.v[t][:, n * 512:(n + 1) * 512]
                if bi == 0:
                    kb.op("dve", lambda e, py=py, sgb=sgb, ms=ms: e.tensor_tensor(out=ms, in0=py[:, :], in1=sgb[:, :], op=ALU.mult),
                          reads=[py, sgb], writes=[wk.v[t]])
                else:
                    kb.op("dve", lambda e, py=py, sgb=sgb: e.tensor_tensor(out=sgb[:, :], in0=py[:, :], in1=sgb[:, :], op=ALU.mult),
                          reads=[py], writes=[sgb])
                    kb.op("dve", lambda e, sgb=sgb, ms=ms: e.tensor_tensor(out=ms, in0=ms, in1=sgb[:, :], op=ALU.add),
                          reads=[sgb], writes=[wk.v[t]])
    kb.barrier()
    mT = View(wk.hT[:, 0:8192].rearrange("p (k t) -> p k t", t=BLK))
    for t in range(4):
        kb.op("act", lambda e, t=t: e.activation(out=wk.xb[:, :], in_=wk.v[t][:, :], func=AF.Copy), reads=[wk.v[t]], writes=[wk.xb])
        for half in range(2):
            ptb = pg.pt[half]
            for k in range(8):
                kc = half * 8 + k
                kb.op("pe", lambda e, kc=kc, k=k, ptb=ptb: e.transpose(out=ptb[:, k * 128:(k + 1) * 128], in_=wk.xb[:, kc * 128:(kc + 1) * 128],
                                                                      identity=pg.identb[:, :]), reads=[wk.xb, pg.identb], writes=[ptb])
            kb.op("dve", lambda e, ptb=ptb, half=half, t=t: e.tensor_copy(
                out=mT[:, half * 8:half * 8 + 8, t * 128:(t + 1) * 128], in_=ptb[:, :].rearrange("p (k t) -> p k t", t=128)),
                reads=[ptb], writes=[mT])
        r0 = b * BLK + t * 128
        kb.dma("sp", wk.v[t][:, :], nm["x1"][0][r0:r0 + 128, :], reads=[pg.R(nm["x1"][1])], writes=[wk.v[t]])
    for n in range(4):
        so_, wo = pg.load_w(nm["w_out"][0], nm["w_out"][1], 0, 16, n * 512, 512)
        for t in range(4):
            ph = pg.bank()
            for kc in range(16):
                mm(pg, ph[:, :], mT[:, kc, t * 128:(t + 1) * 128], wo[:, kc, :], kc == 0, kc == 15, [mT, so_], ph)
            vs = wk.v[t][:, n * 512:(n + 1) * 512]
            kb.op("dve", lambda e, ph=ph, vs=vs: e.scalar_tensor_tensor(out=vs, in0=vs, scalar=ALPHA, in1=ph[:, :], op0=ALU.mult, op1=ALU.add),
                  reads=[ph], writes=[wk.v[t]])
    kb.barrier()
    for t in range(4):
        r0 = b * BLK + t * 128
        emit_ln(pg, wk.v[t], wk.v[t][:, :], wk.gbc, wk.bbc, wk.st, nm["x2"][0][r0:r0 + 128, :], nm["x2"][1],
                nm["x2T"][0][:, r0:r0 + 128], nm["x2T"][1], wk.xb, wk.xts)


class XAState:
    def __init__(self, pg, wk, nm):
        kb = pg.kb
        kb.barrier()
        self.KxT = Buf(kb, "xaK", [128, 1024], BF16)
        self.Vx = Buf(kb, "xaV", [128, 1024], BF16)
        memT = View(wk.hT[:, 0:4096].rearrange("p (k m) -> p k m", m=256))
        for mt in range(2):
            vb = wk.v[mt]
            kb.dma("sp", vb[:, :], nm["mem"][0][mt * 128:(mt + 1) * 128, :], reads=[pg.R(nm["mem"][1])], writes=[vb])
            kb.op("act", lambda e, vb=vb: e.activation(out=wk.xb[:, :], in_=vb[:, :], func=AF.Copy), reads=[vb], writes=[wk.xb])
            for half in range(2):
                ptb = pg.pt[half]
                for k in range(8):
                    kc = half * 8 + k
                    kb.op("pe", lambda e, kc=kc, k=k, ptb=ptb: e.transpose(out=ptb[:, k * 128:(k + 1) * 128], in_=wk.xb[:, kc * 128:(kc + 1) * 128],
                                                                          identity=pg.identb[:, :]), reads=[wk.xb, pg.identb], writes=[ptb])
                kb.op("dve", lambda e, ptb=ptb, half=half, mt=mt: e.tensor_copy(
                    out=memT[:, half * 8:half * 8 + 8, mt * 128:(mt + 1) * 128], in_=ptb[:, :].rearrange("p (k t) -> p k t", t=128)),
                    reads=[ptb], writes=[memT])
        K3 = self.KxT[:, :].rearrange("p (h m) -> p h m", m=256)
        V3 = self.Vx[:, :].rearrange("p (a c) -> p a c", c=512)
        sk_, wkk = pg.load_w(nm["xa_w_kv"][0], nm["xa_w_kv"][1], 0, 16, 0, 512)
        for h in range(4):
            ps = pg.bank()
            for kc in range(16):
                mm(pg, ps[:, 0:256], wkk[:, kc, h * 128:(h + 1) * 128], memT[:, kc, :], kc == 0, kc == 15, [sk_, memT], ps)
            kb.op("act", lambda e, ps=ps, h=h: e.activation(out=K3[:, h, :], in_=ps[:, 0:256], func=AF.Copy), reads=[ps], writes=[self.KxT])
        sv_, wvv = pg.load_w(nm["xa_w_kv"][0], nm["xa_w_kv"][1], 0, 16, 512, 512)
        for mt in range(2):
            ps = pg.bank()
            for kc in range(16):
                mm(pg, ps[:, :], memT[:, kc, mt * 128:(mt + 1) * 128], wvv[:, kc, :], kc == 0, kc == 15, [sv_, memT], ps)
            kb.op("act", lambda e, ps=ps, mt=mt: e.activation(out=V3[:, mt, :], in_=ps[:, :], func=AF.Copy), reads=[ps], writes=[self.Vx])
        self.K3, self.V3 = K3, V3
        kb.barrier()


def emit_xattn_block(pg, wk, xa, b, nm):
    kb = pg.kb
    kb.barrier()
    load_ln_params(pg, wk, nm["ln_g2"][0], nm["ln_b2"][0], nm["ln_g2"][1], nm["ln_b2"][1])
    xT = load_xT_block(pg, wk, nm["x2T"][0], nm["x2T"][1], b)
    qT = View(wk.hT[:, 0:2048].rearrange("p (h t) -> p h t", t=BLK))
    oxT = View(wk.hT[:, 2048:4096].rearrange("p (h t) -> p h t", t=BLK))
    Pb = [View(wk.hT[:, 4096 + i * 256:4096 + (i + 1) * 256]) for i in range(2)]
    PTb = [View(wk.hT[:, 4608 + i * 256:4608 + (i + 1) * 256]) for i in range(2)]
    ob = View(wk.hT[:, 5120:5632])
    rs = View(wk.sg[0][:, 0:8])
    sq_, wq = pg.load_w(nm["xa_w_q"][0], nm["xa_w_q"][1], 0, 16, 0, 512)
    for h in range(4):
        ps = fm_proj(pg, wq, sq_, h * 128, 128, xT, wk.xT)
        kb.op("act", lambda e, ps=ps, h=h: e.activation(out=qT[:, h, :], in_=ps[:, :], func=AF.Copy), reads=[ps], writes=[qT])
    for t in range(4):
        r0 = b * BLK + t * 128
        kb.dma("sp", wk.v[t][:, :], nm["x2"][0][r0:r0 + 128, :], reads=[pg.R(nm["x2"][1])], writes=[wk.v[t]])
        for h in range(4):
            ps = pg.bank()
            mm(pg, ps[:, 0:256], qT[:, h, t * 128:(t + 1) * 128], xa.K3[:, h, :], True, True, [qT, xa.KxT], ps)
            p = Pb[h % 2]
            kb.op("act", lambda e, ps=ps, p=p, h=h: e.activation(out=p[:, :], in_=ps[:, 0:256], func=AF.Exp, scale=128.0 ** -0.5,
                                                                 accum_out=rs[:, h:h + 1]), reads=[ps], writes=[p, rs])
            ptb = pg.pt[h % 2]
            for i in range(2):
                kb.op("pe", lambda e, i=i, p=p, ptb=ptb: e.transpose(out=ptb[:, i * 128:(i + 1) * 128], in_=p[:, i * 128:(i + 1) * 128],
                                                                    identity=pg.identb[:, :]), reads=[p, pg.identb], writes=[ptb])
            pt_ = PTb[h % 2]
            kb.op("dve", lambda e, ptb=ptb, pt_=pt_: e.tensor_copy(out=pt_[:, :], in_=ptb[:, 0:256]), reads=[ptb], writes=[pt_])
            po = pg.bank()
            for i in range(2):
                mm(pg, po[:, 0:128], pt_[:, i * 128:(i + 1) * 128], xa.V3[:, i, h * 128:(h + 1) * 128], i == 0, i == 1, [pt_, xa.Vx], po)
            kb.op("dve", lambda e, h=h: e.reciprocal(out=rs[:, 4 + h:5 + h], in_=rs[:, h:h + 1]), reads=[rs], writes=[rs])
            kb.op("act", lambda e, po=po, h=h: e.activation(out=ob[:, h * 128:(h + 1) * 128], in_=po[:, 0:128], func=AF.Copy, scale=rs[:, 4 + h:5 + h]),
                  reads=[po, rs], writes=[ob])
        ptb = pg.pt[0]
        for h in range(4):
            kb.op("pe", lambda e, h=h, ptb=ptb: e.transpose(out=ptb[:, h * 128:(h + 1) * 128], in_=ob[:, h * 128:(h + 1) * 128],
                                                            identity=pg.identb[:, :]), reads=[ob, pg.identb], writes=[ptb])
        kb.op("dve", lambda e, ptb=ptb, t=t: e.tensor_copy(out=oxT[:, :, t * 128:(t + 1) * 128], in_=ptb[:, 0:512].rearrange("p (h t) -> p h t", t=128)),
              reads=[ptb], writes=[oxT])
    for n in range(4):
        so_, wo = pg.load_w(nm["xa_w_o"][0], nm["xa_w_o"][1], 0, 4, n * 512, 512)
        for t in range(4):
            ph = pg.bank()
            for kc in range(4):
                mm(pg, ph[:, :], oxT[:, kc, t * 128:(t + 1) * 128], wo[:, kc, :], kc == 0, kc == 3, [oxT, so_], ph)
            vs = wk.v[t][:, n * 512:(n + 1) * 512]
            kb.op("dve", lambda e, ph=ph, vs=vs: e.scalar_tensor_tensor(out=vs, in0=vs, scalar=ALPHA, in1=ph[:, :], op0=ALU.mult, op1=ALU.add),
                  reads=[ph], writes=[wk.v[t]])
    kb.barrier()
    for t in range(4):
        r0 = b * BLK + t * 128
        emit_ln(pg, wk.v[t], wk.v[t][:, :], wk.gbc, wk.bbc, wk.st, nm["x3"][0][r0:r0 + 128, :], nm["x3"][1],
                nm["x3T"][0][:, r0:r0 + 128], nm["x3T"][1], wk.xb, wk.xts)


A_OUT = [("x1", [TL, D], F32), ("x1T", [D, TL], BF16), ("QnT", [8, 128, TL], BF16), ("QrT", [8, 64, TL], BF16),
         ("QdT", [8, 128, TL], BF16), ("IqT", [8, 128, TL], BF16), ("iw", [TL, 16], F32), ("gqT", [4, 128, TL], BF16),
         ("gU0", [NT, 128, 1024], F32), ("gA", [NT, 128, 8], F32), ("goi", [TL, 1024], F32), ("gsr", [TL, 1024], BF16),
         ("KnT_o", [8, 128, TL], BF16), ("KrT_o", [64, TL], BF16), ("Vm_o", [TL, 1024], BF16), ("KdT_o", [8, 128, TL], BF16),
         ("Vd_o", [TL, 1024], BF16), ("IkT_o", [64, TL], BF16), ("gUt_o", [NT, 128, 1024], F32), ("gAt_o", [NT, 128, 4], F32)]
A_LOCAL = ["x1", "x1T", "QnT", "QrT", "QdT", "IqT", "iw", "gqT", "gU0", "gA", "goi", "gsr"]
B_GLOBAL = [("KnT_g", [8, 128, SEQ], BF16), ("KrT_g", [64, SEQ], BF16), ("Vm_g", [8, 128, SEQ], BF16), ("KdT_g", [8, 128, SEQ], BF16),
            ("Vd_g", [8, 128, SEQ], BF16), ("IkT_g", [64, SEQ], BF16), ("gUt_g", [SEQ // 128, 128, 1024], F32), ("gAt_g", [128, 512], F32)]
A_W = [("ffn1_g", [D, DFF]), ("ffn1_u", [D, DFF]), ("ffn1_d", [DFF, D]), ("ln_g0", [D]), ("ln_b0", [D]), ("w_in", [D, IN_W]),
       ("w_uq", [512, 1536]), ("w_ukv", [512, 2048]), ("sm", [128, 8]), ("wg2", [16, 512]), ("bg", [1, 512])]
B_W = [("w_br_mla", [1024, D]), ("w_br_dsa", [1024, D]), ("w_br_gla", [1024, D]), ("w_merge", [D, 3 * D]), ("b_merge", [1, 3 * D]),
       ("w_out", [D, D]), ("xa_w_q", [D, 512]), ("xa_w_kv", [D, 1024]), ("xa_w_o", [512, D]), ("mem", [256, D]),
       ("ln_g1", [D]), ("ln_b1", [D]), ("ln_g2", [D]), ("ln_b2", [D]), ("ffn2_g", [D, DFF]), ("ffn2_u", [D, DFF]), ("ffn2_d", [DFF, D]),
       ("ln_g3", [D]), ("ln_b3", [D]), ("gla_norm", [1024])]


def build_program(has_B, has_A, dbg_out=()):
    pg = Prog()
    kb = pg.kb
    nmB, nmA = {}, {}
    if has_B:
        for k in A_LOCAL:
            shape, dt = next((s, d) for (n, s, d) in A_OUT if n == k)
            nmB[k] = (pg.inp("i_" + k, shape, dt), "i_" + k)
        for (k, shape, dt) in B_GLOBAL:
            nmB[k] = (pg.inp(k, shape, dt), k)
        nmB["cmask"] = (pg.inp("cmask", [128, 1024], F32), "cmask")
        for (k, shape) in B_W:
            nmB[k] = (pg.inp("B_" + k, shape, F32), "B_" + k)
        for (k, shape, dt) in (("MnegD", [NT, 128, SEQ], BF16), ("oT_mla", [1024, TL], BF16), ("oT_dsa", [1024, TL], BF16),
                               ("oT_gla", [1024, TL], BF16), ("x2", [TL, D], F32), ("x2T", [D, TL], BF16),
                               ("x3", [TL, D], F32), ("x3T", [D, TL], BF16), ("x4T", [D, TL], BF16)):
            if k in dbg_out:
                nmB[k] = (pg.out(k, shape, dt), k)
            else:
                nmB[k] = (pg.scr(k, shape, dt), k)
        if has_A:
            nmB["x4"] = (pg.out("x4", [TL, D], F32), "x4") if "x4" in dbg_out else (pg.scr("x4", [TL, D], F32), "x4")
        else:
            nmB["x4"] = (pg.out("y", [TL, D], F32), "y")
    if has_A:
        for (k, shape) in A_W:
            nmA[k] = (pg.inp("A_" + k, shape, F32), "A_" + k)
        nmA["pos"] = (pg.inp("pos", [TL], I32), "pos")
        nmA["tabs"] = (pg.scr("tabs", [4, 128, TL], F32), "tabs")
        for (k, shape, dt) in A_OUT:
            nmA[k] = (pg.out("o_" + k, shape, dt), "o_" + k)
        nmA["xT"] = nmA["x1T"]
        if not has_B:
            nmA["x0"] = (pg.inp("x_in", [TL, D], F32), "x_in")
            nmA["x0T"] = (pg.scr("x0T", [D, TL], BF16), "x0T")
        else:
            nmA["x0"] = nmB["x4"]
            nmA["x0T"] = nmB["x4T"]
    if has_B:
        emit_indexer(pg, nmB)
        emit_attention(pg, nmB, "mla")
        emit_attention(pg, nmB, "dsa")
        emit_gla_out(pg, nmB)
    kb.push()
    wk = Work(pg)
    if has_A:
        mp = MixParams(pg, "A")
        mp.load(pg, nmA["sm"][0], nmA["sm"][1], nmA["wg2"][0], nmA["wg2"][1], nmA["bg"][0], nmA["bg"][1])
        emit_rope_tables(pg, wk, nmA["pos"][0], nmA["pos"][1], nmA["tabs"][0], nmA["tabs"][1])
        if not has_B:
            emit_prep_xT(pg, wk, nmA["x0"], nmA["x0T"])
    if has_B:
        xa = XAState(pg, wk, nmB)
        for b in range(NB):
            emit_merge_block(pg, wk, b, nmB)
        for b in range(NB):
            emit_xattn_block(pg, wk, xa, b, nmB)
        for b in range(NB):
            emit_ffn_block(pg, wk, b, nmB["x3"], nmB["x3T"], nmB["ffn2_g"], nmB["ffn2_u"], nmB["ffn2_d"],
                           (nmB["ln_g3"], nmB["ln_b3"]), nmB["x4"], nmB["x4T"])
    if has_A:
        for b in range(NB):
            emit_ffn_block(pg, wk, b, nmA["x0"], nmA["x0T"], nmA["ffn1_g"], nmA["ffn1_u"], nmA["ffn1_d"],
                           (nmA["ln_g0"], nmA["ln_b0"]), nmA["x1"], nmA["x1T"])
        for b in range(NB):
            emit_mixproj_block(pg, wk, mp, b, {"n": nmA})
    kb.pop()
    kb.finish()
    return pg


def _loc(a, c):
    return np.ascontiguousarray(a.reshape(NT, NCORES, 128, *a.shape[1:])[:, c].reshape(TL, *a.shape[1:]))


def _glob_cols(parts):
    lead = parts[0].shape[:-1]
    st = np.stack([p.reshape(*lead, NT, 128) for p in parts], axis=-2)
    return np.ascontiguousarray(st.reshape(*lead, SEQ))


def _glob_rows(parts):
    f = parts[0].shape[1:]
    st = np.stack([p.reshape(NT, 128, *f) for p in parts], axis=1)
    return np.ascontiguousarray(st.reshape(SEQ, *f))


def _vlay(v):
    return np.ascontiguousarray(v.reshape(SEQ // 128, 128, 8, 128).transpose(2, 1, 0, 3).reshape(8, 128, SEQ))


def _a_weights(inp, l):
    sm = np.zeros((128, 8), np.float32)
    sm[:, 0:4] = inp["mla_q_norm"][l].reshape(4, 128).T
    sm[:, 4:8] = inp["mla_kv_norm"][l].reshape(4, 128).T
    return {"A_ffn1_g": inp["ffn_w_gate"][l, 0], "A_ffn1_u": inp["ffn_w_up"][l, 0], "A_ffn1_d": inp["ffn_w_down"][l, 0],
            "A_ln_g0": inp["ln_gain"][l, 0], "A_ln_b0": inp["ln_bias"][l, 0], "A_w_in": inp["w_in"][l],
            "A_w_uq": inp["mla_w_uq"][l], "A_w_ukv": inp["mla_w_ukv"][l], "A_sm": sm,
            "A_wg2": inp["gla_w_gate2"][l], "A_bg": inp["gla_b_gate"][l][None, :]}


def _b_weights(inp, l):
    return {"B_w_br_mla": inp["w_branch_mla"][l], "B_w_br_dsa": inp["w_branch_dsa"][l], "B_w_br_gla": inp["w_branch_gla"][l],
            "B_w_merge": inp["w_merge"][l], "B_b_merge": inp["b_merge"][l][None, :], "B_w_out": inp["w_out"][l],
            "B_xa_w_q": inp["xa_w_q"][l], "B_xa_w_kv": inp["xa_w_kv"][l], "B_xa_w_o": inp["xa_w_o"][l], "B_mem": inp["mem"][0],
            "B_ln_g1": inp["ln_gain"][l, 1], "B_ln_b1": inp["ln_bias"][l, 1], "B_ln_g2": inp["ln_gain"][l, 2], "B_ln_b2": inp["ln_bias"][l, 2],
            "B_ffn2_g": inp["ffn_w_gate"][l, 1], "B_ffn2_u": inp["ffn_w_up"][l, 1], "B_ffn2_d": inp["ffn_w_down"][l, 1],
            "B_ln_g3": inp["ln_gain"][l, 3], "B_ln_b3": inp["ln_bias"][l, 3], "B_gla_norm": inp["gla_norm"][l]}


def _gather(res):
    loc = [{"i_" + k: r["o_" + k] for k in A_LOCAL} for r in res]
    g = {"KnT_g": _glob_cols([r["o_KnT_o"] for r in res]), "KrT_g": _glob_cols([r["o_KrT_o"] for r in res]),
         "KdT_g": _glob_cols([r["o_KdT_o"] for r in res]), "IkT_g": _glob_cols([r["o_IkT_o"] for r in res]),
         "Vm_g": _vlay(_glob_rows([r["o_Vm_o"] for r in res])), "Vd_g": _vlay(_glob_rows([r["o_Vd_o"] for r in res]))}
    ut = np.stack([r["o_gUt_o"] for r in res], axis=1)
    g["gUt_g"] = np.ascontiguousarray(ut.reshape(SEQ // 128, 128, 1024))
    at = np.stack([r["o_gAt_o"] for r in res], axis=1)
    g["gAt_g"] = np.ascontiguousarray(at.reshape(SEQ // 128, 128, 4).transpose(1, 0, 2).reshape(128, 512))
    return loc, g


_PROGS = {}


def _prog(has_B, has_A):
    key = (has_B, has_A)
    if key not in _PROGS:
        _PROGS[key] = build_program(has_B, has_A)
    return _PROGS[key]


def kernel(**inp):
    inp = {k: np.asarray(v) for k, v in inp.items()}
    cores = list(range(NCORES))
    consts = [make_consts(c) for c in cores]
    cmask = [make_cmask(c) for c in cores]
    pos = [_loc(inp["positions"][0].astype(np.int32), c) for c in cores]
    pg = _prog(False, True)
    wa = _a_weights(inp, 0)
    maps = [dict(wa, consts=consts[c], pos=pos[c], x_in=_loc(inp["x"][0], c)) for c in cores]
    res = run_bass_kernel_spmd(pg.nc, maps, core_ids=cores).results
    loc, g = _gather(res)
    pg = _prog(True, True)
    wb, wa = _b_weights(inp, 0), _a_weights(inp, 1)
    maps = [dict(wb, **wa, **g, **loc[c], consts=consts[c], cmask=cmask[c], pos=pos[c]) for c in cores]
    res = run_bass_kernel_spmd(pg.nc, maps, core_ids=cores).results
    loc, g = _gather(res)
    pg = _prog(True, False)
    wb = _b_weights(inp, 1)
    maps = [dict(wb, **g, **loc[c], consts=consts[c], cmask=cmask[c]) for c in cores]
    res = run_bass_kernel_spmd(pg.nc, maps, core_ids=cores).results
    y = _glob_rows([r["y"] for r in res])
    return y[None].astype(np.float32)
```

```python
import contextlib
import math
import numpy as np
import ml_dtypes
import concourse.bass as bass
import concourse.mybir as mybir
from concourse.bass_utils import run_bass_kernel_spmd

F32 = mybir.dt.float32
BF16 = mybir.dt.bfloat16
I32 = mybir.dt.int32
AF = mybir.ActivationFunctionType
ALU = mybir.AluOpType
NPBF = ml_dtypes.bfloat16

NCORES = 8
SEQ = 16384
D = 2048
DFF = 5632
TL = SEQ // NCORES
NT = TL // 128
BLK = 512
NB = TL // BLK
ALPHA = 4.0 ** 0.25
EPS = 1e-5
NEG = -30000.0
IN_W = 8352

ENG_NAMES = ["pe", "act", "dve", "pool", "sp"]


class Reg:
    __slots__ = ("w", "r")

    def __init__(self):
        self.w = None
        self.r = {}


class Buf:
    _uid = [0]

    def __init__(self, kb, name, shape, dt, psum=False):
        Buf._uid[0] += 1
        name = f"{name}_{Buf._uid[0]}"
        if psum:
            self.t = kb.scopes[-1].enter_context(kb.nc.psum_tensor(name, shape, dt))
            self.psum = True
        else:
            self.psum = False
            self.t = kb.scopes[-1].enter_context(kb.nc.sbuf_tensor(name, shape, dt))
        self.r = Reg()

    def __getitem__(self, idx):
        return self.t[idx]


class View:
    def __init__(self, ap):
        self.ap = ap
        self.r = Reg()

    def __getitem__(self, idx):
        return self.ap[idx]


class KB:
    def __init__(self, nc):
        self.nc = nc
        self.es = contextlib.ExitStack()
        self.scopes = [self.es]
        engs = [nc.tensor, nc.scalar, nc.vector, nc.gpsimd, nc.sync]
        self.eng = dict(zip(ENG_NAMES, engs))
        self.idx = {n: i for i, n in enumerate(ENG_NAMES)}
        self.sems = [self.es.enter_context(nc.semaphore("s_" + n)) for n in ENG_NAMES]
        self.cnt = [0] * len(ENG_NAMES)
        self.waited = [dict() for _ in ENG_NAMES]
        self.dpool = {}
        for q, n in (("sp", 24), ("pool", 16), ("act", 8)):
            lst = []
            for i in range(n):
                self.sems.append(self.es.enter_context(nc.semaphore(f"d_{q}{i}")))
                self.cnt.append(0)
                lst.append(len(self.sems) - 1)
            self.dpool[q] = [lst, 0]
        self.ninstr = 0

    def _wait(self, ei, deps):
        w = self.waited[ei]
        for (si, v) in deps.items():
            if si == ei and ei == 0:
                continue
            if w.get(si, 0) < v:
                self.eng[ENG_NAMES[ei]].wait_ge(self.sems[si], v)
                w[si] = v

    @staticmethod
    def _deps(reads, writes):
        deps = {}

        def add(t):
            if t is not None and deps.get(t[0], 0) < t[1]:
                deps[t[0]] = t[1]
        for r in reads:
            add(r.w)
        for r in writes:
            add(r.w)
            for si, v in r.r.items():
                add((si, v))
        return deps

    @staticmethod
    def _mark(t, reads, writes):
        for r in reads:
            if r.r.get(t[0], 0) < t[1]:
                r.r[t[0]] = t[1]
        for r in writes:
            r.w = t
            r.r = {}

    def op(self, eng, fn, reads=(), writes=()):
        if self.ninstr >= getattr(self, "maxops", 1 << 60):
            return None
        ei = self.idx[eng]
        writes = list(writes) + [x for x in reads if isinstance(x, Buf) and x.psum]
        reads = [x for x in reads if not (isinstance(x, Buf) and x.psum)]
        reads = [x if isinstance(x, Reg) else x.r for x in reads]
        writes = [x if isinstance(x, Reg) else x.r for x in writes]
        self._wait(ei, self._deps(reads, writes))
        ins = fn(self.eng[eng])
        self.cnt[ei] += 1
        ins.then_inc(self.sems[ei], 1)
        self._mark((ei, self.cnt[ei]), reads, writes)
        self.ninstr += 1
        return ins

    def dma(self, q, out, in_, reads=(), writes=()):
        if self.ninstr >= getattr(self, "maxops", 1 << 60):
            return None
        ei = self.idx[q]
        reads = [x if isinstance(x, Reg) else x.r for x in reads]
        writes = [x if isinstance(x, Reg) else x.r for x in writes]
        lst, nxt = self.dpool[q]
        si = lst[nxt]
        self.dpool[q][1] = (nxt + 1) % len(lst)
        deps = self._deps(reads, writes)
        if self.cnt[si] > 0 and deps.get(si, 0) < self.cnt[si]:
            deps[si] = self.cnt[si]
        self._wait(ei, deps)
        self.cnt[si] += 16
        self.eng[q].dma_start(out=out, in_=in_).then_inc(self.sems[si], 16)
        self._mark((si, self.cnt[si]), reads, writes)
        self.ninstr += 1

    def push(self):
        self.barrier()
        self.scopes.append(contextlib.ExitStack())

    def pop(self):
        self.barrier()
        self.scopes.pop().close()

    def barrier(self):
        for ei in range(len(ENG_NAMES)):
            deps = {si: v for si, v in enumerate(self.cnt) if v > 0 and si != ei}
            self._wait(ei, deps)

    def finish(self):
        deps = {si: v for si, v in enumerate(self.cnt) if v > 0 and si != self.idx["sp"]}
        self._wait(self.idx["sp"], deps)
        self.es.close()


C_ID, C_ONES, C_TRI, C_RTRI, C_MASKF, C_P128, C_P64, C_CH, C_F64, C_F128, C_EI, C_EPS, C_END = (
    0, 128, 256, 384, 512, 640, 768, 896, 898, 899, 900, 908, 909)


def make_consts(core):
    c = np.zeros((128, C_END), np.float32)
    c[:, C_ID:C_ID + 128] = np.eye(128)
    c[:, C_ONES:C_ONES + 128] = 1.0
    s = np.arange(128)[:, None]
    t = np.arange(128)[None, :]
    same = (s // 64) == (t // 64)
    c[:, C_TRI:C_TRI + 128] = (same & (s <= t))
    c[:, C_RTRI:C_RTRI + 128] = (same & (s > t))
    c[:, C_MASKF:C_MASKF + 128] = (same & (s <= t))
    for half, col in ((64, C_P128), (32, C_P64)):
        p = np.zeros((128, 128), np.float32)
        for m in range(128):
            if (m % (2 * half)) < half:
                p[m + half, m] = -1.0
            else:
                p[m - half, m] = 1.0
        c[:, col:col + 128] = p
    c[:, C_CH] = (np.arange(128) < 64)
    c[:, C_CH + 1] = (np.arange(128) >= 64)
    f64 = (1.0 / (np.float32(10000.0) ** (np.arange(0, 64, 2, dtype=np.float32) / np.float32(64)))).astype(np.float32)
    f128 = (1.0 / (np.float32(10000.0) ** (np.arange(0, 128, 2, dtype=np.float32) / np.float32(128)))).astype(np.float32)
    c[:, C_F64] = f64[np.arange(128) % 32]
    c[:, C_F128] = f128[np.arange(128) % 64]
    c[:, C_EI + core] = 1.0
    c[:, C_EPS] = EPS
    return c


def make_cmask(core):
    m = np.zeros((128, 1024), np.float32)
    q = np.arange(128)[:, None]
    for i in range(8):
        blk = m[:, i * 128:(i + 1) * 128]
        if i > core:
            blk[:] = NEG
        elif i == core:
            blk[np.arange(128)[None, :] > q] = NEG
    return m


class Prog:
    def __init__(self):
        self.nc = bass.Bass("TRN2", target_bir_lowering=False)
        self.kb = KB(self.nc)
        self.in_names = []
        self.out_names = []
        self.dreg = {}
        kb = self.kb
        self.consts_d = self.inp("consts", [128, C_END], F32)
        self.consts = Buf(kb, "consts_sb", [128, C_END], F32)
        kb.dma("sp", self.consts[:, :], self.consts_d[:, :], writes=[self.consts])
        self.identb = Buf(kb, "identb", [128, 128], BF16)
        kb.op("dve", lambda e: e.tensor_copy(out=self.identb[:, :], in_=self.consts[:, C_ID:C_ID + 128]),
              reads=[self.consts], writes=[self.identb])
        self.onesb = Buf(kb, "onesb", [1, 128], BF16)
        kb.op("dve", lambda e: e.tensor_copy(out=self.onesb[:, :], in_=self.consts[0:1, C_ONES:C_ONES + 128]),
              reads=[self.consts], writes=[self.onesb])
        self.pb = [Buf(kb, f"pb{i}", [128, 512], F32, psum=True) for i in range(6)]
        self.pt = [Buf(kb, f"pt{i}", [128, 1024], BF16, psum=True) for i in range(2)]
        self.ws = None
        self.wsn = 0
        self.pbn = 0
        self.uid = 0

    def _dt(self, name, shape, dt, kind):
        t = self.nc.dram_tensor(name, list(shape), dt, kind=kind)
        ap = t.ap()
        self.dreg[name] = Reg()
        return ap

    def inp(self, name, shape, dt):
        self.in_names.append(name)
        return self._dt(name, shape, dt, "ExternalInput")

    def out(self, name, shape, dt):
        self.out_names.append(name)
        return self._dt(name, shape, dt, "ExternalOutput")

    def scr(self, name, shape, dt):
        return self._dt(name, shape, dt, "Internal")

    def R(self, name):
        return self.dreg[name]

    def cc(self, col, n=128, rows=128):
        return self.consts[0:rows, col:col + n]

    def load_w(self, w_ap, wname, k0, kc, c0, n):
        slot = self.ws[self.wsn]
        self.wsn = (self.wsn + 1) % len(self.ws)
        assert kc * n <= 8192
        view = slot[:, 0:kc * n].rearrange("p (k n) -> p k n", n=n)
        src = w_ap[k0:k0 + kc * 128, c0:c0 + n].rearrange("(k p) n -> p k n", p=128)
        self.kb.dma("pool", view, src, reads=[self.R(wname)], writes=[slot])
        return slot, view

    def bank(self):
        b = self.pb[self.pbn]
        self.pbn = (self.pbn + 1) % 4
        return b


def mm(pg, out_ap, lhsT, rhs, start, stop, reads, wbank):
    pg.kb.op("pe", lambda e: e.matmul(out_ap, lhsT=lhsT, rhs=rhs, start=start, stop=stop),
             reads=reads, writes=[wbank])


def emit_transpose_out(pg, src_buf, src_ap, ncol_chunks, xb, xts, dst_ap, dst_name, q="sp"):
    kb = pg.kb
    kb.op("act", lambda e: e.activation(out=xb[:, 0:ncol_chunks * 128], in_=src_ap, func=AF.Copy),
          reads=[src_buf], writes=[xb])
    for half in range((ncol_chunks + 7) // 8):
        n = min(8, ncol_chunks - half * 8)
        ptb = pg.pt[half % 2]
        for k in range(n):
            kc = half * 8 + k
            kb.op("pe", lambda e, kc=kc, k=k, ptb=ptb: e.transpose(out=ptb[:, k * 128:(k + 1) * 128],
                                                                    in_=xb[:, kc * 128:(kc + 1) * 128],
                                                                    identity=pg.identb[:, :]),
                  reads=[xb, pg.identb], writes=[ptb])
        kb.op("dve", lambda e, ptb=ptb, half=half, n=n: e.tensor_copy(
            out=xts[:, half * 1024:half * 1024 + n * 128], in_=ptb[:, 0:n * 128]),
            reads=[ptb], writes=[xts])
    kb.dma(q, dst_ap.rearrange("(k p) t -> p k t", p=128),
           xts[:, 0:ncol_chunks * 128].rearrange("p (k t) -> p k t", t=128),
           reads=[xts], writes=[pg.R(dst_name)])


def emit_ln(pg, vbuf, vap, gbc, bbc, st, x_out_ap, x_out_name, xT_out_ap, xT_out_name, xb, xts):
    kb = pg.kb
    stats, mv, rstd = st
    for c in range(4):
        kb.op("dve", lambda e, c=c: e.bn_stats(out=stats[:, c * 6:(c + 1) * 6], in_=vap[:, c * 512:(c + 1) * 512]),
              reads=[vbuf], writes=[stats])
    kb.op("dve", lambda e: e.bn_aggr(out=mv[:, 0:2], in_=stats[:, 0:24]), reads=[stats], writes=[mv])
    kb.op("act", lambda e: e.activation(out=rstd[:, 0:1], in_=mv[:, 1:2], func=AF.Ln, bias=pg.cc(C_EPS, 1), scale=1.0),
          reads=[mv, pg.consts], writes=[rstd])
    kb.op("act", lambda e: e.activation(out=rstd[:, 1:2], in_=rstd[:, 0:1], func=AF.Exp, scale=-0.5),
          reads=[rstd], writes=[rstd])
    kb.op("dve", lambda e: e.tensor_scalar(out=vap, in0=vap, scalar1=mv[:, 0:1], scalar2=rstd[:, 1:2],
                                           op0=ALU.subtract, op1=ALU.mult),
          reads=[vbuf, mv, rstd], writes=[vbuf])
    kb.op("dve", lambda e: e.tensor_tensor(out=vap, in0=vap, in1=gbc[:, :], op=ALU.mult), reads=[vbuf, gbc], writes=[vbuf])
    kb.op("dve", lambda e: e.tensor_tensor(out=vap, in0=vap, in1=bbc[:, :], op=ALU.add), reads=[vbuf, bbc], writes=[vbuf])
    kb.dma("sp", x_out_ap, vap, reads=[vbuf], writes=[pg.R(x_out_name)])
    emit_transpose_out(pg, vbuf, vap, 16, xb, xts, xT_out_ap, xT_out_name)


class Work:
    def __init__(self, pg):
        kb = pg.kb
        pg.uid += 1
        u = str(pg.uid)
        pg.ws = [Buf(kb, f"ws{i}_" + u, [128, 8192], BF16) for i in range(3)]
        self.v = [Buf(kb, f"v{i}", [128, D], F32) for i in range(4)]
        self.xT = Buf(kb, "xTblk", [128, 16 * BLK], BF16)
        self.hT = Buf(kb, "hT", [128, 22 * BLK], BF16)
        self.gbc = Buf(kb, "gbc", [128, D], F32)
        self.bbc = Buf(kb, "bbc", [128, D], F32)
        self.sg = [Buf(kb, f"sg{i}", [128, BLK], F32) for i in range(2)]
        self.xb = Buf(kb, "xb", [128, D], BF16)
        self.xts = Buf(kb, "xts", [128, D], BF16)
        self.stats = Buf(kb, "stats", [128, 24], F32)
        self.mv = Buf(kb, "mv", [128, 2], F32)
        self.rstd = Buf(kb, "rstd", [128, 2], F32)
        self.st = (self.stats, self.mv, self.rstd)


def load_xT_block(pg, wk, xT_ap, xT_name, b):
    view = wk.xT[:, :].rearrange("p (k t) -> p k t", t=BLK)
    pg.kb.dma("sp", view, xT_ap[:, b * BLK:(b + 1) * BLK].rearrange("(k p) t -> p k t", p=128),
              reads=[pg.R(xT_name)], writes=[wk.xT])
    return view


def load_ln_params(pg, wk, g_ap, b_ap, gname, bname):
    pg.kb.dma("sp", wk.gbc[:, :], g_ap.partition_broadcast(128), reads=[pg.R(gname)], writes=[wk.gbc])
    pg.kb.dma("sp", wk.bbc[:, :], b_ap.partition_broadcast(128), reads=[pg.R(bname)], writes=[wk.bbc])


def emit_ffn_block(pg, wk, b, x_in, xT_in, wg, wu, wd, ln, x_out, xT_out, first=False):
    kb = pg.kb
    if b == 0 or first:
        kb.barrier()
        load_ln_params(pg, wk, ln[0][0], ln[1][0], ln[0][1], ln[1][1])
    xT = load_xT_block(pg, wk, xT_in[0], xT_in[1], b)
    for t in range(4):
        kb.dma("sp", wk.v[t][:, :], x_in[0][b * BLK + t * 128: b * BLK + (t + 1) * 128, :],
               reads=[pg.R(x_in[1])], writes=[wk.v[t]])
    hT = wk.hT[:, :].rearrange("p (f t) -> p f t", t=BLK)
    for half in range(2):
        for grp in range(6):
            nf = 4 if grp < 5 else 2
            f0 = half * 22 + grp * 4
            sg_, wgv = pg.load_w(wg[0], wg[1], 0, 16, f0 * 128, nf * 128)
            su_, wuv = pg.load_w(wu[0], wu[1], 0, 16, f0 * 128, nf * 128)
            for fi in range(nf):
                pgate = pg.bank()
                pup = pg.bank()
                for kc in range(16):
                    mm(pg, pgate[:, :], wgv[:, kc, fi * 128:(fi + 1) * 128], xT[:, kc, :], kc == 0, kc == 15,
                       [sg_, wk.xT], pgate)
                for kc in range(16):
                    mm(pg, pup[:, :], wuv[:, kc, fi * 128:(fi + 1) * 128], xT[:, kc, :], kc == 0, kc == 15,
                       [su_, wk.xT], pup)
                sgb = wk.sg[(grp * 4 + fi) % 2]
                kb.op("act", lambda e, pgate=pgate, sgb=sgb: e.activation(out=sgb[:, :], in_=pgate[:, :], func=AF.Silu),
                      reads=[pgate], writes=[sgb])
                fl = grp * 4 + fi
                kb.op("dve", lambda e, pup=pup, sgb=sgb, fl=fl: e.scalar_tensor_tensor(
                    out=hT[:, fl, :], in0=pup[:, :], scalar=0.5, in1=sgb[:, :], op0=ALU.mult, op1=ALU.mult),
                    reads=[pup, sgb], writes=[wk.hT])
        for n in range(4):
            s0, w0 = pg.load_w(wd[0], wd[1], (half * 22) * 128, 11, n * 512, 512)
            s1, w1 = pg.load_w(wd[0], wd[1], (half * 22 + 11) * 128, 11, n * 512, 512)
            for t in range(4):
                pd = pg.bank()
                for q_, (s_, w_) in enumerate(((s0, w0), (s1, w1))):
                    for fc in range(11):
                        f = q_ * 11 + fc
                        mm(pg, pd[:, :], hT[:, f, t * 128:(t + 1) * 128], w_[:, fc, :], f == 0, f == 21, [wk.hT, s_], pd)
                vslice = wk.v[t][:, n * 512:(n + 1) * 512]
                kb.op("dve", lambda e, pd=pd, vslice=vslice, half=half: e.scalar_tensor_tensor(
                    out=vslice, in0=vslice, scalar=(ALPHA if half == 0 else 1.0), in1=pd[:, :], op0=ALU.mult, op1=ALU.add),
                    reads=[pd, wk.v[t]], writes=[wk.v[t]])
    for t in range(4):
        r0 = b * BLK + t * 128
        emit_ln(pg, wk.v[t], wk.v[t][:, :], wk.gbc, wk.bbc, wk.st,
                x_out[0][r0:r0 + 128, :], x_out[1], xT_out[0][:, r0:r0 + 128], xT_out[1], wk.xb, wk.xts)


def emit_prep_xT(pg, wk, x_in, xT_out):
    for j in range(NT):
        vb = wk.v[j % 4]
        pg.kb.dma("sp", vb[:, :], x_in[0][j * 128:(j + 1) * 128, :], reads=[pg.R(x_in[1])], writes=[vb])
        emit_transpose_out(pg, vb, vb[:, :], 16, wk.xb, wk.xts, xT_out[0][:, j * 128:(j + 1) * 128], xT_out[1])


TWO_PI = 2.0 * math.pi
CW1 = 6.28125
CW2 = TWO_PI - 6.28125


def emit_rope_tables(pg, wk, pos_ap, pos_name, tabs, tabs_name):
    kb = pg.kb
    posf, ang, kf, tmp = wk.v[0], wk.v[1], wk.v[2], wk.v[3]
    ki = View(wk.xT[:, 0:2 * TL].bitcast(I32))
    kb.dma("pool", posf[:, :], pos_ap.partition_broadcast(128), reads=[pg.R(pos_name)], writes=[posf])
    PI = math.pi

    def fold(buf):
        kb.op("dve", lambda e: e.tensor_scalar(out=tmp[:, :], in0=buf[:, :], scalar1=PI, scalar2=-TWO_PI,
                                               op0=ALU.is_gt, op1=ALU.mult), reads=[buf], writes=[tmp])
        kb.op("dve", lambda e: e.tensor_tensor(out=buf[:, :], in0=buf[:, :], in1=tmp[:, :], op=ALU.add),
              reads=[buf, tmp], writes=[buf])
        kb.op("dve", lambda e: e.tensor_scalar(out=tmp[:, :], in0=buf[:, :], scalar1=-PI, scalar2=TWO_PI,
                                               op0=ALU.is_lt, op1=ALU.mult), reads=[buf], writes=[tmp])
        kb.op("dve", lambda e: e.tensor_tensor(out=buf[:, :], in0=buf[:, :], in1=tmp[:, :], op=ALU.add),
              reads=[buf, tmp], writes=[buf])
        kb.op("dve", lambda e: e.tensor_scalar(out=buf[:, :], in0=buf[:, :], scalar1=-PI, scalar2=PI,
                                               op0=ALU.max, op1=ALU.min), reads=[buf], writes=[buf])

    for ti, fcol in ((0, C_F64), (1, C_F128)):
        kb.op("dve", lambda e: e.tensor_scalar(out=ang[:, :], in0=posf[:, :], scalar1=pg.cc(fcol, 1), scalar2=None,
                                               op0=ALU.mult), reads=[posf, pg.consts], writes=[ang])
        kb.op("dve", lambda e: e.tensor_scalar(out=ki[:, :], in0=ang[:, :], scalar1=1.0 / TWO_PI, scalar2=None,
                                               op0=ALU.mult), reads=[ang], writes=[ki, wk.xT])
        kb.op("dve", lambda e: e.tensor_copy(out=kf[:, :], in_=ki[:, :]), reads=[ki, wk.xT], writes=[kf])
        kb.op("dve", lambda e: e.scalar_tensor_tensor(out=ang[:, :], in0=kf[:, :], scalar=-CW1, in1=ang[:, :],
                                                      op0=ALU.mult, op1=ALU.add), reads=[kf, ang], writes=[ang])
        kb.op("dve", lambda e: e.scalar_tensor_tensor(out=ang[:, :], in0=kf[:, :], scalar=-CW2, in1=ang[:, :],
                                                      op0=ALU.mult, op1=ALU.add), reads=[kf, ang], writes=[ang])
        fold(ang)
        kb.op("act", lambda e: e.activation(out=kf[:, :], in_=ang[:, :], func=AF.Sin), reads=[ang], writes=[kf])
        kb.dma("sp", tabs[2 * ti + 1], kf[:, :], reads=[kf], writes=[pg.R(tabs_name)])
        kb.op("dve", lambda e: e.tensor_scalar(out=ang[:, :], in0=ang[:, :], scalar1=PI / 2, scalar2=None, op0=ALU.add),
              reads=[ang], writes=[ang])
        fold(ang)
        kb.op("act", lambda e: e.activation(out=kf[:, :], in_=ang[:, :], func=AF.Sin), reads=[ang], writes=[kf])
        kb.dma("sp", tabs[2 * ti], kf[:, :], reads=[kf], writes=[pg.R(tabs_name)])
    kb.barrier()


O_CQ, O_CKV, O_KR, O_DQ, O_DK, O_DV, O_IQ, O_IK, O_IW, O_GQ, O_GK, O_GV, O_GLR, O_GR = (
    0, 512, 1024, 1088, 2112, 3136, 4160, 5184, 5248, 5264, 5776, 6288, 7312, 7328)


class MixParams:
    def __init__(self, pg, tag):
        kb = pg.kb
        self.sm = Buf(kb, "sm" + tag, [128, 8], F32)
        self.wg2 = Buf(kb, "wg2" + tag, [16, 512], F32)
        self.bg = Buf(kb, "bg" + tag, [1, 512], F32)

    def load(self, pg, sm_ap, sm_name, wg2_ap, wg2_name, bg_ap, bg_name):
        kb = pg.kb
        kb.dma("sp", self.sm[:, :], sm_ap[:, :], reads=[pg.R(sm_name)], writes=[self.sm])
        kb.dma("sp", self.wg2[:, :], wg2_ap[:, :], reads=[pg.R(wg2_name)], writes=[self.wg2])
        kb.dma("sp", self.bg[:, :], bg_ap[:, :], reads=[pg.R(bg_name)], writes=[self.bg])


def fm_proj(pg, wview, wslot, c0, M, xT, xTbuf, K=16):
    ps = pg.bank()
    for kc in range(K):
        mm(pg, ps[0:M, :], wview[:, kc, c0:c0 + M], xT[:, kc, :], kc == 0, kc == K - 1, [wslot, xTbuf], ps)
    return ps


def tm_proj(pg, wview_cols, wslot, N, xT, xTbuf, t, K=16, outv=None):
    ps = pg.bank()
    o = ps[:, 0:N] if outv is None else outv(ps)
    for kc in range(K):
        mm(pg, o, xT[:, kc, t * 128:(t + 1) * 128], wview_cols(kc), kc == 0, kc == K - 1, [wslot, xTbuf], ps)
    return ps


def emit_mixproj_block(pg, wk, mp, b, L):
    kb = pg.kb
    kb.barrier()
    nm = L["n"]
    tok = slice(b * BLK, (b + 1) * BLK)
    xT = load_xT_block(pg, wk, nm["xT"][0], nm["xT"][1], b)
    xTb = wk.xT
    tab = View(wk.v[0][:, :].rearrange("p (a t) -> p a t", t=BLK))
    kb.dma("sp", tab[:, :, :], nm["tabs"][0][:, :, tok].rearrange("a p t -> p a t"), reads=[pg.R(nm["tabs"][1])], writes=[tab])
    C64, S64, C128, S128 = (tab[:, i, :] for i in range(4))
    cT = View(wk.v[1][:, :].rearrange("p (a t) -> p a t", t=BLK))
    sq = View(wk.v[2][:, :].rearrange("p (a t) -> p a t", t=BLK))
    rstd = View(wk.v[3][:, 0:512])
    xs = View(wk.v[3][:, 512:1024])
    t1 = View(wk.v[3][:, 1024:1536])
    cnT = View(wk.hT[:, 0:2048].rearrange("p (a t) -> p a t", t=BLK))
    stg = [View(wk.hT[:, 2048 + i * 512: 2048 + (i + 1) * 512]) for i in range(4)]
    stgn = [0]

    def stage():
        s = stg[stgn[0] % 4]
        stgn[0] += 1
        return s

    def store_fm(ps, M, dst_ap, dst_name):
        s = stage()
        kb.op("act", lambda e: e.activation(out=s[0:M, :], in_=ps[0:M, :], func=AF.Copy), reads=[ps], writes=[s])
        kb.dma("sp", dst_ap, s[0:M, :], reads=[s], writes=[pg.R(dst_name)])

    def rope_store(ps, M, Cb, Sb, pcol, dst_ap, dst_name):
        kb.op("act", lambda e: e.activation(out=xs[0:M, :], in_=ps[0:M, :], func=AF.Copy), reads=[ps], writes=[xs])
        ps2 = pg.bank()
        mm(pg, ps2[0:M, :], pg.consts[0:M, pcol:pcol + M], xs[0:M, :], True, True, [pg.consts, xs], ps2)
        kb.op("dve", lambda e: e.tensor_tensor(out=t1[0:M, :], in0=xs[0:M, :], in1=Cb[0:M, :], op=ALU.mult),
              reads=[xs, tab], writes=[t1])
        kb.op("dve", lambda e: e.tensor_tensor(out=xs[0:M, :], in0=ps2[0:M, :], in1=Sb[0:M, :], op=ALU.mult),
              reads=[ps2, tab, xs], writes=[xs])
        s = stage()
        kb.op("dve", lambda e: e.tensor_tensor(out=s[0:M, :], in0=t1[0:M, :], in1=xs[0:M, :], op=ALU.add),
              reads=[t1, xs], writes=[s])
        kb.dma("sp", dst_ap, s[0:M, :], reads=[s], writes=[pg.R(dst_name)])

    def rms_fm(c0, gcol0):
        slot, wv = pg.load_w(nm["w_in"][0], nm["w_in"][1], 0, 16, c0, 512)
        for c in range(4):
            ps = fm_proj(pg, wv, slot, c * 128, 128, xT, xTb)
            kb.op("act", lambda e, ps=ps, c=c: e.activation(out=sq[:, c, :], in_=ps[:, :], func=(AF.Copy if getattr(pg, "nosq", False) else AF.Square)), reads=[ps], writes=[sq])
            kb.op("dve", lambda e, ps=ps, c=c: e.tensor_copy(out=cT[:, c, :], in_=ps[:, :]), reads=[ps], writes=[cT])
        if getattr(pg, 'dbg', 99) < 2.1:
            return
        ss = pg.bank()
        for c in range(4):
            mm(pg, ss[:, :], pg.cc(C_ONES), sq[:, c, :], c == 0, c == 3, [pg.consts, sq], ss)
        if getattr(pg, 'dbg', 99) < 2.2:
            return
        kb.op("act", lambda e: e.activation(out=rstd[:, :], in_=ss[:, :], func=AF.Ln, bias=pg.cc(C_EPS, 1), scale=1.0 / 512),
              reads=[ss, pg.consts], writes=[rstd])
        kb.op("act", lambda e: e.activation(out=rstd[:, :], in_=rstd[:, :], func=AF.Exp, scale=-0.5), reads=[rstd], writes=[rstd])
        for c in range(4):
            kb.op("dve", lambda e, c=c: e.scalar_tensor_tensor(out=cnT[:, c, :], in0=cT[:, c, :], scalar=mp.sm[:, gcol0 + c:gcol0 + c + 1],
                                                               in1=rstd[:, :], op0=ALU.mult, op1=ALU.mult),
                  reads=[cT, mp.sm, rstd], writes=[cnT])

    if getattr(pg, 'dbg', 99) < 2:
        return
    rms_fm(O_CQ, 0)
    if getattr(pg, 'dbg', 99) < 2.3:
        return
    slot, wq = pg.load_w(nm["w_uq"][0], nm["w_uq"][1], 0, 4, 0, 1536)
    for h in range(8):
        ps = fm_proj(pg, wq, slot, h * 192, 128, cnT, cnT, K=4)
        store_fm(ps, 128, nm["QnT"][0][h, :, tok], nm["QnT"][1])
        if getattr(pg, 'dbg', 99) < 2.6:
            continue
        ps = fm_proj(pg, wq, slot, h * 192 + 128, 64, cnT, cnT, K=4)
        rope_store(ps, 64, C64, S64, C_P64, nm["QrT"][0][h, :, tok], nm["QrT"][1])
    if getattr(pg, 'dbg', 99) < 3:
        return
    rms_fm(O_CKV, 4)
    slot, wkv = pg.load_w(nm["w_ukv"][0], nm["w_ukv"][1], 0, 4, 0, 2048)
    for h in range(8):
        ps = fm_proj(pg, wkv, slot, h * 256, 128, cnT, cnT, K=4)
        store_fm(ps, 128, nm["KnT_o"][0][h, :, tok], nm["KnT_o"][1])
    for t in range(4):
        for i in range(2):
            wsel = lambda kc, i=i: wkv[:, kc, :].rearrange("p (h two d) -> p h two d", two=2, d=128)[:, 4 * i:4 * i + 4, 1, :]
            ps = tm_proj(pg, wsel, slot, 512, cnT, cnT, t, K=4, outv=lambda p: p[:, :].rearrange("p (h d) -> p h d", d=128))
            s = stage()
            kb.op("act", lambda e, ps=ps, s=s: e.activation(out=s[:, :], in_=ps[:, :], func=AF.Copy), reads=[ps], writes=[s])
            r0 = b * BLK + t * 128
            kb.dma("sp", nm["Vm_o"][0][r0:r0 + 128, i * 512:(i + 1) * 512], s[:, :], reads=[s], writes=[pg.R(nm["Vm_o"][1])])
    if getattr(pg, 'dbg', 99) < 4:
        return
    slot, wv = pg.load_w(nm["w_in"][0], nm["w_in"][1], 0, 16, O_KR, 64)
    ps = fm_proj(pg, wv, slot, 0, 64, xT, xTb)
    rope_store(ps, 64, C64, S64, C_P64, nm["KrT_o"][0][:, tok], nm["KrT_o"][1])
    for (c0, key, Cb, Sb, pcol) in ((O_DQ, "QdT", C128, S128, C_P128), (O_DK, "KdT_o", C128, S128, C_P128),
                                    (O_IQ, "IqT", C64, S64, C_P64)):
        for g in range(2):
            slot, wv = pg.load_w(nm["w_in"][0], nm["w_in"][1], 0, 16, c0 + g * 512, 512)
            for hh in range(4):
                ps = fm_proj(pg, wv, slot, hh * 128, 128, xT, xTb)
                rope_store(ps, 128, Cb, Sb, pcol, nm[key][0][g * 4 + hh, :, tok], nm[key][1])
    if getattr(pg, 'dbg', 99) < 5:
        return
    for g in range(2):
        slot, wv = pg.load_w(nm["w_in"][0], nm["w_in"][1], 0, 16, O_DV + g * 512, 512)
        for t in range(4):
            ps = tm_proj(pg, lambda kc, wv=wv: wv[:, kc, :], slot, 512, xT, xTb, t)
            s = stage()
            kb.op("act", lambda e, ps=ps, s=s: e.activation(out=s[:, :], in_=ps[:, :], func=AF.Copy), reads=[ps], writes=[s])
            r0 = b * BLK + t * 128
            kb.dma("sp", nm["Vd_o"][0][r0:r0 + 128, g * 512:(g + 1) * 512], s[:, :], reads=[s], writes=[pg.R(nm["Vd_o"][1])])
    slot, wv = pg.load_w(nm["w_in"][0], nm["w_in"][1], 0, 16, O_IK, 80)
    ps = fm_proj(pg, wv, slot, 0, 64, xT, xTb)
    rope_store(ps, 64, C64, S64, C_P64, nm["IkT_o"][0][:, tok], nm["IkT_o"][1])
    for t in range(4):
        ps = tm_proj(pg, lambda kc, wv=wv: wv[:, kc, 64:80], slot, 16, xT, xTb, t)
        kb.op("act", lambda e, ps=ps: e.activation(out=xs[:, 0:16], in_=ps[:, 0:16], func=AF.Copy), reads=[ps], writes=[xs])
        r0 = b * BLK + t * 128
        kb.dma("sp", nm["iw"][0][r0:r0 + 128, :], xs[:, 0:16], reads=[xs], writes=[pg.R(nm["iw"][1])])
    if getattr(pg, 'dbg', 99) < 6:
        return
    emit_gla_proj_block(pg, wk, mp, b, L, xT)


def emit_gla_proj_block(pg, wk, mp, b, L, xT):
    kb = pg.kb
    kb.barrier()
    nm = L["n"]
    xTb = wk.xT
    win = nm["w_in"]
    gq_raw = View(wk.v[0][:, :].rearrange("p (a t) -> p a t", t=512))
    gk_raw = View(wk.v[1][:, :].rearrange("p (a t) -> p a t", t=512))
    lbuf = View(wk.v[2][:, 0:512])
    Eq = View(wk.v[2][:, 512:1024])
    Ek = View(wk.v[2][:, 1024:1536])
    Er = View(wk.v[2][:, 1536:2048])
    oi = View(wk.v[3][:, 0:1024])
    U0 = View(wk.v[3][:, 1024:2048])
    Ut = View(wk.gbc[:, 0:1024])
    gv = View(wk.hT[:, 0:4096].rearrange("p (a t) -> p a t", t=1024))
    sgr = View(wk.hT[:, 4096:8192].rearrange("p (a t) -> p a t", t=1024))
    qt = View(wk.hT[:, 8192:8704])
    kt = View(wk.hT[:, 8704:9216])
    kh = View(wk.hT[:, 9216:9728])
    qkT = View(wk.hT[:, 9728:10752].rearrange("p (a t) -> p a t", t=128))
    AT = View(wk.hT[:, 10752:11264].rearrange("p (a t) -> p a t", t=128))
    glrT = View(wk.sg[0][0:16, :])
    Acol = View(wk.sg[1][:, 0:8])
    Atl = View(wk.sg[1][:, 8:12])

    slot, wv = pg.load_w(win[0], win[1], 0, 16, O_GLR, 16)
    ps = fm_proj(pg, wv, slot, 0, 16, xT, xTb)
    kb.op("act", lambda e: e.activation(out=glrT[:, :], in_=ps[0:16, :], func=AF.Copy), reads=[ps], writes=[glrT])
    for (c0, dst) in ((O_GQ, gq_raw), (O_GK, gk_raw)):
        slot, wv = pg.load_w(win[0], win[1], 0, 16, c0, 512)
        for t in range(4):
            ps = tm_proj(pg, lambda kc, wv=wv: wv[:, kc, :], slot, 512, xT, xTb, t)
            kb.op("act", lambda e, ps=ps, t=t, dst=dst: e.activation(out=dst[:, t, :], in_=ps[:, :], func=AF.Copy), reads=[ps], writes=[dst])
    for (c0, dst, fn) in ((O_GV, gv, AF.Copy), (O_GR, sgr, AF.Silu)):
        for g in range(2):
            slot, wv = pg.load_w(win[0], win[1], 0, 16, c0 + g * 512, 512)
            for t in range(4):
                ps = tm_proj(pg, lambda kc, wv=wv: wv[:, kc, :], slot, 512, xT, xTb, t)
                kb.op("act", lambda e, ps=ps, t=t, g=g, dst=dst, fn=fn: e.activation(out=dst[:, t, g * 512:(g + 1) * 512], in_=ps[:, :], func=fn),
                      reads=[ps], writes=[dst])
    for t in range(4):
        r0 = b * BLK + t * 128
        kb.dma("sp", nm["gsr"][0][r0:r0 + 128, :], sgr[:, t, :], reads=[sgr], writes=[pg.R(nm["gsr"][1])])
    for t in range(4):
        j = b * 4 + t
        r0 = b * BLK + t * 128
        Z = pg.bank()
        mm(pg, Z[:, :], glrT[0:16, t * 128:(t + 1) * 128], mp.wg2[0:16, :], True, False, [glrT, mp.wg2], Z)
        mm(pg, Z[:, :], pg.consts[0:1, C_ONES:C_ONES + 128], mp.bg[0:1, :], False, True, [pg.consts, mp.bg], Z)
        kb.op("act", lambda e: e.activation(out=Eq[:, :], in_=Z[:, :], func=AF.Exp, scale=-1.0), reads=[Z], writes=[Eq])
        kb.op("act", lambda e: e.activation(out=lbuf[:, :], in_=Eq[:, :], func=AF.Ln, bias=pg.cc(C_ONES, 1), scale=1.0),
              reads=[Eq, pg.consts], writes=[lbuf])
        cum = pg.bank()
        mm(pg, cum[:, :], pg.cc(C_TRI), lbuf[:, :], True, True, [pg.consts, lbuf], cum)
        rev = pg.bank()
        mm(pg, rev[:, :], pg.cc(C_RTRI), lbuf[:, :], True, True, [pg.consts, lbuf], rev)
        tot = pg.pb[4]
        for h in range(4):
            mm(pg, tot[:, h * 2:h * 2 + 2], lbuf[:, h * 128:(h + 1) * 128], pg.consts[:, C_CH:C_CH + 2], True, True, [pg.consts, lbuf], tot)
        kb.op("act", lambda e: e.activation(out=Eq[:, :], in_=cum[:, :], func=AF.Exp, scale=-1.0 / 16), reads=[cum], writes=[Eq])
        kb.op("act", lambda e: e.activation(out=Ek[:, :], in_=cum[:, :], func=AF.Exp, scale=1.0 / 16), reads=[cum], writes=[Ek])
        kb.op("act", lambda e: e.activation(out=Er[:, :], in_=rev[:, :], func=AF.Exp, scale=-1.0 / 16), reads=[rev], writes=[Er])
        kb.op("act", lambda e: e.activation(out=Acol[:, :], in_=tot[:, 0:8], func=AF.Exp, scale=-1.0 / 16), reads=[tot], writes=[Acol])
        kb.op("dve", lambda e, t=t: e.scalar_tensor_tensor(out=qt[:, :], in0=gq_raw[:, t, :], scalar=128.0 ** -0.5, in1=Eq[:, :],
                                                          op0=ALU.mult, op1=ALU.mult), reads=[gq_raw, Eq], writes=[qt])
        kb.op("dve", lambda e, t=t: e.tensor_tensor(out=kt[:, :], in0=gk_raw[:, t, :], in1=Ek[:, :], op=ALU.mult), reads=[gk_raw, Ek], writes=[kt])
        kb.op("dve", lambda e, t=t: e.tensor_tensor(out=kh[:, :], in0=gk_raw[:, t, :], in1=Er[:, :], op=ALU.mult), reads=[gk_raw, Er], writes=[kh])
        ptb = pg.pt[0]
        for i in range(8):
            src = qt if i < 4 else kt
            hh = i % 4
            kb.op("pe", lambda e, i=i, src=src, hh=hh: e.transpose(out=ptb[:, i * 128:(i + 1) * 128], in_=src[:, hh * 128:(hh + 1) * 128],
                                                                   identity=pg.identb[:, :]), reads=[src, pg.identb], writes=[ptb])
        kb.op("dve", lambda e: e.tensor_copy(out=qkT[:, :, :], in_=ptb[:, :].rearrange("p (a t) -> p a t", t=128)), reads=[ptb], writes=[qkT])
        kb.dma("sp", nm["gqT"][0][:, :, r0:r0 + 128].rearrange("h p t -> p h t"), qkT[:, 0:4, :], reads=[qkT], writes=[pg.R(nm["gqT"][1])])
        for h in range(4):
            at = pg.bank()
            mm(pg, at[:, 0:128], qkT[:, 4 + h, :], qkT[:, h, :], True, True, [qkT], at)
            kb.op("dve", lambda e, at=at, h=h: e.tensor_tensor(out=AT[:, h, :], in0=at[:, 0:128], in1=pg.cc(C_MASKF), op=ALU.mult),
                  reads=[at, pg.consts], writes=[AT])
        for h in range(4):
            o = pg.bank()
            mm(pg, o[:, 0:256], AT[:, h, :], gv[:, t, h * 256:(h + 1) * 256], True, True, [AT, gv], o)
            kb.op("act", lambda e, o=o, h=h: e.activation(out=oi[:, h * 256:(h + 1) * 256], in_=o[:, 0:256], func=AF.Copy), reads=[o], writes=[oi])
        kb.dma("sp", nm["goi"][0][r0:r0 + 128, :], oi[:, :], reads=[oi], writes=[pg.R(nm["goi"][1])])
        for h in range(4):
            u0 = pg.bank()
            mm(pg, u0[:, 0:256], kh[0:64, h * 128:(h + 1) * 128], gv[0:64, t, h * 256:(h + 1) * 256], True, True, [kh, gv], u0)
            u1 = pg.bank()
            mm(pg, u1[:, 0:256], kh[64:128, h * 128:(h + 1) * 128], gv[64:128, t, h * 256:(h + 1) * 256], True, True, [kh, gv], u1)
            kb.op("act", lambda e, u0=u0, h=h: e.activation(out=U0[:, h * 256:(h + 1) * 256], in_=u0[:, 0:256], func=AF.Copy), reads=[u0], writes=[U0])
            kb.op("dve", lambda e, u1=u1, h=h: e.scalar_tensor_tensor(out=Ut[:, h * 256:(h + 1) * 256], in0=U0[:, h * 256:(h + 1) * 256],
                                                                      scalar=Acol[:, 2 * h + 1:2 * h + 2], in1=u1[:, 0:256],
                                                                      op0=ALU.mult, op1=ALU.add), reads=[U0, Acol, u1], writes=[Ut])
        kb.dma("sp", nm["gU0"][0][j], U0[:, :], reads=[U0], writes=[pg.R(nm["gU0"][1])])
        kb.dma("sp", nm["gUt_o"][0][j], Ut[:, :], reads=[Ut], writes=[pg.R(nm["gUt_o"][1])])
        Ac3 = Acol[:, :].rearrange("p (h c) -> p h c", c=2)
        kb.op("dve", lambda e: e.tensor_tensor(out=Atl[:, :], in0=Ac3[:, :, 0], in1=Ac3[:, :, 1], op=ALU.mult), reads=[Acol], writes=[Atl])
        kb.dma("sp", nm["gA"][0][j], Acol[:, :], reads=[Acol], writes=[pg.R(nm["gA"][1])])
        kb.dma("sp", nm["gAt_o"][0][j], Atl[:, :], reads=[Atl], writes=[pg.R(nm["gAt_o"][1])])
    kb.barrier()


BIS_L = -64.0
BIS_W = 128.0
BIS_N = 24


def emit_indexer(pg, nm, tiles=None):
    kb = pg.kb
    kb.push()
    ik2 = Buf(kb, "ik2", [128, SEQ], BF16)
    score = Buf(kb, "score", [128, SEQ], F32)
    mneg = Buf(kb, "mnegw", [128, SEQ], BF16)
    cm = Buf(kb, "cm32", [128, 1024], F32)
    iq = Buf(kb, "iqt", [128, 1024], BF16)
    iwt = Buf(kb, "iwt", [128, 16], F32)
    wab = Buf(kb, "wab", [128, 16], F32)
    wsg = Buf(kb, "wsg", [128, 16], F32)
    mid = Buf(kb, "mid", [128, 1], F32)
    cnt = Buf(kb, "cnt", [128, 1], F32)
    dlt = Buf(kb, "dlt", [128, 1], F32)
    nmid = Buf(kb, "nmid", [128, 1], F32)
    sgs = Buf(kb, "sgs", [128, 1], F32)
    ajunk = Buf(kb, "ajunk", [128, 10240], BF16)
    accBs = [Buf(kb, f"accB{i}", [128, 512], F32) for i in range(2)]
    kb.dma("sp", ik2[0:64, :], nm["IkT_g"][0][:, :], reads=[pg.R(nm["IkT_g"][1])], writes=[ik2])
    kb.dma("sp", ik2[64:128, :], nm["IkT_g"][0][:, :], reads=[pg.R(nm["IkT_g"][1])], writes=[ik2])
    kb.dma("sp", cm[:, :], nm["cmask"][0][:, :], reads=[pg.R(nm["cmask"][1])], writes=[cm])
    iq3 = iq[:, :].rearrange("p (a t) -> p a t", t=128)
    for j in (range(NT) if tiles is None else tiles):
        nk = 1024 * (j + 1)
        ts = slice(j * 128, (j + 1) * 128)
        kb.dma("sp", iq3, nm["IqT"][0][:, :, ts].rearrange("a p t -> p a t"), reads=[pg.R(nm["IqT"][1])], writes=[iq])
        kb.dma("sp", iwt[:, :], nm["iw"][0][ts, :], reads=[pg.R(nm["iw"][1])], writes=[iwt])
        kb.op("act", lambda e: e.activation(out=wab[:, :], in_=iwt[:, :], func=AF.Abs, scale=(64.0 ** -0.5) * (16.0 ** -0.5)),
              reads=[iwt], writes=[wab])
        kb.op("act", lambda e: e.activation(out=wsg[:, :], in_=iwt[:, :], func=AF.Sign), reads=[iwt], writes=[wsg])
        for kbi in range(2 * (j + 1)):
            ks = slice(kbi * 512, (kbi + 1) * 512)
            diag = kbi >= 2 * j
            accB = accBs[kbi % 2]
            for hh in range(16):
                p0 = (hh % 2) * 64
                ps = pg.bank()
                mm(pg, ps[:, :], iq3[p0:p0 + 64, hh // 2, :], ik2[p0:p0 + 64, ks], True, True, [iq, ik2], ps)
                kb.op("act", lambda e, ps=ps, hh=hh: e.activation(out=ps[:, :], in_=ps[:, :], func=AF.Relu, scale=wab[:, hh:hh + 1]),
                      reads=[ps, wab], writes=[ps])
                if hh % 2 == 0:
                    dst, dreg = score[:, ks], score
                else:
                    dst, dreg = accB[:, :], accB
                if hh == 0 and diag:
                    in1 = cm[:, (kbi - 2 * j) * 512:(kbi - 2 * j + 1) * 512]
                    kb.op("dve", lambda e, ps=ps, dst=dst, hh=hh, in1=in1: e.scalar_tensor_tensor(
                        out=dst, in0=ps[:, :], scalar=wsg[:, hh:hh + 1], in1=in1, op0=ALU.mult, op1=ALU.add),
                        reads=[ps, wsg, cm], writes=[dreg])
                elif hh < 2:
                    kb.op("dve", lambda e, ps=ps, dst=dst, hh=hh: e.tensor_scalar(out=dst, in0=ps[:, :], scalar1=wsg[:, hh:hh + 1], scalar2=None,
                                                                               op0=ALU.mult), reads=[ps, wsg], writes=[dreg])
                else:
                    kb.op("dve", lambda e, ps=ps, dst=dst, hh=hh: e.scalar_tensor_tensor(
                        out=dst, in0=ps[:, :], scalar=wsg[:, hh:hh + 1], in1=dst, op0=ALU.mult, op1=ALU.add),
                        reads=[ps, wsg], writes=[dreg])
            kb.op("dve", lambda e, ks=ks, accB=accB: e.tensor_tensor(out=score[:, ks], in0=score[:, ks], in1=accB[:, :], op=ALU.add),
                  reads=[accB], writes=[score])
        nd = max(512, int(round(0.40 * nk / 512)) * 512)
        na = nk - nd
        kb.op("dve", lambda e: e.memset(mid[:, :], BIS_L + BIS_W / 2), writes=[mid])
        for k in range(1, BIS_N + 1):
            kb.op("dve", lambda e: e.tensor_scalar(out=mneg[:, 0:nd], in0=score[:, 0:nd], scalar1=mid[:, 0:1], scalar2=None,
                                                   op0=ALU.is_ge, op1=ALU.add, accum_out=cnt[:, 0:1]),
                  reads=[score, mid], writes=[mneg, cnt])
            kb.op("act", lambda e: e.activation(out=ajunk[:, 0:na], in_=score[:, nd:nk], func=AF.Sign, bias=mid[:, 0:1], scale=-1.0,
                                                accum_out=sgs[:, 0:1]), reads=[score, mid], writes=[ajunk, sgs])
            kb.op("dve", lambda e: e.scalar_tensor_tensor(out=cnt[:, :], in0=sgs[:, :], scalar=-0.5, in1=cnt[:, :], op0=ALU.mult, op1=ALU.add),
                  reads=[sgs], writes=[cnt])
            hk = BIS_W / 2 ** (k + 1) if k < BIS_N else BIS_W / 2 ** BIS_N
            mul = 2 * hk if k < BIS_N else hk
            kb.op("dve", lambda e, mul=mul: e.tensor_scalar(out=dlt[:, :], in0=cnt[:, :], scalar1=255.5 - 0.5 * na, scalar2=mul,
                                                            op0=ALU.is_ge, op1=ALU.mult), reads=[cnt], writes=[dlt])
            kb.op("dve", lambda e, hk=hk: e.scalar_tensor_tensor(out=mid[:, :], in0=dlt[:, :], scalar=-hk, in1=mid[:, :],
                                                                op0=ALU.add, op1=ALU.add), reads=[dlt, mid], writes=[mid])
        kb.op("dve", lambda e: e.tensor_scalar(out=mneg[:, 0:nk], in0=score[:, 0:nk], scalar1=mid[:, 0:1], scalar2=NEG,
                                               op0=ALU.is_lt, op1=ALU.mult), reads=[score, mid], writes=[mneg])
        kb.dma("sp", nm["MnegD"][0][j, :, 0:nk], mneg[:, 0:nk], reads=[mneg], writes=[pg.R(nm["MnegD"][1])])
        if "thr_dbg" in nm:
            kb.dma("sp", nm["thr_dbg"][0][j], mid[:, :], reads=[mid], writes=[pg.R(nm["thr_dbg"][1])])
    kb.pop()


def emit_attention(pg, nm, kind, heads=None, tiles=None):
    kb = pg.kb
    kb.push()
    mla = kind == "mla"
    K = Buf(kb, "attK", [128, SEQ], BF16)
    V = Buf(kb, "attV", [128, 128 * 129], BF16)
    V3 = V[:, :].rearrange("p (n d) -> p n d", d=129)
    Kg = [View(K[:, g * 1024:(g + 1) * 1024]) for g in range(16)]
    Vg = [View(V3[:, g * 8:(g + 1) * 8, :]) for g in range(16)]
    kb.op("dve", lambda e: e.memset(V[:, :], 1.0), writes=[V] + Vg)
    if mla:
        KR = Buf(kb, "attKR", [128, SEQ], BF16)
        kb.op("dve", lambda e: e.memset(KR[:, :], 0.0), writes=[KR])
        cm32 = Buf(kb, "attcm32", [128, 1024], F32)
        cmb = Buf(kb, "attcmb", [128, 1024], BF16)
        kb.dma("sp", KR[0:64, :], nm["KrT_g"][0][:, :], reads=[pg.R(nm["KrT_g"][1])], writes=[KR])
        kb.dma("sp", cm32[:, :], nm["cmask"][0][:, :], reads=[pg.R(nm["cmask"][1])], writes=[cm32])
        kb.op("dve", lambda e: e.tensor_copy(out=cmb[:, :], in_=cm32[:, :]), reads=[cm32], writes=[cmb])
        Qr = [Buf(kb, f"attQr{i}", [128, 128], BF16) for i in range(2)]
        for q_ in Qr:
            kb.op("dve", lambda e, q_=q_: e.memset(q_[:, :], 0.0), writes=[q_])
        scale = 192.0 ** -0.5
        Kd, Vd, Qd, oT = nm["KnT_g"], nm["Vm_g"], nm["QnT"], nm["oT_mla"]
    else:
        MN = [Buf(kb, f"attMN{i}", [128, SEQ], BF16) for i in range(2)]
        scale = 128.0 ** -0.5
        Kd, Vd, Qd, oT = nm["KdT_g"], nm["Vd_g"], nm["QdT"], nm["oT_dsa"]
    Qn = [Buf(kb, f"attQn{i}", [128, 128], BF16) for i in range(2)]
    PT = [Buf(kb, f"attPT{i}", [128, 512], BF16) for i in range(3)]
    rsum = Buf(kb, "attrsum", [128, 2], F32)
    ob = Buf(kb, "attob", [128, 128], BF16)
    oTs = Buf(kb, "attoTs", [128, 128], BF16)
    Ob = [pg.pb[4], pg.pb[5]]
    tasks = []
    it = 0
    for h in (range(8) if heads is None else heads):
        first = True
        for j in (range(NT) if tiles is None else tiles):
            nblk = 2 * (j + 1)
            for kbi in range(nblk):
                tasks.append(dict(h=h, j=j, kbi=kbi, nblk=nblk, it=it, newhead=(first and kbi == 0)))
            first = False
            it += 1
    state = {"maxg": -1}

    def stageA(s_, T):
        h, j, kbi, it_ = T["h"], T["j"], T["kbi"], T["it"]
        ts = slice(j * 128, (j + 1) * 128)
        nk = 1024 * (j + 1)
        qn = Qn[it_ % 2]
        if kbi == 0:
            if T["newhead"]:
                state["maxg"] = -1
            for g in range(state["maxg"] + 1, j + 1):
                gs = slice(g * 1024, (g + 1) * 1024)
                kb.dma("sp", Kg[g][:, :], Kd[0][h, :, gs], reads=[pg.R(Kd[1])], writes=[Kg[g]])
                kb.dma("sp", Vg[g][:, :, 0:128], Vd[0][h, :, gs].rearrange("p (n d) -> p n d", d=128), reads=[pg.R(Vd[1])], writes=[Vg[g]])
            state["maxg"] = max(state["maxg"], j)
            kb.dma("pool", qn[:, :], Qd[0][h, :, ts], reads=[pg.R(Qd[1])], writes=[qn])
            if mla:
                kb.dma("pool", Qr[it_ % 2][0:64, :], nm["QrT"][0][h, :, ts], reads=[pg.R(nm["QrT"][1])], writes=[Qr[it_ % 2]])
            else:
                kb.dma("pool", MN[it_ % 2][:, 0:nk], nm["MnegD"][0][j, :, 0:nk], reads=[pg.R(nm["MnegD"][1])], writes=[MN[it_ % 2]])
        g = kbi // 2
        diag = g == j
        ps = pg.bank()
        for i in range(4):
            k0 = kbi * 512 + i * 128
            o = ps[:, i * 128:(i + 1) * 128]
            if mla:
                qr = Qr[it_ % 2]
                mm(pg, o, K[:, k0:k0 + 128], qn[:, :], True, False, [qn, Kg[g]], ps)
                mm(pg, o, KR[:, k0:k0 + 128], qr[:, :], False, not diag, [qr, KR], ps)
                if diag:
                    c0 = k0 - j * 1024
                    mm(pg, o, cmb[:, c0:c0 + 128], pg.identb[:, :], False, True, [pg.identb, cmb], ps)
            else:
                mn = MN[it_ % 2]
                mm(pg, o, K[:, k0:k0 + 128], qn[:, :], True, False, [qn, Kg[g]], ps)
                mm(pg, o, mn[:, k0:k0 + 128], pg.identb[:, :], False, True, [pg.identb, mn], ps)
        p = PT[s_ % 3]
        kb.op("act", lambda e: e.activation(out=p[:, :], in_=ps[:, :], func=AF.Exp, scale=scale), reads=[ps], writes=[p])

    def stageC(s_, T):
        h, j, kbi, nblk, it_ = T["h"], T["j"], T["kbi"], T["nblk"], T["it"]
        g = kbi // 2
        pt_ = PT[s_ % 3]
        O = Ob[it_ % 2]
        for i in range(4):
            n = kbi * 4 + i
            mm(pg, O[:, 0:129], pt_[:, i * 128:(i + 1) * 128], V3[:, n, :],
               kbi == 0 and i == 0, kbi == nblk - 1 and i == 3, [pt_, Vg[g]], O)
        if kbi == nblk - 1:
            ts = slice(j * 128, (j + 1) * 128)
            kb.op("dve", lambda e: e.reciprocal(out=rsum[:, 1:2], in_=O[:, 128:129]), reads=[O], writes=[rsum])
            kb.op("act", lambda e: e.activation(out=ob[:, :], in_=O[:, 0:128], func=AF.Copy, scale=rsum[:, 1:2]), reads=[O, rsum], writes=[ob])
            ptb = pg.pt[it_ % 2]
            kb.op("pe", lambda e: e.transpose(out=ptb[:, 0:128], in_=ob[:, :], identity=pg.identb[:, :]), reads=[ob, pg.identb], writes=[ptb])
            kb.op("dve", lambda e: e.tensor_copy(out=oTs[:, :], in_=ptb[:, 0:128]), reads=[ptb], writes=[oTs])
            kb.dma("sp", oT[0][h * 128:(h + 1) * 128, ts], oTs[:, :], reads=[oTs], writes=[pg.R(oT[1])])

    N = len(tasks)
    for s_ in range(N + 1):
        if s_ < N:
            stageA(s_, tasks[s_])
        if 0 <= s_ - 1 < N:
            stageC(s_ - 1, tasks[s_ - 1])
    kb.pop()


def emit_gla_out(pg, nm, tiles=None):
    kb = pg.kb
    kb.push()
    S = Buf(kb, "glaS", [128, 1024], F32)
    snap = Buf(kb, "glasnap", [128, 1024], F32)
    At = Buf(kb, "glaAt", [128, 512], F32)
    Ub = [Buf(kb, f"glaUb{i}", [128, 1024], F32) for i in range(3)]
    U0b = Buf(kb, "glaU0", [128, 1024], F32)
    Ab = Buf(kb, "glaAb", [128, 8], F32)
    qTb = Buf(kb, "glaqT", [128, 512], BF16)
    qA = Buf(kb, "glaqA", [128, 512], BF16)
    qB = Buf(kb, "glaqB", [128, 512], BF16)
    oib = Buf(kb, "glaoi", [128, 1024], F32)
    srb = Buf(kb, "glasr", [128, 1024], BF16)
    gn = Buf(kb, "glagn", [128, 1024], F32)
    S0b = Buf(kb, "glaS0b", [128, 1024], BF16)
    S1b = Buf(kb, "glaS1b", [128, 1024], BF16)
    junk = Buf(kb, "glajunk", [128, 256], F32)
    ss = Buf(kb, "glass", [128, 8], F32)
    ob = Buf(kb, "glaob", [128, 1024], BF16)
    oTs = Buf(kb, "glaoTs", [128, 1024], BF16)
    kb.op("dve", lambda e: e.memset(S[:, :], 0.0), writes=[S])
    kb.op("dve", lambda e: e.memset(qA[:, :], 0.0), writes=[qA])
    kb.op("dve", lambda e: e.memset(qB[:, :], 0.0), writes=[qB])
    kb.dma("sp", At[:, :], nm["gAt_g"][0][:, :], reads=[pg.R(nm["gAt_g"][1])], writes=[At])
    kb.dma("sp", gn[:, :], nm["gla_norm"][0].partition_broadcast(128), reads=[pg.R(nm["gla_norm"][1])], writes=[gn])
    q3 = qTb[:, :].rearrange("p (h t) -> p h t", t=128)
    qA3 = qA[:, :].rearrange("p (h t) -> p h t", t=128)
    qB3 = qB[:, :].rearrange("p (h t) -> p h t", t=128)
    for j in range(NT):
        for i in range(8):
            g = 8 * j + i
            ub = Ub[g % 3]
            kb.dma("sp", ub[:, :], nm["gUt_g"][0][g], reads=[pg.R(nm["gUt_g"][1])], writes=[ub])
            if i == 0:
                kb.op("dve", lambda e: e.tensor_scalar(out=snap[:, :], in0=S[:, :], scalar1=pg.cc(C_EI, 1), scalar2=None, op0=ALU.mult),
                      reads=[S, pg.consts], writes=[snap])
            else:
                kb.op("dve", lambda e, i=i: e.scalar_tensor_tensor(out=snap[:, :], in0=S[:, :], scalar=pg.cc(C_EI + i, 1), in1=snap[:, :],
                                                                   op0=ALU.mult, op1=ALU.add), reads=[S, pg.consts, snap], writes=[snap])
            for h in range(4):
                hs = slice(h * 256, (h + 1) * 256)
                kb.op("dve", lambda e, hs=hs, g=g, h=h, ub=ub: e.scalar_tensor_tensor(
                    out=S[:, hs], in0=S[:, hs], scalar=At[:, g * 4 + h:g * 4 + h + 1], in1=ub[:, hs], op0=ALU.mult, op1=ALU.add),
                    reads=[At, ub], writes=[S])
        if tiles is not None and j not in tiles:
            continue
        ts = slice(j * 128, (j + 1) * 128)
        kb.dma("pool", U0b[:, :], nm["gU0"][0][j], reads=[pg.R(nm["gU0"][1])], writes=[U0b])
        kb.dma("pool", Ab[:, :], nm["gA"][0][j], reads=[pg.R(nm["gA"][1])], writes=[Ab])
        kb.dma("pool", q3, nm["gqT"][0][:, :, ts].rearrange("h p t -> p h t"), reads=[pg.R(nm["gqT"][1])], writes=[qTb])
        kb.dma("pool", oib[:, :], nm["goi"][0][ts, :], reads=[pg.R(nm["goi"][1])], writes=[oib])
        kb.dma("pool", srb[:, :], nm["gsr"][0][ts, :], reads=[pg.R(nm["gsr"][1])], writes=[srb])
        kb.op("act", lambda e: e.activation(out=S0b[:, :], in_=snap[:, :], func=AF.Copy), reads=[snap], writes=[S0b])
        for h in range(4):
            hs = slice(h * 256, (h + 1) * 256)
            kb.op("dve", lambda e, hs=hs, h=h: e.scalar_tensor_tensor(out=S1b[:, hs], in0=snap[:, hs], scalar=Ab[:, 2 * h:2 * h + 1],
                                                                      in1=U0b[:, hs], op0=ALU.mult, op1=ALU.add),
                  reads=[snap, Ab, U0b], writes=[S1b])
        kb.op("dve", lambda e: e.tensor_copy(out=qA3[:, :, 0:64], in_=q3[:, :, 0:64]), reads=[qTb], writes=[qA])
        kb.op("dve", lambda e: e.tensor_copy(out=qB3[:, :, 64:128], in_=q3[:, :, 64:128]), reads=[qTb], writes=[qB])
        for h in range(4):
            hs = slice(h * 256, (h + 1) * 256)
            o = pg.bank()
            mm(pg, o[:, 0:256], qA3[:, h, :], S0b[:, hs], True, False, [qA, S0b], o)
            mm(pg, o[:, 0:256], qB3[:, h, :], S1b[:, hs], False, True, [qB, S1b], o)
            kb.op("dve", lambda e, o=o, hs=hs: e.tensor_tensor(out=oib[:, hs], in0=o[:, 0:256], in1=oib[:, hs], op=ALU.add), reads=[o], writes=[oib])
            kb.op("act", lambda e, hs=hs, h=h: e.activation(out=junk[:, :], in_=oib[:, hs], func=AF.Square, accum_out=ss[:, h:h + 1]),
                  reads=[oib], writes=[junk, ss])
        kb.op("act", lambda e: e.activation(out=ss[:, 4:8], in_=ss[:, 0:4], func=AF.Ln, bias=pg.cc(C_EPS, 1), scale=1.0 / 256),
              reads=[ss, pg.consts], writes=[ss])
        kb.op("act", lambda e: e.activation(out=ss[:, 4:8], in_=ss[:, 4:8], func=AF.Exp, scale=-0.5), reads=[ss], writes=[ss])
        for h in range(4):
            hs = slice(h * 256, (h + 1) * 256)
            kb.op("dve", lambda e, hs=hs, h=h: e.scalar_tensor_tensor(out=oib[:, hs], in0=oib[:, hs], scalar=ss[:, 4 + h:5 + h], in1=gn[:, hs],
                                                                      op0=ALU.mult, op1=ALU.mult), reads=[ss, gn], writes=[oib])
        kb.op("dve", lambda e: e.tensor_tensor(out=ob[:, :], in0=oib[:, :], in1=srb[:, :], op=ALU.mult), reads=[oib, srb], writes=[ob])
        ptb = pg.pt[j % 2]
        for i in range(8):
            kb.op("pe", lambda e, i=i, ptb=ptb: e.transpose(out=ptb[:, i * 128:(i + 1) * 128], in_=ob[:, i * 128:(i + 1) * 128],
                                                            identity=pg.identb[:, :]), reads=[ob, pg.identb], writes=[ptb])
        kb.op("dve", lambda e, ptb=ptb: e.tensor_copy(out=oTs[:, :], in_=ptb[:, :]), reads=[ptb], writes=[oTs])
        kb.dma("sp", nm["oT_gla"][0][:, ts].rearrange("(k p) t -> p k t", p=128), oTs[:, :].rearrange("p (k t) -> p k t", t=128),
               reads=[oTs], writes=[pg.R(nm["oT_gla"][1])])
    kb.pop()


def emit_merge_block(pg, wk, b, nm):
    kb = pg.kb
    kb.barrier()
    load_ln_params(pg, wk, nm["ln_g1"][0], nm["ln_b1"][0], nm["ln_g1"][1], nm["ln_b1"][1])
    tok = slice(b * BLK, (b + 1) * BLK)
    xT = load_xT_block(pg, wk, nm["x1T"][0], nm["x1T"][1], b)
    oTm = View(wk.hT[:, 0:4096].rearrange("p (k t) -> p k t", t=BLK))
    oTd = View(wk.hT[:, 4096:8192].rearrange("p (k t) -> p k t", t=BLK))
    oTg0 = View(wk.xb[:, :].rearrange("p (k t) -> p k t", t=BLK))
    oTg1 = View(wk.xts[:, :].rearrange("p (k t) -> p k t", t=BLK))
    bm = View(wk.stats[0:1, 0:24])
    brow = View(wk.hT[0:1, 8192:8192 + 512])
    kb.dma("sp", oTm[:, :, :], nm["oT_mla"][0][:, tok].rearrange("(k p) t -> p k t", p=128), reads=[pg.R(nm["oT_mla"][1])], writes=[oTm])
    kb.dma("sp", oTd[:, :, :], nm["oT_dsa"][0][:, tok].rearrange("(k p) t -> p k t", p=128), reads=[pg.R(nm["oT_dsa"][1])], writes=[oTd])
    kb.dma("sp", oTg0[:, :, :], nm["oT_gla"][0][0:512, tok].rearrange("(k p) t -> p k t", p=128), reads=[pg.R(nm["oT_gla"][1])], writes=[oTg0])
    kb.dma("sp", oTg1[:, :, :], nm["oT_gla"][0][512:1024, tok].rearrange("(k p) t -> p k t", p=128), reads=[pg.R(nm["oT_gla"][1])], writes=[oTg1])
    brs = [("w_br_mla", lambda kc: oTm[:, kc, :], [oTm]), ("w_br_dsa", lambda kc: oTd[:, kc, :], [oTd]),
           ("w_br_gla", lambda kc: (oTg0[:, kc, :] if kc < 4 else oTg1[:, kc - 4, :]), [oTg0, oTg1])]
    for n in range(4):
        for bi, (wname, osel, oregs) in enumerate(brs):
            sm_, wm = pg.load_w(nm["w_merge"][0], nm["w_merge"][1], 0, 16, bi * 2048 + n * 512, 512)
            sb_, wb = pg.load_w(nm[wname][0], nm[wname][1], 0, 8, n * 512, 512)
            c0 = bi * 2048 + n * 512
            kb.dma("pool", brow[:, :], nm["b_merge"][0][0:1, c0:c0 + 512], reads=[pg.R(nm["b_merge"][1])], writes=[brow])
            for t in range(4):
                pgt = pg.bank()
                for kc in range(16):
                    mm(pg, pgt[:, :], xT[:, kc, t * 128:(t + 1) * 128], wm[:, kc, :], kc == 0, False, [wk.xT, sm_], pgt)
                mm(pg, pgt[:, :], pg.onesb[0:1, :], brow[0:1, :], False, True, [pg.onesb, brow], pgt)
                sgb = wk.sg[t % 2]
                kb.op("act", lambda e, pgt=pgt, sgb=sgb: e.activation(out=sgb[:, :], in_=pgt[:, :], func=AF.Sigmoid), reads=[pgt], writes=[sgb])
                py = pg.bank()
                for kc in range(8):
                    mm(pg, py[:, :], osel(kc)[:, t * 128:(t + 1) * 128], wb[:, kc, :], kc == 0, kc == 7, oregs + [sb_], py)
                ms = wk.v[t][:, n * 512:(n + 1) * 512]
                if bi == 0:
                    kb.op("dve", lambda e, py=py, sgb=sgb, ms=ms: e.tensor_tensor(out=ms, in0=py[:, :], in1=sgb[:, :], op=ALU.mult),
                          reads=[py, sgb], writes=[wk.v[t]])
                else:
                    kb.op("dve", lambda e, py=py, sgb=sgb: e.tensor_tensor(out=sgb[:, :], in0=py[:, :], in1=sgb[:, :], op=ALU.mult),
                          reads=[py], writes=[sgb])
                    kb.op("dve", lambda e, sgb=sgb, ms=ms: e.tensor_tensor(out=ms, in0=ms, in1=sgb[:, :], op=ALU.add),
                          reads=[sgb], writes=[wk.v[t]])
    kb.barrier()
    mT = View(wk.hT[:, 0:8192].rearrange("p (k t) -> p k t", t=BLK))
    for t in range(4):
        kb.op("act", lambda e, t=t: e.activation(out=wk.xb[:, :], in_=wk.v[t][:, :], func=AF.Copy), reads=[wk.v[t]], writes=[wk.xb])
        for half in range(2):
            ptb = pg.pt[half]
            for k in range(8):
                kc = half * 8 + k
                kb.op("pe", lambda e, kc=kc, k=k, ptb=ptb: e.transpose(out=ptb[:, k * 128:(k + 1) * 128], in_=wk.xb[:, kc * 128:(kc + 1) * 128],
                                                                      identity=pg.identb[:, :]), reads=[wk.xb, pg.identb], writes=[ptb])
            kb.op("dve", lambda e, ptb=ptb, half=half, t=t: e.tensor_copy(
                out=mT[:, half * 8:half * 8 + 8, t * 128:(t + 1) * 128], in_=ptb[:, :].rearrange("p (k t) -> p k t", t=128)),
                reads=[ptb], writes=[mT])
        r0 = b * BLK + t * 128
        kb.dma("sp", wk.v[t][:, :], nm["x1"][0][r0:r0 + 128, :], reads=[pg.R(nm["x1"][1])], writes=[wk.v[t]])
    for n in range(4):
        so_, wo = pg.load_w(nm["w_out"][0], nm["w_out"][1], 0, 16, n * 512, 512)
        for t in range(4):
            ph = pg.bank()
            for kc in range(16):
                mm(pg, ph[:, :], mT[:, kc, t * 128:(t + 1) * 128], wo[:, kc, :], kc == 0, kc == 15, [mT, so_], ph)
            vs = wk.v[t][:, n * 512:(n + 1) * 512]
            kb.op("dve", lambda e, ph=ph, vs=vs: e.scalar_tensor_tensor(out=vs, in0=vs, scalar=ALPHA, in1=ph[:, :], op0=ALU.mult, op1=ALU.add),
                  reads=[ph], writes=[wk.v[t]])
    kb.barrier()
    for t in range(4):
        r0 = b * BLK + t * 128
        emit_ln(pg, wk.v[t], wk.v[t][:, :], wk.gbc, wk.bbc, wk.st, nm["x2"][0][r0:r0 + 128, :], nm["x2"][1],
                nm["x2T"][0][:, r0:r0 + 128], nm["x2T"][1], wk.xb, wk.xts)


class XAState:
    def __init__(self, pg, wk, nm):
        kb = pg.kb
        kb.barrier()
        self.KxT = Buf(kb, "xaK", [128, 1024], BF16)
        self.Vx = Buf(kb, "xaV", [128, 1024], BF16)
        memT = View(wk.hT[:, 0:4096].rearrange("p (k m) -> p k m", m=256))
        for mt in range(2):
            vb = wk.v[mt]
            kb.dma("sp", vb[:, :], nm["mem"][0][mt * 128:(mt + 1) * 128, :], reads=[pg.R(nm["mem"][1])], writes=[vb])
            kb.op("act", lambda e, vb=vb: e.activation(out=wk.xb[:, :], in_=vb[:, :], func=AF.Copy), reads=[vb], writes=[wk.xb])
            for half in range(2):
                ptb = pg.pt[half]
                for k in range(8):
                    kc = half * 8 + k
                    kb.op("pe", lambda e, kc=kc, k=k, ptb=ptb: e.transpose(out=ptb[:, k * 128:(k + 1) * 128], in_=wk.xb[:, kc * 128:(kc + 1) * 128],
                                                                          identity=pg.identb[:, :]), reads=[wk.xb, pg.identb], writes=[ptb])
                kb.op("dve", lambda e, ptb=ptb, half=half, mt=mt: e.tensor_copy(
                    out=memT[:, half * 8:half * 8 + 8, mt * 128:(mt + 1) * 128], in_=ptb[:, :].rearrange("p (k t) -> p k t", t=128)),
                    reads=[ptb], writes=[memT])
        K3 = self.KxT[:, :].rearrange("p (h m) -> p h m", m=256)
        V3 = self.Vx[:, :].rearrange("p (a c) -> p a c", c=512)
        sk_, wkk = pg.load_w(nm["xa_w_kv"][0], nm["xa_w_kv"][1], 0, 16, 0, 512)
        for h in range(4):
            ps = pg.bank()
            for kc in range(16):
                mm(pg, ps[:, 0:256], wkk[:, kc, h * 128:(h + 1) * 128], memT[:, kc, :], kc == 0, kc == 15, [sk_, memT], ps)
            kb.op("act", lambda e, ps=ps, h=h: e.activation(out=K3[:, h, :], in_=ps[:, 0:256], func=AF.Copy), reads=[ps], writes=[self.KxT])
        sv_, wvv = pg.load_w(nm["xa_w_kv"][0], nm["xa_w_kv"][1], 0, 16, 512, 512)
        for mt in range(2):
            ps = pg.bank()
            for kc in range(16):
                mm(pg, ps[:, :], memT[:, kc, mt * 128:(mt + 1) * 128], wvv[:, kc, :], kc == 0, kc == 15, [sv_, memT], ps)
            kb.op("act", lambda e, ps=ps, mt=mt: e.activation(out=V3[:, mt, :], in_=ps[:, :], func=AF.Copy), reads=[ps], writes=[self.Vx])
        self.K3, self.V3 = K3, V3
        kb.barrier()


def emit_xattn_block(pg, wk, xa, b, nm):
    kb = pg.kb
    kb.barrier()
    load_ln_params(pg, wk, nm["ln_g2"][0], nm["ln_b2"][0], nm["ln_g2"][1], nm["ln_b2"][1])
    xT = load_xT_block(pg, wk, nm["x2T"][0], nm["x2T"][1], b)
    qT = View(wk.hT[:, 0:2048].rearrange("p (h t) -> p h t", t=BLK))
    oxT = View(wk.hT[:, 2048:4096].rearrange("p (h t) -> p h t", t=BLK))
    Pb = [View(wk.hT[:, 4096 + i * 256:4096 + (i + 1) * 256]) for i in range(2)]
    PTb = [View(wk.hT[:, 4608 + i * 256:4608 + (i + 1) * 256]) for i in range(2)]
    ob = View(wk.hT[:, 5120:5632])
    rs = View(wk.sg[0][:, 0:8])
    sq_, wq = pg.load_w(nm["xa_w_q"][0], nm["xa_w_q"][1], 0, 16, 0, 512)
    for h in range(4):
        ps = fm_proj(pg, wq, sq_, h * 128, 128, xT, wk.xT)
        kb.op("act", lambda e, ps=ps, h=h: e.activation(out=qT[:, h, :], in_=ps[:, :], func=AF.Copy), reads=[ps], writes=[qT])
    for t in range(4):
        r0 = b * BLK + t * 128
        kb.dma("sp", wk.v[t][:, :], nm["x2"][0][r0:r0 + 128, :], reads=[pg.R(nm["x2"][1])], writes=[wk.v[t]])
        for h in range(4):
            ps = pg.bank()
            mm(pg, ps[:, 0:256], qT[:, h, t * 128:(t + 1) * 128], xa.K3[:, h, :], True, True, [qT, xa.KxT], ps)
            p = Pb[h % 2]
            kb.op("act", lambda e, ps=ps, p=p, h=h: e.activation(out=p[:, :], in_=ps[:, 0:256], func=AF.Exp, scale=128.0 ** -0.5,
                                                                 accum_out=rs[:, h:h + 1]), reads=[ps], writes=[p, rs])
            ptb = pg.pt[h % 2]
            for i in range(2):
                kb.op("pe", lambda e, i=i, p=p, ptb=ptb: e.transpose(out=ptb[:, i * 128:(i + 1) * 128], in_=p[:, i * 128:(i + 1) * 128],
                                                                    identity=pg.identb[:, :]), reads=[p, pg.identb], writes=[ptb])
            pt_ = PTb[h % 2]
            kb.op("dve", lambda e, ptb=ptb, pt_=pt_: e.tensor_copy(out=pt_[:, :], in_=ptb[:, 0:256]), reads=[ptb], writes=[pt_])
            po = pg.bank()
            for i in range(2):
                mm(pg, po[:, 0:128], pt_[:, i * 128:(i + 1) * 128], xa.V3[:, i, h * 128:(h + 1) * 128], i == 0, i == 1, [pt_, xa.Vx], po)
            kb.op("dve", lambda e, h=h: e.reciprocal(out=rs[:, 4 + h:5 + h], in_=rs[:, h:h + 1]), reads=[rs], writes=[rs])
            kb.op("act", lambda e, po=po, h=h: e.activation(out=ob[:, h * 128:(h + 1) * 128], in_=po[:, 0:128], func=AF.Copy, scale=rs[:, 4 + h:5 + h]),
                  reads=[po, rs], writes=[ob])
        ptb = pg.pt[0]
        for h in range(4):
            kb.op("pe", lambda e, h=h, ptb=ptb: e.transpose(out=ptb[:, h * 128:(h + 1) * 128], in_=ob[:, h * 128:(h + 1) * 128],
                                                            identity=pg.identb[:, :]), reads=[ob, pg.identb], writes=[ptb])
        kb.op("dve", lambda e, ptb=ptb, t=t: e.tensor_copy(out=oxT[:, :, t * 128:(t + 1) * 128], in_=ptb[:, 0:512].rearrange("p (h t) -> p h t", t=128)),
              reads=[ptb], writes=[oxT])
    for n in range(4):
        so_, wo = pg.load_w(nm["xa_w_o"][0], nm["xa_w_o"][1], 0, 4, n * 512, 512)
        for t in range(4):
            ph = pg.bank()
            for kc in range(4):
                mm(pg, ph[:, :], oxT[:, kc, t * 128:(t + 1) * 128], wo[:, kc, :], kc == 0, kc == 3, [oxT, so_], ph)
            vs = wk.v[t][:, n * 512:(n + 1) * 512]
            kb.op("dve", lambda e, ph=ph, vs=vs: e.scalar_tensor_tensor(out=vs, in0=vs, scalar=ALPHA, in1=ph[:, :], op0=ALU.mult, op1=ALU.add),
                  reads=[ph], writes=[wk.v[t]])
    kb.barrier()
    for t in range(4):
        r0 = b * BLK + t * 128
        emit_ln(pg, wk.v[t], wk.v[t][:, :], wk.gbc, wk.bbc, wk.st, nm["x3"][0][r0:r0 + 128, :], nm["x3"][1],
                nm["x3T"][0][:, r0:r0 + 128], nm["x3T"][1], wk.xb, wk.xts)


A_OUT = [("x1", [TL, D], F32), ("x1T", [D, TL], BF16), ("QnT", [8, 128, TL], BF16), ("QrT", [8, 64, TL], BF16),
         ("QdT", [8, 128, TL], BF16), ("IqT", [8, 128, TL], BF16), ("iw", [TL, 16], F32), ("gqT", [4, 128, TL], BF16),
         ("gU0", [NT, 128, 1024], F32), ("gA", [NT, 128, 8], F32), ("goi", [TL, 1024], F32), ("gsr", [TL, 1024], BF16),
         ("KnT_o", [8, 128, TL], BF16), ("KrT_o", [64, TL], BF16), ("Vm_o", [TL, 1024], BF16), ("KdT_o", [8, 128, TL], BF16),
         ("Vd_o", [TL, 1024], BF16), ("IkT_o", [64, TL], BF16), ("gUt_o", [NT, 128, 1024], F32), ("gAt_o", [NT, 128, 4], F32)]
A_LOCAL = ["x1", "x1T", "QnT", "QrT", "QdT", "IqT", "iw", "gqT", "gU0", "gA", "goi", "gsr"]
B_GLOBAL = [("KnT_g", [8, 128, SEQ], BF16), ("KrT_g", [64, SEQ], BF16), ("Vm_g", [8, 128, SEQ], BF16), ("KdT_g", [8, 128, SEQ], BF16),
            ("Vd_g", [8, 128, SEQ], BF16), ("IkT_g", [64, SEQ], BF16), ("gUt_g", [SEQ // 128, 128, 1024], F32), ("gAt_g", [128, 512], F32)]
A_W = [("ffn1_g", [D, DFF]), ("ffn1_u", [D, DFF]), ("ffn1_d", [DFF, D]), ("ln_g0", [D]), ("ln_b0", [D]), ("w_in", [D, IN_W]),
       ("w_uq", [512, 1536]), ("w_ukv", [512, 2048]), ("sm", [128, 8]), ("wg2", [16, 512]), ("bg", [1, 512])]
B_W = [("w_br_mla", [1024, D]), ("w_br_dsa", [1024, D]), ("w_br_gla", [1024, D]), ("w_merge", [D, 3 * D]), ("b_merge", [1, 3 * D]),
       ("w_out", [D, D]), ("xa_w_q", [D, 512]), ("xa_w_kv", [D, 1024]), ("xa_w_o", [512, D]), ("mem", [256, D]),
       ("ln_g1", [D]), ("ln_b1", [D]), ("ln_g2", [D]), ("ln_b2", [D]), ("ffn2_g", [D, DFF]), ("ffn2_u", [D, DFF]), ("ffn2_d", [DFF, D]),
       ("ln_g3", [D]), ("ln_b3", [D]), ("gla_norm", [1024])]


def build_program(has_B, has_A, dbg_out=()):
    pg = Prog()
    kb = pg.kb
    nmB, nmA = {}, {}
    if has_B:
        for k in A_LOCAL:
            shape, dt = next((s, d) for (n, s, d) in A_OUT if n == k)
            nmB[k] = (pg.inp("i_" + k, shape, dt), "i_" + k)
        for (k, shape, dt) in B_GLOBAL:
            nmB[k] = (pg.inp(k, shape, dt), k)
        nmB["cmask"] = (pg.inp("cmask", [128, 1024], F32), "cmask")
        for (k, shape) in B_W:
            nmB[k] = (pg.inp("B_" + k, shape, F32), "B_" + k)
        for (k, shape, dt) in (("MnegD", [NT, 128, SEQ], BF16), ("oT_mla", [1024, TL], BF16), ("oT_dsa", [1024, TL], BF16),
                               ("oT_gla", [1024, TL], BF16), ("x2", [TL, D], F32), ("x2T", [D, TL], BF16),
                               ("x3", [TL, D], F32), ("x3T", [D, TL], BF16), ("x4T", [D, TL], BF16)):
            if k in dbg_out:
                nmB[k] = (pg.out(k, shape, dt), k)
            else:
                nmB[k] = (pg.scr(k, shape, dt), k)
        if has_A:
            nmB["x4"] = (pg.out("x4", [TL, D], F32), "x4") if "x4" in dbg_out else (pg.scr("x4", [TL, D], F32), "x4")
        else:
            nmB["x4"] = (pg.out("y", [TL, D], F32), "y")
    if has_A:
        for (k, shape) in A_W:
            nmA[k] = (pg.inp("A_" + k, shape, F32), "A_" + k)
        nmA["pos"] = (pg.inp("pos", [TL], I32), "pos")
        nmA["tabs"] = (pg.scr("tabs", [4, 128, TL], F32), "tabs")
        for (k, shape, dt) in A_OUT:
            nmA[k] = (pg.out("o_" + k, shape, dt), "o_" + k)
        nmA["xT"] = nmA["x1T"]
        if not has_B:
            nmA["x0"] = (pg.inp("x_in", [TL, D], F32), "x_in")
            nmA["x0T"] = (pg.scr("x0T", [D, TL], BF16), "x0T")
        else:
            nmA["x0"] = nmB["x4"]
            nmA["x0T"] = nmB["x4T"]
    if has_B:
        emit_indexer(pg, nmB)
        emit_attention(pg, nmB, "mla")
        emit_attention(pg, nmB, "dsa")
        emit_gla_out(pg, nmB)
    kb.push()
    wk = Work(pg)
    if has_A:
        mp = MixParams(pg, "A")
        mp.load(pg, nmA["sm"][0], nmA["sm"][1], nmA["wg2"][0], nmA["wg2"][1], nmA["bg"][0], nmA["bg"][1])
        emit_rope_tables(pg, wk, nmA["pos"][0], nmA["pos"][1], nmA["tabs"][0], nmA["tabs"][1])
        if not has_B:
            emit_prep_xT(pg, wk, nmA["x0"], nmA["x0T"])
    if has_B:
        xa = XAState(pg, wk, nmB)
        for b in range(NB):
            emit_merge_block(pg, wk, b, nmB)
        for b in range(NB):
            emit_xattn_block(pg, wk, xa, b, nmB)
        for b in range(NB):
            emit_ffn_block(pg, wk, b, nmB["x3"], nmB["x3T"], nmB["ffn2_g"], nmB["ffn2_u"], nmB["ffn2_d"],
                           (nmB["ln_g3"], nmB["ln_b3"]), nmB["x4"], nmB["x4T"])
    if has_A:
        for b in range(NB):
            emit_ffn_block(pg, wk, b, nmA["x0"], nmA["x0T"], nmA["ffn1_g"], nmA["ffn1_u"], nmA["ffn1_d"],
                           (nmA["ln_g0"], nmA["ln_b0"]), nmA["x1"], nmA["x1T"])
        for b in range(NB):
            emit_mixproj_block(pg, wk, mp, b, {"n": nmA})
    kb.pop()
    kb.finish()
    return pg


def _loc(a, c):
    return np.ascontiguousarray(a.reshape(NT, NCORES, 128, *a.shape[1:])[:, c].reshape(TL, *a.shape[1:]))


def _glob_cols(parts):
    lead = parts[0].shape[:-1]
    st = np.stack([p.reshape(*lead, NT, 128) for p in parts], axis=-2)
    return np.ascontiguousarray(st.reshape(*lead, SEQ))


def _glob_rows(parts):
    f = parts[0].shape[1:]
    st = np.stack([p.reshape(NT, 128, *f) for p in parts], axis=1)
    return np.ascontiguousarray(st.reshape(SEQ, *f))


def _vlay(v):
    return np.ascontiguousarray(v.reshape(SEQ // 128, 128, 8, 128).transpose(2, 1, 0, 3).reshape(8, 128, SEQ))


def _a_weights(inp, l):
    sm = np.zeros((128, 8), np.float32)
    sm[:, 0:4] = inp["mla_q_norm"][l].reshape(4, 128).T
    sm[:, 4:8] = inp["mla_kv_norm"][l].reshape(4, 128).T
    return {"A_ffn1_g": inp["ffn_w_gate"][l, 0], "A_ffn1_u": inp["ffn_w_up"][l, 0], "A_ffn1_d": inp["ffn_w_down"][l, 0],
            "A_ln_g0": inp["ln_gain"][l, 0], "A_ln_b0": inp["ln_bias"][l, 0], "A_w_in": inp["w_in"][l],
            "A_w_uq": inp["mla_w_uq"][l], "A_w_ukv": inp["mla_w_ukv"][l], "A_sm": sm,
            "A_wg2": inp["gla_w_gate2"][l], "A_bg": inp["gla_b_gate"][l][None, :]}


def _b_weights(inp, l):
    return {"B_w_br_mla": inp["w_branch_mla"][l], "B_w_br_dsa": inp["w_branch_dsa"][l], "B_w_br_gla": inp["w_branch_gla"][l],
            "B_w_merge": inp["w_merge"][l], "B_b_merge": inp["b_merge"][l][None, :], "B_w_out": inp["w_out"][l],
            "B_xa_w_q": inp["xa_w_q"][l], "B_xa_w_kv": inp["xa_w_kv"][l], "B_xa_w_o": inp["xa_w_o"][l], "B_mem": inp["mem"][0],
            "B_ln_g1": inp["ln_gain"][l, 1], "B_ln_b1": inp["ln_bias"][l, 1], "B_ln_g2": inp["ln_gain"][l, 2], "B_ln_b2": inp["ln_bias"][l, 2],
            "B_ffn2_g": inp["ffn_w_gate"][l, 1], "B_ffn2_u": inp["ffn_w_up"][l, 1], "B_ffn2_d": inp["ffn_w_down"][l, 1],
            "B_ln_g3": inp["ln_gain"][l, 3], "B_ln_b3": inp["ln_bias"][l, 3], "B_gla_norm": inp["gla_norm"][l]}


def _gather(res):
    loc = [{"i_" + k: r["o_" + k] for k in A_LOCAL} for r in res]
    g = {"KnT_g": _glob_cols([r["o_KnT_o"] for r in res]), "KrT_g": _glob_cols([r["o_KrT_o"] for r in res]),
         "KdT_g": _glob_cols([r["o_KdT_o"] for r in res]), "IkT_g": _glob_cols([r["o_IkT_o"] for r in res]),
         "Vm_g": _vlay(_glob_rows([r["o_Vm_o"] for r in res])), "Vd_g": _vlay(_glob_rows([r["o_Vd_o"] for r in res]))}
    ut = np.stack([r["o_gUt_o"] for r in res], axis=1)
    g["gUt_g"] = np.ascontiguousarray(ut.reshape(SEQ // 128, 128, 1024))
    at = np.stack([r["o_gAt_o"] for r in res], axis=1)
    g["gAt_g"] = np.ascontiguousarray(at.reshape(SEQ // 128, 128, 4).transpose(1, 0, 2).reshape(128, 512))
    return loc, g


_PROGS = {}


def _prog(has_B, has_A):
    key = (has_B, has_A)
    if key not in _PROGS:
        _PROGS[key] = build_program(has_B, has_A)
    return _PROGS[key]


def kernel(**inp):
    inp = {k: np.asarray(v) for k, v in inp.items()}
    cores = list(range(NCORES))
    consts = [make_consts(c) for c in cores]
    cmask = [make_cmask(c) for c in cores]
    pos = [_loc(inp["positions"][0].astype(np.int32), c) for c in cores]
    pg = _prog(False, True)
    wa = _a_weights(inp, 0)
    maps = [dict(wa, consts=consts[c], pos=pos[c], x_in=_loc(inp["x"][0], c)) for c in cores]
    res = run_bass_kernel_spmd(pg.nc, maps, core_ids=cores).results
    loc, g = _gather(res)
    pg = _prog(True, True)
    wb, wa = _b_weights(inp, 0), _a_weights(inp, 1)
    maps = [dict(wb, **wa, **g, **loc[c], consts=consts[c], cmask=cmask[c], pos=pos[c]) for c in cores]
    res = run_bass_kernel_spmd(pg.nc, maps, core_ids=cores).results
    loc, g = _gather(res)
    pg = _prog(True, False)
    wb = _b_weights(inp, 1)
    maps = [dict(wb, **g, **loc[c], consts=consts[c], cmask=cmask[c]) for c in cores]
    res = run_bass_kernel_spmd(pg.nc, maps, core_ids=cores).results
    y = _glob_rows([r["y"] for r in res])
    return y[None].astype(np.float32)
```

```python
import contextlib
import math
import numpy as np
import ml_dtypes
import concourse.bass as bass
import concourse.mybir as mybir
from concourse.bass_utils import run_bass_kernel_spmd

F32 = mybir.dt.float32
BF16 = mybir.dt.bfloat16
I32 = mybir.dt.int32
AF = mybir.ActivationFunctionType
ALU = mybir.AluOpType
NPBF = ml_dtypes.bfloat16

NCORES = 8
SEQ = 16384
D = 2048
DFF = 5632
TL = SEQ // NCORES
NT = TL // 128
BLK = 512
NB = TL // BLK
ALPHA = 4.0 ** 0.25
EPS = 1e-5
NEG = -30000.0
IN_W = 8352

ENG_NAMES = ["pe", "act", "dve", "pool", "sp"]


class Reg:
    __slots__ = ("w", "r")

    def __init__(self):
        self.w = None
        self.r = {}


class Buf:
    _uid = [0]

    def __init__(self, kb, name, shape, dt, psum=False):
        Buf._uid[0] += 1
        name = f"{name}_{Buf._uid[0]}"
        if psum:
            self.t = kb.scopes[-1].enter_context(kb.nc.psum_tensor(name, shape, dt))
            self.psum = True
        else:
            self.psum = False
            self.t = kb.scopes[-1].enter_context(kb.nc.sbuf_tensor(name, shape, dt))
        self.r = Reg()

    def __getitem__(self, idx):
        return self.t[idx]


class View:
    def __init__(self, ap):
        self.ap = ap
        self.r = Reg()

    def __getitem__(self, idx):
        return self.ap[idx]


class KB:
    def __init__(self, nc):
        self.nc = nc
        self.es = contextlib.ExitStack()
        self.scopes = [self.es]
        engs = [nc.tensor, nc.scalar, nc.vector, nc.gpsimd, nc.sync]
        self.eng = dict(zip(ENG_NAMES, engs))
        self.idx = {n: i for i, n in enumerate(ENG_NAMES)}
        self.sems = [self.es.enter_context(nc.semaphore("s_" + n)) for n in ENG_NAMES]
        self.cnt = [0] * len(ENG_NAMES)
        self.waited = [dict() for _ in ENG_NAMES]
        self.dpool = {}
        for q, n in (("sp", 24), ("pool", 16), ("act", 8)):
            lst = []
            for i in range(n):
                self.sems.append(self.es.enter_context(nc.semaphore(f"d_{q}{i}")))
                self.cnt.append(0)
                lst.append(len(self.sems) - 1)
            self.dpool[q] = [lst, 0]
        self.ninstr = 0

    def _wait(self, ei, deps):
        w = self.waited[ei]
        for (si, v) in deps.items():
            if si == ei and ei == 0:
                continue
            if w.get(si, 0) < v:
                self.eng[ENG_NAMES[ei]].wait_ge(self.sems[si], v)
                w[si] = v

    @staticmethod
    def _deps(reads, writes):
        deps = {}

        def add(t):
            if t is not None and deps.get(t[0], 0) < t[1]:
                deps[t[0]] = t[1]
        for r in reads:
            add(r.w)
        for r in writes:
            add(r.w)
            for si, v in r.r.items():
                add((si, v))
        return deps

    @staticmethod
    def _mark(t, reads, writes):
        for r in reads:
            if r.r.get(t[0], 0) < t[1]:
                r.r[t[0]] = t[1]
        for r in writes:
            r.w = t
            r.r = {}

    def op(self, eng, fn, reads=(), writes=()):
        if self.ninstr >= getattr(self, "maxops", 1 << 60):
            return None
        ei = self.idx[eng]
        writes = list(writes) + [x for x in reads if isinstance(x, Buf) and x.psum]
        reads = [x for x in reads if not (isinstance(x, Buf) and x.psum)]
        reads = [x if isinstance(x, Reg) else x.r for x in reads]
        writes = [x if isinstance(x, Reg) else x.r for x in writes]
        self._wait(ei, self._deps(reads, writes))
        ins = fn(self.eng[eng])
        self.cnt[ei] += 1
        ins.then_inc(self.sems[ei], 1)
        self._mark((ei, self.cnt[ei]), reads, writes)
        self.ninstr += 1
        return ins

    def dma(self, q, out, in_, reads=(), writes=()):
        if self.ninstr >= getattr(self, "maxops", 1 << 60):
            return None
        ei = self.idx[q]
        reads = [x if isinstance(x, Reg) else x.r for x in reads]
        writes = [x if isinstance(x, Reg) else x.r for x in writes]
        lst, nxt = self.dpool[q]
        si = lst[nxt]
        self.dpool[q][1] = (nxt + 1) % len(lst)
        deps = self._deps(reads, writes)
        if self.cnt[si] > 0 and deps.get(si, 0) < self.cnt[si]:
            deps[si] = self.cnt[si]
        self._wait(ei, deps)
        self.cnt[si] += 16
        self.eng[q].dma_start(out=out, in_=in_).then_inc(self.sems[si], 16)
        self._mark((si, self.cnt[si]), reads, writes)
        self.ninstr += 1

    def push(self):
        self.barrier()
        self.scopes.append(contextlib.ExitStack())

    def pop(self):
        self.barrier()
        self.scopes.pop().close()

    def barrier(self):
        for ei in range(len(ENG_NAMES)):
            deps = {si: v for si, v in enumerate(self.cnt) if v > 0 and si != ei}
            self._wait(ei, deps)

    def finish(self):
        deps = {si: v for si, v in enumerate(self.cnt) if v > 0 and si != self.idx["sp"]}
        self._wait(self.idx["sp"], deps)
        self.es.close()


C_ID, C_ONES, C_TRI, C_RTRI, C_MASKF, C_P128, C_P64, C_CH, C_F64, C_F128, C_EI, C_EPS, C_END = (
    0, 128, 256, 384, 512, 640, 768, 896, 898, 899, 900, 908, 909)


def make_consts(core):
    c = np.zeros((128, C_END), np.float32)
    c[:, C_ID:C_ID + 128] = np.eye(128)
    c[:, C_ONES:C_ONES + 128] = 1.0
    s = np.arange(128)[:, None]
    t = np.arange(128)[None, :]
    same = (s // 64) == (t // 64)
    c[:, C_TRI:C_TRI + 128] = (same & (s <= t))
    c[:, C_RTRI:C_RTRI + 128] = (same & (s > t))
    c[:, C_MASKF:C_MASKF + 128] = (same & (s <= t))
    for half, col in ((64, C_P128), (32, C_P64)):
        p = np.zeros((128, 128), np.float32)
        for m in range(128):
            if (m % (2 * half)) < half:
                p[m + half, m] = -1.0
            else:
                p[m - half, m] = 1.0
        c[:, col:col + 128] = p
    c[:, C_CH] = (np.arange(128) < 64)
    c[:, C_CH + 1] = (np.arange(128) >= 64)
    f64 = (1.0 / (np.float32(10000.0) ** (np.arange(0, 64, 2, dtype=np.float32) / np.float32(64)))).astype(np.float32)
    f128 = (1.0 / (np.float32(10000.0) ** (np.arange(0, 128, 2, dtype=np.float32) / np.float32(128)))).astype(np.float32)
    c[:, C_F64] = f64[np.arange(128) % 32]
    c[:, C_F128] = f128[np.arange(128) % 64]
    c[:, C_EI + core] = 1.0
    c[:, C_EPS] = EPS
    return c


def make_cmask(core):
    m = np.zeros((128, 1024), np.float32)
    q = np.arange(128)[:, None]
    for i in range(8):
        blk = m[:, i * 128:(i + 1) * 128]
        if i > core:
            blk[:] = NEG
        elif i == core:
            blk[np.arange(128)[None, :] > q] = NEG
    return m


class Prog:
    def __init__(self):
        self.nc = bass.Bass("TRN2", target_bir_lowering=False)
        self.kb = KB(self.nc)
        self.in_names = []
        self.out_names = []
        self.dreg = {}
        kb = self.kb
        self.consts_d = self.inp("consts", [128, C_END], F32)
        self.consts = Buf(kb, "consts_sb", [128, C_END], F32)
        kb.dma("sp", self.consts[:, :], self.consts_d[:, :], writes=[self.consts])
        self.identb = Buf(kb, "identb", [128, 128], BF16)
        kb.op("dve", lambda e: e.tensor_copy(out=self.identb[:, :], in_=self.consts[:, C_ID:C_ID + 128]),
              reads=[self.consts], writes=[self.identb])
        self.onesb = Buf(kb, "onesb", [1, 128], BF16)
        kb.op("dve", lambda e: e.tensor_copy(out=self.onesb[:, :], in_=self.consts[0:1, C_ONES:C_ONES + 128]),
              reads=[self.consts], writes=[self.onesb])
        self.pb = [Buf(kb, f"pb{i}", [128, 512], F32, psum=True) for i in range(6)]
        self.pt = [Buf(kb, f"pt{i}", [128, 1024], BF16, psum=True) for i in range(2)]
        self.ws = None
        self.wsn = 0
        self.pbn = 0
        self.uid = 0

    def _dt(self, name, shape, dt, kind):
        t = self.nc.dram_tensor(name, list(shape), dt, kind=kind)
        ap = t.ap()
        self.dreg[name] = Reg()
        return ap

    def inp(self, name, shape, dt):
        self.in_names.append(name)
        return self._dt(name, shape, dt, "ExternalInput")

    def out(self, name, shape, dt):
        self.out_names.append(name)
        return self._dt(name, shape, dt, "ExternalOutput")

    def scr(self, name, shape, dt):
        return self._dt(name, shape, dt, "Internal")

    def R(self, name):
        return self.dreg[name]

    def cc(self, col, n=128, rows=128):
        return self.consts[0:rows, col:col + n]

    def load_w(self, w_ap, wname, k0, kc, c0, n):
        slot = self.ws[self.wsn]
        self.wsn = (self.wsn + 1) % len(self.ws)
        assert kc * n <= 8192
        view = slot[:, 0:kc * n].rearrange("p (k n) -> p k n", n=n)
        src = w_ap[k0:k0 + kc * 128, c0:c0 + n].rearrange("(k p) n -> p k n", p=128)
        self.kb.dma("pool", view, src, reads=[self.R(wname)], writes=[slot])
        return slot, view

    def bank(self):
        b = self.pb[self.pbn]
        self.pbn = (self.pbn + 1) % getattr(self, "nbank", 4)
        return b


def mm(pg, out_ap, lhsT, rhs, start, stop, reads, wbank):
    pg.kb.op("pe", lambda e: e.matmul(out_ap, lhsT=lhsT, rhs=rhs, start=start, stop=stop),
             reads=reads, writes=[wbank])


def emit_transpose_out(pg, src_buf, src_ap, ncol_chunks, xb, xts, dst_ap, dst_name, q="sp"):
    kb = pg.kb
    kb.op("act", lambda e: e.activation(out=xb[:, 0:ncol_chunks * 128], in_=src_ap, func=AF.Copy),
          reads=[src_buf], writes=[xb])
    for half in range((ncol_chunks + 7) // 8):
        n = min(8, ncol_chunks - half * 8)
        ptb = pg.pt[half % 2]
        for k in range(n):
            kc = half * 8 + k
            kb.op("pe", lambda e, kc=kc, k=k, ptb=ptb: e.transpose(out=ptb[:, k * 128:(k + 1) * 128],
                                                                    in_=xb[:, kc * 128:(kc + 1) * 128],
                                                                    identity=pg.identb[:, :]),
                  reads=[xb, pg.identb], writes=[ptb])
        kb.op("dve", lambda e, ptb=ptb, half=half, n=n: e.tensor_copy(
            out=xts[:, half * 1024:half * 1024 + n * 128], in_=ptb[:, 0:n * 128]),
            reads=[ptb], writes=[xts])
    kb.dma(q, dst_ap.rearrange("(k p) t -> p k t", p=128),
           xts[:, 0:ncol_chunks * 128].rearrange("p (k t) -> p k t", t=128),
           reads=[xts], writes=[pg.R(dst_name)])


def emit_ln(pg, vbuf, vap, gbc, bbc, st, x_out_ap, x_out_name, xT_out_ap, xT_out_name, xb, xts):
    kb = pg.kb
    stats, mv, rstd = st
    for c in range(4):
        kb.op("dve", lambda e, c=c: e.bn_stats(out=stats[:, c * 6:(c + 1) * 6], in_=vap[:, c * 512:(c + 1) * 512]),
              reads=[vbuf], writes=[stats])
    kb.op("dve", lambda e: e.bn_aggr(out=mv[:, 0:2], in_=stats[:, 0:24]), reads=[stats], writes=[mv])
    kb.op("act", lambda e: e.activation(out=rstd[:, 0:1], in_=mv[:, 1:2], func=AF.Ln, bias=pg.cc(C_EPS, 1), scale=1.0),
          reads=[mv, pg.consts], writes=[rstd])
    kb.op("act", lambda e: e.activation(out=rstd[:, 1:2], in_=rstd[:, 0:1], func=AF.Exp, scale=-0.5),
          reads=[rstd], writes=[rstd])
    kb.op("dve", lambda e: e.tensor_scalar(out=vap, in0=vap, scalar1=mv[:, 0:1], scalar2=rstd[:, 1:2],
                                           op0=ALU.subtract, op1=ALU.mult),
          reads=[vbuf, mv, rstd], writes=[vbuf])
    kb.op("dve", lambda e: e.tensor_tensor(out=vap, in0=vap, in1=gbc[:, :], op=ALU.mult), reads=[vbuf, gbc], writes=[vbuf])
    kb.op("dve", lambda e: e.tensor_tensor(out=vap, in0=vap, in1=bbc[:, :], op=ALU.add), reads=[vbuf, bbc], writes=[vbuf])
    kb.dma("sp", x_out_ap, vap, reads=[vbuf], writes=[pg.R(x_out_name)])
    emit_transpose_out(pg, vbuf, vap, 16, xb, xts, xT_out_ap, xT_out_name)


class Work:
    def __init__(self, pg):
        kb = pg.kb
        pg.uid += 1
        u = str(pg.uid)
        pg.ws = [Buf(kb, f"ws{i}_" + u, [128, 8192], BF16) for i in range(3)]
        self.v = [Buf(kb, f"v{i}", [128, D], F32) for i in range(4)]
        self.xT = Buf(kb, "xTblk", [128, 16 * BLK], BF16)
        self.hT = Buf(kb, "hT", [128, 22 * BLK], BF16)
        self.gbc = Buf(kb, "gbc", [128, D], F32)
        self.bbc = Buf(kb, "bbc", [128, D], F32)
        self.sg = [Buf(kb, f"sg{i}", [128, BLK], F32) for i in range(2)]
        self.xb = Buf(kb, "xb", [128, D], BF16)
        self.xts = Buf(kb, "xts", [128, D], BF16)
        self.stats = Buf(kb, "stats", [128, 24], F32)
        self.mv = Buf(kb, "mv", [128, 2], F32)
        self.rstd = Buf(kb, "rstd", [128, 2], F32)
        self.st = (self.stats, self.mv, self.rstd)


def load_xT_block(pg, wk, xT_ap, xT_name, b):
    view = wk.xT[:, :].rearrange("p (k t) -> p k t", t=BLK)
    pg.kb.dma("sp", view, xT_ap[:, b * BLK:(b + 1) * BLK].rearrange("(k p) t -> p k t", p=128),
              reads=[pg.R(xT_name)], writes=[wk.xT])
    return view


def load_ln_params(pg, wk, g_ap, b_ap, gname, bname):
    pg.kb.dma("sp", wk.gbc[:, :], g_ap.partition_broadcast(128), reads=[pg.R(gname)], writes=[wk.gbc])
    pg.kb.dma("sp", wk.bbc[:, :], b_ap.partition_broadcast(128), reads=[pg.R(bname)], writes=[wk.bbc])


def emit_ffn_block(pg, wk, b, x_in, xT_in, wg, wu, wd, ln, x_out, xT_out, first=False):
    kb = pg.kb
    if b == 0 or first:
        kb.barrier()
        load_ln_params(pg, wk, ln[0][0], ln[1][0], ln[0][1], ln[1][1])
    xT = load_xT_block(pg, wk, xT_in[0], xT_in[1], b)
    for t in range(4):
        kb.dma("sp", wk.v[t][:, :], x_in[0][b * BLK + t * 128: b * BLK + (t + 1) * 128, :],
               reads=[pg.R(x_in[1])], writes=[wk.v[t]])
    hT = wk.hT[:, :].rearrange("p (f t) -> p f t", t=BLK)
    for half in range(2):
        for grp in range(6):
            nf = 4 if grp < 5 else 2
            f0 = half * 22 + grp * 4
            sg_, wgv = pg.load_w(wg[0], wg[1], 0, 16, f0 * 128, nf * 128)
            su_, wuv = pg.load_w(wu[0], wu[1], 0, 16, f0 * 128, nf * 128)
            for fi in range(nf):
                pgate = pg.bank()
                pup = pg.bank()
                for kc in range(16):
                    mm(pg, pgate[:, :], wgv[:, kc, fi * 128:(fi + 1) * 128], xT[:, kc, :], kc == 0, kc == 15,
                       [sg_, wk.xT], pgate)
                for kc in range(16):
                    mm(pg, pup[:, :], wuv[:, kc, fi * 128:(fi + 1) * 128], xT[:, kc, :], kc == 0, kc == 15,
                       [su_, wk.xT], pup)
                sgb = wk.sg[(grp * 4 + fi) % 2]
                kb.op("act", lambda e, pgate=pgate, sgb=sgb: e.activation(out=sgb[:, :], in_=pgate[:, :], func=AF.Silu),
                      reads=[pgate], writes=[sgb])
                fl = grp * 4 + fi
                kb.op("dve", lambda e, pup=pup, sgb=sgb, fl=fl: e.scalar_tensor_tensor(
                    out=hT[:, fl, :], in0=pup[:, :], scalar=0.5, in1=sgb[:, :], op0=ALU.mult, op1=ALU.mult),
                    reads=[pup, sgb], writes=[wk.hT])
        for n in range(4):
            s0, w0 = pg.load_w(wd[0], wd[1], (half * 22) * 128, 11, n * 512, 512)
            s1, w1 = pg.load_w(wd[0], wd[1], (half * 22 + 11) * 128, 11, n * 512, 512)
            for t in range(4):
                pd = pg.bank()
                for q_, (s_, w_) in enumerate(((s0, w0), (s1, w1))):
                    for fc in range(11):
                        f = q_ * 11 + fc
                        mm(pg, pd[:, :], hT[:, f, t * 128:(t + 1) * 128], w_[:, fc, :], f == 0, f == 21, [wk.hT, s_], pd)
                vslice = wk.v[t][:, n * 512:(n + 1) * 512]
                kb.op("dve", lambda e, pd=pd, vslice=vslice, half=half: e.scalar_tensor_tensor(
                    out=vslice, in0=vslice, scalar=(ALPHA if half == 0 else 1.0), in1=pd[:, :], op0=ALU.mult, op1=ALU.add),
                    reads=[pd, wk.v[t]], writes=[wk.v[t]])
    for t in range(4):
        r0 = b * BLK + t * 128
        emit_ln(pg, wk.v[t], wk.v[t][:, :], wk.gbc, wk.bbc, wk.st,
                x_out[0][r0:r0 + 128, :], x_out[1], xT_out[0][:, r0:r0 + 128], xT_out[1], wk.xb, wk.xts)


def emit_prep_xT(pg, wk, x_in, xT_out):
    for j in range(NT):
        vb = wk.v[j % 4]
        pg.kb.dma("sp", vb[:, :], x_in[0][j * 128:(j + 1) * 128, :], reads=[pg.R(x_in[1])], writes=[vb])
        emit_transpose_out(pg, vb, vb[:, :], 16, wk.xb, wk.xts, xT_out[0][:, j * 128:(j + 1) * 128], xT_out[1])


TWO_PI = 2.0 * math.pi
CW1 = 6.28125
CW2 = TWO_PI - 6.28125


def emit_rope_tables(pg, wk, pos_ap, pos_name, tabs, tabs_name):
    kb = pg.kb
    posf, ang, kf, tmp = wk.v[0], wk.v[1], wk.v[2], wk.v[3]
    ki = View(wk.xT[:, 0:2 * TL].bitcast(I32))
    kb.dma("pool", posf[:, :], pos_ap.partition_broadcast(128), reads=[pg.R(pos_name)], writes=[posf])
    PI = math.pi

    def fold(buf):
        kb.op("dve", lambda e: e.tensor_scalar(out=tmp[:, :], in0=buf[:, :], scalar1=PI, scalar2=-TWO_PI,
                                               op0=ALU.is_gt, op1=ALU.mult), reads=[buf], writes=[tmp])
        kb.op("dve", lambda e: e.tensor_tensor(out=buf[:, :], in0=buf[:, :], in1=tmp[:, :], op=ALU.add),
              reads=[buf, tmp], writes=[buf])
        kb.op("dve", lambda e: e.tensor_scalar(out=tmp[:, :], in0=buf[:, :], scalar1=-PI, scalar2=TWO_PI,
                                               op0=ALU.is_lt, op1=ALU.mult), reads=[buf], writes=[tmp])
        kb.op("dve", lambda e: e.tensor_tensor(out=buf[:, :], in0=buf[:, :], in1=tmp[:, :], op=ALU.add),
              reads=[buf, tmp], writes=[buf])
        kb.op("dve", lambda e: e.tensor_scalar(out=buf[:, :], in0=buf[:, :], scalar1=-PI, scalar2=PI,
                                               op0=ALU.max, op1=ALU.min), reads=[buf], writes=[buf])

    for ti, fcol in ((0, C_F64), (1, C_F128)):
        kb.op("dve", lambda e: e.tensor_scalar(out=ang[:, :], in0=posf[:, :], scalar1=pg.cc(fcol, 1), scalar2=None,
                                               op0=ALU.mult), reads=[posf, pg.consts], writes=[ang])
        kb.op("dve", lambda e: e.tensor_scalar(out=ki[:, :], in0=ang[:, :], scalar1=1.0 / TWO_PI, scalar2=None,
                                               op0=ALU.mult), reads=[ang], writes=[ki, wk.xT])
        kb.op("dve", lambda e: e.tensor_copy(out=kf[:, :], in_=ki[:, :]), reads=[ki, wk.xT], writes=[kf])
        kb.op("dve", lambda e: e.scalar_tensor_tensor(out=ang[:, :], in0=kf[:, :], scalar=-CW1, in1=ang[:, :],
                                                      op0=ALU.mult, op1=ALU.add), reads=[kf, ang], writes=[ang])
        kb.op("dve", lambda e: e.scalar_tensor_tensor(out=ang[:, :], in0=kf[:, :], scalar=-CW2, in1=ang[:, :],
                                                      op0=ALU.mult, op1=ALU.add), reads=[kf, ang], writes=[ang])
        fold(ang)
        kb.op("act", lambda e: e.activation(out=kf[:, :], in_=ang[:, :], func=AF.Sin), reads=[ang], writes=[kf])
        kb.dma("sp", tabs[2 * ti + 1], kf[:, :], reads=[kf], writes=[pg.R(tabs_name)])
        kb.op("dve", lambda e: e.tensor_scalar(out=ang[:, :], in0=ang[:, :], scalar1=PI / 2, scalar2=None, op0=ALU.add),
              reads=[ang], writes=[ang])
        fold(ang)
        kb.op("act", lambda e: e.activation(out=kf[:, :], in_=ang[:, :], func=AF.Sin), reads=[ang], writes=[kf])
        kb.dma("sp", tabs[2 * ti], kf[:, :], reads=[kf], writes=[pg.R(tabs_name)])
    kb.barrier()


O_CQ, O_CKV, O_KR, O_DQ, O_DK, O_DV, O_IQ, O_IK, O_IW, O_GQ, O_GK, O_GV, O_GLR, O_GR = (
    0, 512, 1024, 1088, 2112, 3136, 4160, 5184, 5248, 5264, 5776, 6288, 7312, 7328)


class MixParams:
    def __init__(self, pg, tag):
        kb = pg.kb
        self.sm = Buf(kb, "sm" + tag, [128, 8], F32)
        self.wg2 = Buf(kb, "wg2" + tag, [16, 512], F32)
        self.bg = Buf(kb, "bg" + tag, [1, 512], F32)

    def load(self, pg, sm_ap, sm_name, wg2_ap, wg2_name, bg_ap, bg_name):
        kb = pg.kb
        kb.dma("sp", self.sm[:, :], sm_ap[:, :], reads=[pg.R(sm_name)], writes=[self.sm])
        kb.dma("sp", self.wg2[:, :], wg2_ap[:, :], reads=[pg.R(wg2_name)], writes=[self.wg2])
        kb.dma("sp", self.bg[:, :], bg_ap[:, :], reads=[pg.R(bg_name)], writes=[self.bg])


def fm_proj(pg, wview, wslot, c0, M, xT, xTbuf, K=16):
    ps = pg.bank()
    for kc in range(K):
        mm(pg, ps[0:M, :], wview[:, kc, c0:c0 + M], xT[:, kc, :], kc == 0, kc == K - 1, [wslot, xTbuf], ps)
    return ps


def tm_proj(pg, wview_cols, wslot, N, xT, xTbuf, t, K=16, outv=None):
    ps = pg.bank()
    o = ps[:, 0:N] if outv is None else outv(ps)
    for kc in range(K):
        mm(pg, o, xT[:, kc, t * 128:(t + 1) * 128], wview_cols(kc), kc == 0, kc == K - 1, [wslot, xTbuf], ps)
    return ps


def emit_mixproj_block(pg, wk, mp, b, L):
    kb = pg.kb
    kb.barrier()
    nm = L["n"]
    tok = slice(b * BLK, (b + 1) * BLK)
    xT = load_xT_block(pg, wk, nm["xT"][0], nm["xT"][1], b)
    xTb = wk.xT
    tab = View(wk.v[0][:, :].rearrange("p (a t) -> p a t", t=BLK))
    kb.dma("sp", tab[:, :, :], nm["tabs"][0][:, :, tok].rearrange("a p t -> p a t"), reads=[pg.R(nm["tabs"][1])], writes=[tab])
    C64, S64, C128, S128 = (tab[:, i, :] for i in range(4))
    cT = View(wk.v[1][:, :].rearrange("p (a t) -> p a t", t=BLK))
    sq = View(wk.v[2][:, :].rearrange("p (a t) -> p a t", t=BLK))
    rstd = View(wk.v[3][:, 0:512])
    xs = View(wk.v[3][:, 512:1024])
    t1 = View(wk.v[3][:, 1024:1536])
    cnT = View(wk.hT[:, 0:2048].rearrange("p (a t) -> p a t", t=BLK))
    stg = [View(wk.hT[:, 2048 + i * 512: 2048 + (i + 1) * 512]) for i in range(4)]
    stgn = [0]

    def stage():
        s = stg[stgn[0] % 4]
        stgn[0] += 1
        return s

    def store_fm(ps, M, dst_ap, dst_name):
        s = stage()
        kb.op("act", lambda e: e.activation(out=s[0:M, :], in_=ps[0:M, :], func=AF.Copy), reads=[ps], writes=[s])
        kb.dma("sp", dst_ap, s[0:M, :], reads=[s], writes=[pg.R(dst_name)])

    def rope_store(ps, M, Cb, Sb, pcol, dst_ap, dst_name):
        kb.op("act", lambda e: e.activation(out=xs[0:M, :], in_=ps[0:M, :], func=AF.Copy), reads=[ps], writes=[xs])
        ps2 = pg.bank()
        mm(pg, ps2[0:M, :], pg.consts[0:M, pcol:pcol + M], xs[0:M, :], True, True, [pg.consts, xs], ps2)
        kb.op("dve", lambda e: e.tensor_tensor(out=t1[0:M, :], in0=xs[0:M, :], in1=Cb[0:M, :], op=ALU.mult),
              reads=[xs, tab], writes=[t1])
        kb.op("dve", lambda e: e.tensor_tensor(out=xs[0:M, :], in0=ps2[0:M, :], in1=Sb[0:M, :], op=ALU.mult),
              reads=[ps2, tab, xs], writes=[xs])
        s = stage()
        kb.op("dve", lambda e: e.tensor_tensor(out=s[0:M, :], in0=t1[0:M, :], in1=xs[0:M, :], op=ALU.add),
              reads=[t1, xs], writes=[s])
        kb.dma("sp", dst_ap, s[0:M, :], reads=[s], writes=[pg.R(dst_name)])

    def rms_fm(c0, gcol0):
        slot, wv = pg.load_w(nm["w_in"][0], nm["w_in"][1], 0, 16, c0, 512)
        for c in range(4):
            ps = fm_proj(pg, wv, slot, c * 128, 128, xT, xTb)
            kb.op("act", lambda e, ps=ps, c=c: e.activation(out=sq[:, c, :], in_=ps[:, :], func=(AF.Copy if getattr(pg, "nosq", False) else AF.Square)), reads=[ps], writes=[sq])
            kb.op("dve", lambda e, ps=ps, c=c: e.tensor_copy(out=cT[:, c, :], in_=ps[:, :]), reads=[ps], writes=[cT])
        if getattr(pg, 'dbg', 99) < 2.1:
            return
        ss = pg.bank()
        for c in range(4):
            mm(pg, ss[:, :], pg.cc(C_ONES), sq[:, c, :], c == 0, c == 3, [pg.consts, sq], ss)
        if getattr(pg, 'dbg', 99) < 2.2:
            return
        kb.op("act", lambda e: e.activation(out=rstd[:, :], in_=ss[:, :], func=AF.Ln, bias=pg.cc(C_EPS, 1), scale=1.0 / 512),
              reads=[ss, pg.consts], writes=[rstd])
        kb.op("act", lambda e: e.activation(out=rstd[:, :], in_=rstd[:, :], func=AF.Exp, scale=-0.5), reads=[rstd], writes=[rstd])
        for c in range(4):
            kb.op("dve", lambda e, c=c: e.scalar_tensor_tensor(out=cnT[:, c, :], in0=cT[:, c, :], scalar=mp.sm[:, gcol0 + c:gcol0 + c + 1],
                                                               in1=rstd[:, :], op0=ALU.mult, op1=ALU.mult),
                  reads=[cT, mp.sm, rstd], writes=[cnT])

    if getattr(pg, 'dbg', 99) < 2:
        return
    rms_fm(O_CQ, 0)
    if getattr(pg, 'dbg', 99) < 2.3:
        return
    slot, wq = pg.load_w(nm["w_uq"][0], nm["w_uq"][1], 0, 4, 0, 1536)
    for h in range(8):
        ps = fm_proj(pg, wq, slot, h * 192, 128, cnT, cnT, K=4)
        store_fm(ps, 128, nm["QnT"][0][h, :, tok], nm["QnT"][1])
        if getattr(pg, 'dbg', 99) < 2.6:
            continue
        ps = fm_proj(pg, wq, slot, h * 192 + 128, 64, cnT, cnT, K=4)
        rope_store(ps, 64, C64, S64, C_P64, nm["QrT"][0][h, :, tok], nm["QrT"][1])
    if getattr(pg, 'dbg', 99) < 3:
        return
    rms_fm(O_CKV, 4)
    slot, wkv = pg.load_w(nm["w_ukv"][0], nm["w_ukv"][1], 0, 4, 0, 2048)
    for h in range(8):
        ps = fm_proj(pg, wkv, slot, h * 256, 128, cnT, cnT, K=4)
        store_fm(ps, 128, nm["KnT_o"][0][h, :, tok], nm["KnT_o"][1])
    for t in range(4):
        for i in range(2):
            wsel = lambda kc, i=i: wkv[:, kc, :].rearrange("p (h two d) -> p h two d", two=2, d=128)[:, 4 * i:4 * i + 4, 1, :]
            ps = tm_proj(pg, wsel, slot, 512, cnT, cnT, t, K=4, outv=lambda p: p[:, :].rearrange("p (h d) -> p h d", d=128))
            s = stage()
            kb.op("act", lambda e, ps=ps, s=s: e.activation(out=s[:, :], in_=ps[:, :], func=AF.Copy), reads=[ps], writes=[s])
            r0 = b * BLK + t * 128
            kb.dma("sp", nm["Vm_o"][0][r0:r0 + 128, i * 512:(i + 1) * 512], s[:, :], reads=[s], writes=[pg.R(nm["Vm_o"][1])])
    if getattr(pg, 'dbg', 99) < 4:
        return
    slot, wv = pg.load_w(nm["w_in"][0], nm["w_in"][1], 0, 16, O_KR, 64)
    ps = fm_proj(pg, wv, slot, 0, 64, xT, xTb)
    rope_store(ps, 64, C64, S64, C_P64, nm["KrT_o"][0][:, tok], nm["KrT_o"][1])
    for (c0, key, Cb, Sb, pcol) in ((O_DQ, "QdT", C128, S128, C_P128), (O_DK, "KdT_o", C128, S128, C_P128),
                                    (O_IQ, "IqT", C64, S64, C_P64)):
        for g in range(2):
            slot, wv = pg.load_w(nm["w_in"][0], nm["w_in"][1], 0, 16, c0 + g * 512, 512)
            for hh in range(4):
                ps = fm_proj(pg, wv, slot, hh * 128, 128, xT, xTb)
                rope_store(ps, 128, Cb, Sb, pcol, nm[key][0][g * 4 + hh, :, tok], nm[key][1])
    if getattr(pg, 'dbg', 99) < 5:
        return
    for g in range(2):
        slot, wv = pg.load_w(nm["w_in"][0], nm["w_in"][1], 0, 16, O_DV + g * 512, 512)
        for t in range(4):
            ps = tm_proj(pg, lambda kc, wv=wv: wv[:, kc, :], slot, 512, xT, xTb, t)
            s = stage()
            kb.op("act", lambda e, ps=ps, s=s: e.activation(out=s[:, :], in_=ps[:, :], func=AF.Copy), reads=[ps], writes=[s])
            r0 = b * BLK + t * 128
            kb.dma("sp", nm["Vd_o"][0][r0:r0 + 128, g * 512:(g + 1) * 512], s[:, :], reads=[s], writes=[pg.R(nm["Vd_o"][1])])
    slot, wv = pg.load_w(nm["w_in"][0], nm["w_in"][1], 0, 16, O_IK, 80)
    ps = fm_proj(pg, wv, slot, 0, 64, xT, xTb)
    rope_store(ps, 64, C64, S64, C_P64, nm["IkT_o"][0][:, tok], nm["IkT_o"][1])
    for t in range(4):
        ps = tm_proj(pg, lambda kc, wv=wv: wv[:, kc, 64:80], slot, 16, xT, xTb, t)
        kb.op("act", lambda e, ps=ps: e.activation(out=xs[:, 0:16], in_=ps[:, 0:16], func=AF.Copy), reads=[ps], writes=[xs])
        r0 = b * BLK + t * 128
        kb.dma("sp", nm["iw"][0][r0:r0 + 128, :], xs[:, 0:16], reads=[xs], writes=[pg.R(nm["iw"][1])])
    if getattr(pg, 'dbg', 99) < 6:
        return
    emit_gla_proj_block(pg, wk, mp, b, L, xT)


def emit_gla_proj_block(pg, wk, mp, b, L, xT):
    kb = pg.kb
    kb.barrier()
    nm = L["n"]
    xTb = wk.xT
    win = nm["w_in"]
    gq_raw = View(wk.v[0][:, :].rearrange("p (a t) -> p a t", t=512))
    gk_raw = View(wk.v[1][:, :].rearrange("p (a t) -> p a t", t=512))
    lbuf = View(wk.v[2][:, 0:512])
    Eq = View(wk.v[2][:, 512:1024])
    Ek = View(wk.v[2][:, 1024:1536])
    Er = View(wk.v[2][:, 1536:2048])
    oi = View(wk.v[3][:, 0:1024])
    U0 = View(wk.v[3][:, 1024:2048])
    Ut = View(wk.gbc[:, 0:1024])
    gv = View(wk.hT[:, 0:4096].rearrange("p (a t) -> p a t", t=1024))
    sgr = View(wk.hT[:, 4096:8192].rearrange("p (a t) -> p a t", t=1024))
    qt = View(wk.hT[:, 8192:8704])
    kt = View(wk.hT[:, 8704:9216])
    kh = View(wk.hT[:, 9216:9728])
    qkT = View(wk.hT[:, 9728:10752].rearrange("p (a t) -> p a t", t=128))
    AT = View(wk.hT[:, 10752:11264].rearrange("p (a t) -> p a t", t=128))
    glrT = View(wk.sg[0][0:16, :])
    Acol = View(wk.sg[1][:, 0:8])
    Atl = View(wk.sg[1][:, 8:12])

    slot, wv = pg.load_w(win[0], win[1], 0, 16, O_GLR, 16)
    ps = fm_proj(pg, wv, slot, 0, 16, xT, xTb)
    kb.op("act", lambda e: e.activation(out=glrT[:, :], in_=ps[0:16, :], func=AF.Copy), reads=[ps], writes=[glrT])
    for (c0, dst) in ((O_GQ, gq_raw), (O_GK, gk_raw)):
        slot, wv = pg.load_w(win[0], win[1], 0, 16, c0, 512)
        for t in range(4):
            ps = tm_proj(pg, lambda kc, wv=wv: wv[:, kc, :], slot, 512, xT, xTb, t)
            kb.op("act", lambda e, ps=ps, t=t, dst=dst: e.activation(out=dst[:, t, :], in_=ps[:, :], func=AF.Copy), reads=[ps], writes=[dst])
    for (c0, dst, fn) in ((O_GV, gv, AF.Copy), (O_GR, sgr, AF.Silu)):
        for g in range(2):
            slot, wv = pg.load_w(win[0], win[1], 0, 16, c0 + g * 512, 512)
            for t in range(4):
                ps = tm_proj(pg, lambda kc, wv=wv: wv[:, kc, :], slot, 512, xT, xTb, t)
                kb.op("act", lambda e, ps=ps, t=t, g=g, dst=dst, fn=fn: e.activation(out=dst[:, t, g * 512:(g + 1) * 512], in_=ps[:, :], func=fn),
                      reads=[ps], writes=[dst])
    for t in range(4):
        r0 = b * BLK + t * 128
        kb.dma("sp", nm["gsr"][0][r0:r0 + 128, :], sgr[:, t, :], reads=[sgr], writes=[pg.R(nm["gsr"][1])])
    for t in range(4):
        j = b * 4 + t
        r0 = b * BLK + t * 128
        Z = pg.bank()
        mm(pg, Z[:, :], glrT[0:16, t * 128:(t + 1) * 128], mp.wg2[0:16, :], True, False, [glrT, mp.wg2], Z)
        mm(pg, Z[:, :], pg.consts[0:1, C_ONES:C_ONES + 128], mp.bg[0:1, :], False, True, [pg.consts, mp.bg], Z)
        kb.op("act", lambda e: e.activation(out=Eq[:, :], in_=Z[:, :], func=AF.Exp, scale=-1.0), reads=[Z], writes=[Eq])
        kb.op("act", lambda e: e.activation(out=lbuf[:, :], in_=Eq[:, :], func=AF.Ln, bias=pg.cc(C_ONES, 1), scale=1.0),
              reads=[Eq, pg.consts], writes=[lbuf])
        cum = pg.bank()
        mm(pg, cum[:, :], pg.cc(C_TRI), lbuf[:, :], True, True, [pg.consts, lbuf], cum)
        rev = pg.bank()
        mm(pg, rev[:, :], pg.cc(C_RTRI), lbuf[:, :], True, True, [pg.consts, lbuf], rev)
        tot = pg.pb[4]
        for h in range(4):
            mm(pg, tot[:, h * 2:h * 2 + 2], lbuf[:, h * 128:(h + 1) * 128], pg.consts[:, C_CH:C_CH + 2], True, True, [pg.consts, lbuf], tot)
        kb.op("act", lambda e: e.activation(out=Eq[:, :], in_=cum[:, :], func=AF.Exp, scale=-1.0 / 16), reads=[cum], writes=[Eq])
        kb.op("act", lambda e: e.activation(out=Ek[:, :], in_=cum[:, :], func=AF.Exp, scale=1.0 / 16), reads=[cum], writes=[Ek])
        kb.op("act", lambda e: e.activation(out=Er[:, :], in_=rev[:, :], func=AF.Exp, scale=-1.0 / 16), reads=[rev], writes=[Er])
        kb.op("act", lambda e: e.activation(out=Acol[:, :], in_=tot[:, 0:8], func=AF.Exp, scale=-1.0 / 16), reads=[tot], writes=[Acol])
        kb.op("dve", lambda e, t=t: e.scalar_tensor_tensor(out=qt[:, :], in0=gq_raw[:, t, :], scalar=128.0 ** -0.5, in1=Eq[:, :],
                                                          op0=ALU.mult, op1=ALU.mult), reads=[gq_raw, Eq], writes=[qt])
        kb.op("dve", lambda e, t=t: e.tensor_tensor(out=kt[:, :], in0=gk_raw[:, t, :], in1=Ek[:, :], op=ALU.mult), reads=[gk_raw, Ek], writes=[kt])
        kb.op("dve", lambda e, t=t: e.tensor_tensor(out=kh[:, :], in0=gk_raw[:, t, :], in1=Er[:, :], op=ALU.mult), reads=[gk_raw, Er], writes=[kh])
        ptb = pg.pt[0]
        for i in range(8):
            src = qt if i < 4 else kt
            hh = i % 4
            kb.op("pe", lambda e, i=i, src=src, hh=hh: e.transpose(out=ptb[:, i * 128:(i + 1) * 128], in_=src[:, hh * 128:(hh + 1) * 128],
                                                                   identity=pg.identb[:, :]), reads=[src, pg.identb], writes=[ptb])
        kb.op("dve", lambda e: e.tensor_copy(out=qkT[:, :, :], in_=ptb[:, :].rearrange("p (a t) -> p a t", t=128)), reads=[ptb], writes=[qkT])
        kb.dma("sp", nm["gqT"][0][:, :, r0:r0 + 128].rearrange("h p t -> p h t"), qkT[:, 0:4, :], reads=[qkT], writes=[pg.R(nm["gqT"][1])])
        for h in range(4):
            at = pg.bank()
            mm(pg, at[:, 0:128], qkT[:, 4 + h, :], qkT[:, h, :], True, True, [qkT], at)
            kb.op("dve", lambda e, at=at, h=h: e.tensor_tensor(out=AT[:, h, :], in0=at[:, 0:128], in1=pg.cc(C_MASKF), op=ALU.mult),
                  reads=[at, pg.consts], writes=[AT])
        for h in range(4):
            o = pg.bank()
            mm(pg, o[:, 0:256], AT[:, h, :], gv[:, t, h * 256:(h + 1) * 256], True, True, [AT, gv], o)
            kb.op("act", lambda e, o=o, h=h: e.activation(out=oi[:, h * 256:(h + 1) * 256], in_=o[:, 0:256], func=AF.Copy), reads=[o], writes=[oi])
        kb.dma("sp", nm["goi"][0][r0:r0 + 128, :], oi[:, :], reads=[oi], writes=[pg.R(nm["goi"][1])])
        for h in range(4):
            u0 = pg.bank()
            mm(pg, u0[:, 0:256], kh[0:64, h * 128:(h + 1) * 128], gv[0:64, t, h * 256:(h + 1) * 256], True, True, [kh, gv], u0)
            u1 = pg.bank()
            mm(pg, u1[:, 0:256], kh[64:128, h * 128:(h + 1) * 128], gv[64:128, t, h * 256:(h + 1) * 256], True, True, [kh, gv], u1)
            kb.op("act", lambda e, u0=u0, h=h: e.activation(out=U0[:, h * 256:(h + 1) * 256], in_=u0[:, 0:256], func=AF.Copy), reads=[u0], writes=[U0])
            kb.op("dve", lambda e, u1=u1, h=h: e.scalar_tensor_tensor(out=Ut[:, h * 256:(h + 1) * 256], in0=U0[:, h * 256:(h + 1) * 256],
                                                                      scalar=Acol[:, 2 * h + 1:2 * h + 2], in1=u1[:, 0:256],
                                                                      op0=ALU.mult, op1=ALU.add), reads=[U0, Acol, u1], writes=[Ut])
        kb.dma("sp", nm["gU0"][0][j], U0[:, :], reads=[U0], writes=[pg.R(nm["gU0"][1])])
        kb.dma("sp", nm["gUt_o"][0][j], Ut[:, :], reads=[Ut], writes=[pg.R(nm["gUt_o"][1])])
        Ac3 = Acol[:, :].rearrange("p (h c) -> p h c", c=2)
        kb.op("dve", lambda e: e.tensor_tensor(out=Atl[:, :], in0=Ac3[:, :, 0], in1=Ac3[:, :, 1], op=ALU.mult), reads=[Acol], writes=[Atl])
        kb.dma("sp", nm["gA"][0][j], Acol[:, :], reads=[Acol], writes=[pg.R(nm["gA"][1])])
        kb.dma("sp", nm["gAt_o"][0][j], Atl[:, :], reads=[Atl], writes=[pg.R(nm["gAt_o"][1])])
    kb.barrier()


BIS_L = -32.0
BIS_W = 64.0
BIS_N = 22


def emit_indexer(pg, nm, tiles=None):
    kb = pg.kb
    kb.push()
    pg.nbank = 6
    ik2 = Buf(kb, "ik2", [128, SEQ], BF16)
    score = Buf(kb, "score", [128, SEQ], F32)
    mneg = Buf(kb, "mnegw", [128, SEQ], BF16)
    cm = Buf(kb, "cm32", [128, 1024], F32)
    iq = Buf(kb, "iqt", [128, 1024], BF16)
    iwt = Buf(kb, "iwt", [128, 16], F32)
    wab = Buf(kb, "wab", [128, 16], F32)
    wsg = Buf(kb, "wsg", [128, 16], F32)
    mid = Buf(kb, "mid", [128, 1], F32)
    cnt = Buf(kb, "cnt", [128, 1], F32)
    dlt = Buf(kb, "dlt", [128, 1], F32)
    nmid = Buf(kb, "nmid", [128, 1], F32)
    sgs = Buf(kb, "sgs", [128, 1], F32)
    ajunk = Buf(kb, "ajunk", [128, 10240], BF16)
    accBs = [Buf(kb, f"accB{i}", [128, 512], F32) for i in range(2)]
    kb.dma("sp", ik2[0:64, :], nm["IkT_g"][0][:, :], reads=[pg.R(nm["IkT_g"][1])], writes=[ik2])
    kb.dma("sp", ik2[64:128, :], nm["IkT_g"][0][:, :], reads=[pg.R(nm["IkT_g"][1])], writes=[ik2])
    kb.dma("sp", cm[:, :], nm["cmask"][0][:, :], reads=[pg.R(nm["cmask"][1])], writes=[cm])
    iq3 = iq[:, :].rearrange("p (a t) -> p a t", t=128)
    for j in (range(NT) if tiles is None else tiles):
        nk = 1024 * (j + 1)
        ts = slice(j * 128, (j + 1) * 128)
        kb.dma("sp", iq3, nm["IqT"][0][:, :, ts].rearrange("a p t -> p a t"), reads=[pg.R(nm["IqT"][1])], writes=[iq])
        kb.dma("sp", iwt[:, :], nm["iw"][0][ts, :], reads=[pg.R(nm["iw"][1])], writes=[iwt])
        kb.op("act", lambda e: e.activation(out=wab[:, :], in_=iwt[:, :], func=AF.Abs, scale=(64.0 ** -0.5) * (16.0 ** -0.5)),
              reads=[iwt], writes=[wab])
        kb.op("act", lambda e: e.activation(out=wsg[:, :], in_=iwt[:, :], func=AF.Sign), reads=[iwt], writes=[wsg])
        for kbi in range(2 * (j + 1)):
            ks = slice(kbi * 512, (kbi + 1) * 512)
            diag = kbi >= 2 * j
            accB = accBs[kbi % 2]
            for hh in range(16):
                p0 = (hh % 2) * 64
                ps = pg.bank()
                mm(pg, ps[:, :], iq3[p0:p0 + 64, hh // 2, :], ik2[p0:p0 + 64, ks], True, True, [iq, ik2], ps)
                kb.op("act", lambda e, ps=ps, hh=hh: e.activation(out=ps[:, :], in_=ps[:, :], func=AF.Relu, scale=wab[:, hh:hh + 1]),
                      reads=[ps, wab], writes=[ps])
                if hh % 2 == 0:
                    dst, dreg = score[:, ks], score
                else:
                    dst, dreg = accB[:, :], accB
                if hh == 0 and diag:
                    in1 = cm[:, (kbi - 2 * j) * 512:(kbi - 2 * j + 1) * 512]
                    kb.op("dve", lambda e, ps=ps, dst=dst, hh=hh, in1=in1: e.scalar_tensor_tensor(
                        out=dst, in0=ps[:, :], scalar=wsg[:, hh:hh + 1], in1=in1, op0=ALU.mult, op1=ALU.add),
                        reads=[ps, wsg, cm], writes=[dreg])
                elif hh < 2:
                    kb.op("dve", lambda e, ps=ps, dst=dst, hh=hh: e.tensor_scalar(out=dst, in0=ps[:, :], scalar1=wsg[:, hh:hh + 1], scalar2=None,
                                                                               op0=ALU.mult), reads=[ps, wsg], writes=[dreg])
                else:
                    kb.op("dve", lambda e, ps=ps, dst=dst, hh=hh: e.scalar_tensor_tensor(
                        out=dst, in0=ps[:, :], scalar=wsg[:, hh:hh + 1], in1=dst, op0=ALU.mult, op1=ALU.add),
                        reads=[ps, wsg], writes=[dreg])
            kb.op("dve", lambda e, ks=ks, accB=accB: e.tensor_tensor(out=score[:, ks], in0=score[:, ks], in1=accB[:, :], op=ALU.add),
                  reads=[accB], writes=[score])
        nd = max(512, int(round(0.40 * nk / 512)) * 512)
        na = nk - nd
        kb.op("dve", lambda e: e.memset(mid[:, :], BIS_L + BIS_W / 2), writes=[mid])
        for k in range(1, BIS_N + 1):
            kb.op("dve", lambda e: e.tensor_scalar(out=mneg[:, 0:nd], in0=score[:, 0:nd], scalar1=mid[:, 0:1], scalar2=None,
                                                   op0=ALU.is_ge, op1=ALU.add, accum_out=cnt[:, 0:1]),
                  reads=[score, mid], writes=[mneg, cnt])
            kb.op("act", lambda e: e.activation(out=ajunk[:, 0:na], in_=score[:, nd:nk], func=AF.Sign, bias=mid[:, 0:1], scale=-1.0,
                                                accum_out=sgs[:, 0:1]), reads=[score, mid], writes=[ajunk, sgs])
            kb.op("dve", lambda e: e.scalar_tensor_tensor(out=cnt[:, :], in0=sgs[:, :], scalar=-0.5, in1=cnt[:, :], op0=ALU.mult, op1=ALU.add),
                  reads=[sgs], writes=[cnt])
            hk = BIS_W / 2 ** (k + 1) if k < BIS_N else BIS_W / 2 ** BIS_N
            mul = 2 * hk if k < BIS_N else hk
            kb.op("dve", lambda e, mul=mul: e.tensor_scalar(out=dlt[:, :], in0=cnt[:, :], scalar1=255.5 - 0.5 * na, scalar2=mul,
                                                            op0=ALU.is_ge, op1=ALU.mult), reads=[cnt], writes=[dlt])
            kb.op("dve", lambda e, hk=hk: e.scalar_tensor_tensor(out=mid[:, :], in0=dlt[:, :], scalar=-hk, in1=mid[:, :],
                                                                op0=ALU.add, op1=ALU.add), reads=[dlt, mid], writes=[mid])
        kb.op("dve", lambda e: e.tensor_scalar(out=mneg[:, 0:nk], in0=score[:, 0:nk], scalar1=mid[:, 0:1], scalar2=NEG,
                                               op0=ALU.is_lt, op1=ALU.mult), reads=[score, mid], writes=[mneg])
        kb.dma("sp", nm["MnegD"][0][j, :, 0:nk], mneg[:, 0:nk], reads=[mneg], writes=[pg.R(nm["MnegD"][1])])
        if "thr_dbg" in nm:
            kb.dma("sp", nm["thr_dbg"][0][j], mid[:, :], reads=[mid], writes=[pg.R(nm["thr_dbg"][1])])
    pg.nbank = 4
    pg.pbn = 0
    kb.pop()


def emit_attention(pg, nm, kind, heads=None, tiles=None):
    kb = pg.kb
    kb.push()
    mla = kind == "mla"
    K = Buf(kb, "attK", [128, SEQ], BF16)
    V = Buf(kb, "attV", [128, 128 * 129], BF16)
    V3 = V[:, :].rearrange("p (n d) -> p n d", d=129)
    Kg = [View(K[:, g * 1024:(g + 1) * 1024]) for g in range(16)]
    Vg = [View(V3[:, g * 8:(g + 1) * 8, :]) for g in range(16)]
    kb.op("dve", lambda e: e.memset(V[:, :], 1.0), writes=[V] + Vg)
    if mla:
        KR = Buf(kb, "attKR", [128, SEQ], BF16)
        kb.op("dve", lambda e: e.memset(KR[:, :], 0.0), writes=[KR])
        cm32 = Buf(kb, "attcm32", [128, 1024], F32)
        cmb = Buf(kb, "attcmb", [128, 1024], BF16)
        kb.dma("sp", KR[0:64, :], nm["KrT_g"][0][:, :], reads=[pg.R(nm["KrT_g"][1])], writes=[KR])
        kb.dma("sp", cm32[:, :], nm["cmask"][0][:, :], reads=[pg.R(nm["cmask"][1])], writes=[cm32])
        kb.op("dve", lambda e: e.tensor_copy(out=cmb[:, :], in_=cm32[:, :]), reads=[cm32], writes=[cmb])
        Qr = [Buf(kb, f"attQr{i}", [128, 128], BF16) for i in range(2)]
        for q_ in Qr:
            kb.op("dve", lambda e, q_=q_: e.memset(q_[:, :], 0.0), writes=[q_])
        scale = 192.0 ** -0.5
        Kd, Vd, Qd, oT = nm["KnT_g"], nm["Vm_g"], nm["QnT"], nm["oT_mla"]
    else:
        MN = [Buf(kb, f"attMN{i}", [128, SEQ], BF16) for i in range(2)]
        scale = 128.0 ** -0.5
        Kd, Vd, Qd, oT = nm["KdT_g"], nm["Vd_g"], nm["QdT"], nm["oT_dsa"]
    Qn = [Buf(kb, f"attQn{i}", [128, 128], BF16) for i in range(2)]
    PT = [Buf(kb, f"attPT{i}", [128, 512], BF16) for i in range(3)]
    rsum = Buf(kb, "attrsum", [128, 2], F32)
    ob = Buf(kb, "attob", [128, 128], BF16)
    oTs = Buf(kb, "attoTs", [128, 128], BF16)
    Ob = [pg.pb[4], pg.pb[5]]
    tasks = []
    it = 0
    for h in (range(8) if heads is None else heads):
        first = True
        for j in (range(NT) if tiles is None else tiles):
            nblk = 2 * (j + 1)
            for kbi in range(nblk):
                tasks.append(dict(h=h, j=j, kbi=kbi, nblk=nblk, it=it, newhead=(first and kbi == 0)))
            first = False
            it += 1
    state = {"maxg": -1}

    def stageA(s_, T):
        h, j, kbi, it_ = T["h"], T["j"], T["kbi"], T["it"]
        ts = slice(j * 128, (j + 1) * 128)
        nk = 1024 * (j + 1)
        qn = Qn[it_ % 2]
        if kbi == 0:
            if T["newhead"]:
                state["maxg"] = -1
            for g in range(state["maxg"] + 1, j + 1):
                gs = slice(g * 1024, (g + 1) * 1024)
                kb.dma("sp", Kg[g][:, :], Kd[0][h, :, gs], reads=[pg.R(Kd[1])], writes=[Kg[g]])
                kb.dma("sp", Vg[g][:, :, 0:128], Vd[0][h, :, gs].rearrange("p (n d) -> p n d", d=128), reads=[pg.R(Vd[1])], writes=[Vg[g]])
            state["maxg"] = max(state["maxg"], j)
            kb.dma("pool", qn[:, :], Qd[0][h, :, ts], reads=[pg.R(Qd[1])], writes=[qn])
            if mla:
                kb.dma("pool", Qr[it_ % 2][0:64, :], nm["QrT"][0][h, :, ts], reads=[pg.R(nm["QrT"][1])], writes=[Qr[it_ % 2]])
            else:
                kb.dma("pool", MN[it_ % 2][:, 0:nk], nm["MnegD"][0][j, :, 0:nk], reads=[pg.R(nm["MnegD"][1])], writes=[MN[it_ % 2]])
        g = kbi // 2
        diag = g == j
        ps = pg.bank()
        for i in range(4):
            k0 = kbi * 512 + i * 128
            o = ps[:, i * 128:(i + 1) * 128]
            if mla:
                qr = Qr[it_ % 2]
                mm(pg, o, K[:, k0:k0 + 128], qn[:, :], True, False, [qn, Kg[g]], ps)
                mm(pg, o, KR[:, k0:k0 + 128], qr[:, :], False, not diag, [qr, KR], ps)
                if diag:
                    c0 = k0 - j * 1024
                    mm(pg, o, cmb[:, c0:c0 + 128], pg.identb[:, :], False, True, [pg.identb, cmb], ps)
            else:
                mn = MN[it_ % 2]
                mm(pg, o, K[:, k0:k0 + 128], qn[:, :], True, False, [qn, Kg[g]], ps)
                mm(pg, o, mn[:, k0:k0 + 128], pg.identb[:, :], False, True, [pg.identb, mn], ps)
        p = PT[s_ % 3]
        kb.op("act", lambda e: e.activation(out=p[:, :], in_=ps[:, :], func=AF.Exp, scale=scale), reads=[ps], writes=[p])

    def stageC(s_, T):
        h, j, kbi, nblk, it_ = T["h"], T["j"], T["kbi"], T["nblk"], T["it"]
        g = kbi // 2
        pt_ = PT[s_ % 3]
        O = Ob[it_ % 2]
        for i in range(4):
            n = kbi * 4 + i
            mm(pg, O[:, 0:129], pt_[:, i * 128:(i + 1) * 128], V3[:, n, :],
               kbi == 0 and i == 0, kbi == nblk - 1 and i == 3, [pt_, Vg[g]], O)
        if kbi == nblk - 1:
            ts = slice(j * 128, (j + 1) * 128)
            kb.op("dve", lambda e: e.reciprocal(out=rsum[:, 1:2], in_=O[:, 128:129]), reads=[O], writes=[rsum])
            kb.op("act", lambda e: e.activation(out=ob[:, :], in_=O[:, 0:128], func=AF.Copy, scale=rsum[:, 1:2]), reads=[O, rsum], writes=[ob])
            ptb = pg.pt[it_ % 2]
            kb.op("pe", lambda e: e.transpose(out=ptb[:, 0:128], in_=ob[:, :], identity=pg.identb[:, :]), reads=[ob, pg.identb], writes=[ptb])
            kb.op("dve", lambda e: e.tensor_copy(out=oTs[:, :], in_=ptb[:, 0:128]), reads=[ptb], writes=[oTs])
            kb.dma("sp", oT[0][h * 128:(h + 1) * 128, ts], oTs[:, :], reads=[oTs], writes=[pg.R(oT[1])])

    N = len(tasks)
    for s_ in range(N + 1):
        if s_ < N:
            stageA(s_, tasks[s_])
        if 0 <= s_ - 1 < N:
            stageC(s_ - 1, tasks[s_ - 1])
    kb.pop()


def emit_gla_out(pg, nm, tiles=None):
    kb = pg.kb
    kb.push()
    S = Buf(kb, "glaS", [128, 1024], F32)
    snap = Buf(kb, "glasnap", [128, 1024], F32)
    At = Buf(kb, "glaAt", [128, 512], F32)
    Ub = [Buf(kb, f"glaUb{i}", [128, 1024], F32) for i in range(3)]
    U0b = Buf(kb, "glaU0", [128, 1024], F32)
    Ab = Buf(kb, "glaAb", [128, 8], F32)
    qTb = Buf(kb, "glaqT", [128, 512], BF16)
    qA = Buf(kb, "glaqA", [128, 512], BF16)
    qB = Buf(kb, "glaqB", [128, 512], BF16)
    oib = Buf(kb, "glaoi", [128, 1024], F32)
    srb = Buf(kb, "glasr", [128, 1024], BF16)
    gn = Buf(kb, "glagn", [128, 1024], F32)
    S0b = Buf(kb, "glaS0b", [128, 1024], BF16)
    S1b = Buf(kb, "glaS1b", [128, 1024], BF16)
    junk = Buf(kb, "glajunk", [128, 256], F32)
    ss = Buf(kb, "glass", [128, 8], F32)
    ob = Buf(kb, "glaob", [128, 1024], BF16)
    oTs = Buf(kb, "glaoTs", [128, 1024], BF16)
    kb.op("dve", lambda e: e.memset(S[:, :], 0.0), writes=[S])
    kb.op("dve", lambda e: e.memset(qA[:, :], 0.0), writes=[qA])
    kb.op("dve", lambda e: e.memset(qB[:, :], 0.0), writes=[qB])
    kb.dma("sp", At[:, :], nm["gAt_g"][0][:, :], reads=[pg.R(nm["gAt_g"][1])], writes=[At])
    kb.dma("sp", gn[:, :], nm["gla_norm"][0].partition_broadcast(128), reads=[pg.R(nm["gla_norm"][1])], writes=[gn])
    q3 = qTb[:, :].rearrange("p (h t) -> p h t", t=128)
    qA3 = qA[:, :].rearrange("p (h t) -> p h t", t=128)
    qB3 = qB[:, :].rearrange("p (h t) -> p h t", t=128)
    for j in range(NT):
        for i in range(8):
            g = 8 * j + i
            ub = Ub[g % 3]
            kb.dma("sp", ub[:, :], nm["gUt_g"][0][g], reads=[pg.R(nm["gUt_g"][1])], writes=[ub])
            if i == 0:
                kb.op("dve", lambda e: e.tensor_scalar(out=snap[:, :], in0=S[:, :], scalar1=pg.cc(C_EI, 1), scalar2=None, op0=ALU.mult),
                      reads=[S, pg.consts], writes=[snap])
            else:
                kb.op("dve", lambda e, i=i: e.scalar_tensor_tensor(out=snap[:, :], in0=S[:, :], scalar=pg.cc(C_EI + i, 1), in1=snap[:, :],
                                                                   op0=ALU.mult, op1=ALU.add), reads=[S, pg.consts, snap], writes=[snap])
            for h in range(4):
                hs = slice(h * 256, (h + 1) * 256)
                kb.op("dve", lambda e, hs=hs, g=g, h=h, ub=ub: e.scalar_tensor_tensor(
                    out=S[:, hs], in0=S[:, hs], scalar=At[:, g * 4 + h:g * 4 + h + 1], in1=ub[:, hs], op0=ALU.mult, op1=ALU.add),
                    reads=[At, ub], writes=[S])
        if tiles is not None and j not in tiles:
            continue
        ts = slice(j * 128, (j + 1) * 128)
        kb.dma("pool", U0b[:, :], nm["gU0"][0][j], reads=[pg.R(nm["gU0"][1])], writes=[U0b])
        kb.dma("pool", Ab[:, :], nm["gA"][0][j], reads=[pg.R(nm["gA"][1])], writes=[Ab])
        kb.dma("pool", q3, nm["gqT"][0][:, :, ts].rearrange("h p t -> p h t"), reads=[pg.R(nm["gqT"][1])], writes=[qTb])
        kb.dma("pool", oib[:, :], nm["goi"][0][ts, :], reads=[pg.R(nm["goi"][1])], writes=[oib])
        kb.dma("pool", srb[:, :], nm["gsr"][0][ts, :], reads=[pg.R(nm["gsr"][1])], writes=[srb])
        kb.op("act", lambda e: e.activation(out=S0b[:, :], in_=snap[:, :], func=AF.Copy), reads=[snap], writes=[S0b])
        for h in range(4):
            hs = slice(h * 256, (h + 1) * 256)
            kb.op("dve", lambda e, hs=hs, h=h: e.scalar_tensor_tensor(out=S1b[:, hs], in0=snap[:, hs], scalar=Ab[:, 2 * h:2 * h + 1],
                                                                      in1=U0b[:, hs], op0=ALU.mult, op1=ALU.add),
                  reads=[snap, Ab, U0b], writes=[S1b])
        kb.op("dve", lambda e: e.tensor_copy(out=qA3[:, :, 0:64], in_=q3[:, :, 0:64]), reads=[qTb], writes=[qA])
        kb.op("dve", lambda e: e.tensor_copy(out=qB3[:, :, 64:128], in_=q3[:, :, 64:128]), reads=[qTb], writes=[qB])
        for h in range(4):
            hs = slice(h * 256, (h + 1) * 256)
            o = pg.bank()
            mm(pg, o[:, 0:256], qA3[:, h, :], S0b[:, hs], True, False, [qA, S0b], o)
            mm(pg, o[:, 0:256], qB3[:, h, :], S1b[:, hs], False, True, [qB, S1b], o)
            kb.op("dve", lambda e, o=o, hs=hs: e.tensor_tensor(out=oib[:, hs], in0=o[:, 0:256], in1=oib[:, hs], op=ALU.add), reads=[o], writes=[oib])
            kb.op("act", lambda e, hs=hs, h=h: e.activation(out=junk[:, :], in_=oib[:, hs], func=AF.Square, accum_out=ss[:, h:h + 1]),
                  reads=[oib], writes=[junk, ss])
        kb.op("act", lambda e: e.activation(out=ss[:, 4:8], in_=ss[:, 0:4], func=AF.Ln, bias=pg.cc(C_EPS, 1), scale=1.0 / 256),
              reads=[ss, pg.consts], writes=[ss])
        kb.op("act", lambda e: e.activation(out=ss[:, 4:8], in_=ss[:, 4:8], func=AF.Exp, scale=-0.5), reads=[ss], writes=[ss])
        for h in range(4):
            hs = slice(h * 256, (h + 1) * 256)
            kb.op("dve", lambda e, hs=hs, h=h: e.scalar_tensor_tensor(out=oib[:, hs], in0=oib[:, hs], scalar=ss[:, 4 + h:5 + h], in1=gn[:, hs],
                                                                      op0=ALU.mult, op1=ALU.mult), reads=[ss, gn], writes=[oib])
        kb.op("dve", lambda e: e.tensor_tensor(out=ob[:, :], in0=oib[:, :], in1=srb[:, :], op=ALU.mult), reads=[oib, srb], writes=[ob])
        ptb = pg.pt[j % 2]
        for i in range(8):
            kb.op("pe", lambda e, i=i, ptb=ptb: e.transpose(out=ptb[:, i * 128:(i + 1) * 128], in_=ob[:, i * 128:(i + 1) * 128],
                                                            identity=pg.identb[:, :]), reads=[ob, pg.identb], writes=[ptb])
        kb.op("dve", lambda e, ptb=ptb: e.tensor_copy(out=oTs[:, :], in_=ptb[:, :]), reads=[ptb], writes=[oTs])
        kb.dma("sp", nm["oT_gla"][0][:, ts].rearrange("(k p) t -> p k t", p=128), oTs[:, :].rearrange("p (k t) -> p k t", t=128),
               reads=[oTs], writes=[pg.R(nm["oT_gla"][1])])
    kb.pop()


def emit_merge_block(pg, wk, b, nm):
    kb = pg.kb
    kb.barrier()
    load_ln_params(pg, wk, nm["ln_g1"][0], nm["ln_b1"][0], nm["ln_g1"][1], nm["ln_b1"][1])
    tok = slice(b * BLK, (b + 1) * BLK)
    xT = load_xT_block(pg, wk, nm["x1T"][0], nm["x1T"][1], b)
    oTm = View(wk.hT[:, 0:4096].rearrange("p (k t) -> p k t", t=BLK))
    oTd = View(wk.hT[:, 4096:8192].rearrange("p (k t) -> p k t", t=BLK))
    oTg0 = View(wk.xb[:, :].rearrange("p (k t) -> p k t", t=BLK))
    oTg1 = View(wk.xts[:, :].rearrange("p (k t) -> p k t", t=BLK))
    bm = View(wk.stats[0:1, 0:24])
    brow = View(wk.hT[0:1, 8192:8192 + 512])
    kb.dma("sp", oTm[:, :, :], nm["oT_mla"][0][:, tok].rearrange("(k p) t -> p k t", p=128), reads=[pg.R(nm["oT_mla"][1])], writes=[oTm])
    kb.dma("sp", oTd[:, :, :], nm["oT_dsa"][0][:, tok].rearrange("(k p) t -> p k t", p=128), reads=[pg.R(nm["oT_dsa"][1])], writes=[oTd])
    kb.dma("sp", oTg0[:, :, :], nm["oT_gla"][0][0:512, tok].rearrange("(k p) t -> p k t", p=128), reads=[pg.R(nm["oT_gla"][1])], writes=[oTg0])
    kb.dma("sp", oTg1[:, :, :], nm["oT_gla"][0][512:1024, tok].rearrange("(k p) t -> p k t", p=128), reads=[pg.R(nm["oT_gla"][1])], writes=[oTg1])
    brs = [("w_br_mla", lambda kc: oTm[:, kc, :], [oTm]), ("w_br_dsa", lambda kc: oTd[:, kc, :], [oTd]),
           ("w_br_gla", lambda kc: (oTg0[:, kc, :] if kc < 4 else oTg1[:, kc - 4, :]), [oTg0, oTg1])]
    for n in range(4):
        for bi, (wname, osel, oregs) in enumerate(brs):
            sm_, wm = pg.load_w(nm["w_merge"][0], nm["w_merge"][1], 0, 16, bi * 2048 + n * 512, 512)
            sb_, wb = pg.load_w(nm[wname][0], nm[wname][1], 0, 8, n * 512, 512)
            c0 = bi * 2048 + n * 512
            kb.dma("pool", brow[:, :], nm["b_merge"][0][0:1, c0:c0 + 512], reads=[pg.R(nm["b_merge"][1])], writes=[brow])
            for t in range(4):
                pgt = pg.bank()
                for kc in range(16):
                    mm(pg, pgt[:, :], xT[:, kc, t * 128:(t + 1) * 128], wm[:, kc, :], kc == 0, False, [wk.xT, sm_], pgt)
                mm(pg, pgt[:, :], pg.onesb[0:1, :], brow[0:1, :], False, True, [pg.onesb, brow], pgt)
                sgb = wk.sg[t % 2]
                kb.op("act", lambda e, pgt=pgt, sgb=sgb: e.activation(out=sgb[:, :], in_=pgt[:, :], func=AF.Sigmoid), reads=[pgt], writes=[sgb])
                py = pg.bank()
                for kc in range(8):
                    mm(pg, py[:, :], osel(kc)[:, t * 128:(t + 1) * 128], wb[:, kc, :], kc == 0, kc == 7, oregs + [sb_], py)
                ms = wk.v[t][:, n * 512:(n + 1) * 512]
                if bi == 0:
                    kb.op("dve", lambda e, py=py, sgb=sgb, ms=ms: e.tensor_tensor(out=ms, in0=py[:, :], in1=sgb[:, :], op=ALU.mult),
                          reads=[py, sgb], writes=[wk.v[t]])
                else:
                    kb.op("dve", lambda e, py=py, sgb=sgb: e.tensor_tensor(out=sgb[:, :], in0=py[:, :], in1=sgb[:, :], op=ALU.mult),
                          reads=[py], writes=[sgb])
                    kb.op("dve", lambda e, sgb=sgb, ms=ms: e.tensor_tensor(out=ms, in0=ms, in1=sgb[:, :], op=ALU.add),
                          reads=[sgb], writes=[wk.v[t]])
    kb.barrier()
    mT = View(wk.hT[:, 0:8192].rearrange("p (k t) -> p k t", t=BLK))
    for t in range(4):
        kb.op("act", lambda e, t=t: e.activation(out=wk.xb[:, :], in_=wk.v[t][:, :], func=AF.Copy), reads=[wk.v[t]], writes=[wk.xb])
        for half in range(2):
            ptb = pg.pt[half]
            for k in range(8):
                kc = half * 8 + k
                kb.op("pe", lambda e, kc=kc, k=k, ptb=ptb: e.transpose(out=ptb[:, k * 128:(k + 1) * 128], in_=wk.xb[:, kc * 128:(kc + 1) * 128],
                                                                      identity=pg.identb[:, :]), reads=[wk.xb, pg.identb], writes=[ptb])
            kb.op("dve", lambda e, ptb=ptb, half=half, t=t: e.tensor_copy(
                out=mT[:, half * 8:half * 8 + 8, t * 128:(t + 1) * 128], in_=ptb[:, :].rearrange("p (k t) -> p k t", t=128)),
                reads=[ptb], writes=[mT])
        r0 = b * BLK + t * 128
        kb.dma("sp", wk.v[t][:, :], nm["x1"][0][r0:r0 + 128, :], reads=[pg.R(nm["x1"][1])], writes=[wk.v[t]])
    for n in range(4):
        so_, wo = pg.load_w(nm["w_out"][0], nm["w_out"][1], 0, 16, n * 512, 512)
        for t in range(4):
            ph = pg.bank()
            for kc in range(16):
                mm(pg, ph[:, :], mT[:, kc, t * 128:(t + 1) * 128], wo[:, kc, :], kc == 0, kc == 15, [mT, so_], ph)
            vs = wk.v[t][:, n * 512:(n + 1) * 512]
            kb.op("dve", lambda e, ph=ph, vs=vs: e.scalar_tensor_tensor(out=vs, in0=vs, scalar=ALPHA, in1=ph[:, :], op0=ALU.mult, op1=ALU.add),
                  reads=[ph], writes=[wk.v[t]])
    kb.barrier()
    for t in range(4):
        r0 = b * BLK + t * 128
        emit_ln(pg, wk.v[t], wk.v[t][:, :], wk.gbc, wk.bbc, wk.st, nm["x2"][0][r0:r0 + 128, :], nm["x2"][1],
                nm["x2T"][0][:, r0:r0 + 128], nm["x2T"][1], wk.xb, wk.xts)


class XAState:
    def __init__(self, pg, wk, nm):
        kb = pg.kb
        kb.barrier()
        self.KxT = Buf(kb, "xaK", [128, 1024], BF16)
        self.Vx = Buf(kb, "xaV", [128, 1024], BF16)
        memT = View(wk.hT[:, 0:4096].rearrange("p (k m) -> p k m", m=256))
        for mt in range(2):
            vb = wk.v[mt]
            kb.dma("sp", vb[:, :], nm["mem"][0][mt * 128:(mt + 1) * 128, :], reads=[pg.R(nm["mem"][1])], writes=[vb])
            kb.op("act", lambda e, vb=vb: e.activation(out=wk.xb[:, :], in_=vb[:, :], func=AF.Copy), reads=[vb], writes=[wk.xb])
            for half in range(2):
                ptb = pg.pt[half]
                for k in range(8):
                    kc = half * 8 + k
                    kb.op("pe", lambda e, kc=kc, k=k, ptb=ptb: e.transpose(out=ptb[:, k * 128:(k + 1) * 128], in_=wk.xb[:, kc * 128:(kc + 1) * 128],
                                                                          identity=pg.identb[:, :]), reads=[wk.xb, pg.identb], writes=[ptb])
                kb.op("dve", lambda e, ptb=ptb, half=half, mt=mt: e.tensor_copy(
                    out=memT[:, half * 8:half * 8 + 8, mt * 128:(mt + 1) * 128], in_=ptb[:, :].rearrange("p (k t) -> p k t", t=128)),
                    reads=[ptb], writes=[memT])
        K3 = self.KxT[:, :].rearrange("p (h m) -> p h m", m=256)
        V3 = self.Vx[:, :].rearrange("p (a c) -> p a c", c=512)
        sk_, wkk = pg.load_w(nm["xa_w_kv"][0], nm["xa_w_kv"][1], 0, 16, 0, 512)
        for h in range(4):
            ps = pg.bank()
            for kc in range(16):
                mm(pg, ps[:, 0:256], wkk[:, kc, h * 128:(h + 1) * 128], memT[:, kc, :], kc == 0, kc == 15, [sk_, memT], ps)
            kb.op("act", lambda e, ps=ps, h=h: e.activation(out=K3[:, h, :], in_=ps[:, 0:256], func=AF.Copy), reads=[ps], writes=[self.KxT])
        sv_, wvv = pg.load_w(nm["xa_w_kv"][0], nm["xa_w_kv"][1], 0, 16, 512, 512)
        for mt in range(2):
            ps = pg.bank()
            for kc in range(16):
                mm(pg, ps[:, :], memT[:, kc, mt * 128:(mt + 1) * 128], wvv[:, kc, :], kc == 0, kc == 15, [sv_, memT], ps)
            kb.op("act", lambda e, ps=ps, mt=mt: e.activation(out=V3[:, mt, :], in_=ps[:, :], func=AF.Copy), reads=[ps], writes=[self.Vx])
        self.K3, self.V3 = K3, V3
        kb.barrier()


def emit_xattn_block(pg, wk, xa, b, nm):
    kb = pg.kb
    kb.barrier()
    load_ln_params(pg, wk, nm["ln_g2"][0], nm["ln_b2"][0], nm["ln_g2"][1], nm["ln_b2"][1])
    xT = load_xT_block(pg, wk, nm["x2T"][0], nm["x2T"][1], b)
    qT = View(wk.hT[:, 0:2048].rearrange("p (h t) -> p h t", t=BLK))
    oxT = View(wk.hT[:, 2048:4096].rearrange("p (h t) -> p h t", t=BLK))
    Pb = [View(wk.hT[:, 4096 + i * 256:4096 + (i + 1) * 256]) for i in range(2)]
    PTb = [View(wk.hT[:, 4608 + i * 256:4608 + (i + 1) * 256]) for i in range(2)]
    ob = View(wk.hT[:, 5120:5632])
    rs = View(wk.sg[0][:, 0:8])
    sq_, wq = pg.load_w(nm["xa_w_q"][0], nm["xa_w_q"][1], 0, 16, 0, 512)
    for h in range(4):
        ps = fm_proj(pg, wq, sq_, h * 128, 128, xT, wk.xT)
        kb.op("act", lambda e, ps=ps, h=h: e.activation(out=qT[:, h, :], in_=ps[:, :], func=AF.Copy), reads=[ps], writes=[qT])
    for t in range(4):
        r0 = b * BLK + t * 128
        kb.dma("sp", wk.v[t][:, :], nm["x2"][0][r0:r0 + 128, :], reads=[pg.R(nm["x2"][1])], writes=[wk.v[t]])
        for h in range(4):
            ps = pg.bank()
            mm(pg, ps[:, 0:256], qT[:, h, t * 128:(t + 1) * 128], xa.K3[:, h, :], True, True, [qT, xa.KxT], ps)
            p = Pb[h % 2]
            kb.op("act", lambda e, ps=ps, p=p, h=h: e.activation(out=p[:, :], in_=ps[:, 0:256], func=AF.Exp, scale=128.0 ** -0.5,
                                                                 accum_out=rs[:, h:h + 1]), reads=[ps], writes=[p, rs])
            ptb = pg.pt[h % 2]
            for i in range(2):
                kb.op("pe", lambda e, i=i, p=p, ptb=ptb: e.transpose(out=ptb[:, i * 128:(i + 1) * 128], in_=p[:, i * 128:(i + 1) * 128],
                                                                    identity=pg.identb[:, :]), reads=[p, pg.identb], writes=[ptb])
            pt_ = PTb[h % 2]
            kb.op("dve", lambda e, ptb=ptb, pt_=pt_: e.tensor_copy(out=pt_[:, :], in_=ptb[:, 0:256]), reads=[ptb], writes=[pt_])
            po = pg.bank()
            for i in range(2):
                mm(pg, po[:, 0:128], pt_[:, i * 128:(i + 1) * 128], xa.V3[:, i, h * 128:(h + 1) * 128], i == 0, i == 1, [pt_, xa.Vx], po)
            kb.op("dve", lambda e, h=h: e.reciprocal(out=rs[:, 4 + h:5 + h], in_=rs[:, h:h + 1]), reads=[rs], writes=[rs])
            kb.op("act", lambda e, po=po, h=h: e.activation(out=ob[:, h * 128:(h + 1) * 128], in_=po[:, 0:128], func=AF.Copy, scale=rs[:, 4 + h:5 + h]),
                  reads=[po, rs], writes=[ob])
        ptb = pg.pt[0]
        for h in range(4):
            kb.op("pe", lambda e, h=h, ptb=ptb: e.transpose(out=ptb[:, h * 128:(h + 1) * 128], in_=ob[:, h * 128:(h + 1) * 128],
                                                            identity=pg.identb[:, :]), reads=[ob, pg.identb], writes=[ptb])
        kb.op("dve", lambda e, ptb=ptb, t=t: e.tensor_copy(out=oxT[:, :, t * 128:(t + 1) * 128], in_=ptb[:, 0:512].rearrange("p (h t) -> p h t", t=128)),
              reads=[ptb], writes=[oxT])
    for n in range(4):
        so_, wo = pg.load_w(nm["xa_w_o"][0], nm["xa_w_o"][1], 0, 4, n * 512, 512)
        for t in range(4):
            ph = pg.bank()
            for kc in range(4):
                mm(pg, ph[:, :], oxT[:, kc, t * 128:(t + 1) * 128], wo[:, kc, :], kc == 0, kc == 3, [oxT, so_], ph)
            vs = wk.v[t][:, n * 512:(n + 1) * 512]
            kb.op("dve", lambda e, ph=ph, vs=vs: e.scalar_tensor_tensor(out=vs, in0=vs, scalar=ALPHA, in1=ph[:, :], op0=ALU.mult, op1=ALU.add),
                  reads=[ph], writes=[wk.v[t]])
    kb.barrier()
    for t in range(4):
        r0 = b * BLK + t * 128
        emit_ln(pg, wk.v[t], wk.v[t][:, :], wk.gbc, wk.bbc, wk.st, nm["x3"][0][r0:r0 + 128, :], nm["x3"][1],
                nm["x3T"][0][:, r0:r0 + 128], nm["x3T"][1], wk.xb, wk.xts)


A_OUT = [("x1", [TL, D], F32), ("x1T", [D, TL], BF16), ("QnT", [8, 128, TL], BF16), ("QrT", [8, 64, TL], BF16),
         ("QdT", [8, 128, TL], BF16), ("IqT", [8, 128, TL], BF16), ("iw", [TL, 16], F32), ("gqT", [4, 128, TL], BF16),
         ("gU0", [NT, 128, 1024], F32), ("gA", [NT, 128, 8], F32), ("goi", [TL, 1024], F32), ("gsr", [TL, 1024], BF16),
         ("KnT_o", [8, 128, TL], BF16), ("KrT_o", [64, TL], BF16), ("Vm_o", [TL, 1024], BF16), ("KdT_o", [8, 128, TL], BF16),
         ("Vd_o", [TL, 1024], BF16), ("IkT_o", [64, TL], BF16), ("gUt_o", [NT, 128, 1024], F32), ("gAt_o", [NT, 128, 4], F32)]
A_LOCAL = ["x1", "x1T", "QnT", "QrT", "QdT", "IqT", "iw", "gqT", "gU0", "gA", "goi", "gsr"]
B_GLOBAL = [("KnT_g", [8, 128, SEQ], BF16), ("KrT_g", [64, SEQ], BF16), ("Vm_g", [8, 128, SEQ], BF16), ("KdT_g", [8, 128, SEQ], BF16),
            ("Vd_g", [8, 128, SEQ], BF16), ("IkT_g", [64, SEQ], BF16), ("gUt_g", [SEQ // 128, 128, 1024], F32), ("gAt_g", [128, 512], F32)]
A_W = [("ffn1_g", [D, DFF]), ("ffn1_u", [D, DFF]), ("ffn1_d", [DFF, D]), ("ln_g0", [D]), ("ln_b0", [D]), ("w_in", [D, IN_W]),
       ("w_uq", [512, 1536]), ("w_ukv", [512, 2048]), ("sm", [128, 8]), ("wg2", [16, 512]), ("bg", [1, 512])]
B_W = [("w_br_mla", [1024, D]), ("w_br_dsa", [1024, D]), ("w_br_gla", [1024, D]), ("w_merge", [D, 3 * D]), ("b_merge", [1, 3 * D]),
       ("w_out", [D, D]), ("xa_w_q", [D, 512]), ("xa_w_kv", [D, 1024]), ("xa_w_o", [512, D]), ("mem", [256, D]),
       ("ln_g1", [D]), ("ln_b1", [D]), ("ln_g2", [D]), ("ln_b2", [D]), ("ffn2_g", [D, DFF]), ("ffn2_u", [D, DFF]), ("ffn2_d", [DFF, D]),
       ("ln_g3", [D]), ("ln_b3", [D]), ("gla_norm", [1024])]


def build_program(has_B, has_A, dbg_out=()):
    pg = Prog()
    kb = pg.kb
    nmB, nmA = {}, {}
    if has_B:
        for k in A_LOCAL:
            shape, dt = next((s, d) for (n, s, d) in A_OUT if n == k)
            nmB[k] = (pg.inp("i_" + k, shape, dt), "i_" + k)
        for (k, shape, dt) in B_GLOBAL:
            nmB[k] = (pg.inp(k, shape, dt), k)
        nmB["cmask"] = (pg.inp("cmask", [128, 1024], F32), "cmask")
        for (k, shape) in B_W:
            nmB[k] = (pg.inp("B_" + k, shape, F32), "B_" + k)
        for (k, shape, dt) in (("MnegD", [NT, 128, SEQ], BF16), ("oT_mla", [1024, TL], BF16), ("oT_dsa", [1024, TL], BF16),
                               ("oT_gla", [1024, TL], BF16), ("x2", [TL, D], F32), ("x2T", [D, TL], BF16),
                               ("x3", [TL, D], F32), ("x3T", [D, TL], BF16), ("x4T", [D, TL], BF16)):
            if k in dbg_out:
                nmB[k] = (pg.out(k, shape, dt), k)
            else:
                nmB[k] = (pg.scr(k, shape, dt), k)
        if has_A:
            nmB["x4"] = (pg.out("x4", [TL, D], F32), "x4") if "x4" in dbg_out else (pg.scr("x4", [TL, D], F32), "x4")
        else:
            nmB["x4"] = (pg.out("y", [TL, D], F32), "y")
    if has_A:
        for (k, shape) in A_W:
            nmA[k] = (pg.inp("A_" + k, shape, F32), "A_" + k)
        nmA["pos"] = (pg.inp("pos", [TL], I32), "pos")
        nmA["tabs"] = (pg.scr("tabs", [4, 128, TL], F32), "tabs")
        for (k, shape, dt) in A_OUT:
            nmA[k] = (pg.out("o_" + k, shape, dt), "o_" + k)
        nmA["xT"] = nmA["x1T"]
        if not has_B:
            nmA["x0"] = (pg.inp("x_in", [TL, D], F32), "x_in")
            nmA["x0T"] = (pg.scr("x0T", [D, TL], BF16), "x0T")
        else:
            nmA["x0"] = nmB["x4"]
            nmA["x0T"] = nmB["x4T"]
    if has_B:
        emit_indexer(pg, nmB)
        emit_attention(pg, nmB, "mla")
        emit_attention(pg, nmB, "dsa")
        emit_gla_out(pg, nmB)
    kb.push()
    wk = Work(pg)
    if has_A:
        mp = MixParams(pg, "A")
        mp.load(pg, nmA["sm"][0], nmA["sm"][1], nmA["wg2"][0], nmA["wg2"][1], nmA["bg"][0], nmA["bg"][1])
        emit_rope_tables(pg, wk, nmA["pos"][0], nmA["pos"][1], nmA["tabs"][0], nmA["tabs"][1])
        if not has_B:
            emit_prep_xT(pg, wk, nmA["x0"], nmA["x0T"])
    if has_B:
        xa = XAState(pg, wk, nmB)
        for b in range(NB):
            emit_merge_block(pg, wk, b, nmB)
        for b in range(NB):
            emit_xattn_block(pg, wk, xa, b, nmB)
        for b in range(NB):
            emit_ffn_block(pg, wk, b, nmB["x3"], nmB["x3T"], nmB["ffn2_g"], nmB["ffn2_u"], nmB["ffn2_d"],
                           (nmB["ln_g3"], nmB["ln_b3"]), nmB["x4"], nmB["x4T"])
    if has_A:
        for b in range(NB):
            emit_ffn_block(pg, wk, b, nmA["x0"], nmA["x0T"], nmA["ffn1_g"], nmA["ffn1_u"], nmA["ffn1_d"],
                           (nmA["ln_g0"], nmA["ln_b0"]), nmA["x1"], nmA["x1T"])
        for b in range(NB):
            emit_mixproj_block(pg, wk, mp, b, {"n": nmA})
    kb.pop()
    kb.finish()
    return pg


def _loc(a, c):
    return np.ascontiguousarray(a.reshape(NT, NCORES, 128, *a.shape[1:])[:, c].reshape(TL, *a.shape[1:]))


def _glob_cols(parts):
    lead = parts[0].shape[:-1]
    st = np.stack([p.reshape(*lead, NT, 128) for p in parts], axis=-2)
    return np.ascontiguousarray(st.reshape(*lead, SEQ))


def _glob_rows(parts):
    f = parts[0].shape[1:]
    st = np.stack([p.reshape(NT, 128, *f) for p in parts], axis=1)
    return np.ascontiguousarray(st.reshape(SEQ, *f))


def _vlay(v):
    return np.ascontiguousarray(v.reshape(SEQ // 128, 128, 8, 128).transpose(2, 1, 0, 3).reshape(8, 128, SEQ))


def _a_weights(inp, l):
    sm = np.zeros((128, 8), np.float32)
    sm[:, 0:4] = inp["mla_q_norm"][l].reshape(4, 128).T
    sm[:, 4:8] = inp["mla_kv_norm"][l].reshape(4, 128).T
    return {"A_ffn1_g": inp["ffn_w_gate"][l, 0], "A_ffn1_u": inp["ffn_w_up"][l, 0], "A_ffn1_d": inp["ffn_w_down"][l, 0],
            "A_ln_g0": inp["ln_gain"][l, 0], "A_ln_b0": inp["ln_bias"][l, 0], "A_w_in": inp["w_in"][l],
            "A_w_uq": inp["mla_w_uq"][l], "A_w_ukv": inp["mla_w_ukv"][l], "A_sm": sm,
            "A_wg2": inp["gla_w_gate2"][l], "A_bg": inp["gla_b_gate"][l][None, :]}


def _b_weights(inp, l):
    return {"B_w_br_mla": inp["w_branch_mla"][l], "B_w_br_dsa": inp["w_branch_dsa"][l], "B_w_br_gla": inp["w_branch_gla"][l],
            "B_w_merge": inp["w_merge"][l], "B_b_merge": inp["b_merge"][l][None, :], "B_w_out": inp["w_out"][l],
            "B_xa_w_q": inp["xa_w_q"][l], "B_xa_w_kv": inp["xa_w_kv"][l], "B_xa_w_o": inp["xa_w_o"][l], "B_mem": inp["mem"][0],
            "B_ln_g1": inp["ln_gain"][l, 1], "B_ln_b1": inp["ln_bias"][l, 1], "B_ln_g2": inp["ln_gain"][l, 2], "B_ln_b2": inp["ln_bias"][l, 2],
            "B_ffn2_g": inp["ffn_w_gate"][l, 1], "B_ffn2_u": inp["ffn_w_up"][l, 1], "B_ffn2_d": inp["ffn_w_down"][l, 1],
            "B_ln_g3": inp["ln_gain"][l, 3], "B_ln_b3": inp["ln_bias"][l, 3], "B_gla_norm": inp["gla_norm"][l]}


def _gather(res):
    loc = [{"i_" + k: r["o_" + k] for k in A_LOCAL} for r in res]
    g = {"KnT_g": _glob_cols([r["o_KnT_o"] for r in res]), "KrT_g": _glob_cols([r["o_KrT_o"] for r in res]),
         "KdT_g": _glob_cols([r["o_KdT_o"] for r in res]), "IkT_g": _glob_cols([r["o_IkT_o"] for r in res]),
         "Vm_g": _vlay(_glob_rows([r["o_Vm_o"] for r in res])), "Vd_g": _vlay(_glob_rows([r["o_Vd_o"] for r in res]))}
    ut = np.stack([r["o_gUt_o"] for r in res], axis=1)
    g["gUt_g"] = np.ascontiguousarray(ut.reshape(SEQ // 128, 128, 1024))
    at = np.stack([r["o_gAt_o"] for r in res], axis=1)
    g["gAt_g"] = np.ascontiguousarray(at.reshape(SEQ // 128, 128, 4).transpose(1, 0, 2).reshape(128, 512))
    return loc, g


_PROGS = {}


def _prog(has_B, has_A):
    key = (has_B, has_A)
    if key not in _PROGS:
        _PROGS[key] = build_program(has_B, has_A)
    return _PROGS[key]


def kernel(**inp):
    inp = {k: np.asarray(v) for k, v in inp.items()}
    cores = list(range(NCORES))
    consts = [make_consts(c) for c in cores]
    cmask = [make_cmask(c) for c in cores]
    pos = [_loc(inp["positions"][0].astype(np.int32), c) for c in cores]
    pg = _prog(False, True)
    wa = _a_weights(inp, 0)
    maps = [dict(wa, consts=consts[c], pos=pos[c], x_in=_loc(inp["x"][0], c)) for c in cores]
    res = run_bass_kernel_spmd(pg.nc, maps, core_ids=cores).results
    loc, g = _gather(res)
    pg = _prog(True, True)
    wb, wa = _b_weights(inp, 0), _a_weights(inp, 1)
    maps = [dict(wb, **wa, **g, **loc[c], consts=consts[c], cmask=cmask[c], pos=pos[c]) for c in cores]
    res = run_bass_kernel_spmd(pg.nc, maps, core_ids=cores).results
    loc, g = _gather(res)
    pg = _prog(True, False)
    wb = _b_weights(inp, 1)
    maps = [dict(wb, **g, **loc[c], consts=consts[c], cmask=cmask[c]) for c in cores]
    res = run_bass_kernel_spmd(pg.nc, maps, core_ids=cores).results
    y = _glob_rows([r["y"] for r in res])
    return y[None].astype(np.float32)
```
